# Optimizing a Trainium2 kernel written in Bass

```python
import math
import jax, jax.numpy as jnp
from jax import lax
import numpy as np

D_MODEL = 1024
BATCH = 32
SEQ = 256
DEPTH = 2
DEC_BATCH = 8
DEC_SEQ = 2048
PAST_LEN = 256

GRID_W = 64
RWKV_HEAD = 64
N_RWKV = 4
D_RWKV = N_RWKV * RWKV_HEAD
LORA_W = 64
LORA_A = 64
GN_EPS = 64e-5
MLA_HEADS = 8
MLA_NOPE = 64
MLA_ROPE = 32
MLA_V = 64
D_MLA = MLA_HEADS * MLA_V
Q_RANK = 256
KV_RANK = 128
ROPE_BASE = 10000.0
ROPE_PAIRS_AXIS = MLA_ROPE // 4
SOFTMAX_SCALE = (MLA_NOPE + MLA_ROPE) ** -0.5
Q_BLOCK = 128
LRU_BLOCKS = 4
LRU_BLOCK = 64
D_LRU = LRU_BLOCKS * LRU_BLOCK
CONV_W = 4
CONV_LEFT = 2
LRU_C = 8.0
IN_SIZES = (D_RWKV, D_RWKV, D_RWKV, D_RWKV, Q_RANK, KV_RANK, MLA_ROPE, D_MLA, D_LRU, D_LRU)
D_IN = sum(IN_SIZES)
D_MIX = D_RWKV + D_MLA + D_LRU
ALPHA = (2 * DEPTH) ** 0.25
OUT_INIT = (8 * DEPTH) ** -0.25
LN_EPS = 1e-5

kernel_name = "hybrid_rwkv7_mla_rglru_diffusion_step"


def layer_norm(x, g=None, b=None):
    xf = x.astype(jnp.float32)
    mu = jnp.mean(xf, -1, keepdims=True)
    var = jnp.mean(jnp.square(xf - mu), -1, keepdims=True)
    y = (xf - mu) * lax.rsqrt(var + LN_EPS)
    if g is not None:
        y = y * g.astype(jnp.float32) + b.astype(jnp.float32)
    return y.astype(x.dtype)


def rms_norm(x, g):
    xf = x.astype(jnp.float32)
    y = xf * lax.rsqrt(jnp.mean(xf * xf, -1, keepdims=True) + 1e-6) * g.astype(jnp.float32)
    return y.astype(x.dtype)


def adaln(cvec, w, b):
    m = jax.nn.silu(cvec) @ w + b
    return jnp.split(m[:, None, :], 3, axis=-1)


def split_in(h):
    idx = [int(v) for v in np.cumsum(IN_SIZES)[:-1]]
    return jnp.split(h, idx, axis=-1)


def axial_rope_tables(L):
    rows = L // GRID_W
    row = jnp.repeat(jnp.arange(rows), GRID_W).astype(jnp.float32)
    col = jnp.tile(jnp.arange(GRID_W), rows).astype(jnp.float32)
    inv = ROPE_BASE ** (-jnp.arange(ROPE_PAIRS_AXIS, dtype=jnp.float32) / ROPE_PAIRS_AXIS)
    ang = jnp.concatenate([row[:, None] * inv, col[:, None] * inv], axis=-1)
    return jnp.cos(ang), jnp.sin(ang)


def apply_rope(x, cos, sin):
    xp = x.astype(jnp.float32).reshape(x.shape[:-1] + (MLA_ROPE // 2, 2))
    x1, x2 = xp[..., 0], xp[..., 1]
    y = jnp.stack([x1 * cos - x2 * sin, x1 * sin + x2 * cos], axis=-1)
    return y.reshape(x.shape).astype(x.dtype)


def rwkv_step(S, inp):
    r_t, w_t, k_t, v_t, kk_t, a_t = inp
    sa = jnp.einsum('ebhvk,ebhk->ebhv', S, -kk_t)
    S = (S * w_t[..., None, :] + sa[..., :, None] * (kk_t * a_t)[..., None, :]
         + v_t[..., :, None] * k_t[..., None, :])
    return S, jnp.einsum('ebhvk,ebhk->ebhv', S, r_t)


def bidir(t):
    return jnp.stack([t, jnp.flip(t, 1)])


def flip_second(t):
    return jnp.stack([t[0], jnp.flip(t[1], 1)])


def rwkv_branch(u, r, k, v, g, S0, l, P):
    B, L, _ = r.shape
    f32 = jnp.float32
    heads = lambda t: t.reshape(t.shape[:-1] + (N_RWKV, RWKV_HEAD))
    wl = P['rw_w0'][l][:, None, None, :] + jnp.einsum(
        'eblr,erc->eblc', jnp.tanh(jnp.einsum('bld,edr->eblr', u, P['rw_w1'][l])), P['rw_w2'][l])
    decay = jnp.exp(-jnp.exp(-jax.nn.softplus(-wl.astype(f32)) - 0.5))
    a = jax.nn.sigmoid((P['rw_a0'][l][:, None, None, :] + jnp.einsum(
        'eblr,erc->eblc', jnp.einsum('bld,edr->eblr', u, P['rw_a1'][l]), P['rw_a2'][l])).astype(f32))
    rf, kf, vf = r.astype(f32), k.astype(f32), v.astype(f32)
    kk = heads(kf * P['rw_kk'][l])
    kk = kk / jnp.maximum(jnp.sqrt(jnp.sum(kk * kk, -1, keepdims=True)), 1e-12)
    k_dir = kf[None] * (1.0 + (a - 1.0) * P['rw_ka'][l])
    seq = (bidir(heads(rf)), flip_second(heads(decay)), flip_second(heads(k_dir)),
           bidir(heads(vf)), bidir(kk), flip_second(heads(a)))
    seq = tuple(jnp.moveaxis(t, 2, 0) for t in seq)
    S_fin, ys = lax.scan(rwkv_step, S0.astype(f32), seq)
    ys = jnp.moveaxis(ys, 0, 2)
    y = ys[0] + jnp.flip(ys[1], 1)
    mu = jnp.mean(y, -1, keepdims=True)
    var = jnp.mean(jnp.square(y - mu), -1, keepdims=True)
    yn = (y - mu) * lax.rsqrt(var + GN_EPS)
    bonus = jnp.sum(heads(rf) * heads(kf) * P['rw_rk'][l], -1, keepdims=True) * heads(vf)
    yn = yn.reshape(B, L, D_RWKV) * P['rw_lnx_g'][l] + P['rw_lnx_b'][l]
    y = (yn + bonus.reshape(B, L, D_RWKV)) * jax.nn.silu(g.astype(f32))
    return y.astype(u.dtype), S_fin


def mla_queries(cq, l, P):
    B, L, _ = cq.shape
    q = (rms_norm(cq, P['mla_qnorm'][l]) @ P['mla_wuq'][l]).reshape(B, L, MLA_HEADS, MLA_NOPE + MLA_ROPE)
    return q[..., :MLA_NOPE], q[..., MLA_NOPE:]


def mla_keyvals(ckv_n, l, P):
    B, L, _ = ckv_n.shape
    kv = (ckv_n @ P['mla_wukv'][l]).reshape(B, L, MLA_HEADS, MLA_NOPE + MLA_V)
    return kv[..., :MLA_NOPE], kv[..., MLA_NOPE:]


def block_attention(qn, qr, kn, kr, v):
    B, Lq = qn.shape[:2]
    nb = Lq // Q_BLOCK

    def one_block(args):
        qn_b, qr_b = args
        s = (jnp.einsum('bqhd,bkhd->bhqk', qn_b, kn)
             + jnp.einsum('bqhr,bkr->bhqk', qr_b, kr)).astype(jnp.float32) * SOFTMAX_SCALE
        p = jax.nn.softmax(s, axis=-1).astype(v.dtype)
        return jnp.einsum('bhqk,bkhd->bqhd', p, v)

    to_blocks = lambda t: jnp.moveaxis(t.reshape((B, nb, Q_BLOCK) + t.shape[2:]), 1, 0)
    out = lax.map(one_block, (to_blocks(qn), to_blocks(qr)))
    return jnp.moveaxis(out, 0, 1).reshape(B, Lq, MLA_HEADS * MLA_V)


def centred_conv(x, w, b):
    L = x.shape[1]
    xp = jnp.pad(x, ((0, 0), (CONV_LEFT, CONV_W - 1 - CONV_LEFT), (0, 0)))
    y = b
    for j in range(CONV_W):
        y = y + xp[:, j:j + L] * w[j]
    return y


def lin_combine(p, q):
    a1, b1 = p
    a2, b2 = q
    return a1 * a2, a2 * b1 + b2


def lru_branch(xl, gl, h0, l, P):
    B, L, _ = xl.shape
    f32 = jnp.float32
    xc = centred_conv(xl, P['lru_conv_w'][l], P['lru_conv_b'][l])
    xblk = xc.reshape(B, L, LRU_BLOCKS, LRU_BLOCK)
    gr = jnp.einsum('blgi,egio->eblgo', xblk, P['lru_wa'][l]).reshape(2, B, L, D_LRU) + P['lru_ba'][l][:, None, None, :]
    gi = jnp.einsum('blgi,egio->eblgo', xblk, P['lru_wx'][l]).reshape(2, B, L, D_LRU) + P['lru_bx'][l][:, None, None, :]
    log_a = -LRU_C * jax.nn.sigmoid(gr.astype(f32)) * jax.nn.softplus(-P['lru_lambda'][l].astype(f32))[:, None, None, :]
    a = jnp.exp(log_a)
    bterm = jnp.sqrt(-jnp.expm1(2.0 * log_a)) * jax.nn.sigmoid(gi.astype(f32)) * xc.astype(f32)[None]
    a = flip_second(a)
    bterm = flip_second(bterm)
    bterm = bterm.at[:, :, 0].add(a[:, :, 0] * h0.astype(f32))
    _, h = lax.associative_scan(lin_combine, (a, bterm), axis=2)
    h_fin = h[:, :, -1]
    y = (h[0] + jnp.flip(h[1], 1)) * jax.nn.silu(gl.astype(f32))
    return y.astype(xl.dtype), h_fin


def trunk_layer(x, cvec, l, P, ctx=None):
    B, L, _ = x.shape
    shift, scale, gate = adaln(cvec, P['w_mod'][l], P['b_mod'][l])
    u = layer_norm(x) * (1.0 + scale) + shift
    r, k, v, g, cq, ckv, kr, gm, xl, gl = split_in(u @ P['w_in'][l])
    ckv_n = rms_norm(ckv, P['mla_kvnorm'][l])
    qn, qr = mla_queries(cq, l, P)
    kn, vv = mla_keyvals(ckv_n, l, P)
    if ctx is None:
        S0 = jnp.zeros((2, B, N_RWKV, RWKV_HEAD, RWKV_HEAD), jnp.float32)
        h0 = jnp.zeros((2, B, D_LRU), jnp.float32)
        kr_all = kr
    else:
        S_ctx, ckv_ctx, kr_ctx, h_ctx = ctx
        S0 = jnp.moveaxis(S_ctx, 1, 0)
        h0 = jnp.moveaxis(h_ctx, 1, 0)
        cos, sin = axial_rope_tables(L)
        qr = apply_rope(qr, cos[:, None, :], sin[:, None, :])
        kn_c, v_c = mla_keyvals(ckv_ctx.astype(x.dtype), l, P)
        kn = jnp.concatenate([kn_c, kn], axis=1)
        vv = jnp.concatenate([v_c, vv], axis=1)
        kr_all = jnp.concatenate([kr_ctx.astype(kr.dtype), apply_rope(kr, cos, sin)], axis=1)
    y_rw, S_fin = rwkv_branch(u, r, k, v, g, S0, l, P)
    y_mla = block_attention(qn, qr, kn, kr_all, vv) * jax.nn.silu(gm)
    y_lru, h_fin = lru_branch(xl, gl, h0, l, P)
    out = jnp.concatenate([y_rw, y_mla, y_lru], axis=-1) @ P['w_out'][l]
    x = layer_norm(ALPHA * x + gate * out, P['ln_g'][l], P['ln_b'][l])
    return x, (jnp.moveaxis(S_fin, 0, 1), ckv_n, kr, jnp.moveaxis(h_fin, 0, 1))


def setup_inputs(seed: int = 0) -> dict:
    key = jax.random.key(seed)
    ks = iter(jax.random.split(key, 48))
    D = D_MODEL

    def nrm(shape, s):
        return s * jax.random.normal(next(ks), shape, jnp.float32)

    a_c = jax.random.uniform(next(ks), (DEPTH, 2, D_LRU), jnp.float32, 0.9, 0.999)
    s_lam = a_c ** (1.0 / LRU_C)
    return {
        'x_prompt': nrm((BATCH, SEQ, D), 1.0),
        'x_sample': nrm((DEC_BATCH, DEC_SEQ, D), 1.0),
        'state_rwkv': nrm((DEC_BATCH, DEPTH, 2, N_RWKV, RWKV_HEAD, RWKV_HEAD), 0.1),
        'cache_mla_ckv': nrm((DEC_BATCH, DEPTH, PAST_LEN, KV_RANK), 1.0),
        'cache_mla_krope': nrm((DEC_BATCH, DEPTH, PAST_LEN, MLA_ROPE), 1.0),
        'state_lru': nrm((DEC_BATCH, DEPTH, 2, D_LRU), 0.5),
        'c': nrm((DEC_BATCH, D), 1.0),
        'c_ctx': nrm((D,), 1.0),
        'w_mod': nrm((DEPTH, D, 3 * D), 0.5 * D ** -0.5),
        'b_mod': nrm((DEPTH, 3 * D), 0.02),
        'w_in': nrm((DEPTH, D, D_IN), D ** -0.5),
        'rw_w0': jax.random.uniform(next(ks), (DEPTH, 2, D_RWKV), jnp.float32, -2.0, 2.0),
        'rw_w1': nrm((DEPTH, 2, D, LORA_W), D ** -0.5),
        'rw_w2': nrm((DEPTH, 2, LORA_W, D_RWKV), 0.1 * LORA_W ** -0.5),
        'rw_a0': nrm((DEPTH, 2, D_RWKV), 0.1),
        'rw_a1': nrm((DEPTH, 2, D, LORA_A), D ** -0.5),
        'rw_a2': nrm((DEPTH, 2, LORA_A, D_RWKV), 0.1 * LORA_A ** -0.5),
        'rw_kk': 0.85 + nrm((DEPTH, D_RWKV), 0.02),
        'rw_ka': 1.0 + nrm((DEPTH, D_RWKV), 0.02),
        'rw_rk': nrm((DEPTH, N_RWKV, RWKV_HEAD), 0.1),
        'rw_lnx_g': 1.0 + nrm((DEPTH, D_RWKV), 0.02),
        'rw_lnx_b': nrm((DEPTH, D_RWKV), 0.02),
        'mla_qnorm': 1.0 + nrm((DEPTH, Q_RANK), 0.02),
        'mla_wuq': nrm((DEPTH, Q_RANK, MLA_HEADS * (MLA_NOPE + MLA_ROPE)), Q_RANK ** -0.5),
        'mla_kvnorm': 1.0 + nrm((DEPTH, KV_RANK), 0.02),
        'mla_wukv': nrm((DEPTH, KV_RANK, MLA_HEADS * (MLA_NOPE + MLA_V)), KV_RANK ** -0.5),
        'lru_conv_w': nrm((DEPTH, CONV_W, D_LRU), CONV_W ** -0.5),
        'lru_conv_b': nrm((DEPTH, D_LRU), 0.02),
        'lru_wa': nrm((DEPTH, 2, LRU_BLOCKS, LRU_BLOCK, LRU_BLOCK), LRU_BLOCK ** -0.5),
        'lru_ba': nrm((DEPTH, 2, D_LRU), 0.02),
        'lru_wx': nrm((DEPTH, 2, LRU_BLOCKS, LRU_BLOCK, LRU_BLOCK), LRU_BLOCK ** -0.5),
        'lru_bx': nrm((DEPTH, 2, D_LRU), 0.02),
        'lru_lambda': jnp.log(s_lam) - jnp.log1p(-s_lam),
        'w_out': nrm((DEPTH, D_MIX, D), OUT_INIT * D_MIX ** -0.5),
        'ln_g': 1.0 + nrm((DEPTH, D), 0.02),
        'ln_b': nrm((DEPTH, D), 0.02),
    }


def reference(x_prompt, x_sample, state_rwkv, cache_mla_ckv, cache_mla_krope, state_lru, c,
              c_ctx, w_mod, b_mod, w_in, rw_w0, rw_w1, rw_w2, rw_a0, rw_a1, rw_a2, rw_kk, rw_ka,
              rw_rk, rw_lnx_g, rw_lnx_b, mla_qnorm, mla_wuq, mla_kvnorm, mla_wukv, lru_conv_w,
              lru_conv_b, lru_wa, lru_ba, lru_wx, lru_bx, lru_lambda, w_out, ln_g, ln_b):
    P = dict(w_mod=w_mod, b_mod=b_mod, w_in=w_in, rw_w0=rw_w0, rw_w1=rw_w1, rw_w2=rw_w2,
             rw_a0=rw_a0, rw_a1=rw_a1, rw_a2=rw_a2, rw_kk=rw_kk, rw_ka=rw_ka, rw_rk=rw_rk,
             rw_lnx_g=rw_lnx_g, rw_lnx_b=rw_lnx_b, mla_qnorm=mla_qnorm, mla_wuq=mla_wuq,
             mla_kvnorm=mla_kvnorm, mla_wukv=mla_wukv, lru_conv_w=lru_conv_w,
             lru_conv_b=lru_conv_b, lru_wa=lru_wa, lru_ba=lru_ba, lru_wx=lru_wx, lru_bx=lru_bx,
             lru_lambda=lru_lambda, w_out=w_out, ln_g=ln_g, ln_b=ln_b)
    xp = x_prompt
    rw_states, ckvs, krs, lru_states = [], [], [], []
    for l in range(DEPTH):
        xp, (S_l, ckv_l, kr_l, h_l) = trunk_layer(xp, c_ctx[None, :], l, P)
        rw_states.append(S_l)
        ckvs.append(ckv_l)
        krs.append(kr_l)
        lru_states.append(h_l)
    xs = x_sample
    for l in range(DEPTH):
        xs, _ = trunk_layer(xs, c, l, P, ctx=(state_rwkv[:, l], cache_mla_ckv[:, l],
                                              cache_mla_krope[:, l], state_lru[:, l]))
    new_state_rwkv = jnp.stack(rw_states, axis=1).astype(x_prompt.dtype)
    new_cache_mla_ckv = jnp.stack(ckvs, axis=1).astype(x_prompt.dtype)
    new_cache_mla_krope = jnp.stack(krs, axis=1).astype(x_prompt.dtype)
    new_state_lru = jnp.stack(lru_states, axis=1).astype(x_prompt.dtype)
    return (xp, xs, new_state_rwkv, new_cache_mla_ckv, new_cache_mla_krope, new_state_lru)
```

```python
import numpy as np
from collections import deque
from contextlib import ExitStack
import concourse.bass as bass
import concourse.mybir as mybir
from concourse.bass_utils import run_bass_kernel_spmd

F32 = mybir.dt.float32
BF16 = mybir.dt.bfloat16
ALU = mybir.AluOpType
AF = mybir.ActivationFunctionType

D = 1024
DEPTH = 2
ALPHA = (2 * DEPTH) ** 0.25
LN_EPS = 1e-5
GN_EPS = 64e-5
SM_SCALE = 96.0 ** -0.5
WDEC = float(np.exp(-0.5))
C_R, C_K, C_V, C_G, C_CQ, C_CKV, C_KR, C_GM, C_XL, C_GL = 0, 256, 512, 768, 1024, 1280, 1408, 1440, 1952, 2208

ENGS = ("pe", "act", "dve", "pool", "sp")


class Prog:
    N_DSEM = 40

    def __init__(self, nc, stack, same_engine_sync=True):
        self.nc = nc
        self.ops = {e: [] for e in ENGS}
        self.cnt = {e: 0 for e in ENGS}
        self.esem = {e: stack.enter_context(nc.semaphore("es_" + e)) for e in ENGS}
        self.dsem = [stack.enter_context(nc.semaphore("ds%d" % i)) for i in range(self.N_DSEM)]
        self.dcum = [0] * self.N_DSEM
        self.dnext = 0
        self.know_e = {e: {f: 0 for f in ENGS} for e in ENGS}
        self.know_d = {e: [0] * self.N_DSEM for e in ENGS}
        self.last_w = {}
        self.readers = {}
        self.same_engine_sync = same_engine_sync
        self.ps = [stack.enter_context(nc.psum_tensor("psb%d" % i, [128, 512], F32)) for i in range(8)]
        self.free_banks = deque(range(8))

    def pefull(self, on):
        self.pe_full = on and self.allow_pe_full

    allow_pe_full = ()

    def bank(self):
        assert self.free_banks, "out of PSUM banks"
        return self.free_banks.popleft()

    def release(self, b):
        self.free_banks.append(b)

    def _need(self, eng, ev):
        if ev is None:
            return
        if ev[0] == "c":
            _, f, n = ev
            if self.know_e[eng][f] >= n:
                return
            self.know_e[eng][f] = n
            self.ops[eng].append(("wc", f, n))
        else:
            _, i, v = ev
            if self.know_d[eng][i] >= v:
                return
            self.know_d[eng][i] = v
            self.ops[eng].append(("wd", i, v))

    relax = ("pe", "act", "dve")

    pe_cont = False
    pe_full = False

    def _same(self, eng, ev):
        if eng == "pe":
            return (self.pe_cont or self.pe_full) and ev is not None and ev[0] == "c" and ev[1] == eng
        return (eng in self.relax) and ev is not None and ev[0] == "c" and ev[1] == eng

    def _deps(self, eng, reads, writes):
        for k in reads:
            ev = self.last_w.get(k)
            if self._same(eng, ev) and not self.same_engine_sync:
                continue
            self._need(eng, ev)
        for k in writes:
            ev = self.last_w.get(k)
            if not self._same(eng, ev):
                self._need(eng, ev)
            for ev in self.readers.get(k, ()):
                if not self._same(eng, ev):
                    self._need(eng, ev)

    def _commit(self, ev, reads, writes):
        for k in reads:
            self.readers.setdefault(k, []).append(ev)
        for k in writes:
            self.last_w[k] = ev
            self.readers[k] = []

    def op(self, eng, fn, r=(), w=()):
        self._deps(eng, r, w)
        self.cnt[eng] += 1
        n = self.cnt[eng]
        self.ops[eng].append(("ins", fn, n))
        self._commit(("c", eng, n), r, w)

    def dma(self, eng, out, in_, r=(), w=(), **kw):
        self._deps(eng, r, w)
        i = self.dnext
        self.dnext = (self.dnext + 1) % self.N_DSEM
        if self.dcum[i] > 0:
            self._need(eng, ("d", i, self.dcum[i]))
        self.dcum[i] += 16
        v = self.dcum[i]
        sem = self.dsem[i]
        self.ops[eng].append(
            ("raw", lambda E, sem=sem, out=out, in_=in_, kw=kw: E.dma_start(out=out, in_=in_, **kw).then_inc(sem, 16)))
        self._commit(("d", i, v), r, w)

    def barrier(self):
        for e in ENGS:
            for f in ENGS:
                if self.cnt[f] > 0:
                    self._need(e, ("c", f, self.cnt[f]))
            for i in range(self.N_DSEM):
                if self.dcum[i] > 0:
                    self._need(e, ("d", i, self.dcum[i]))
        self.last_w = {}
        self.readers = {}

    def emit(self):
        self.barrier()
        ops = self.ops
        marked = {e: set() for e in ENGS}
        for e in ENGS:
            for rec in ops[e]:
                if rec[0] == "wc":
                    marked[rec[1]].add(rec[2])
        val = {}
        for e in ENGS:
            c = 0
            for n in sorted(marked[e]):
                c += 1
                val[(e, n)] = c
            assert c < 60000, (e, c)
        esem, dsem = self.esem, self.dsem

        def run(E, e):
            sem = esem[e]
            for rec in ops[e]:
                k = rec[0]
                if k == "ins":
                    ins = rec[1](E)
                    if rec[2] in marked[e]:
                        ins.then_inc(sem, 1)
                elif k == "wc":
                    E.wait_ge(esem[rec[1]], val[(rec[1], rec[2])])
                elif k == "wd":
                    E.wait_ge(dsem[rec[1]], rec[2])
                else:
                    rec[1](E)

        with self.nc.Block() as block:
            @block.tensor
            def _(E):
                run(E, "pe")

            @block.scalar
            def _(E):
                run(E, "act")

            @block.vector
            def _(E):
                run(E, "dve")

            @block.gpsimd
            def _(E):
                run(E, "pool")

            @block.sync
            def _(E):
                run(E, "sp")

    def mm(self, out, lhsT, rhs, start=True, stop=True, r=(), w=()):
        self.pe_cont = (not start) and ("pe" in self.relax)
        self.op("pe", lambda E: E.matmul(out, lhsT, rhs, start=start, stop=stop), r, w)
        self.pe_cont = False

    def act(self, out, in_, func, r=(), w=(), bias=None, scale=None, accum=None):
        kw = {}
        if bias is not None:
            kw["bias"] = bias
        if scale is not None:
            kw["scale"] = scale
        if accum is not None:
            kw["accum_out"] = accum
        self.op("act", lambda E: E.activation(out, in_, func, **kw), r, w)

    def cp(self, eng, out, in_, r=(), w=()):
        if eng == "act":
            self.op("act", lambda E: E.copy(out, in_), r, w)
        else:
            self.op(eng, lambda E: E.tensor_copy(out, in_), r, w)

    def ts(self, eng, out, in0, s1, s2, op0, op1, r=(), w=()):
        self.op(eng, lambda E: E.tensor_scalar(out, in0, s1, s2, op0, op1), r, w)

    def ts1(self, eng, out, in0, s1, op0, r=(), w=()):
        self.op(eng, lambda E: E.tensor_single_scalar(out, in0, s1, op0), r, w)

    def tt(self, eng, out, in0, in1, op, r=(), w=()):
        self.op(eng, lambda E: E.tensor_tensor(out, in0, in1, op), r, w)

    def stt(self, eng, out, in0, scalar, in1, op0, op1, r=(), w=()):
        self.op(eng, lambda E: E.scalar_tensor_tensor(out, in0, scalar, in1, op0, op1), r, w)

    def rsqrt(self, out, in_, eps, scale, tx, tt_, r=(), w=(), kx=None, kt=None, small=False):
        self.act(out, in_, AF.Ln, bias=eps, scale=scale, r=r, w=w)
        self.act(out, out, AF.Exp, scale=-0.5, r=w, w=w)

    def memset(self, eng, ap, val, w=()):
        self.op(eng, lambda E: E.memset(ap, val), (), w)


class Group:
    def __init__(self, name, nseq, L, j, ctx):
        self.name, self.nseq, self.L, self.j, self.ctx = name, nseq, L, j, ctx
        self.T = nseq * L
        self.NT = self.T // 128
        self.LT = L // 128
        self.BW = min(512, L)
        self.CB = self.BW // 128
        self.NB = L // self.BW


def build(dbg=None):
    dbg = dbg or {}
    stop = dbg.get("stop", "")
    groups_sel = dbg.get("groups", "SP")
    nlayers = dbg.get("nlayers", DEPTH)
    nc = bass.Bass("TRN2", target_bir_lowering=False)

    def din(name, shape):
        return nc.dram_tensor(name, list(shape), F32, kind="ExternalInput").ap()

    def dout(name, shape):
        return nc.dram_tensor(name, list(shape), F32, kind="ExternalOutput").ap()

    xs = din("xs", [2048, D]); xp = din("xp", [1024, D])
    st_rwkv = din("st_rwkv", [2, 2, 4, 64, 64]); c_ckv = din("c_ckv", [2, 256, 128]); c_kr = din("c_kr", [2, 256, 32])
    st_lru = din("st_lru", [2, 2, 256]); cvec = din("cvec", [2, D])
    w_mod = din("w_mod", [2, D, 3 * D]); b_mod = din("b_mod", [2, 3 * D]); w_in = din("w_in", [2, D, 2464])
    rw_w0 = din("rw_w0", [2, 2, 256]); rw_w1 = din("rw_w1", [2, 2, D, 64]); rw_w2 = din("rw_w2", [2, 2, 64, 256])
    rw_a0 = din("rw_a0", [2, 2, 256]); rw_a1 = din("rw_a1", [2, 2, D, 64]); rw_a2 = din("rw_a2", [2, 2, 64, 256])
    rw_kk = din("rw_kk", [2, 256]); rw_ka = din("rw_ka", [2, 256]); rw_rk = din("rw_rk", [2, 256])
    rw_lnx_g = din("rw_lnx_g", [2, 256]); rw_lnx_b = din("rw_lnx_b", [2, 256])
    mla_qnorm = din("mla_qnorm", [2, 256]); mla_wuq = din("mla_wuq", [2, 256, 768])
    mla_kvnorm = din("mla_kvnorm", [2, 128]); mla_wukv = din("mla_wukv", [2, 128, 1024])
    lru_conv_w = din("lru_conv_w", [2, 4, 256]); lru_conv_b = din("lru_conv_b", [2, 256])
    lru_wa = din("lru_wa", [2, 2, 4, 64, 64]); lru_ba = din("lru_ba", [2, 2, 256])
    lru_wx = din("lru_wx", [2, 2, 4, 64, 64]); lru_bx = din("lru_bx", [2, 2, 256]); lru_lambda = din("lru_lambda", [2, 2, 256])
    w_out = din("w_out", [2, D, D]); ln_g = din("ln_g", [2, D]); ln_b = din("ln_b", [2, D])
    k_ident = din("k_ident", [128, 128]); k_jrev = din("k_jrev", [128, 128])
    k_mask1 = din("k_mask1", [128, 512]); k_mask2 = din("k_mask2", [128, 512]); k_mask3 = din("k_mask3", [128, 512])
    k_bones = din("k_bones", [128, 128]); k_cos = din("k_cos", [32, 2048]); k_sin = din("k_sin", [32, 2048])

    y_s = dout("y_s", [2048, D]); y_p = dout("y_p", [1024, D])
    o_rwkv = dout("o_rwkv", [4, 2, 2, 4, 64, 64]); o_ckv = dout("o_ckv", [4, 2, 256, 128])
    o_kr = dout("o_kr", [4, 2, 256, 32]); o_lru = dout("o_lru", [4, 2, 2, 256])
    xmid = {"S": nc.dram_tensor("xmid_s", [2048, D], F32).ap(), "P": nc.dram_tensor("xmid_p", [1024, D], F32).ap()}

    dumps = []

    with ExitStack() as st:
        P = Prog(nc, st, same_engine_sync=dbg.get("ses", True))
        P.relax = dbg.get("relax", Prog.relax)
        P.allow_pe_full = dbg.get("pe_full", Prog.allow_pe_full)
        ps = P.ps

        uniq = [0]

        def sb(stack, name, shape, dt):
            uniq[0] += 1
            return stack.enter_context(nc.sbuf_tensor("%s_%d" % (name, uniq[0]), list(shape), dt))

        def dump(name, ap, shape, rkeys):
            if name not in dbg.get("dump", ()):
                return
            o = dout("dbg_" + name, shape)
            P.dma("pool", o, ap, r=rkeys, allow_slow_non_contiguous=True)
            dumps.append(name)

        identf = sb(st, "identf", [128, 128], F32); jrevf = sb(st, "jrevf", [128, 128], F32)
        identb = sb(st, "identb", [128, 128], BF16); jrevb = sb(st, "jrevb", [128, 128], BF16)
        mask1 = sb(st, "mask1", [128, 512], BF16); mask2 = sb(st, "mask2", [128, 512], BF16); mask3 = sb(st, "mask3", [128, 512], BF16)
        bones = sb(st, "bones", [128, 128], F32); onesf = sb(st, "onesf", [128, 128], F32)
        P.dma("sp", identf[:], k_ident, w=["identf"]); P.dma("sp", jrevf[:], k_jrev, w=["jrevf"])
        P.dma("pool", identb[:], k_ident, w=["identb"]); P.dma("pool", jrevb[:], k_jrev, w=["jrevb"])
        P.dma("pool", mask1[:], k_mask1, w=["mask1"]); P.dma("pool", mask2[:], k_mask2, w=["mask2"]); P.dma("pool", mask3[:], k_mask3, w=["mask3"])
        P.dma("sp", bones[:], k_bones, w=["bones"])
        P.memset("dve", onesf[:], 1.0, w=["onesf"])
        mod = [sb(st, "mod%d" % l, [128, 24, 2], F32) for l in range(2)]
        sc1 = [sb(st, "sc1_%d" % l, [128, 8, 2], F32) for l in range(2)]
        gta = [sb(st, "gta%d" % l, [128, 8, 2], F32) for l in range(2)]

        with ExitStack() as ph:
            cv = sb(ph, "cv", [128, 8, 2], F32); scv = sb(ph, "scv", [128, 8, 2], F32)
            bm = sb(ph, "bm", [128, 24], F32)
            wm = [sb(ph, "wm%d" % i, [128, 8, 512], F32) for i in range(2)]
            for j in range(2):
                P.dma("sp", cv[:, :, j], cvec[j].rearrange("(kc p) -> p kc", p=128), w=["cv"], allow_slow_non_contiguous=True)
            P.act(scv[:], cv[:], AF.Silu, r=["cv"], w=["scv"])
            modrow = sb(ph, "modrow", [2, 3 * D], F32); bmrow = sb(ph, "bmrow", [2, 3 * D], F32)
            for l in range(2):
                P.dma("sp", bmrow[:], b_mod[l:l + 1, :].to_broadcast([2, 3 * D]), w=["bmrow"])
                for ob in range(6):
                    wmt = wm[ob % 2]; wk = "wm%d" % (ob % 2)
                    P.dma("sp", wmt[:], w_mod[l, :, ob * 512:(ob + 1) * 512].rearrange("(kc p) n -> p kc n", p=128), w=[wk])
                    b = P.bank(); bk = "ps%d" % b
                    for kc in range(8):
                        P.mm(ps[b][0:2, :], scv[:, kc, :], wmt[:, kc, :], start=(kc == 0), stop=(kc == 7), r=[wk, "scv"], w=[bk])
                    P.tt("dve", modrow[:, ob * 512:(ob + 1) * 512], ps[b][0:2, :], bmrow[:, ob * 512:(ob + 1) * 512], ALU.add,
                         r=[bk, "bmrow"], w=["modrow"])
                    P.release(b)
                b = P.bank(); bk = "ps%d" % b
                for oc in range(24):
                    P.mm(ps[b][:, oc * 2:oc * 2 + 2], modrow[0:2, oc * 128:(oc + 1) * 128], identf[0:2, 0:2], r=["modrow", "identf"], w=[bk])
                P.cp("dve", mod[l][:], ps[b][:, 0:48].rearrange("p (a b) -> p a b", b=2), r=[bk], w=["mod%d" % l])
                P.release(b)
                P.ts1("dve", sc1[l][:], mod[l][:, 8:16, :], 1.0, ALU.add, r=["mod%d" % l], w=["sc1_%d" % l])
                P.ts1("dve", gta[l][:], mod[l][:, 16:24, :], 1.0 / ALPHA, ALU.mult, r=["mod%d" % l], w=["gta%d" % l])
            dump("mod0", mod[0][:], [128, 24, 2], ["mod0"])
            P.barrier()
        if stop == "0":
            P.emit()
            return nc, dumps

        groups = []
        if "S" in groups_sel:
            groups.append(Group("S", 1, 2048, 0, True))
        if "P" in groups_sel:
            groups.append(Group("P", 4, 256, 1, False))
        xin0 = {"S": xs, "P": xp}
        xfin = {"S": y_s, "P": y_p}
        env = dict(nc=nc, P=P, sb=sb, dump=dump, stop=stop, identf=identf, jrevf=jrevf, identb=identb, jrevb=jrevb,
                   mask1=mask1, mask2=mask2, mask3=mask3, bones=bones, onesf=onesf, k_cos=k_cos, k_sin=k_sin,
                   mod=mod, sc1=sc1, gta=gta, dbg=dbg,
                   W=dict(w_in=w_in, rw_w0=rw_w0, rw_w1=rw_w1, rw_w2=rw_w2, rw_a0=rw_a0, rw_a1=rw_a1, rw_a2=rw_a2,
                          rw_kk=rw_kk, rw_ka=rw_ka, rw_rk=rw_rk, rw_lnx_g=rw_lnx_g, rw_lnx_b=rw_lnx_b,
                          mla_qnorm=mla_qnorm, mla_wuq=mla_wuq, mla_kvnorm=mla_kvnorm, mla_wukv=mla_wukv,
                          lru_conv_w=lru_conv_w, lru_conv_b=lru_conv_b, lru_wa=lru_wa, lru_ba=lru_ba, lru_wx=lru_wx,
                          lru_bx=lru_bx, lru_lambda=lru_lambda, w_out=w_out, ln_g=ln_g, ln_b=ln_b),
                   st_rwkv=st_rwkv, c_ckv=c_ckv, c_kr=c_kr, st_lru=st_lru,
                   o_rwkv=o_rwkv, o_ckv=o_ckv, o_kr=o_kr, o_lru=o_lru)
        for g in groups:
            for l in range(nlayers):
                x_in = xin0[g.name] if l == 0 else xmid[g.name]
                x_out = xfin[g.name] if l == nlayers - 1 else xmid[g.name]
                done = layer(env, g, l, x_in, x_out, "xin%s%d" % (g.name, l), "xin%s%d" % (g.name, l + 1))
                if done:
                    P.emit()
                    return nc, dumps
        P.emit()
    return nc, dumps


def layer(env, g, l, x_in, x_out, kin, kout):
    nc, P, sb, dump, stop = env["nc"], env["P"], env["sb"], env["dump"], env["stop"]
    ps = P.ps
    W = env["W"]
    identb, jrevb, identf, jrevf = env["identb"], env["jrevb"], env["identf"], env["jrevf"]
    T, NT, LT, L, j = g.T, g.NT, g.LT, g.L, g.j
    tag = "%s%d" % (g.name, l)
    w_in = W["w_in"]

    def wslice(c0, c1):
        return w_in[l, :, c0:c1].rearrange("(kc p) n -> p kc n", p=128)

    def ppar(stack, name, src):
        t = sb(stack, name, [128, 2], F32)
        P.dma("sp", t[:], src.rearrange("(fc p) -> p fc", p=128), w=[name], allow_slow_non_contiguous=True)
        return t

    with ExitStack() as lay:
        u = sb(lay, "u", [128, 8, T], BF16)
        y_rw = sb(lay, "y_rw", [128, 2, T], BF16)
        y_lru = sb(lay, "y_lru", [128, 2, T], BF16)
        with ExitStack() as revs:
            ur = sb(revs, "ur", [128, 8, T], BF16)
            with ExitStack() as ph:
                NBUF = 3
                xb = [sb(ph, "xa%d" % i, [128, D], F32) for i in range(NBUF)]
                xn = [sb(ph, "xn%d" % i, [128, D], BF16) for i in range(NBUF)]
                st6 = sb(ph, "st6", [128, 12], F32); mv = sb(ph, "mv", [128, 2], F32); rs = sb(ph, "rs", [128, 1], F32)
                rq1 = sb(ph, "rq1", [128, 1], F32); rq2 = sb(ph, "rq2", [128, 1], F32)

                def stage1(tt):
                    xt = xb[tt % NBUF]; xk = "xa%d" % (tt % NBUF); xnt = xn[tt % NBUF]; xnk = "xn%d" % (tt % NBUF)
                    P.dma("sp", xt[:], x_in[tt * 128:(tt + 1) * 128, :], r=["%s:%d" % (kin, tt)], w=[xk])
                    P.op("dve", lambda E, xt=xt: E.bn_stats(st6[:, 0:6], xt[:, 0:512]), r=[xk], w=["st6a"])
                    P.op("dve", lambda E, xt=xt: E.bn_stats(st6[:, 6:12], xt[:, 512:1024]), r=[xk], w=["st6b"])
                    P.op("dve", lambda E: E.bn_aggr(mv[:], st6[:]), r=["st6a", "st6b"], w=["mv"])
                    P.rsqrt(rs[:], mv[:, 1:2], LN_EPS, 1.0, rq1[:], rq2[:], r=["mv"], w=["rs"], kx="rq1", kt="rq2", small=True)
                    P.ts("dve", xnt[:], xt[:], mv[:, 0:1], rs[:], ALU.subtract, ALU.mult, r=[xk, "mv", "rs"], w=[xnk])

                def stage2(tt):
                    s_, i = divmod(tt, LT)
                    rt = s_ * LT + (LT - 1 - i)
                    xnt = xn[tt % NBUF]; xnk = "xn%d" % (tt % NBUF)
                    for (dst, dk, perm, pk, tpos) in ((u, "u", identb, "identb", tt), (ur, "ur", jrevb, "jrevb", rt)):
                        for half in range(2):
                            b = P.bank(); bk = "ps%d" % b
                            for q in range(4):
                                kc = half * 4 + q
                                P.mm(ps[b][:, q * 128:(q + 1) * 128], xnt[:, kc * 128:(kc + 1) * 128], perm[:], r=[xnk, pk], w=[bk])
                            for q in range(4):
                                kc = half * 4 + q
                                if dk == "u":
                                    P.act(dst[:, kc, tpos * 128:(tpos + 1) * 128], ps[b][:, q * 128:(q + 1) * 128], AF.Identity,
                                          scale=env["sc1"][l][:, kc, j:j + 1], bias=env["mod"][l][:, kc, j:j + 1],
                                          r=[bk, "sc1_%d" % l, "mod%d" % l], w=["%s:%d" % (dk, tpos)])
                                else:
                                    P.ts("dve", dst[:, kc, tpos * 128:(tpos + 1) * 128], ps[b][:, q * 128:(q + 1) * 128],
                                         env["sc1"][l][:, kc, j:j + 1], env["mod"][l][:, kc, j:j + 1], ALU.mult, ALU.add,
                                         r=[bk, "sc1_%d" % l, "mod%d" % l], w=["%s:%d" % (dk, tpos)])
                            P.release(b)

                for tt in range(NT + 1):
                    if tt < NT:
                        stage1(tt)
                    if tt >= 1:
                        stage2(tt - 1)
                dump("u_" + tag, u[:], [128, 8, T], ["u:%d" % t for t in range(NT)])
                dump("ur_" + tag, ur[:], [128, 8, T], ["ur:%d" % t for t in range(NT)])
                P.barrier()
            if stop == "A":
                return True
            ukeys = ["u:%d" % t for t in range(NT)]
            urkeys = ["ur:%d" % t for t in range(NT)]
            skip = env["dbg"].get("skip", "")
            threaded = (g.nseq > 1) and skip == "" and stop in ("", "E") and env["dbg"].get("threads", True)
            if threaded:
                y_mla = sb(revs, "y_mla", [128, 4, T], BF16)
                with ExitStack() as shared:
                    tw = env["dbg"].get("tw", (6, 1, 1))
                    run_gen(interleave_w([(phase_rwkv(env, g, l, u, ur, y_rw, ukeys, urkeys, wslice, ppar, tag, threaded=True, ext=shared), tw[0]),
                                          (phase_lru(env, g, l, u, ur, y_lru, ukeys, urkeys, wslice, ppar, tag, threaded=True, ext=shared), tw[1]),
                                          (phase_mla(env, g, l, u, y_mla, ukeys, wslice, ppar, tag, threaded=True, ext=shared), tw[2])]))
                    P.barrier()
                phase_out(env, g, l, (y_rw, y_mla, y_lru), x_in, x_out, kin, kout, tag)
                return stop == "E"
            if "B" not in skip:
                run_gen(phase_rwkv(env, g, l, u, ur, y_rw, ukeys, urkeys, wslice, ppar, tag))
            if stop == "B":
                return True
            if "C" not in skip:
                run_gen(phase_lru(env, g, l, u, ur, y_lru, ukeys, urkeys, wslice, ppar, tag))
            if stop == "C":
                return True
        y_mla = sb(lay, "y_mla", [128, 4, T], BF16)
        if "D" not in skip:
            run_gen(phase_mla(env, g, l, u, y_mla, ukeys, wslice, ppar, tag))
        if stop == "D":
            return True
        phase_out(env, g, l, (y_rw, y_mla, y_lru), x_in, x_out, kin, kout, tag)
        if stop == "E":
            return True
    return False


def _unpack(env):
    return env["nc"], env["P"], env["sb"], env["dump"], env["W"]


def proj_fm(P, out, w, c0, M, src, t0, N, r, wk):
    for kc in range(8):
        P.mm(out, w[:, kc, c0:c0 + M], src[:, kc, t0:t0 + N], start=(kc == 0), stop=(kc == 7), r=r, w=[wk])


def phase_out(env, g, l, y, x_in, x_out, kin, kout, tag):
    nc, P, sb, dump, W = _unpack(env)
    ps = P.ps
    identf = env["identf"]
    NT, j = g.NT, g.j
    ykeys = ["y:%d" % c for c in range(8)]
    with ExitStack() as ph:
        w_o = sb(ph, "w_o", [128, 8, D], BF16)
        P.dma("pool", w_o[:], W["w_out"][l].rearrange("(kc p) n -> p kc n", p=128), w=["w_o"])
        gate_bc = sb(ph, "gate_bc", [128, D], F32); g_bc = sb(ph, "g_bc", [128, D], F32); b_bc = sb(ph, "b_bc", [128, D], F32)
        P.dma("sp", g_bc[:], W["ln_g"][l:l + 1, :].to_broadcast([128, D]), w=["g_bc"])
        P.dma("sp", b_bc[:], W["ln_b"][l:l + 1, :].to_broadcast([128, D]), w=["b_bc"])
        for half in range(2):
            b = P.bank(); bk = "ps%d" % b
            for q in range(4):
                kc = half * 4 + q
                P.mm(ps[b][:, q * 128:(q + 1) * 128], env["gta"][l][:, kc, j:j + 1].to_broadcast([128, 128]), identf[:],
                     r=["gta%d" % l, "identf"], w=[bk])
            P.cp("act", gate_bc[:, half * 512:(half + 1) * 512], ps[b][:], r=[bk], w=["gate_bc"])
            P.release(b)
        xb = [sb(ph, "xo%d" % i, [128, D], F32) for i in range(3)]
        t1 = [sb(ph, "t1o%d" % i, [128, D], F32) for i in range(3)]
        st6 = sb(ph, "st6o", [128, 12], F32); mv = sb(ph, "mvo", [128, 2], F32); rs = sb(ph, "rso", [128, 1], F32)
        nmr = sb(ph, "nmr", [128, 1], F32); rq1 = sb(ph, "rq1o", [128, 1], F32); rq2 = sb(ph, "rq2o", [128, 1], F32)
        def stage_a(tt):
            xt = xb[tt % 3]; xk = "xo%d" % (tt % 3); tt1 = t1[tt % 3]; tk = "t1o%d" % (tt % 3)
            P.dma("sp", xt[:], x_in[tt * 128:(tt + 1) * 128, :], r=["%s:%d" % (kin, tt)], w=[xk])
            bs = [P.bank(), P.bank()]
            for half in range(2):
                for kc in range(8):
                    yt_, yc_ = (y[0], kc) if kc < 2 else ((y[1], kc - 2) if kc < 6 else (y[2], kc - 6))
                    P.mm(ps[bs[half]][:, :], yt_[:, yc_, tt * 128:(tt + 1) * 128], w_o[:, kc, half * 512:(half + 1) * 512],
                         start=(kc == 0), stop=(kc == 7), r=[ykeys[kc], "w_o"], w=["ps%d" % bs[half]])
            for half in range(2):
                P.tt("dve", tt1[:, half * 512:(half + 1) * 512], ps[bs[half]][:, :], gate_bc[:, half * 512:(half + 1) * 512], ALU.mult,
                     r=["ps%d" % bs[half], "gate_bc"], w=[tk])
                P.release(bs[half])
            P.tt("pool", tt1[:, 0:512], tt1[:, 0:512], xt[:, 0:512], ALU.add, r=[tk, xk], w=[tk + "L"])
            P.tt("dve", tt1[:, 512:1024], tt1[:, 512:1024], xt[:, 512:1024], ALU.add, r=[tk, xk], w=[tk + "R"])

        def stage_a2(tt):
            tt1 = t1[tt % 3]; tk = "t1o%d" % (tt % 3)
            P.op("dve", lambda E, a=tt1: E.bn_stats(st6[:, 6:12], a[:, 512:1024]), r=[tk, tk + "R"], w=["st6ob"])
            P.op("dve", lambda E, a=tt1: E.bn_stats(st6[:, 0:6], a[:, 0:512]), r=[tk, tk + "L"], w=["st6oa"])
            P.op("dve", lambda E: E.bn_aggr(mv[:], st6[:]), r=["st6oa", "st6ob"], w=["mvo"])
            P.rsqrt(rs[:], mv[:, 1:2], LN_EPS / (ALPHA * ALPHA), 1.0, rq1[:], rq2[:], r=["mvo"], w=["rso"], kx="rq1o", kt="rq2o", small=True)
            P.stt("dve", nmr[:], mv[:, 0:1], -1.0, rs[:], ALU.mult, ALU.mult, r=["mvo", "rso"], w=["nmr"])
            P.act(tt1[:], tt1[:], AF.Identity, scale=rs[:, 0:1], bias=nmr[:, 0:1], r=[tk, tk + "L", tk + "R", "rso", "nmr"], w=[tk, tk + "L", tk + "R"])

        def stage_b(tt):
            tt1 = t1[tt % 3]; tk = "t1o%d" % (tt % 3)
            P.tt("dve", tt1[:], tt1[:], g_bc[:], ALU.mult, r=[tk, "g_bc"], w=[tk])
            P.tt("pool", tt1[:], tt1[:], b_bc[:], ALU.add, r=[tk, "b_bc"], w=[tk])
            P.dma("pool", x_out[tt * 128:(tt + 1) * 128, :], tt1[:], r=[tk], w=["%s:%d" % (kout, tt)])

        for tt in range(NT + 1):
            if tt < NT:
                stage_a(tt)
            if tt >= 1:
                stage_b(tt - 1)
            if tt < NT:
                stage_a2(tt)
        P.barrier()


def phase_lru(env, g, l, u, ur, y, ukeys, urkeys, wslice, ppar, tag, threaded=False, ext=None):
    nc, P, sb, dump, W = _unpack(env)
    ps = P.ps
    identf, jrevf = env["identf"], env["jrevf"]
    L, LT, BW, NB, CB = g.L, g.LT, g.BW, g.NB, g.CB
    NS = "NSCAN"
    with (_NoClose(ext) if ext is not None else ExitStack()) as ph:
        w_xl = sb(ph, "w_xl", [128, 8, 256], BF16); w_gl = sb(ph, "w_gl", [128, 8, 256], BF16)
        P.dma("pool", w_xl[:], wslice(C_XL, C_XL + 256), w=["w_xl"])
        P.dma("pool", w_gl[:], wslice(C_GL, C_GL + 256), w=["w_gl"])
        cw = sb(ph, "cw", [128, 2, 4], F32)
        for jj in range(4):
            P.dma("sp", cw[:, :, jj], W["lru_conv_w"][l, jj].rearrange("(fc p) -> p fc", p=128), w=["cw"], allow_slow_non_contiguous=True)
        cbias = ppar(ph, "cbias", W["lru_conv_b"][l])
        Wbd = {}
        for e in range(2):
            for nm, src in (("a", W["lru_wa"]), ("x", W["lru_wx"])):
                t = sb(ph, "wbd%s%d" % (nm, e), [128, 2, 128], BF16)
                k = "wbd%s%d" % (nm, e)
                P.memset("pool", t[:], 0.0, w=[k])
                for gi in range(4):
                    fc, hh = gi // 2, gi % 2
                    P.dma("pool", t[hh * 64:(hh + 1) * 64, fc, hh * 64:(hh + 1) * 64], src[l, e, gi], w=[k])
                Wbd[(e, nm)] = (t, k)
        ba = [ppar(ph, "lba%d" % e, W["lru_ba"][l, e]) for e in range(2)]
        bx = [ppar(ph, "lbx%d" % e, W["lru_bx"][l, e]) for e in range(2)]
        lam = [ppar(ph, "lam%d" % e, W["lru_lambda"][l, e]) for e in range(2)]
        hba = [sb(ph, "hba%d" % e, [128, 2], F32) for e in range(2)]
        hbx = [sb(ph, "hbx%d" % e, [128, 2], F32) for e in range(2)]
        for e in range(2):
            P.ts1("dve", hba[e][:], ba[e][:], 0.5, ALU.mult, r=["lba%d" % e], w=["hba%d" % e])
            P.ts1("dve", hbx[e][:], bx[e][:], 0.5, ALU.mult, r=["lbx%d" % e], w=["hbx%d" % e])
        cneg = [sb(ph, "cneg%d" % e, [128, 2], F32) for e in range(2)]
        hcn = [sb(ph, "hcn%d" % e, [128, 2], F32) for e in range(2)]
        for e in range(2):
            k = "cneg%d" % e
            P.act(cneg[e][:], lam[e][:], AF.Exp, scale=-1.0, r=["lam%d" % e], w=[k])
            P.act(cneg[e][:], cneg[e][:], AF.Ln, bias=1.0, r=[k], w=[k])
            P.ts1("dve", cneg[e][:], cneg[e][:], -8.0, ALU.mult, r=[k], w=[k])
            P.ts1("dve", hcn[e][:], cneg[e][:], 0.5, ALU.mult, r=[k], w=["hcn%d" % e])
        h0 = None
        if g.ctx:
            h0 = [ppar(ph, "lh0%d" % e, env["st_lru"][l, e]) for e in range(2)]
        xlp = sb(ph, "xlp", [128, 2, L + 4], F32); xc = sb(ph, "xc", [128, 2, L], F32); xcb = sb(ph, "xcb", [128, 2, L], BF16)
        hh_ = [sb(ph, "lh%d" % e, [128, 2, L], F32) for e in range(2)]
        NT2 = 2 if g.nseq == 1 else 1
        scr = []
        for i in range(NT2):
            scr.append({nm: sb(ph, "l%s%d" % (nm, i), [128, BW], F32) for nm in ("sr", "si", "aa", "a2", "bt", "hT", "sg", "tsum")})
        git = [0, 0]
        P.memset("dve", xlp[:, :, 0:2], 0.0, w=["xlp"])
        P.memset("dve", xlp[:, :, L + 2:L + 4], 0.0, w=["xlp"])
        for s in range(g.nseq):
            tok0 = s * L
            for e in range(2):
                src, skeys = (u, ukeys) if e == 0 else (ur, urkeys)
                hk = "lh%d" % e
                for blk in range(NB):
                    for fc in range(2):
                        yield from need(P, 1)
                        b = P.bank(); bk = "ps%d" % b
                        proj_fm(P, ps[b][:, 0:BW], w_xl, fc * 128, 128, src, tok0 + blk * BW, BW, ["w_xl"] + skeys, bk)
                        P.cp("act", xlp[:, fc, 2 + blk * BW:2 + (blk + 1) * BW], ps[b][:, 0:BW], r=[bk], w=["xlp"])
                        P.release(b)
                        yield
                dd = [jj - 2 for jj in range(4)] if e == 0 else [2 - jj for jj in range(4)]
                for fc in range(2):
                    P.ts("dve", xc[:, fc, :], xlp[:, fc, 2 + dd[0]:2 + dd[0] + L], cw[:, fc, 0:1], cbias[:, fc:fc + 1], ALU.mult, ALU.add,
                         r=["xlp", "cw", "cbias"], w=["xc"])
                    for jj in range(1, 4):
                        P.stt("dve", xc[:, fc, :], xlp[:, fc, 2 + dd[jj]:2 + dd[jj] + L], cw[:, fc, jj:jj + 1], xc[:, fc, :], ALU.mult, ALU.add,
                              r=["xlp", "cw", "xc"], w=["xc"])
                P.cp("act", xcb[:, 0, :], xc[:, 0, :], r=["xc"], w=["xcb"])
                P.cp("dve", xcb[:, 1, :], xc[:, 1, :], r=["xc"], w=["xcb"])
                yield
                wa_t, wa_k = Wbd[(e, "a")]; wx_t, wx_k = Wbd[(e, "x")]

                def gate_s1(fc, blk, S_, sx):
                    sl = slice(blk * BW, (blk + 1) * BW)
                    sr, si, aa, a2 = S_["sr"], S_["si"], S_["aa"], S_["a2"]
                    b1 = P.bank(); b2 = P.bank()
                    P.mm(ps[b1][:, 0:BW], wa_t[:, fc, :], xcb[:, fc, sl], r=[wa_k, "xcb"], w=["ps%d" % b1])
                    P.mm(ps[b2][:, 0:BW], wx_t[:, fc, :], xcb[:, fc, sl], r=[wx_k, "xcb"], w=["ps%d" % b2])
                    P.act(sr[:], ps[b1][:, 0:BW], AF.Tanh, bias=hba[e][:, fc:fc + 1], scale=0.5, r=["ps%d" % b1, "hba%d" % e], w=["lsr" + sx])
                    P.act(si[:], ps[b2][:, 0:BW], AF.Tanh, bias=hbx[e][:, fc:fc + 1], scale=0.5, r=["ps%d" % b2, "hbx%d" % e], w=["lsi" + sx])
                    P.release(b1); P.release(b2)
                    P.act(aa[:], sr[:], AF.Exp, scale=hcn[e][:, fc:fc + 1], bias=hcn[e][:, fc:fc + 1], r=["lsr" + sx, "hcn%d" % e], w=["laa" + sx])
                    P.act(a2[:], sr[:], AF.Exp, scale=cneg[e][:, fc:fc + 1], bias=cneg[e][:, fc:fc + 1], r=["lsr" + sx, "cneg%d" % e], w=["la2" + sx])

                def gate_s2(fc, blk, S_, sx):
                    sl = slice(blk * BW, (blk + 1) * BW)
                    si, aa, a2, bt = S_["si"], S_["aa"], S_["a2"], S_["bt"]
                    P.act(a2[:], a2[:], AF.Sqrt, bias=1.0, scale=-1.0, r=["la2" + sx], w=["la2" + sx])
                    P.stt("dve", bt[:], si[:], 1.0, a2[:], ALU.add, ALU.mult, r=["la2" + sx, "lsi" + sx], w=["lbt" + sx])
                    P.stt("dve", bt[:], bt[:], 0.5, xc[:, fc, sl], ALU.mult, ALU.mult, r=["lbt" + sx, "xc"], w=["lbt" + sx])
                    if blk == 0:
                        init = h0[e][:, fc:fc + 1] if h0 is not None else 0.0
                        rk = ["lh0%d" % e] if h0 is not None else []
                    else:
                        init = hh_[e][:, fc, blk * BW - 1:blk * BW]
                        rk = [hk]
                    P.op("dve", lambda E, o=hh_[e][:, fc, sl], i0=init, aa=aa, bt=bt: E.tensor_tensor_scan(o, aa[:], bt[:], i0, ALU.mult, ALU.add),
                         r=["laa" + sx, "lbt" + sx] + rk, w=[hk])

                for blk in range(NB):
                    if NT2 == 2:
                        yield from need(P, 4)
                        for fc in range(2):
                            gate_s1(fc, blk, scr[fc], "%d" % fc)
                        yield
                        for fc in range(2):
                            gate_s2(fc, blk, scr[fc], "%d" % fc)
                        yield
                    else:
                        for fc in range(2):
                            yield from need(P, 2)
                            gate_s1(fc, blk, scr[0], "0")
                            gate_s2(fc, blk, scr[0], "0")
                            yield
                if not g.ctx:
                    P.dma("pool", env["o_lru"][s, l, e].rearrange("(fc p) -> p fc", p=128), hh_[e][:, :, L - 1], r=[hk], allow_slow_non_contiguous=True)
            for blk in range(NB):
                sl = slice(blk * BW, (blk + 1) * BW)
                for fc in range(2):
                    S_ = scr[git[1] % NT2]; sx = "%d" % (git[1] % NT2); git[1] += 1
                    hT, sg, tsum = S_["hT"], S_["sg"], S_["tsum"]
                    yield from need(P, 3)
                    bT = P.bank()
                    for ci in range(CB):
                        tr = LT - 1 - (blk * CB + ci)
                        P.mm(ps[bT][:, ci * 128:(ci + 1) * 128], hh_[1][:, fc, tr * 128:(tr + 1) * 128], identf[:], r=["lh1", "identf"], w=["ps%d" % bT])
                    P.cp("act", hT[:], ps[bT][:, 0:BW], r=["ps%d" % bT], w=["lhT" + sx])
                    P.release(bT)
                    bF = P.bank()
                    for ci in range(CB):
                        P.mm(ps[bF][:, ci * 128:(ci + 1) * 128], hT[:, ci * 128:(ci + 1) * 128], jrevf[:], r=["lhT" + sx, "jrevf"], w=["ps%d" % bF])
                    bG = P.bank()
                    proj_fm(P, ps[bG][:, 0:BW], w_gl, fc * 128, 128, u, tok0 + blk * BW, BW, ["w_gl"] + ukeys, "ps%d" % bG)
                    P.act(sg[:], ps[bG][:, 0:BW], AF.Tanh, scale=0.5, r=["ps%d" % bG], w=["lsg" + sx])
                    P.stt("dve", sg[:], sg[:], 1.0, ps[bG][:, 0:BW], ALU.add, ALU.mult, r=["lsg" + sx, "ps%d" % bG], w=["lsg" + sx])
                    P.release(bG)
                    P.tt("dve", tsum[:], ps[bF][:, 0:BW], hh_[0][:, fc, sl], ALU.add, r=["ps%d" % bF, "lh0"], w=["ltsum" + sx])
                    P.release(bF)
                    P.stt("dve", y[:, fc, tok0 + blk * BW:tok0 + (blk + 1) * BW], tsum[:], 0.5, sg[:], ALU.mult, ALU.mult, r=["ltsum" + sx, "lsg" + sx], w=["y:%d" % (6 + fc)])
                    yield
        if not threaded:
            dump("ylru_" + tag, y[:, :, :], [128, 2, g.T], ["y:6", "y:7"])
            P.barrier()


def phase_mla(env, g, l, u, y, ukeys, wslice, ppar, tag, threaded=False, ext=None):
    nc, P, sb, dump, W = _unpack(env)
    ps = P.ps
    identf, onesf = env["identf"], env["onesf"]
    L, LT, BW, NB, CB = g.L, g.LT, g.BW, g.NB, g.CB
    ctx = g.ctx
    K0 = 256 if ctx else 0
    Lk = L + K0
    KT = Lk // 128
    QB = BW
    with (_NoClose(ext) if ext is not None else ExitStack()) as ph:
        w_cq = sb(ph, "w_cq", [128, 8, 256], BF16); P.dma("pool", w_cq[:], wslice(C_CQ, C_CQ + 256), w=["w_cq"])
        w_ck = sb(ph, "w_ck", [128, 8, 128], BF16); P.dma("pool", w_ck[:], wslice(C_CKV, C_CKV + 128), w=["w_ck"])
        w_kr = sb(ph, "w_kr", [128, 8, 96], BF16)
        P.memset("pool", w_kr[:], 0.0, w=["w_kr"])
        P.dma("pool", w_kr[:, :, 64:96], wslice(C_KR, C_KR + 32), w=["w_kr"])
        w_gm = sb(ph, "w_gm", [128, 8, 512], BF16); P.dma("pool", w_gm[:], wslice(C_GM, C_GM + 512), w=["w_gm"])
        wuq = sb(ph, "wuq", [128, 2, 768], BF16)
        P.dma("pool", wuq[:], W["mla_wuq"][l].rearrange("(kc p) n -> p kc n", p=128), w=["wuq"])
        wkk = sb(ph, "wkk", [128, 8, 64], BF16); wkv = sb(ph, "wkv", [128, 8, 64], BF16)
        wv4 = W["mla_wukv"][l].rearrange("r (h x) -> r h x", x=128)
        P.dma("pool", wkk[:], wv4[:, :, 0:64], w=["wkk"])
        P.dma("pool", wkv[:], wv4[:, :, 64:128], w=["wkv"])
        qng = ppar(ph, "qng", W["mla_qnorm"][l])
        kvg = sb(ph, "kvg", [128, 1], F32)
        P.dma("sp", kvg[:], W["mla_kvnorm"][l].rearrange("(p o) -> p o", o=1), w=["kvg"], allow_slow_non_contiguous=True)
        if ctx:
            cosT = sb(ph, "cosT", [128, 512], F32); sinT = sb(ph, "sinT", [128, 512], F32)

            def load_tables(p0, n):
                P.dma("sp", cosT[64:96, 0:n], env["k_cos"][:, p0:p0 + n], w=["cosT"])
                P.dma("sp", sinT[64:96, 0:n], env["k_sin"][:, p0:p0 + n], w=["sinT"])
            w_krs = sb(ph, "w_krs", [128, 8, 96], BF16)
            P.memset("pool", w_krs[:], 0.0, w=["w_krs"])
            P.ts1("dve", w_krs[:, :, 64:96:2], w_kr[:, :, 65:96:2], -1.0, ALU.mult, r=["w_kr"], w=["w_krs"])
            P.cp("dve", w_krs[:, :, 65:96:2], w_kr[:, :, 64:96:2], r=["w_kr"], w=["w_krs"])
            wuqs = sb(ph, "wuqs", [128, 2, 768], BF16)
            P.memset("pool", wuqs[:], 0.0, w=["wuqs"])
            for kc in range(2):
                v4 = wuq[:, kc, :].rearrange("p (h x) -> p h x", x=96)
                v4s = wuqs[:, kc, :].rearrange("p (h x) -> p h x", x=96)
                P.ts1("dve", v4s[:, :, 64:96:2], v4[:, :, 65:96:2], -1.0, ALU.mult, r=["wuq"], w=["wuqs"])
                P.cp("dve", v4s[:, :, 65:96:2], v4[:, :, 64:96:2], r=["wuq"], w=["wuqs"])
        ckvn = sb(ph, "ckvn", [128, Lk], BF16); krall = sb(ph, "krall", [128, Lk], BF16)
        Kt = sb(ph, "Kt", [128, 8, Lk], BF16); V = sb(ph, "Vv", [128, KT, 8, 65], BF16)
        cqn = sb(ph, "cqn", [128, 2, L], BF16); sgm = [sb(ph, "sgm%d" % i, [128, QB], F32) for i in range(3 if KT < 6 else 2)]
        ck = sb(ph, "mck", [128, BW], F32); sq = sb(ph, "msq", [128, BW], F32); sq1 = sb(ph, "msq1", [128, BW], F32)
        rstd = sb(ph, "mrstd", [128, BW], F32); t1 = sb(ph, "mt1", [128, BW], F32); t2 = sb(ph, "mt2", [128, BW], F32)
        if g.nseq == 1:
            sqq = sb(ph, "msqq", [128, BW], F32); rstdq = sb(ph, "mrstdq", [128, BW], F32); t2q = sb(ph, "mt2q", [128, BW], F32)
            kq_ = ("msqq", "mrstdq", "mt2q")
        else:
            sqq, rstdq, t2q = sq, rstd, t2
            kq_ = ("msq", "mrstd", "mt2")
        Pt = [sb(ph, "Pt%d" % i, [128, QB], BF16) for i in range(4)]
        Qh = [sb(ph, "Qh%d" % i, [128, QB], BF16) for i in range(3 if KT < 6 else 2)]
        accs, rden, tmpy = ck, sq, sq1
        P.memset("pool", V[:, :, :, 64:65], 1.0, w=["Vones"])
        if ctx:
            ckc = sb(ph, "ckc", [128, 2, 128], F32); krc = sb(ph, "krc", [128, 2, 96], F32)
        else:
            otile = sb(ph, "otile", [128, CB, 128], F32); okr = sb(ph, "okr", [128, CB, 32], F32)
        qi = 0
        for s in range(g.nseq):
            tok0 = s * L
            if ctx:
                P.dma("sp", ckc[:], env["c_ckv"][l].rearrange("(t p) r -> p t r", p=128), w=["ckc"])
                P.memset("dve", krc[:], 0.0, w=["krc"])
                P.dma("sp", krc[:, :, 64:96], env["c_kr"][l].rearrange("(t p) r -> p t r", p=128), w=["krc"])
                for t in range(2):
                    b = P.bank(); bk = "ps%d" % b
                    P.mm(ps[b][:, 0:128], ckc[:, t, :], identf[:], r=["ckc", "identf"], w=[bk])
                    P.cp("act", ckvn[:, t * 128:(t + 1) * 128], ps[b][:, 0:128], r=[bk], w=["ckvn"])
                    P.release(b)
                    b = P.bank(); bk = "ps%d" % b
                    P.mm(ps[b][0:96, 0:128], krc[:, t, :], identf[:], r=["krc", "identf"], w=[bk])
                    P.cp("act", krall[64:96, t * 128:(t + 1) * 128], ps[b][64:96, 0:128], r=[bk], w=["krall"])
                    P.release(b)
            for blk in range(NB):
                t0 = tok0 + blk * BW
                lsl = slice(blk * BW, (blk + 1) * BW)
                ksl = slice(K0 + blk * BW, K0 + (blk + 1) * BW)
                if ctx:
                    load_tables(blk * BW, BW)
                def ckv_gen(blk=blk, t0=t0, lsl=lsl, ksl=ksl):
                    yield from need(P, 3)
                    b = P.bank(); bk = "ps%d" % b
                    proj_fm(P, ps[b][:, 0:BW], w_ck, 0, 128, u, t0, BW, ["w_ck"] + ukeys, bk)
                    P.cp("act", ck[:], ps[b][:, 0:BW], r=[bk], w=["mck"])
                    P.act(sq[:], ps[b][:, 0:BW], AF.Square, r=[bk], w=["msq"])
                    P.release(b)
                    yield
                    b2 = P.bank(); bk2 = "ps%d" % b2
                    P.mm(ps[b2][:, 0:BW], onesf[:], sq[:], r=["onesf", "msq"], w=[bk2])
                    P.rsqrt(rstd[:], ps[b2][:, 0:BW], 1e-6, 1.0 / 128.0, t1[:], t2[:], r=[bk2], w=["mrstd"], kx="mt1", kt="mt2")
                    P.release(b2)
                    yield
                    P.tt("dve", ck[:], ck[:], rstd[:], ALU.mult, r=["mck", "mrstd"], w=["mck"])
                    P.ts1("dve", ck[:], ck[:], kvg[:, 0:1], ALU.mult, r=["mck", "kvg"], w=["mck"])
                    P.cp("act", ckvn[:, ksl], ck[:], r=["mck"], w=["ckvn"])
                    if not ctx:
                        b3 = P.bank(); bk3 = "ps%d" % b3
                        for ci in range(CB):
                            P.mm(ps[b3][:, ci * 128:(ci + 1) * 128], ck[:, ci * 128:(ci + 1) * 128], identf[:], r=["mck", "identf"], w=[bk3])
                        P.cp("act", otile[:], ps[b3][:, 0:BW].rearrange("p (c r) -> p c r", r=128), r=[bk3], w=["otile"])
                        P.release(b3)
                        P.dma("act", env["o_ckv"][s, l, blk * BW:(blk + 1) * BW, :].rearrange("(c p) r -> p c r", p=128), otile[:], r=["otile"])
                def kr_gen(blk=blk, t0=t0, lsl=lsl, ksl=ksl):
                    yield from need(P, 3)
                    b = P.bank(); bk = "ps%d" % b
                    proj_fm(P, ps[b][0:96, 0:BW], w_kr, 0, 96, u, t0, BW, ["w_kr"] + ukeys, bk)
                    if ctx:
                        b2 = P.bank(); bk2 = "ps%d" % b2
                        proj_fm(P, ps[b2][0:96, 0:BW], w_krs, 0, 96, u, t0, BW, ["w_krs"] + ukeys, bk2)
                        P.tt("dve", t1[64:96, :], ps[b][64:96, 0:BW], cosT[64:96, 0:BW], ALU.mult, r=[bk, "cosT"], w=["mt1"])
                        P.tt("dve", t2[64:96, :], ps[b2][64:96, 0:BW], sinT[64:96, 0:BW], ALU.mult, r=[bk2, "sinT"], w=["mt2"])
                        P.release(b); P.release(b2)
                        P.tt("pool", krall[64:96, ksl], t1[64:96, :], t2[64:96, :], ALU.add, r=["mt1", "mt2"], w=["krall"])
                    else:
                        P.cp("act", t1[64:96, :], ps[b][64:96, 0:BW], r=[bk], w=["mt1"])
                        P.release(b)
                        P.cp("pool", krall[64:96, ksl], t1[64:96, :], r=["mt1"], w=["krall"])
                        b3 = P.bank(); bk3 = "ps%d" % b3
                        for ci in range(CB):
                            P.mm(ps[b3][:, ci * 32:(ci + 1) * 32], t1[64:96, ci * 128:(ci + 1) * 128], identf[64:96, 64:96], r=["mt1", "identf"], w=[bk3])
                        P.cp("act", okr[:], ps[b3][:, 0:CB * 32].rearrange("p (c r) -> p c r", r=32), r=[bk3], w=["okr"])
                        P.release(b3)
                        P.dma("act", env["o_kr"][s, l, blk * BW:(blk + 1) * BW, :].rearrange("(c p) r -> p c r", p=128), okr[:], r=["okr"])
                def cq_gen(blk=blk, t0=t0, lsl=lsl, ksl=ksl):
                    yield from need(P, 3)
                    bq = [P.bank(), P.bank()]
                    for kc in range(2):
                        proj_fm(P, ps[bq[kc]][:, 0:BW], w_cq, kc * 128, 128, u, t0, BW, ["w_cq"] + ukeys, "ps%d" % bq[kc])
                    P.act(sqq[:], ps[bq[0]][:, 0:BW], AF.Square, r=["ps%d" % bq[0]], w=[kq_[0]])
                    P.act(sq1[:], ps[bq[1]][:, 0:BW], AF.Square, r=["ps%d" % bq[1]], w=["msq1"])
                    yield
                    b2 = P.bank(); bk2 = "ps%d" % b2
                    P.mm(ps[b2][:, 0:BW], onesf[:], sqq[:], start=True, stop=False, r=["onesf", kq_[0]], w=[bk2])
                    P.mm(ps[b2][:, 0:BW], onesf[:], sq1[:], start=False, stop=True, r=["onesf", "msq1"], w=[bk2])
                    P.rsqrt(rstdq[:], ps[b2][:, 0:BW], 1e-6, 1.0 / 256.0, t1[:], t2[:], r=[bk2], w=[kq_[1]], kx="mt1", kt="mt2")
                    P.release(b2)
                    yield
                    for kc in range(2):
                        P.tt("dve", t2q[:], ps[bq[kc]][:, 0:BW], rstdq[:], ALU.mult, r=["ps%d" % bq[kc], kq_[1]], w=[kq_[2]])
                        P.release(bq[kc])
                        P.act(cqn[:, kc, lsl], t2q[:], AF.Identity, scale=qng[:, kc:kc + 1], r=[kq_[2], "qng"], w=["cqn"])
                if g.nseq == 1:
                    yield from interleave(ckv_gen(), kr_gen(), cq_gen())
                else:
                    yield from ckv_gen()
                    yield
                    yield from kr_gen()
                    yield
                    yield from cq_gen()
            yield
            nkb = (Lk + 511) // 512
            for kb in range(nkb):
                k0 = kb * 512
                n = min(512, Lk - k0)
                for h in range(8):
                    yield from need(P, 1)
                    b = P.bank(); bk = "ps%d" % b
                    P.mm(ps[b][0:64, 0:n], wkk[:, h, :], ckvn[:, k0:k0 + n], r=["wkk", "ckvn"], w=[bk])
                    P.cp("act" if h % 2 == 0 else "dve", Kt[0:64, h, k0:k0 + n], ps[b][0:64, 0:n], r=[bk], w=["Kt"])
                    P.release(b)
            for h in range(8):
                P.cp("dve" if h % 2 == 0 else "act", Kt[64:96, h, :], krall[64:96, :], r=["krall"], w=["Kt"])
            wkv2 = wkv[:].rearrange("p h x -> p (h x)")
            yield
            for kt in range(KT):
                yield from need(P, 1)
                b = P.bank(); bk = "ps%d" % b
                P.mm(ps[b][:, 0:512], ckvn[:, kt * 128:(kt + 1) * 128], wkv2, r=["wkv", "ckvn"], w=[bk])
                P.cp("act" if kt % 2 == 0 else "dve", V[:, kt, :, 0:64], ps[b][:, 0:512].rearrange("p (h x) -> p h x", x=64), r=[bk], w=["Vv"])
                P.release(b)
            PRE = 2
            DEP = 2 if KT < 6 else 1
            NQ = DEP + 1
            items = [(qb, h) for qb in range(L // QB) for h in range(8)]
            prep = {}
            esets = [(accs, rden, tmpy, "mck", "msq", "msq1")]
            if DEP == 2:
                esets.append((rstd, t1, t2, "mrstd", "mt1", "mt2"))

            def prepare(it):
                qb, h = it
                qs = slice(qb * QB, (qb + 1) * QB)
                if ctx and h == 0:
                    load_tables(qb * QB, QB)
                if h % 2 == 0:
                    gi = (h // 2) % len(sgm)
                    sgt = sgm[gi]; sgk = "sgm%d" % gi
                    b = P.bank(); bk = "ps%d" % b
                    proj_fm(P, ps[b][:, 0:QB], w_gm, (h // 2) * 128, 128, u, tok0 + qb * QB, QB, ["w_gm"] + ukeys, bk)
                    P.act(sgt[:], ps[b][:, 0:QB], AF.Tanh, scale=0.5, r=[bk], w=[sgk])
                    P.stt("dve", sgt[:], sgt[:], 1.0, ps[b][:, 0:QB], ALU.add, ALU.mult, r=[sgk, bk], w=[sgk])
                    P.release(b)
                idx = items.index(it)
                qh = Qh[idx % NQ]; qk = "Qh%d" % (idx % NQ)
                bq_ = P.bank(); bqk = "ps%d" % bq_
                P.mm(ps[bq_][0:96, 0:QB], wuq[:, 0, h * 96:(h + 1) * 96], cqn[:, 0, qs], start=True, stop=False, r=["wuq", "cqn"], w=[bqk])
                P.mm(ps[bq_][0:96, 0:QB], wuq[:, 1, h * 96:(h + 1) * 96], cqn[:, 1, qs], start=False, stop=True, r=["wuq", "cqn"], w=[bqk])
                if ctx:
                    bs_ = P.bank(); bsk = "ps%d" % bs_
                    P.mm(ps[bs_][0:96, 0:QB], wuqs[:, 0, h * 96:(h + 1) * 96], cqn[:, 0, qs], start=True, stop=False, r=["wuqs", "cqn"], w=[bsk])
                    P.mm(ps[bs_][0:96, 0:QB], wuqs[:, 1, h * 96:(h + 1) * 96], cqn[:, 1, qs], start=False, stop=True, r=["wuqs", "cqn"], w=[bsk])
                    P.cp("act", qh[0:64, :], ps[bq_][0:64, 0:QB], r=[bqk], w=[qk])
                    P.tt("dve", t1[64:96, 0:QB], ps[bq_][64:96, 0:QB], cosT[64:96, 0:QB], ALU.mult, r=[bqk, "cosT"], w=["mt1"])
                    P.tt("dve", t2[64:96, 0:QB], ps[bs_][64:96, 0:QB], sinT[64:96, 0:QB], ALU.mult, r=[bsk, "sinT"], w=["mt2"])
                    P.release(bq_); P.release(bs_)
                    P.tt("pool", qh[64:96, :], t1[64:96, 0:QB], t2[64:96, 0:QB], ALU.add, r=["mt1", "mt2"], w=[qk])
                else:
                    P.cp("act", qh[0:96, :], ps[bq_][0:96, 0:QB], r=[bqk], w=[qk])
                    P.release(bq_)
                prep[it] = (qh, qk)

            def tail1(p):
                ac, rd, tm, ka, kr_, km = esets[p["idx"] % len(esets)]
                back = "ps%d" % p["bacc"]
                P.cp("act", ac[0:65, 0:QB], ps[p["bacc"]][0:65, 0:QB], r=[back], w=[ka])
                P.release(p["bacc"])
                P.op("dve", lambda E, ac=ac, rd=rd: E.reciprocal(rd[64:65, 0:QB], ac[64:65, 0:QB]), r=[ka], w=[kr_])

            def tail2(p):
                ac, rd, tm, ka, kr_, km = esets[p["idx"] % len(esets)]
                qb, h = p["it"]
                bb = P.bank(); bbk = "ps%d" % bb
                P.mm(ps[bb][0:64, 0:QB], onesf[64:65, 0:64], rd[64:65, 0:QB], r=["onesf", kr_], w=[bbk])
                pb = (h % 2) * 64; ch = 2 + h // 2
                P.tt("dve", tm[pb:pb + 64, 0:QB], ac[0:64, 0:QB], ps[bb][0:64, 0:QB], ALU.mult, r=[ka, bbk], w=[km])
                P.release(bb)
                gi = (h // 2) % len(sgm)
                sgt = sgm[gi]; sgk = "sgm%d" % gi
                P.stt("dve", y[pb:pb + 64, ch - 2, tok0 + qb * QB:tok0 + (qb + 1) * QB], tm[pb:pb + 64, 0:QB], 0.5, sgt[pb:pb + 64, :], ALU.mult, ALU.mult,
                      r=[km, sgk], w=["y:%d" % ch])

            yield
            yield from need(P, 3)
            for i in range(min(DEP, len(items))):
                prepare(items[i])
            pending = []
            K1 = min(2, KT - 1)
            GAP2 = 10 if KT >= 14 else KT

            def run_pending(slot):
                for p in list(pending):
                    if p["stage"] == 0 and slot >= p["s1"]:
                        tail1(p); p["stage"] = 1
                    if p["stage"] == 1 and slot >= p["s1"] + GAP2:
                        tail2(p); pending.remove(p)

            for idx, it in enumerate(items):
                yield
                yield from need(P, 5)
                qb, h = it
                qh, qk = prep.pop(it)
                sbank = {}

                def issue_S(kt, qh=qh, qk=qk, h=h):
                    bS = P.bank()
                    P.mm(ps[bS][:, 0:QB], Kt[0:96, h, kt * 128:(kt + 1) * 128], qh[0:96, :], r=["Kt", qk], w=["ps%d" % bS])
                    sbank[kt] = bS

                for kt in range(min(PRE, KT)):
                    issue_S(kt)
                if idx + DEP < len(items):
                    prepare(items[idx + DEP])
                bacc = P.bank(); back = "ps%d" % bacc
                for kt in range(KT):
                    if kt + PRE < KT:
                        issue_S(kt + PRE)
                    run_pending(idx * KT + kt)
                    bS = sbank.pop(kt)
                    pt = Pt[kt % len(Pt)]; pk = "Pt%d" % (kt % len(Pt))
                    P.act(pt[:], ps[bS][:, 0:QB], AF.Exp, scale=SM_SCALE, r=["ps%d" % bS], w=[pk])
                    P.release(bS)
                    P.mm(ps[bacc][0:65, 0:QB], V[:, kt, h, :], pt[:], start=(kt == 0), stop=(kt == KT - 1), r=["Vv", "Vones", pk], w=[back])
                pending.append(dict(it=it, idx=idx, bacc=bacc, stage=0, s1=(idx + 1) * KT + K1))
            run_pending(10 ** 9)
        if not threaded:
            dump("ymla_" + tag, y[:, :, :], [128, 4, g.T], ["y:2", "y:3", "y:4", "y:5"])
            P.barrier()


def interleave(*gens):
    gens = list(gens)
    while gens:
        alive = []
        for gn in gens:
            try:
                next(gn)
                alive.append(gn)
            except StopIteration:
                pass
        gens = alive
        yield


class _NoClose:
    def __init__(self, st):
        self.st = st

    def __enter__(self):
        return self.st

    def __exit__(self, *a):
        return False


def need(P, n):
    while len(P.free_banks) < n:
        yield


def interleave_w(pairs):
    pairs = list(pairs)
    while pairs:
        alive = []
        for gn, wgt in pairs:
            ok = True
            for _ in range(wgt):
                try:
                    next(gn)
                except StopIteration:
                    ok = False
                    break
            if ok:
                alive.append((gn, wgt))
        pairs = alive
        yield


def run_gen(gn):
    for _ in gn:
        pass


def phase_rwkv(env, g, l, u, ur, y, ukeys, urkeys, wslice, ppar, tag, threaded=False, ext=None):
    nc, P, sb, dump, W = _unpack(env)
    ps = P.ps
    identf, identb, jrevb, bones, onesf = env["identf"], env["identb"], env["jrevb"], env["bones"], env["onesf"]
    mask1, mask2, mask3 = env["mask1"], env["mask2"], env["mask3"]
    L, LT = g.L, g.LT
    BW, CB = 256, 2
    NB = L // BW
    nth = 2 if (g.nseq > 1 and not threaded) else 1
    NBP = 2 if NB > 1 else 1
    NTR = 2 if g.nseq == 1 else 1

    def rk(b, i):
        return "ps%d" % b

    def rks(b, n):
        return ["ps%d" % b]

    with (_NoClose(ext) if ext is not None else ExitStack()) as ph:
        w_rk = sb(ph, "w_rk", [128, 8, 512], BF16); P.dma("pool", w_rk[:], wslice(C_R, C_R + 512), w=["w_rk"])
        w_vg = sb(ph, "w_vg", [128, 8, 256], BF16); P.dma("pool", w_vg[:], wslice(C_V, C_V + 256), w=["w_vg"])
        w_g_own = sb(ph, "w_g_own", [128, 8, 256], BF16); P.dma("pool", w_g_own[:], wslice(C_G, C_G + 256), w=["w_g_own"])
        lo1 = [sb(ph, "lo1_%d" % e, [128, 8, 128], BF16) for e in range(2)]
        lo2 = [sb(ph, "lo2_%d" % e, [128, 256], BF16) for e in range(2)]
        for e in range(2):
            P.dma("pool", lo1[e][:, :, 0:64], W["rw_w1"][l, e].rearrange("(kc p) r -> p kc r", p=128), w=["lo1_%d" % e])
            P.dma("pool", lo1[e][:, :, 64:128], W["rw_a1"][l, e].rearrange("(kc p) r -> p kc r", p=128), w=["lo1_%d" % e])
            P.dma("pool", lo2[e][0:64, :], W["rw_w2"][l, e], w=["lo2_%d" % e])
            P.dma("pool", lo2[e][64:128, :], W["rw_a2"][l, e], w=["lo2_%d" % e])
        pw0 = [ppar(ph, "pw0_%d" % e, W["rw_w0"][l, e]) for e in range(2)]
        pa0 = [ppar(ph, "pa0_%d" % e, W["rw_a0"][l, e]) for e in range(2)]
        hpw0 = [sb(ph, "hpw0_%d" % e, [128, 2], F32) for e in range(2)]
        hpa0 = [sb(ph, "hpa0_%d" % e, [128, 2], F32) for e in range(2)]
        for e in range(2):
            P.ts1("dve", hpw0[e][:], pw0[e][:], 0.5, ALU.mult, r=["pw0_%d" % e], w=["hpw0_%d" % e])
            P.ts1("dve", hpa0[e][:], pa0[e][:], 0.5, ALU.mult, r=["pa0_%d" % e], w=["hpa0_%d" % e])
        pkk = ppar(ph, "pkk", W["rw_kk"][l]); pka = ppar(ph, "pka", W["rw_ka"][l]); prk = ppar(ph, "prk", W["rw_rk"][l])
        plg = ppar(ph, "plg", W["rw_lnx_g"][l]); plb = ppar(ph, "plb", W["rw_lnx_b"][l])

        class BufSet:
            pass

        ND = env["dbg"].get("ndummy", 0)
        dbank = P.bank() if ND else None

        def dummies(n):
            for _ in range(n * ND):
                P.pe_cont = True
                P.op("pe", lambda E: E.matmul(ps[dbank][:, :], identb[:], mask1[:], start=False, stop=False), (), ())
                P.pe_cont = False

        def alloc_set(tid):
            B = BufSet()
            B.tid = tid
            B.k = lambda name: "%s@%d" % (name, tid)
            n = lambda name: "%s_t%d" % (name, tid)
            B.QR = [[sb(ph, n("QR%d%d" % (bp, e)), [128, 2, CB, 2, 128], BF16) for e in range(2)] for bp in range(NBP)]
            B.AK = [[sb(ph, n("AK%d%d" % (bp, e)), [128, 2, CB, 2, 128], BF16) for e in range(2)] for bp in range(NBP)]
            B.AKgT = [[sb(ph, n("AKgT%d%d" % (bp, e)), [128, CB, 2, 2, 128], BF16) for e in range(2)] for bp in range(NBP)]
            B.Vt = [[sb(ph, n("Vt%d%d" % (bp, e)), [128, CB, 256], BF16) for e in range(2)] for bp in range(NBP)]
            B.GE = [sb(ph, n("GE%d" % bp), [128, 2, 2, CB], F32) for bp in range(NBP)]
            B.yT = [sb(ph, n("yT%d" % e), [128, LT, 256], BF16) for e in range(2)]
            B.tr = []
            for ti in range(NTR):
                T = BufSet()
                T.sfx = "#%d" % ti
                T.tl = sb(ph, n("tl%d" % ti), [128, BW], BF16)
                for nm in ("rT", "kT", "kkT", "sgw", "aT", "cs", "Em", "tq"):
                    setattr(T, nm, sb(ph, n("%s%d" % (nm, ti)), [128, BW], F32))
                T.Ebuf = sb(ph, n("Ebuf%d" % ti), [128, CB, 129], F32)
                T.Ag = sb(ph, n("Ag%d" % ti), [128, BW], BF16); T.Kg = sb(ph, n("Kg%d" % ti), [128, BW], BF16)
                P.memset("dve", T.Ebuf[:, :, 0:1], 1.0, w=[B.k("Ebuf0" + T.sfx)])
                B.tr.append(T)
            B.MA = [[sb(ph, n("MA%d%d" % (par, e)), [128, 4, 2, 128], BF16) for e in range(2)] for par in range(2)]
            B.LK = [[sb(ph, n("LK%d%d" % (par, e)), [128, 4, 2, 128], BF16) for e in range(2)] for par in range(2)]
            B.Pm = [[sb(ph, n("Pm%d%d" % (par, e)), [128, 4, 128], BF16) for e in range(2)] for par in range(2)]
            B.Nn = [sb(ph, n("Nn%d" % e), [128, 4, 128], BF16) for e in range(2)]
            B.M2 = [[sb(ph, n("M2_%d%d" % (e, i)), [128, 4, 128], BF16) for i in range(2)] for e in range(2)]
            B.N2 = [[sb(ph, n("N2_%d%d" % (e, i)), [128, 4, 128], BF16) for i in range(2)] for e in range(2)]
            B.W1 = sb(ph, n("W1"), [128, 512], BF16); B.Uu = sb(ph, n("Uu"), [128, 512], BF16)
            B.H32 = sb(ph, n("H32"), [128, 4, 64], F32); B.Hbf = sb(ph, n("Hbf"), [128, 4, 64], BF16)
            if not g.ctx:
                B.SS = sb(ph, n("SS"), [128, 512], F32)
                B.ssk = B.k("SS")
                B.So = B.SS[0:64, 0:512].rearrange("p (a h k) -> p a h k", a=4, h=2)
            return B

        def v3(t):
            return t[:].rearrange("p (c t) -> p c t", t=128)

        def derive_gen(B, s, blk, e):
            T = B.tr[e % NTR]
            bp = blk % NBP
            k = lambda name: B.k(name + T.sfx)
            kb = lambda name: B.k("%s.%d" % (name, bp))
            tok0 = s * L
            src, skeys = (u, ukeys) if e == 0 else (ur, urkeys)
            t0 = tok0 + blk * BW
            ek = "%d" % e
            rT, kT, kkT, sgw, aT, cs, Em, tq, tl, Ebuf = T.rT, T.kT, T.kkT, T.sgw, T.aT, T.cs, T.Em, T.tq, T.tl, T.Ebuf
            QRe, AKe, AKgTe, Vte, GEb = B.QR[bp][e], B.AK[bp][e], B.AKgT[bp][e], B.Vt[bp][e], B.GE[bp]
            Eend, kka, kdir = sgw, tq, cs
            cs3, Em3, Eend3 = v3(cs), v3(Em), v3(Eend)
            for half in range((CB + 1) // 2):
                yield from need(P, 2)
                b = P.bank(); bk = "ps%d" % b
                nci = min(2, CB - half * 2)
                for q in range(nci):
                    ci = half * 2 + q
                    for kc in range(8):
                        P.mm(ps[b][:, q * 256:(q + 1) * 256], src[:, kc, t0 + ci * 128:t0 + (ci + 1) * 128], w_vg[:, kc, 0:256],
                             start=(kc == 0), stop=(kc == 7), r=["w_vg"] + skeys, w=[rk(b, q)])
                P.cp("act", Vte[:, half * 2:half * 2 + nci, :], ps[b][:, 0:nci * 256].rearrange("p (c v) -> p c v", v=256),
                     r=rks(b, nci), w=[kb("Vt" + ek)])
                P.release(b)
            b = P.bank(); bk = "ps%d" % b
            proj_fm(P, ps[b][:, 0:BW], lo1[e], 0, 128, src, t0, BW, ["lo1_" + ek] + skeys, bk)
            P.act(tl[0:64, :], ps[b][0:64, 0:BW], AF.Tanh, r=[bk], w=[k("tl")])
            P.cp("dve", tl[64:128, :], ps[b][64:128, 0:BW], r=[bk], w=[k("tl")])
            P.release(b)
            yield
            for fc in range(2):
                fs = slice(fc * 128, (fc + 1) * 128)
                yield from need(P, 4)
                bw_ = P.bank(); ba_ = P.bank()
                P.mm(ps[bw_][:, 0:BW], lo2[e][0:64, fs], tl[0:64, :], r=["lo2_" + ek, k("tl")], w=["ps%d" % bw_])
                P.mm(ps[ba_][:, 0:BW], lo2[e][64:128, fs], tl[64:128, :], r=["lo2_" + ek, k("tl")], w=["ps%d" % ba_])
                br_ = P.bank(); bk_ = P.bank()
                proj_fm(P, ps[br_][:, 0:BW], w_rk, fc * 128, 128, src, t0, BW, ["w_rk"] + skeys, "ps%d" % br_)
                proj_fm(P, ps[bk_][:, 0:BW], w_rk, 256 + fc * 128, 128, src, t0, BW, ["w_rk"] + skeys, "ps%d" % bk_)
                P.act(sgw[:], ps[bw_][:, 0:BW], AF.Tanh, bias=hpw0[e][:, fc:fc + 1], scale=0.5, r=["ps%d" % bw_, "hpw0_" + ek], w=[k("sgw")])
                P.act(aT[:], ps[ba_][:, 0:BW], AF.Tanh, bias=hpa0[e][:, fc:fc + 1], scale=0.5, r=["ps%d" % ba_, "hpa0_" + ek], w=[k("aT")])
                P.release(bw_); P.release(ba_)
                P.ts("pool", sgw[:], sgw[:], 0.5, 0.5, ALU.mult, ALU.add, r=[k("sgw")], w=[k("sgw")])
                P.ts("pool", aT[:], aT[:], 0.5, 0.5, ALU.mult, ALU.add, r=[k("aT")], w=[k("aT")])
                P.cp("act", rT[:], ps[br_][:, 0:BW], r=["ps%d" % br_], w=[k("rT")])
                P.cp("dve", kT[:], ps[bk_][:, 0:BW], r=["ps%d" % bk_], w=[k("kT")])
                P.release(br_); P.release(bk_)
                yield
                P.ts1("dve", kkT[:], kT[:], pkk[:, fc:fc + 1], ALU.mult, r=[k("kT"), "pkk"], w=[k("kkT")])
                P.tt("pool", tq[:], kkT[:], kkT[:], ALU.mult, r=[k("kkT")], w=[k("tq")])
                bs_ = P.bank()
                P.mm(ps[bs_][:, 0:BW], bones[:], tq[:], r=["bones", k("tq")], w=["ps%d" % bs_])
                P.rsqrt(tq[:], ps[bs_][:, 0:BW], 1e-24, 1.0, Em[:], cs[:], r=["ps%d" % bs_], w=[k("tq")], kx=k("Em"), kt=k("cs"))
                P.release(bs_)
                for ci in range(CB):
                    P.op("dve", lambda E, ci=ci: E.tensor_tensor_scan(cs[:, ci * 128:(ci + 1) * 128], onesf[:, 0:128],
                                                                      sgw[:, ci * 128:(ci + 1) * 128], 0.0, ALU.mult, ALU.add),
                         r=["onesf", k("sgw")], w=[k("cs")])
                yield
                P.tt("dve", kkT[:], kkT[:], tq[:], ALU.mult, r=[k("kkT"), k("tq")], w=[k("kkT")])
                P.act(Ebuf[:, :, 1:129], cs3, AF.Exp, scale=-WDEC, r=[k("cs")], w=[k("Ebuf")])
                P.act(Em[:], cs[:], AF.Exp, scale=WDEC, r=[k("cs")], w=[k("Em")])
                P.cp("dve", GEb[:, e, fc, :], Ebuf[:, :, 128], r=[k("Ebuf")], w=[kb("GE")])
                P.tt("dve", Eend3, Em3, GEb[:, e, fc, :].unsqueeze(2).to_broadcast([128, CB, 128]), ALU.mult, r=[k("Em"), kb("GE")], w=[k("sgw")])
                P.tt("pool", kka[:], kkT[:], aT[:], ALU.mult, r=[k("kkT"), k("aT")], w=[k("tq")])
                P.ts("dve", aT[:], aT[:], 1.0, pka[:, fc:fc + 1], ALU.subtract, ALU.mult, r=[k("aT"), "pka"], w=[k("aT")])
                P.stt("dve", kdir[:], aT[:], 1.0, kT[:], ALU.add, ALU.mult, r=[k("aT"), k("kT")], w=[k("cs")])
                yield
                P.tt("dve", QRe[:, fc, :, 0, :], v3(kkT), Ebuf[:, :, 0:128], ALU.mult, r=[k("kkT"), k("Ebuf"), k("Ebuf0")], w=[kb("QR" + ek)])
                P.tt("dve", QRe[:, fc, :, 1, :], v3(rT), Ebuf[:, :, 1:129], ALU.mult, r=[k("rT"), k("Ebuf")], w=[kb("QR" + ek)])
                P.tt("pool", AKe[:, fc, :, 0, :], v3(kka), Em3, ALU.mult, r=[k("tq"), k("Em")], w=[kb("AK" + ek)])
                P.tt("pool", AKe[:, fc, :, 1, :], v3(kdir), Em3, ALU.mult, r=[k("cs"), k("Em")], w=[kb("AK" + ek)])
                P.tt("dve", T.Ag[:], kka[:], Eend[:], ALU.mult, r=[k("tq"), k("sgw")], w=[k("Ag")])
                P.tt("pool", T.Kg[:], kdir[:], Eend[:], ALU.mult, r=[k("cs"), k("sgw")], w=[k("Kg")])
                yield
                for ci in range(CB):
                    yield from need(P, 1)
                    b = P.bank()
                    P.mm(ps[b][:, 0:128], T.Ag[:, ci * 128:(ci + 1) * 128], identb[:], r=[k("Ag"), "identb"], w=[rk(b, 0)])
                    P.mm(ps[b][:, 128:256], T.Kg[:, ci * 128:(ci + 1) * 128], identb[:], r=[k("Kg"), "identb"], w=[rk(b, 1)])
                    P.cp("act", AKgTe[:, ci, :, fc, :], ps[b][:, 0:256].rearrange("p (a x) -> p a x", x=128), r=rks(b, 2), w=[kb("AKgT" + ek)])
                    P.release(b)
                yield

        def gn_gen(B, blk, ci, par):
            bp = blk % NBP
            k0 = B.k
            k = lambda name: (B.k("%s.%d" % (name, bp)) if name[:2] in ("QR", "AK", "Vt", "GE") else B.k(name))
            QR, AK, MA, LK, Pm, Nn, M2, N2 = B.QR[bp], B.AK[bp], B.MA[par], B.LK[par], B.Pm[par], B.Nn, B.M2, B.N2
            pk = "%d" % par
            for e in range(2):
                ek = "%d" % e
                for fc in range(2):
                    yield from need(P, 3)
                    b1 = P.bank(); b2 = P.bank()
                    for hh in range(2):
                        pb = hh * 64
                        qr2 = QR[e][pb:pb + 64, fc, ci, :, :].rearrange("p a t -> p (a t)")
                        P.mm(ps[b1][:, hh * 256:(hh + 1) * 256], AK[e][pb:pb + 64, fc, ci, 0, :], qr2, r=[k("AK" + ek), k("QR" + ek)], w=[rk(b1, hh)])
                        P.mm(ps[b2][:, hh * 256:(hh + 1) * 256], AK[e][pb:pb + 64, fc, ci, 1, :], qr2, r=[k("AK" + ek), k("QR" + ek)], w=[rk(b2, hh)])
                    P.tt("dve", MA[e][:, fc * 2:fc * 2 + 2, :, :].rearrange("p h a t -> p (h a t)"), ps[b1][:, :], mask1[:], ALU.mult,
                         r=rks(b1, 2) + ["mask1"], w=[k("MA" + pk + ek)])
                    P.tt("dve", LK[e][:, fc * 2:fc * 2 + 2, :, :].rearrange("p h a t -> p (h a t)"), ps[b2][:, :], mask2[:], ALU.mult,
                         r=rks(b2, 2) + ["mask2"], w=[k("LK" + pk + ek)])
                    P.release(b1); P.release(b2)
                b3 = P.bank()
                for h in range(4):
                    fc, pb = h // 2, (h % 2) * 64
                    P.mm(ps[b3][:, h * 128:(h + 1) * 128], QR[e][pb:pb + 64, fc, ci, 0, :], AK[e][pb:pb + 64, fc, ci, 0, :],
                         r=[k("AK" + ek), k("QR" + ek)], w=[rk(b3, h)])
                P.tt("dve", Nn[e][:].rearrange("p h t -> p (h t)"), ps[b3][:, :], mask3[:], ALU.mult, r=rks(b3, 4) + ["mask3"], w=[k("Nn" + ek)])
                P.release(b3)
                yield
            Mc, Nc, Mk, Nk = {}, {}, {}, {}
            for e in range(2):
                ek = "%d" % e
                P.tt("pool", Pm[e][:], MA[e][:, :, 0, :], identb[:].unsqueeze(1).to_broadcast([128, 4, 128]), ALU.add,
                     r=[k("MA" + pk + ek), "identb"], w=[k("Pm" + pk + ek)])
                Mc[e] = [MA[e][:, h, 0, :] for h in range(4)]; Mk[e] = k("MA" + pk + ek)
                Nc[e] = [Nn[e][:, h, :] for h in range(4)]; Nk[e] = k("Nn" + ek)
            for lev in range(6):
                i2 = lev % 2
                bN, bM, bP = {}, {}, {}
                yield from need(P, 4)
                for e in range(2):
                    bN[e] = P.bank()
                    if lev < 5:
                        bM[e] = P.bank()
                for h in range(4):
                    for e in range(2):
                        P.mm(ps[bN[e]][:, h * 128:(h + 1) * 128], Mc[e][h], Nc[e][h], r=[Mk[e], Nk[e]], w=[rk(bN[e], h)])
                        if lev < 5:
                            P.mm(ps[bM[e]][:, h * 128:(h + 1) * 128], Nc[e][h], Mc[e][h], r=[Mk[e], Nk[e]], w=[rk(bM[e], h)])
                dummies(1)
                for e in range(2):
                    n2k = k("N2_%d%d" % (e, i2))
                    P.cp("act", N2[e][i2][:].rearrange("p h t -> p (h t)"), ps[bN[e]][:, :], r=rks(bN[e], 4), w=[n2k])
                    P.release(bN[e])
                if lev < 5:
                    for e in range(2):
                        m2k = k("M2_%d%d" % (e, i2))
                        P.cp("dve" if e == 0 else "act", M2[e][i2][:].rearrange("p h t -> p (h t)"), ps[bM[e]][:, :], r=rks(bM[e], 4), w=[m2k])
                        P.release(bM[e])
                yield
                yield from need(P, 2)
                for e in range(2):
                    bP[e] = P.bank()
                for h in range(4):
                    for e in range(2):
                        P.mm(ps[bP[e]][:, h * 128:(h + 1) * 128], N2[e][i2][:, h, :], Pm[e][:, h, :],
                             r=[k("N2_%d%d" % (e, i2)), k("Pm" + pk + "%d" % e)], w=[rk(bP[e], h)])
                for e in range(2):
                    P.tt("dve", Pm[e][:].rearrange("p h t -> p (h t)"), Pm[e][:].rearrange("p h t -> p (h t)"), ps[bP[e]][:, :], ALU.add,
                         r=[k("Pm" + pk + "%d" % e)] + rks(bP[e], 4), w=[k("Pm" + pk + "%d" % e)])
                    P.release(bP[e])
                for e in range(2):
                    Mc[e] = [M2[e][i2][:, h, :] for h in range(4)]; Mk[e] = k("M2_%d%d" % (e, i2))
                    Nc[e] = [N2[e][i2][:, h, :] for h in range(4)]; Nk[e] = k("N2_%d%d" % (e, i2))
                yield

        def rec_gen(B, blk, ci, par, tile_i):
            bp = blk % NBP
            k = lambda name: (B.k("%s.%d" % (name, bp)) if name[:2] in ("QR", "AK", "Vt", "GE") else B.k(name))
            QR, AKgT, Vt, MA, LK, Pm = B.QR[bp], B.AKgT[bp], B.Vt[bp], B.MA[par], B.LK[par], B.Pm[par]
            W1, Uu, H32, Hbf, GE = B.W1, B.Uu, B.H32, B.Hbf, B.GE[bp]
            pk = "%d" % par
            chains = [(e, h) for h in range(4) for e in range(2)]
            yield from need(P, 1)
            bW = P.bank()
            for (e, h) in chains:
                ek = "%d" % e
                fc, pb = h // 2, (h % 2) * 64
                c = e * 4 + h
                cb = c * 64
                P.mm(ps[bW][:, cb:cb + 64], QR[e][pb:pb + 64, fc, ci, 0, :], Hbf[pb:pb + 64, e * 2 + fc, :], start=True, stop=False,
                     r=[k("QR" + ek), k("Hbf")], w=[rk(bW, c)])
                P.mm(ps[bW][:, cb:cb + 64], LK[e][:, h, 0, :], Vt[e][:, ci, h * 64:(h + 1) * 64], start=False, stop=True,
                     r=[k("LK" + pk + ek), k("Vt" + ek)], w=[rk(bW, c)])
            P.cp("act", W1[:], ps[bW][:, :], r=rks(bW, 8), w=[k("W1")])
            P.release(bW)
            yield
            yield from need(P, 1)
            bU = P.bank()
            for (e, h) in chains:
                c = e * 4 + h
                cb = c * 64
                P.mm(ps[bU][:, cb:cb + 64], Pm[e][:, h, :], W1[:, cb:cb + 64], r=[k("Pm" + pk + "%d" % e), k("W1")], w=[rk(bU, c)])
            P.ts1("dve", Uu[:], ps[bU][:, :], -1.0, ALU.mult, r=rks(bU, 8), w=[k("Uu")])
            P.release(bU)
            yield
            yield from need(P, 2)
            bY = P.bank(); bH = P.bank()
            for fc in range(2):
                for e in range(2):
                    ek = "%d" % e
                    c0 = (e * 4 + fc * 2) * 64
                    P.mm(ps[bH][:, c0:c0 + 128], AKgT[e][:, ci, 0, fc, :], Uu[:, c0:c0 + 128], start=True, stop=False,
                         r=[k("AKgT" + ek), k("Uu")], w=[rk(bH, 0)])
                    P.mm(ps[bH][:, c0:c0 + 128], AKgT[e][:, ci, 1, fc, :], Vt[e][:, ci, fc * 128:(fc + 1) * 128], start=False, stop=True,
                         r=[k("AKgT" + ek), k("Vt" + ek)], w=[rk(bH, 0)])
            for (e, h) in chains:
                ek = "%d" % e
                fc, pb = h // 2, (h % 2) * 64
                c = e * 4 + h
                cb = c * 64
                P.mm(ps[bY][:, cb:cb + 64], QR[e][pb:pb + 64, fc, ci, 1, :], Hbf[pb:pb + 64, e * 2 + fc, :], start=True, stop=False,
                     r=[k("QR" + ek), k("Hbf")], w=[rk(bY, c)])
                P.mm(ps[bY][:, cb:cb + 64], MA[e][:, h, 1, :], Uu[:, cb:cb + 64], start=False, stop=False, r=[k("MA" + pk + ek), k("Uu")], w=[rk(bY, c)])
                P.mm(ps[bY][:, cb:cb + 64], LK[e][:, h, 1, :], Vt[e][:, ci, h * 64:(h + 1) * 64], start=False, stop=True,
                     r=[k("LK" + pk + ek), k("Vt" + ek)], w=[rk(bY, c)])
            yield
            bH4 = ps[bH][:, :].rearrange("p (a hh v) -> p a hh v", hh=2, v=64)
            for hh in range(2):
                pb = hh * 64
                gb = GE[pb:pb + 64, :, :, ci].rearrange("p e f -> p (e f)").unsqueeze(2).to_broadcast([64, 4, 64])
                P.tt("dve", H32[pb:pb + 64, :, :], H32[pb:pb + 64, :, :], gb, ALU.mult, r=[k("H32"), k("GE")], w=[k("H32")])
                P.tt("dve", H32[pb:pb + 64, :, :], H32[pb:pb + 64, :, :], bH4[pb:pb + 64, :, hh, :], ALU.add, r=[k("H32")] + rks(bH, 8), w=[k("H32")])
            P.release(bH)
            P.cp("act", Hbf[:], H32[:], r=[k("H32")], w=[k("Hbf")])
            for e in range(2):
                P.cp("act" if e == 0 else "pool", B.yT[e][:, tile_i, :], ps[bY][:, e * 256:(e + 1) * 256], r=rks(bY, 8), w=[k("yT%d" % e)]) \
                    if e == 0 else P.cp("act", B.yT[e][:, tile_i, :], ps[bY][:, e * 256:(e + 1) * 256], r=rks(bY, 8), w=[k("yT%d" % e)])
            P.release(bY)
            yield

        def epi_gen(B, s, fcs, ti):
            tok0 = s * L
            T = B.tr[ti]
            k = lambda name: B.k(name + T.sfx) if name in ("rT", "kT", "kkT", "sgw", "aT") else B.k(name)
            rT, kT, kkT, sgw, aT = T.rT, T.kT, T.kkT, T.sgw, T.aT
            yT = B.yT
            w_g, wgk = w_g_own[:], "w_g_own"
            for blk in range(NB):
                t0 = tok0 + blk * BW
                for fc in fcs:
                    fs = slice(fc * 128, (fc + 1) * 128)
                    yield from need(P, 3)
                    bC = P.bank()
                    for ci in range(CB):
                        t = blk * CB + ci
                        tr = LT - 1 - t
                        P.mm(ps[bC][:, ci * 128:(ci + 1) * 128], yT[0][:, t, fs], identb[:], start=True, stop=False, r=[k("yT0"), "identb"], w=[rk(bC, ci)])
                        P.mm(ps[bC][:, ci * 128:(ci + 1) * 128], yT[1][:, tr, fs], jrevb[:], start=False, stop=True, r=[k("yT1"), "jrevb"], w=[rk(bC, ci)])
                    P.cp("act", rT[:], ps[bC][:, 0:BW], r=rks(bC, CB), w=[k("rT")])
                    P.act(kT[:], ps[bC][:, 0:BW], AF.Square, r=rks(bC, CB), w=[k("kT")])
                    P.release(bC)
                    yield
                    yield from need(P, 3)
                    bm_ = P.bank(); bq_ = P.bank()
                    P.mm(ps[bm_][:, 0:BW], bones[:], rT[:], r=["bones", k("rT")], w=["ps%d" % bm_])
                    P.mm(ps[bq_][:, 0:BW], bones[:], kT[:], r=["bones", k("kT")], w=["ps%d" % bq_])
                    P.ts1("dve", kkT[:], ps[bm_][:, 0:BW], 1.0 / 64.0, ALU.mult, r=["ps%d" % bm_], w=[k("kkT")])
                    P.release(bm_)
                    P.tt("pool", sgw[:], kkT[:], kkT[:], ALU.mult, r=[k("kkT")], w=[k("sgw")])
                    P.stt("dve", aT[:], ps[bq_][:, 0:BW], 1.0 / 64.0, sgw[:], ALU.mult, ALU.subtract, r=["ps%d" % bq_, k("sgw")], w=[k("aT")])
                    P.release(bq_)
                    P.ts1("dve", aT[:], aT[:], 0.0, ALU.max, r=[k("aT")], w=[k("aT")])
                    P.rsqrt(aT[:], aT[:], GN_EPS, 1.0, T.Em[:], T.cs[:], r=[k("aT")], w=[k("aT")], kx=B.k("Em" + T.sfx), kt=B.k("cs" + T.sfx))
                    yield
                    P.tt("dve", rT[:], rT[:], kkT[:], ALU.subtract, r=[k("rT"), k("kkT")], w=[k("rT")])
                    P.tt("dve", rT[:], rT[:], aT[:], ALU.mult, r=[k("rT"), k("aT")], w=[k("rT")])
                    P.ts("dve", rT[:], rT[:], plg[:, fc:fc + 1], plb[:, fc:fc + 1], ALU.mult, ALU.add, r=[k("rT"), "plg", "plb"], w=[k("rT")])
                    yield from need(P, 3)
                    br_ = P.bank(); bk_ = P.bank()
                    proj_fm(P, ps[br_][:, 0:BW], w_rk, fc * 128, 128, u, t0, BW, ["w_rk"] + ukeys, "ps%d" % br_)
                    proj_fm(P, ps[bk_][:, 0:BW], w_rk, 256 + fc * 128, 128, u, t0, BW, ["w_rk"] + ukeys, "ps%d" % bk_)
                    P.cp("act", kT[:], ps[bk_][:, 0:BW], r=["ps%d" % bk_], w=[k("kT")])
                    P.release(bk_)
                    P.stt("dve", kT[:], ps[br_][:, 0:BW], prk[:, fc:fc + 1], kT[:], ALU.mult, ALU.mult, r=["ps%d" % br_, "prk", k("kT")], w=[k("kT")])
                    P.release(br_)
                    yield
                    yield from need(P, 3)
                    bv_ = P.bank(); bg_ = P.bank(); bb_ = P.bank()
                    proj_fm(P, ps[bv_][:, 0:BW], w_vg, fc * 128, 128, u, t0, BW, ["w_vg"] + ukeys, "ps%d" % bv_)
                    proj_fm(P, ps[bg_][:, 0:BW], w_g, fc * 128, 128, u, t0, BW, [wgk] + ukeys, "ps%d" % bg_)
                    P.mm(ps[bb_][:, 0:BW], bones[:], kT[:], r=["bones", k("kT")], w=["ps%d" % bb_])
                    P.cp("act", kkT[:], ps[bv_][:, 0:BW], r=["ps%d" % bv_], w=[k("kkT")])
                    P.release(bv_)
                    P.act(sgw[:], ps[bg_][:, 0:BW], AF.Tanh, scale=0.5, r=["ps%d" % bg_], w=[k("sgw")])
                    P.stt("dve", sgw[:], sgw[:], 1.0, ps[bg_][:, 0:BW], ALU.add, ALU.mult, r=[k("sgw"), "ps%d" % bg_], w=[k("sgw")])
                    P.release(bg_)
                    P.tt("dve", kkT[:], kkT[:], ps[bb_][:, 0:BW], ALU.mult, r=[k("kkT"), "ps%d" % bb_], w=[k("kkT")])
                    P.release(bb_)
                    P.tt("pool", rT[:], rT[:], kkT[:], ALU.add, r=[k("rT"), k("kkT")], w=[k("rT")])
                    P.stt("dve", y[:, fc, t0:t0 + BW], rT[:], 0.5, sgw[:], ALU.mult, ALU.mult, r=[k("rT"), k("sgw")], w=["y:%d" % fc])
                    yield

        def unit_gen(B, s):
            k = B.k
            H32, Hbf = B.H32, B.Hbf
            if g.ctx:
                S0 = [B.tr[e % NTR].rT[0:64, 0:256].rearrange("p (f h k) -> p f h k", f=2, h=2) for e in range(2)]
                s0k = [B.k("rT" + B.tr[e % NTR].sfx) for e in range(2)]
                yield from need(P, 1)
                b = P.bank()
                for e in range(2):
                    for fc in range(2):
                        P.dma("sp", S0[e][:, fc, :, :], env["st_rwkv"][l, e, 2 * fc:2 * fc + 2].rearrange("h v k -> v h k"), w=[s0k[e]])
                    for fc in range(2):
                        a = e * 2 + fc
                        P.mm(ps[b][:, a * 64:(a + 1) * 64], S0[e][:, fc, :, :].rearrange("v h k -> v (h k)"), identf[0:64, 0:64],
                             r=[s0k[e], "identf"], w=[rk(b, a)])
                P.cp("dve", H32[:], ps[b][:, 0:256].rearrange("p (a v) -> p a v", v=64), r=rks(b, 4), w=[k("H32")])
                P.release(b)
            else:
                P.memset("dve", H32[:], 0.0, w=[k("H32")])
            P.cp("act", Hbf[:], H32[:], r=[k("H32")], w=[k("Hbf")])
            yield
            gci = 0
            if NTR == 2:
                yield from interleave(derive_gen(B, s, 0, 0), derive_gen(B, s, 0, 1))
            else:
                for e in range(2):
                    yield from derive_gen(B, s, 0, e)
            yield from gn_gen(B, 0, 0, gci % 2)
            for blk in range(NB):
                for ci in range(CB):
                    gens = [rec_gen(B, blk, ci, gci % 2, blk * CB + ci)]
                    if ci + 1 < CB:
                        gens.append(gn_gen(B, blk, ci + 1, (gci + 1) % 2))
                        if blk + 1 < NB:
                            gens.append(derive_gen(B, s, blk + 1, 0))
                            gens.append(derive_gen(B, s, blk + 1, 1))
                    elif blk + 1 < NB:
                        gens.append(gn_gen(B, blk + 1, 0, (gci + 1) % 2))
                    yield from interleave(*gens)
                    gci += 1
            if not g.ctx:
                yield from need(P, 1)
                b = P.bank()
                for a in range(4):
                    P.mm(ps[b][0:64, a * 128:(a + 1) * 128], H32[:, a, :], identf[:], r=[k("H32"), "identf"], w=[rk(b, a)])
                P.cp("act", B.SS[0:64, 0:512], ps[b][0:64, :], r=rks(b, 4), w=[B.ssk])
                P.release(b)
                for e in range(2):
                    for fc in range(2):
                        P.dma("act", env["o_rwkv"][s, l, e, 2 * fc:2 * fc + 2].rearrange("h v k -> v h k"), B.So[:, e * 2 + fc, :, :], r=[B.ssk])
            yield
            if NTR == 2:
                yield from interleave(epi_gen(B, s, (0,), 0), epi_gen(B, s, (1,), 1))
            else:
                yield from epi_gen(B, s, (0, 1), 0)

        sets = [alloc_set(i) for i in range(nth)]
        for s0 in range(0, g.nseq, nth):
            yield from interleave(*[unit_gen(sets[i], s0 + i) for i in range(nth) if s0 + i < g.nseq])
        if dbank is not None:
            P.release(dbank)
        if not threaded:
            dump("yrw_" + tag, y[:, :, :], [128, 2, g.T], ["y:0", "y:1"])
            P.barrier()


def host_consts():
    ident = np.eye(128, dtype=np.float32)
    jrev = np.ascontiguousarray(ident[::-1])
    jj, ii = np.meshgrid(np.arange(128), np.arange(128), indexing="ij")
    mU_s = (jj < ii).astype(np.float32)
    mU_i = (jj <= ii).astype(np.float32)
    mL_s = (ii < jj).astype(np.float32)
    mask1 = np.tile(np.concatenate([-mU_s, mU_i], 1), (1, 2))
    mask2 = np.tile(np.concatenate([mU_s, mU_i], 1), (1, 2))
    mask3 = np.tile(-mL_s, (1, 4))
    bones = np.zeros((128, 128), np.float32)
    bones[:64, :64] = 1.0
    bones[64:, 64:] = 1.0
    Lq = 2048
    row = np.repeat(np.arange(Lq // 64), 64).astype(np.float32)
    col = np.tile(np.arange(64), Lq // 64).astype(np.float32)
    inv = (np.float32(10000.0) ** (-np.arange(8, dtype=np.float32) / np.float32(8))).astype(np.float32)
    ang = np.concatenate([row[:, None] * inv, col[:, None] * inv], -1).astype(np.float32)
    cosT = np.repeat(np.cos(ang).astype(np.float32).T, 2, axis=0)
    sinT = np.repeat(np.sin(ang).astype(np.float32).T, 2, axis=0)
    return dict(k_ident=ident, k_jrev=jrev, k_mask1=np.ascontiguousarray(mask1), k_mask2=np.ascontiguousarray(mask2),
                k_mask3=np.ascontiguousarray(mask3), k_bones=bones, k_cos=np.ascontiguousarray(cosT),
                k_sin=np.ascontiguousarray(sinT))


WEIGHT_NAMES = ["w_mod", "b_mod", "w_in", "rw_w0", "rw_w1", "rw_w2", "rw_a0", "rw_a1", "rw_a2", "rw_kk", "rw_ka", "rw_rk",
                "rw_lnx_g", "rw_lnx_b", "mla_qnorm", "mla_wuq", "mla_kvnorm", "mla_wukv", "lru_conv_w", "lru_conv_b",
                "lru_wa", "lru_ba", "lru_wx", "lru_bx", "lru_lambda", "w_out", "ln_g", "ln_b"]


def make_in_maps(inputs, cores):
    f = lambda a: np.ascontiguousarray(np.asarray(a, dtype=np.float32))
    shared = {k: f(inputs[k]) for k in WEIGHT_NAMES}
    shared["rw_rk"] = shared["rw_rk"].reshape(2, 256)
    shared.update(host_consts())
    maps = []
    for i in cores:
        m = dict(shared)
        m["xs"] = f(inputs["x_sample"][i])
        m["xp"] = f(inputs["x_prompt"][4 * i:4 * i + 4]).reshape(1024, D)
        m["st_rwkv"] = f(inputs["state_rwkv"][i])
        m["c_ckv"] = f(inputs["cache_mla_ckv"][i])
        m["c_kr"] = f(inputs["cache_mla_krope"][i])
        m["st_lru"] = f(inputs["state_lru"][i])
        m["cvec"] = f(np.stack([np.asarray(inputs["c"][i]), np.asarray(inputs["c_ctx"])]))
        maps.append(m)
    return maps


def kernel(**inputs):
    nc = build()[0]
    cores = list(range(8))
    res = run_bass_kernel_spmd(nc, make_in_maps(inputs, cores), core_ids=cores)
    R = res.results
    y_prompt = np.concatenate([r["y_p"].reshape(4, 256, D) for r in R], 0)
    y_sample = np.stack([r["y_s"] for r in R], 0)
    o_rwkv = np.concatenate([r["o_rwkv"] for r in R], 0)
    o_ckv = np.concatenate([r["o_ckv"] for r in R], 0)
    o_kr = np.concatenate([r["o_kr"] for r in R], 0)
    o_lru = np.concatenate([r["o_lru"] for r in R], 0)
    return (y_prompt.astype(np.float32), y_sample.astype(np.float32), o_rwkv.astype(np.float32),
            o_ckv.astype(np.float32), o_kr.astype(np.float32), o_lru.astype(np.float32))
```

```python
import numpy as np
from collections import deque
from contextlib import ExitStack
import concourse.bass as bass
import concourse.mybir as mybir
from concourse.bass_utils import run_bass_kernel_spmd

F32 = mybir.dt.float32
BF16 = mybir.dt.bfloat16
ALU = mybir.AluOpType
AF = mybir.ActivationFunctionType

D = 1024
DEPTH = 2
ALPHA = (2 * DEPTH) ** 0.25
LN_EPS = 1e-5
GN_EPS = 64e-5
SM_SCALE = 96.0 ** -0.5
WDEC = float(np.exp(-0.5))
C_R, C_K, C_V, C_G, C_CQ, C_CKV, C_KR, C_GM, C_XL, C_GL = 0, 256, 512, 768, 1024, 1280, 1408, 1440, 1952, 2208

ENGS = ("pe", "act", "dve", "pool", "sp")


class Prog:
    N_DSEM = 40

    def __init__(self, nc, stack, same_engine_sync=True):
        self.nc = nc
        self.ops = {e: [] for e in ENGS}
        self.cnt = {e: 0 for e in ENGS}
        self.esem = {e: stack.enter_context(nc.semaphore("es_" + e)) for e in ENGS}
        self.dsem = [stack.enter_context(nc.semaphore("ds%d" % i)) for i in range(self.N_DSEM)]
        self.dcum = [0] * self.N_DSEM
        self.dnext = 0
        self.know_e = {e: {f: 0 for f in ENGS} for e in ENGS}
        self.know_d = {e: [0] * self.N_DSEM for e in ENGS}
        self.last_w = {}
        self.readers = {}
        self.same_engine_sync = same_engine_sync
        self.ps = [stack.enter_context(nc.psum_tensor("psb%d" % i, [128, 512], F32)) for i in range(8)]
        self.free_banks = deque(range(8))

    def pefull(self, on):
        self.pe_full = on and self.allow_pe_full

    allow_pe_full = ()

    def bank(self):
        assert self.free_banks, "out of PSUM banks"
        return self.free_banks.popleft()

    def release(self, b):
        self.free_banks.append(b)

    def _need(self, eng, ev):
        if ev is None:
            return
        if ev[0] == "c":
            _, f, n = ev
            if self.know_e[eng][f] >= n:
                return
            self.know_e[eng][f] = n
            self.ops[eng].append(("wc", f, n))
        else:
            _, i, v = ev
            if self.know_d[eng][i] >= v:
                return
            self.know_d[eng][i] = v
            self.ops[eng].append(("wd", i, v))

    relax = ("pe", "act", "dve")

    pe_cont = False
    pe_full = False

    def _same(self, eng, ev):
        if eng == "pe":
            return (self.pe_cont or self.pe_full) and ev is not None and ev[0] == "c" and ev[1] == eng
        return (eng in self.relax) and ev is not None and ev[0] == "c" and ev[1] == eng

    def _deps(self, eng, reads, writes):
        for k in reads:
            ev = self.last_w.get(k)
            if self._same(eng, ev) and not self.same_engine_sync:
                continue
            self._need(eng, ev)
        for k in writes:
            ev = self.last_w.get(k)
            if not self._same(eng, ev):
                self._need(eng, ev)
            for ev in self.readers.get(k, ()):
                if not self._same(eng, ev):
                    self._need(eng, ev)

    def _commit(self, ev, reads, writes):
        for k in reads:
            self.readers.setdefault(k, []).append(ev)
        for k in writes:
            self.last_w[k] = ev
            self.readers[k] = []

    def op(self, eng, fn, r=(), w=()):
        self._deps(eng, r, w)
        self.cnt[eng] += 1
        n = self.cnt[eng]
        self.ops[eng].append(("ins", fn, n))
        self._commit(("c", eng, n), r, w)

    def dma(self, eng, out, in_, r=(), w=(), **kw):
        self._deps(eng, r, w)
        i = self.dnext
        self.dnext = (self.dnext + 1) % self.N_DSEM
        if self.dcum[i] > 0:
            self._need(eng, ("d", i, self.dcum[i]))
        self.dcum[i] += 16
        v = self.dcum[i]
        sem = self.dsem[i]
        self.ops[eng].append(
            ("raw", lambda E, sem=sem, out=out, in_=in_, kw=kw: E.dma_start(out=out, in_=in_, **kw).then_inc(sem, 16)))
        self._commit(("d", i, v), r, w)

    def barrier(self):
        for e in ENGS:
            for f in ENGS:
                if self.cnt[f] > 0:
                    self._need(e, ("c", f, self.cnt[f]))
            for i in range(self.N_DSEM):
                if self.dcum[i] > 0:
                    self._need(e, ("d", i, self.dcum[i]))
        self.last_w = {}
        self.readers = {}

    def emit(self):
        self.barrier()
        ops = self.ops
        marked = {e: set() for e in ENGS}
        for e in ENGS:
            for rec in ops[e]:
                if rec[0] == "wc":
                    marked[rec[1]].add(rec[2])
        val = {}
        for e in ENGS:
            c = 0
            for n in sorted(marked[e]):
                c += 1
                val[(e, n)] = c
            assert c < 60000, (e, c)
        esem, dsem = self.esem, self.dsem

        def run(E, e):
            sem = esem[e]
            for rec in ops[e]:
                k = rec[0]
                if k == "ins":
                    ins = rec[1](E)
                    if rec[2] in marked[e]:
                        ins.then_inc(sem, 1)
                elif k == "wc":
                    E.wait_ge(esem[rec[1]], val[(rec[1], rec[2])])
                elif k == "wd":
                    E.wait_ge(dsem[rec[1]], rec[2])
                else:
                    rec[1](E)

        with self.nc.Block() as block:
            @block.tensor
            def _(E):
                run(E, "pe")

            @block.scalar
            def _(E):
                run(E, "act")

            @block.vector
            def _(E):
                run(E, "dve")

            @block.gpsimd
            def _(E):
                run(E, "pool")

            @block.sync
            def _(E):
                run(E, "sp")

    def mm(self, out, lhsT, rhs, start=True, stop=True, r=(), w=()):
        self.pe_cont = (not start) and ("pe" in self.relax)
        self.op("pe", lambda E: E.matmul(out, lhsT, rhs, start=start, stop=stop), r, w)
        self.pe_cont = False

    def act(self, out, in_, func, r=(), w=(), bias=None, scale=None, accum=None):
        kw = {}
        if bias is not None:
            kw["bias"] = bias
        if scale is not None:
            kw["scale"] = scale
        if accum is not None:
            kw["accum_out"] = accum
        self.op("act", lambda E: E.activation(out, in_, func, **kw), r, w)

    def cp(self, eng, out, in_, r=(), w=()):
        if eng == "act":
            self.op("act", lambda E: E.copy(out, in_), r, w)
        else:
            self.op(eng, lambda E: E.tensor_copy(out, in_), r, w)

    def ts(self, eng, out, in0, s1, s2, op0, op1, r=(), w=()):
        self.op(eng, lambda E: E.tensor_scalar(out, in0, s1, s2, op0, op1), r, w)

    def ts1(self, eng, out, in0, s1, op0, r=(), w=()):
        self.op(eng, lambda E: E.tensor_single_scalar(out, in0, s1, op0), r, w)

    def tt(self, eng, out, in0, in1, op, r=(), w=()):
        self.op(eng, lambda E: E.tensor_tensor(out, in0, in1, op), r, w)

    def stt(self, eng, out, in0, scalar, in1, op0, op1, r=(), w=()):
        self.op(eng, lambda E: E.scalar_tensor_tensor(out, in0, scalar, in1, op0, op1), r, w)

    def rsqrt(self, out, in_, eps, scale, tx, tt_, r=(), w=(), kx=None, kt=None, small=False):
        self.act(out, in_, AF.Ln, bias=eps, scale=scale, r=r, w=w)
        self.act(out, out, AF.Exp, scale=-0.5, r=w, w=w)

    def memset(self, eng, ap, val, w=()):
        self.op(eng, lambda E: E.memset(ap, val), (), w)


class Group:
    def __init__(self, name, nseq, L, j, ctx):
        self.name, self.nseq, self.L, self.j, self.ctx = name, nseq, L, j, ctx
        self.T = nseq * L
        self.NT = self.T // 128
        self.LT = L // 128
        self.BW = min(512, L)
        self.CB = self.BW // 128
        self.NB = L // self.BW


def build(dbg=None):
    dbg = dbg or {}
    stop = dbg.get("stop", "")
    groups_sel = dbg.get("groups", "SP")
    nlayers = dbg.get("nlayers", DEPTH)
    nc = bass.Bass("TRN2", target_bir_lowering=False)

    def din(name, shape):
        return nc.dram_tensor(name, list(shape), F32, kind="ExternalInput").ap()

    def dout(name, shape):
        return nc.dram_tensor(name, list(shape), F32, kind="ExternalOutput").ap()

    xs = din("xs", [2048, D]); xp = din("xp", [1024, D])
    st_rwkv = din("st_rwkv", [2, 2, 4, 64, 64]); c_ckv = din("c_ckv", [2, 256, 128]); c_kr = din("c_kr", [2, 256, 32])
    st_lru = din("st_lru", [2, 2, 256]); cvec = din("cvec", [2, D])
    w_mod = din("w_mod", [2, D, 3 * D]); b_mod = din("b_mod", [2, 3 * D]); w_in = din("w_in", [2, D, 2464])
    rw_w0 = din("rw_w0", [2, 2, 256]); rw_w1 = din("rw_w1", [2, 2, D, 64]); rw_w2 = din("rw_w2", [2, 2, 64, 256])
    rw_a0 = din("rw_a0", [2, 2, 256]); rw_a1 = din("rw_a1", [2, 2, D, 64]); rw_a2 = din("rw_a2", [2, 2, 64, 256])
    rw_kk = din("rw_kk", [2, 256]); rw_ka = din("rw_ka", [2, 256]); rw_rk = din("rw_rk", [2, 256])
    rw_lnx_g = din("rw_lnx_g", [2, 256]); rw_lnx_b = din("rw_lnx_b", [2, 256])
    mla_qnorm = din("mla_qnorm", [2, 256]); mla_wuq = din("mla_wuq", [2, 256, 768])
    mla_kvnorm = din("mla_kvnorm", [2, 128]); mla_wukv = din("mla_wukv", [2, 128, 1024])
    lru_conv_w = din("lru_conv_w", [2, 4, 256]); lru_conv_b = din("lru_conv_b", [2, 256])
    lru_wa = din("lru_wa", [2, 2, 4, 64, 64]); lru_ba = din("lru_ba", [2, 2, 256])
    lru_wx = din("lru_wx", [2, 2, 4, 64, 64]); lru_bx = din("lru_bx", [2, 2, 256]); lru_lambda = din("lru_lambda", [2, 2, 256])
    w_out = din("w_out", [2, D, D]); ln_g = din("ln_g", [2, D]); ln_b = din("ln_b", [2, D])
    k_ident = din("k_ident", [128, 128]); k_jrev = din("k_jrev", [128, 128])
    k_mask1 = din("k_mask1", [128, 512]); k_mask2 = din("k_mask2", [128, 512]); k_mask3 = din("k_mask3", [128, 512])
    k_bones = din("k_bones", [128, 128]); k_cos = din("k_cos", [32, 2048]); k_sin = din("k_sin", [32, 2048])

    y_s = dout("y_s", [2048, D]); y_p = dout("y_p", [1024, D])
    o_rwkv = dout("o_rwkv", [4, 2, 2, 4, 64, 64]); o_ckv = dout("o_ckv", [4, 2, 256, 128])
    o_kr = dout("o_kr", [4, 2, 256, 32]); o_lru = dout("o_lru", [4, 2, 2, 256])
    xmid = {"S": nc.dram_tensor("xmid_s", [2048, D], F32).ap(), "P": nc.dram_tensor("xmid_p", [1024, D], F32).ap()}

    dumps = []

    with ExitStack() as st:
        P = Prog(nc, st, same_engine_sync=dbg.get("ses", True))
        P.relax = dbg.get("relax", Prog.relax)
        P.allow_pe_full = dbg.get("pe_full", Prog.allow_pe_full)
        ps = P.ps

        uniq = [0]

        def sb(stack, name, shape, dt):
            uniq[0] += 1
            return stack.enter_context(nc.sbuf_tensor("%s_%d" % (name, uniq[0]), list(shape), dt))

        def dump(name, ap, shape, rkeys):
            if name not in dbg.get("dump", ()):
                return
            o = dout("dbg_" + name, shape)
            P.dma("pool", o, ap, r=rkeys, allow_slow_non_contiguous=True)
            dumps.append(name)

        identf = sb(st, "identf", [128, 128], F32); jrevf = sb(st, "jrevf", [128, 128], F32)
        identb = sb(st, "identb", [128, 128], BF16); jrevb = sb(st, "jrevb", [128, 128], BF16)
        mask1 = sb(st, "mask1", [128, 512], BF16); mask2 = sb(st, "mask2", [128, 512], BF16); mask3 = sb(st, "mask3", [128, 512], BF16)
        bones = sb(st, "bones", [128, 128], F32); onesf = sb(st, "onesf", [128, 128], F32)
        P.dma("sp", identf[:], k_ident, w=["identf"]); P.dma("sp", jrevf[:], k_jrev, w=["jrevf"])
        P.dma("pool", identb[:], k_ident, w=["identb"]); P.dma("pool", jrevb[:], k_jrev, w=["jrevb"])
        P.dma("pool", mask1[:], k_mask1, w=["mask1"]); P.dma("pool", mask2[:], k_mask2, w=["mask2"]); P.dma("pool", mask3[:], k_mask3, w=["mask3"])
        P.dma("sp", bones[:], k_bones, w=["bones"])
        P.memset("dve", onesf[:], 1.0, w=["onesf"])
        mod = [sb(st, "mod%d" % l, [128, 24, 2], F32) for l in range(2)]
        sc1 = [sb(st, "sc1_%d" % l, [128, 8, 2], F32) for l in range(2)]
        gta = [sb(st, "gta%d" % l, [128, 8, 2], F32) for l in range(2)]

        with ExitStack() as ph:
            cv = sb(ph, "cv", [128, 8, 2], F32); scv = sb(ph, "scv", [128, 8, 2], F32)
            bm = sb(ph, "bm", [128, 24], F32)
            wm = [sb(ph, "wm%d" % i, [128, 8, 512], F32) for i in range(2)]
            for j in range(2):
                P.dma("sp", cv[:, :, j], cvec[j].rearrange("(kc p) -> p kc", p=128), w=["cv"], allow_slow_non_contiguous=True)
            P.act(scv[:], cv[:], AF.Silu, r=["cv"], w=["scv"])
            modrow = sb(ph, "modrow", [2, 3 * D], F32); bmrow = sb(ph, "bmrow", [2, 3 * D], F32)
            for l in range(2):
                P.dma("sp", bmrow[:], b_mod[l:l + 1, :].to_broadcast([2, 3 * D]), w=["bmrow"])
                for ob in range(6):
                    wmt = wm[ob % 2]; wk = "wm%d" % (ob % 2)
                    P.dma("sp", wmt[:], w_mod[l, :, ob * 512:(ob + 1) * 512].rearrange("(kc p) n -> p kc n", p=128), w=[wk])
                    b = P.bank(); bk = "ps%d" % b
                    for kc in range(8):
                        P.mm(ps[b][0:2, :], scv[:, kc, :], wmt[:, kc, :], start=(kc == 0), stop=(kc == 7), r=[wk, "scv"], w=[bk])
                    P.tt("dve", modrow[:, ob * 512:(ob + 1) * 512], ps[b][0:2, :], bmrow[:, ob * 512:(ob + 1) * 512], ALU.add,
                         r=[bk, "bmrow"], w=["modrow"])
                    P.release(b)
                b = P.bank(); bk = "ps%d" % b
                for oc in range(24):
                    P.mm(ps[b][:, oc * 2:oc * 2 + 2], modrow[0:2, oc * 128:(oc + 1) * 128], identf[0:2, 0:2], r=["modrow", "identf"], w=[bk])
                P.cp("dve", mod[l][:], ps[b][:, 0:48].rearrange("p (a b) -> p a b", b=2), r=[bk], w=["mod%d" % l])
                P.release(b)
                P.ts1("dve", sc1[l][:], mod[l][:, 8:16, :], 1.0, ALU.add, r=["mod%d" % l], w=["sc1_%d" % l])
                P.ts1("dve", gta[l][:], mod[l][:, 16:24, :], 1.0 / ALPHA, ALU.mult, r=["mod%d" % l], w=["gta%d" % l])
            dump("mod0", mod[0][:], [128, 24, 2], ["mod0"])
            P.barrier()
        if stop == "0":
            P.emit()
            return nc, dumps

        groups = []
        if "S" in groups_sel:
            groups.append(Group("S", 1, 2048, 0, True))
        if "P" in groups_sel:
            groups.append(Group("P", 4, 256, 1, False))
        xin0 = {"S": xs, "P": xp}
        xfin = {"S": y_s, "P": y_p}
        env = dict(nc=nc, P=P, sb=sb, dump=dump, stop=stop, identf=identf, jrevf=jrevf, identb=identb, jrevb=jrevb,
                   mask1=mask1, mask2=mask2, mask3=mask3, bones=bones, onesf=onesf, k_cos=k_cos, k_sin=k_sin,
                   mod=mod, sc1=sc1, gta=gta, dbg=dbg,
                   W=dict(w_in=w_in, rw_w0=rw_w0, rw_w1=rw_w1, rw_w2=rw_w2, rw_a0=rw_a0, rw_a1=rw_a1, rw_a2=rw_a2,
                          rw_kk=rw_kk, rw_ka=rw_ka, rw_rk=rw_rk, rw_lnx_g=rw_lnx_g, rw_lnx_b=rw_lnx_b,
                          mla_qnorm=mla_qnorm, mla_wuq=mla_wuq, mla_kvnorm=mla_kvnorm, mla_wukv=mla_wukv,
                          lru_conv_w=lru_conv_w, lru_conv_b=lru_conv_b, lru_wa=lru_wa, lru_ba=lru_ba, lru_wx=lru_wx,
                          lru_bx=lru_bx, lru_lambda=lru_lambda, w_out=w_out, ln_g=ln_g, ln_b=ln_b),
                   st_rwkv=st_rwkv, c_ckv=c_ckv, c_kr=c_kr, st_lru=st_lru,
                   o_rwkv=o_rwkv, o_ckv=o_ckv, o_kr=o_kr, o_lru=o_lru)
        for g in groups:
            for l in range(nlayers):
                x_in = xin0[g.name] if l == 0 else xmid[g.name]
                x_out = xfin[g.name] if l == nlayers - 1 else xmid[g.name]
                done = layer(env, g, l, x_in, x_out, "xin%s%d" % (g.name, l), "xin%s%d" % (g.name, l + 1))
                if done:
                    P.emit()
                    return nc, dumps
        P.emit()
    return nc, dumps


def layer(env, g, l, x_in, x_out, kin, kout):
    nc, P, sb, dump, stop = env["nc"], env["P"], env["sb"], env["dump"], env["stop"]
    ps = P.ps
    W = env["W"]
    identb, jrevb, identf, jrevf = env["identb"], env["jrevb"], env["identf"], env["jrevf"]
    T, NT, LT, L, j = g.T, g.NT, g.LT, g.L, g.j
    tag = "%s%d" % (g.name, l)
    w_in = W["w_in"]

    def wslice(c0, c1):
        return w_in[l, :, c0:c1].rearrange("(kc p) n -> p kc n", p=128)

    def ppar(stack, name, src):
        t = sb(stack, name, [128, 2], F32)
        P.dma("sp", t[:], src.rearrange("(fc p) -> p fc", p=128), w=[name], allow_slow_non_contiguous=True)
        return t

    with ExitStack() as lay:
        u = sb(lay, "u", [128, 8, T], BF16)
        y_rw = sb(lay, "y_rw", [128, 2, T], BF16)
        y_lru = sb(lay, "y_lru", [128, 2, T], BF16)
        with ExitStack() as revs:
            ur = sb(revs, "ur", [128, 8, T], BF16)
            with ExitStack() as ph:
                NBUF = 3
                xb = [sb(ph, "xa%d" % i, [128, D], F32) for i in range(NBUF)]
                xn = [sb(ph, "xn%d" % i, [128, D], BF16) for i in range(NBUF)]
                st6 = sb(ph, "st6", [128, 12], F32); mv = sb(ph, "mv", [128, 2], F32); rs = sb(ph, "rs", [128, 1], F32)
                rq1 = sb(ph, "rq1", [128, 1], F32); rq2 = sb(ph, "rq2", [128, 1], F32)

                def stage1(tt):
                    xt = xb[tt % NBUF]; xk = "xa%d" % (tt % NBUF); xnt = xn[tt % NBUF]; xnk = "xn%d" % (tt % NBUF)
                    P.dma("sp", xt[:], x_in[tt * 128:(tt + 1) * 128, :], r=["%s:%d" % (kin, tt)], w=[xk])
                    P.op("dve", lambda E, xt=xt: E.bn_stats(st6[:, 0:6], xt[:, 0:512]), r=[xk], w=["st6a"])
                    P.op("dve", lambda E, xt=xt: E.bn_stats(st6[:, 6:12], xt[:, 512:1024]), r=[xk], w=["st6b"])
                    P.op("dve", lambda E: E.bn_aggr(mv[:], st6[:]), r=["st6a", "st6b"], w=["mv"])
                    P.rsqrt(rs[:], mv[:, 1:2], LN_EPS, 1.0, rq1[:], rq2[:], r=["mv"], w=["rs"], kx="rq1", kt="rq2", small=True)
                    P.ts("dve", xnt[:], xt[:], mv[:, 0:1], rs[:], ALU.subtract, ALU.mult, r=[xk, "mv", "rs"], w=[xnk])

                def stage2(tt):
                    s_, i = divmod(tt, LT)
                    rt = s_ * LT + (LT - 1 - i)
                    xnt = xn[tt % NBUF]; xnk = "xn%d" % (tt % NBUF)
                    for (dst, dk, perm, pk, tpos) in ((u, "u", identb, "identb", tt), (ur, "ur", jrevb, "jrevb", rt)):
                        for half in range(2):
                            b = P.bank(); bk = "ps%d" % b
                            for q in range(4):
                                kc = half * 4 + q
                                P.mm(ps[b][:, q * 128:(q + 1) * 128], xnt[:, kc * 128:(kc + 1) * 128], perm[:], r=[xnk, pk], w=[bk])
                            for q in range(4):
                                kc = half * 4 + q
                                if dk == "u":
                                    P.act(dst[:, kc, tpos * 128:(tpos + 1) * 128], ps[b][:, q * 128:(q + 1) * 128], AF.Identity,
                                          scale=env["sc1"][l][:, kc, j:j + 1], bias=env["mod"][l][:, kc, j:j + 1],
                                          r=[bk, "sc1_%d" % l, "mod%d" % l], w=["%s:%d" % (dk, tpos)])
                                else:
                                    P.ts("dve", dst[:, kc, tpos * 128:(tpos + 1) * 128], ps[b][:, q * 128:(q + 1) * 128],
                                         env["sc1"][l][:, kc, j:j + 1], env["mod"][l][:, kc, j:j + 1], ALU.mult, ALU.add,
                                         r=[bk, "sc1_%d" % l, "mod%d" % l], w=["%s:%d" % (dk, tpos)])
                            P.release(b)

                for tt in range(NT + 1):
                    if tt < NT:
                        stage1(tt)
                    if tt >= 1:
                        stage2(tt - 1)
                dump("u_" + tag, u[:], [128, 8, T], ["u:%d" % t for t in range(NT)])
                dump("ur_" + tag, ur[:], [128, 8, T], ["ur:%d" % t for t in range(NT)])
                P.barrier()
            if stop == "A":
                return True
            ukeys = ["u:%d" % t for t in range(NT)]
            urkeys = ["ur:%d" % t for t in range(NT)]
            skip = env["dbg"].get("skip", "")
            threaded = (g.nseq > 1) and skip == "" and stop in ("", "E") and env["dbg"].get("threads", True)
            if threaded:
                y_mla = sb(revs, "y_mla", [128, 4, T], BF16)
                with ExitStack() as shared:
                    tw = env["dbg"].get("tw", (6, 1, 1))
                    run_gen(interleave_w([(phase_rwkv(env, g, l, u, ur, y_rw, ukeys, urkeys, wslice, ppar, tag, threaded=True, ext=shared), tw[0]),
                                          (phase_lru(env, g, l, u, ur, y_lru, ukeys, urkeys, wslice, ppar, tag, threaded=True, ext=shared), tw[1]),
                                          (phase_mla(env, g, l, u, y_mla, ukeys, wslice, ppar, tag, threaded=True, ext=shared), tw[2])]))
                    P.barrier()
                phase_out(env, g, l, (y_rw, y_mla, y_lru), x_in, x_out, kin, kout, tag)
                return stop == "E"
            if "B" not in skip:
                run_gen(phase_rwkv(env, g, l, u, ur, y_rw, ukeys, urkeys, wslice, ppar, tag))
            if stop == "B":
                return True
            if "C" not in skip:
                run_gen(phase_lru(env, g, l, u, ur, y_lru, ukeys, urkeys, wslice, ppar, tag))
            if stop == "C":
                return True
        y_mla = sb(lay, "y_mla", [128, 4, T], BF16)
        if "D" not in skip:
            run_gen(phase_mla(env, g, l, u, y_mla, ukeys, wslice, ppar, tag))
        if stop == "D":
            return True
        phase_out(env, g, l, (y_rw, y_mla, y_lru), x_in, x_out, kin, kout, tag)
        if stop == "E":
            return True
    return False


def _unpack(env):
    return env["nc"], env["P"], env["sb"], env["dump"], env["W"]


def proj_fm(P, out, w, c0, M, src, t0, N, r, wk):
    for kc in range(8):
        P.mm(out, w[:, kc, c0:c0 + M], src[:, kc, t0:t0 + N], start=(kc == 0), stop=(kc == 7), r=r, w=[wk])


def phase_out(env, g, l, y, x_in, x_out, kin, kout, tag):
    nc, P, sb, dump, W = _unpack(env)
    ps = P.ps
    identf = env["identf"]
    NT, j = g.NT, g.j
    ykeys = ["y:%d" % c for c in range(8)]
    with ExitStack() as ph:
        w_o = sb(ph, "w_o", [128, 8, D], BF16)
        P.dma("pool", w_o[:], W["w_out"][l].rearrange("(kc p) n -> p kc n", p=128), w=["w_o"])
        gate_bc = sb(ph, "gate_bc", [128, D], F32); g_bc = sb(ph, "g_bc", [128, D], F32); b_bc = sb(ph, "b_bc", [128, D], F32)
        P.dma("sp", g_bc[:], W["ln_g"][l:l + 1, :].to_broadcast([128, D]), w=["g_bc"])
        P.dma("sp", b_bc[:], W["ln_b"][l:l + 1, :].to_broadcast([128, D]), w=["b_bc"])
        for half in range(2):
            b = P.bank(); bk = "ps%d" % b
            for q in range(4):
                kc = half * 4 + q
                P.mm(ps[b][:, q * 128:(q + 1) * 128], env["gta"][l][:, kc, j:j + 1].to_broadcast([128, 128]), identf[:],
                     r=["gta%d" % l, "identf"], w=[bk])
            P.cp("act", gate_bc[:, half * 512:(half + 1) * 512], ps[b][:], r=[bk], w=["gate_bc"])
            P.release(b)
        xb = [sb(ph, "xo%d" % i, [128, D], F32) for i in range(3)]
        t1 = [sb(ph, "t1o%d" % i, [128, D], F32) for i in range(3)]
        st6 = sb(ph, "st6o", [128, 12], F32); mv = sb(ph, "mvo", [128, 2], F32); rs = sb(ph, "rso", [128, 1], F32)
        nmr = sb(ph, "nmr", [128, 1], F32); rq1 = sb(ph, "rq1o", [128, 1], F32); rq2 = sb(ph, "rq2o", [128, 1], F32)
        def stage_a(tt):
            xt = xb[tt % 3]; xk = "xo%d" % (tt % 3); tt1 = t1[tt % 3]; tk = "t1o%d" % (tt % 3)
            P.dma("sp", xt[:], x_in[tt * 128:(tt + 1) * 128, :], r=["%s:%d" % (kin, tt)], w=[xk])
            bs = [P.bank(), P.bank()]
            for half in range(2):
                for kc in range(8):
                    yt_, yc_ = (y[0], kc) if kc < 2 else ((y[1], kc - 2) if kc < 6 else (y[2], kc - 6))
                    P.mm(ps[bs[half]][:, :], yt_[:, yc_, tt * 128:(tt + 1) * 128], w_o[:, kc, half * 512:(half + 1) * 512],
                         start=(kc == 0), stop=(kc == 7), r=[ykeys[kc], "w_o"], w=["ps%d" % bs[half]])
            for half in range(2):
                P.tt("dve", tt1[:, half * 512:(half + 1) * 512], ps[bs[half]][:, :], gate_bc[:, half * 512:(half + 1) * 512], ALU.mult,
                     r=["ps%d" % bs[half], "gate_bc"], w=[tk])
                P.release(bs[half])
            P.tt("pool", tt1[:, 0:512], tt1[:, 0:512], xt[:, 0:512], ALU.add, r=[tk, xk], w=[tk + "L"])
            P.tt("dve", tt1[:, 512:1024], tt1[:, 512:1024], xt[:, 512:1024], ALU.add, r=[tk, xk], w=[tk + "R"])

        def stage_a2(tt):
            tt1 = t1[tt % 3]; tk = "t1o%d" % (tt % 3)
            P.op("dve", lambda E, a=tt1: E.bn_stats(st6[:, 6:12], a[:, 512:1024]), r=[tk, tk + "R"], w=["st6ob"])
            P.op("dve", lambda E, a=tt1: E.bn_stats(st6[:, 0:6], a[:, 0:512]), r=[tk, tk + "L"], w=["st6oa"])
            P.op("dve", lambda E: E.bn_aggr(mv[:], st6[:]), r=["st6oa", "st6ob"], w=["mvo"])
            P.rsqrt(rs[:], mv[:, 1:2], LN_EPS / (ALPHA * ALPHA), 1.0, rq1[:], rq2[:], r=["mvo"], w=["rso"], kx="rq1o", kt="rq2o", small=True)
            P.stt("dve", nmr[:], mv[:, 0:1], -1.0, rs[:], ALU.mult, ALU.mult, r=["mvo", "rso"], w=["nmr"])
            P.act(tt1[:], tt1[:], AF.Identity, scale=rs[:, 0:1], bias=nmr[:, 0:1], r=[tk, tk + "L", tk + "R", "rso", "nmr"], w=[tk, tk + "L", tk + "R"])

        def stage_b(tt):
            tt1 = t1[tt % 3]; tk = "t1o%d" % (tt % 3)
            P.tt("dve", tt1[:], tt1[:], g_bc[:], ALU.mult, r=[tk, "g_bc"], w=[tk])
            P.tt("pool", tt1[:], tt1[:], b_bc[:], ALU.add, r=[tk, "b_bc"], w=[tk])
            P.dma("pool", x_out[tt * 128:(tt + 1) * 128, :], tt1[:], r=[tk], w=["%s:%d" % (kout, tt)])

        for tt in range(NT + 1):
            if tt < NT:
                stage_a(tt)
            if tt >= 1:
                stage_b(tt - 1)
            if tt < NT:
                stage_a2(tt)
        P.barrier()


def phase_lru(env, g, l, u, ur, y, ukeys, urkeys, wslice, ppar, tag, threaded=False, ext=None):
    nc, P, sb, dump, W = _unpack(env)
    ps = P.ps
    identf, jrevf = env["identf"], env["jrevf"]
    L, LT, BW, NB, CB = g.L, g.LT, g.BW, g.NB, g.CB
    NS = "NSCAN"
    with (_NoClose(ext) if ext is not None else ExitStack()) as ph:
        w_xl = sb(ph, "w_xl", [128, 8, 256], BF16); w_gl = sb(ph, "w_gl", [128, 8, 256], BF16)
        P.dma("pool", w_xl[:], wslice(C_XL, C_XL + 256), w=["w_xl"])
        P.dma("pool", w_gl[:], wslice(C_GL, C_GL + 256), w=["w_gl"])
        cw = sb(ph, "cw", [128, 2, 4], F32)
        for jj in range(4):
            P.dma("sp", cw[:, :, jj], W["lru_conv_w"][l, jj].rearrange("(fc p) -> p fc", p=128), w=["cw"], allow_slow_non_contiguous=True)
        cbias = ppar(ph, "cbias", W["lru_conv_b"][l])
        Wbd = {}
        for e in range(2):
            for nm, src in (("a", W["lru_wa"]), ("x", W["lru_wx"])):
                t = sb(ph, "wbd%s%d" % (nm, e), [128, 2, 128], BF16)
                k = "wbd%s%d" % (nm, e)
                P.memset("pool", t[:], 0.0, w=[k])
                for gi in range(4):
                    fc, hh = gi // 2, gi % 2
                    P.dma("pool", t[hh * 64:(hh + 1) * 64, fc, hh * 64:(hh + 1) * 64], src[l, e, gi], w=[k])
                Wbd[(e, nm)] = (t, k)
        ba = [ppar(ph, "lba%d" % e, W["lru_ba"][l, e]) for e in range(2)]
        bx = [ppar(ph, "lbx%d" % e, W["lru_bx"][l, e]) for e in range(2)]
        lam = [ppar(ph, "lam%d" % e, W["lru_lambda"][l, e]) for e in range(2)]
        hba = [sb(ph, "hba%d" % e, [128, 2], F32) for e in range(2)]
        hbx = [sb(ph, "hbx%d" % e, [128, 2], F32) for e in range(2)]
        for e in range(2):
            P.ts1("dve", hba[e][:], ba[e][:], 0.5, ALU.mult, r=["lba%d" % e], w=["hba%d" % e])
            P.ts1("dve", hbx[e][:], bx[e][:], 0.5, ALU.mult, r=["lbx%d" % e], w=["hbx%d" % e])
        cneg = [sb(ph, "cneg%d" % e, [128, 2], F32) for e in range(2)]
        hcn = [sb(ph, "hcn%d" % e, [128, 2], F32) for e in range(2)]
        for e in range(2):
            k = "cneg%d" % e
            P.act(cneg[e][:], lam[e][:], AF.Exp, scale=-1.0, r=["lam%d" % e], w=[k])
            P.act(cneg[e][:], cneg[e][:], AF.Ln, bias=1.0, r=[k], w=[k])
            P.ts1("dve", cneg[e][:], cneg[e][:], -8.0, ALU.mult, r=[k], w=[k])
            P.ts1("dve", hcn[e][:], cneg[e][:], 0.5, ALU.mult, r=[k], w=["hcn%d" % e])
        h0 = None
        if g.ctx:
            h0 = [ppar(ph, "lh0%d" % e, env["st_lru"][l, e]) for e in range(2)]
        xlp = sb(ph, "xlp", [128, 2, L + 4], F32); xc = sb(ph, "xc", [128, 2, L], F32); xcb = sb(ph, "xcb", [128, 2, L], BF16)
        hh_ = [sb(ph, "lh%d" % e, [128, 2, L], F32) for e in range(2)]
        NT2 = 2 if g.nseq == 1 else 1
        scr = []
        for i in range(NT2):
            scr.append({nm: sb(ph, "l%s%d" % (nm, i), [128, BW], F32) for nm in ("sr", "si", "aa", "a2", "bt", "hT", "sg", "tsum")})
        git = [0, 0]
        P.memset("dve", xlp[:, :, 0:2], 0.0, w=["xlp"])
        P.memset("dve", xlp[:, :, L + 2:L + 4], 0.0, w=["xlp"])
        for s in range(g.nseq):
            tok0 = s * L
            for e in range(2):
                src, skeys = (u, ukeys) if e == 0 else (ur, urkeys)
                hk = "lh%d" % e
                for blk in range(NB):
                    for fc in range(2):
                        yield from need(P, 1)
                        b = P.bank(); bk = "ps%d" % b
                        proj_fm(P, ps[b][:, 0:BW], w_xl, fc * 128, 128, src, tok0 + blk * BW, BW, ["w_xl"] + skeys, bk)
                        P.cp("act", xlp[:, fc, 2 + blk * BW:2 + (blk + 1) * BW], ps[b][:, 0:BW], r=[bk], w=["xlp"])
                        P.release(b)
                        yield
                dd = [jj - 2 for jj in range(4)] if e == 0 else [2 - jj for jj in range(4)]
                for fc in range(2):
                    P.ts("dve", xc[:, fc, :], xlp[:, fc, 2 + dd[0]:2 + dd[0] + L], cw[:, fc, 0:1], cbias[:, fc:fc + 1], ALU.mult, ALU.add,
                         r=["xlp", "cw", "cbias"], w=["xc"])
                    for jj in range(1, 4):
                        P.stt("dve", xc[:, fc, :], xlp[:, fc, 2 + dd[jj]:2 + dd[jj] + L], cw[:, fc, jj:jj + 1], xc[:, fc, :], ALU.mult, ALU.add,
                              r=["xlp", "cw", "xc"], w=["xc"])
                P.cp("act", xcb[:, 0, :], xc[:, 0, :], r=["xc"], w=["xcb"])
                P.cp("dve", xcb[:, 1, :], xc[:, 1, :], r=["xc"], w=["xcb"])
                yield
                wa_t, wa_k = Wbd[(e, "a")]; wx_t, wx_k = Wbd[(e, "x")]

                def gate_s1(fc, blk, S_, sx):
                    sl = slice(blk * BW, (blk + 1) * BW)
                    sr, si, aa, a2 = S_["sr"], S_["si"], S_["aa"], S_["a2"]
                    b1 = P.bank(); b2 = P.bank()
                    P.mm(ps[b1][:, 0:BW], wa_t[:, fc, :], xcb[:, fc, sl], r=[wa_k, "xcb"], w=["ps%d" % b1])
                    P.mm(ps[b2][:, 0:BW], wx_t[:, fc, :], xcb[:, fc, sl], r=[wx_k, "xcb"], w=["ps%d" % b2])
                    P.act(sr[:], ps[b1][:, 0:BW], AF.Tanh, bias=hba[e][:, fc:fc + 1], scale=0.5, r=["ps%d" % b1, "hba%d" % e], w=["lsr" + sx])
                    P.act(si[:], ps[b2][:, 0:BW], AF.Tanh, bias=hbx[e][:, fc:fc + 1], scale=0.5, r=["ps%d" % b2, "hbx%d" % e], w=["lsi" + sx])
                    P.release(b1); P.release(b2)
                    P.act(aa[:], sr[:], AF.Exp, scale=hcn[e][:, fc:fc + 1], bias=hcn[e][:, fc:fc + 1], r=["lsr" + sx, "hcn%d" % e], w=["laa" + sx])
                    P.act(a2[:], sr[:], AF.Exp, scale=cneg[e][:, fc:fc + 1], bias=cneg[e][:, fc:fc + 1], r=["lsr" + sx, "cneg%d" % e], w=["la2" + sx])

                def gate_s2(fc, blk, S_, sx):
                    sl = slice(blk * BW, (blk + 1) * BW)
                    si, aa, a2, bt = S_["si"], S_["aa"], S_["a2"], S_["bt"]
                    P.act(a2[:], a2[:], AF.Sqrt, bias=1.0, scale=-1.0, r=["la2" + sx], w=["la2" + sx])
                    P.stt("dve", bt[:], si[:], 1.0, a2[:], ALU.add, ALU.mult, r=["la2" + sx, "lsi" + sx], w=["lbt" + sx])
                    P.stt("dve", bt[:], bt[:], 0.5, xc[:, fc, sl], ALU.mult, ALU.mult, r=["lbt" + sx, "xc"], w=["lbt" + sx])
                    if blk == 0:
                        init = h0[e][:, fc:fc + 1] if h0 is not None else 0.0
                        rk = ["lh0%d" % e] if h0 is not None else []
                    else:
                        init = hh_[e][:, fc, blk * BW - 1:blk * BW]
                        rk = [hk]
                    P.op("dve", lambda E, o=hh_[e][:, fc, sl], i0=init, aa=aa, bt=bt: E.tensor_tensor_scan(o, aa[:], bt[:], i0, ALU.mult, ALU.add),
                         r=["laa" + sx, "lbt" + sx] + rk, w=[hk])

                for blk in range(NB):
                    if NT2 == 2:
                        yield from need(P, 4)
                        for fc in range(2):
                            gate_s1(fc, blk, scr[fc], "%d" % fc)
                        yield
                        for fc in range(2):
                            gate_s2(fc, blk, scr[fc], "%d" % fc)
                        yield
                    else:
                        for fc in range(2):
                            yield from need(P, 2)
                            gate_s1(fc, blk, scr[0], "0")
                            gate_s2(fc, blk, scr[0], "0")
                            yield
                if not g.ctx:
                    P.dma("pool", env["o_lru"][s, l, e].rearrange("(fc p) -> p fc", p=128), hh_[e][:, :, L - 1], r=[hk], allow_slow_non_contiguous=True)
            for blk in range(NB):
                sl = slice(blk * BW, (blk + 1) * BW)
                for fc in range(2):
                    S_ = scr[git[1] % NT2]; sx = "%d" % (git[1] % NT2); git[1] += 1
                    hT, sg, tsum = S_["hT"], S_["sg"], S_["tsum"]
                    yield from need(P, 3)
                    bT = P.bank()
                    for ci in range(CB):
                        tr = LT - 1 - (blk * CB + ci)
                        P.mm(ps[bT][:, ci * 128:(ci + 1) * 128], hh_[1][:, fc, tr * 128:(tr + 1) * 128], identf[:], r=["lh1", "identf"], w=["ps%d" % bT])
                    P.cp("act", hT[:], ps[bT][:, 0:BW], r=["ps%d" % bT], w=["lhT" + sx])
                    P.release(bT)
                    bF = P.bank()
                    for ci in range(CB):
                        P.mm(ps[bF][:, ci * 128:(ci + 1) * 128], hT[:, ci * 128:(ci + 1) * 128], jrevf[:], r=["lhT" + sx, "jrevf"], w=["ps%d" % bF])
                    bG = P.bank()
                    proj_fm(P, ps[bG][:, 0:BW], w_gl, fc * 128, 128, u, tok0 + blk * BW, BW, ["w_gl"] + ukeys, "ps%d" % bG)
                    P.act(sg[:], ps[bG][:, 0:BW], AF.Tanh, scale=0.5, r=["ps%d" % bG], w=["lsg" + sx])
                    P.stt("dve", sg[:], sg[:], 1.0, ps[bG][:, 0:BW], ALU.add, ALU.mult, r=["lsg" + sx, "ps%d" % bG], w=["lsg" + sx])
                    P.release(bG)
                    P.tt("dve", tsum[:], ps[bF][:, 0:BW], hh_[0][:, fc, sl], ALU.add, r=["ps%d" % bF, "lh0"], w=["ltsum" + sx])
                    P.release(bF)
                    P.stt("dve", y[:, fc, tok0 + blk * BW:tok0 + (blk + 1) * BW], tsum[:], 0.5, sg[:], ALU.mult, ALU.mult, r=["ltsum" + sx, "lsg" + sx], w=["y:%d" % (6 + fc)])
                    yield
        if not threaded:
            dump("ylru_" + tag, y[:, :, :], [128, 2, g.T], ["y:6", "y:7"])
            P.barrier()


def phase_mla(env, g, l, u, y, ukeys, wslice, ppar, tag, threaded=False, ext=None):
    nc, P, sb, dump, W = _unpack(env)
    ps = P.ps
    identf, onesf = env["identf"], env["onesf"]
    L, LT, BW, NB, CB = g.L, g.LT, g.BW, g.NB, g.CB
    ctx = g.ctx
    K0 = 256 if ctx else 0
    Lk = L + K0
    KT = Lk // 128
    QB = BW
    with (_NoClose(ext) if ext is not None else ExitStack()) as ph:
        w_cq = sb(ph, "w_cq", [128, 8, 256], BF16); P.dma("pool", w_cq[:], wslice(C_CQ, C_CQ + 256), w=["w_cq"])
        w_ck = sb(ph, "w_ck", [128, 8, 128], BF16); P.dma("pool", w_ck[:], wslice(C_CKV, C_CKV + 128), w=["w_ck"])
        w_kr = sb(ph, "w_kr", [128, 8, 96], BF16)
        P.memset("pool", w_kr[:], 0.0, w=["w_kr"])
        P.dma("pool", w_kr[:, :, 64:96], wslice(C_KR, C_KR + 32), w=["w_kr"])
        w_gm = sb(ph, "w_gm", [128, 8, 512], BF16); P.dma("pool", w_gm[:], wslice(C_GM, C_GM + 512), w=["w_gm"])
        wuq = sb(ph, "wuq", [128, 2, 768], BF16)
        P.dma("pool", wuq[:], W["mla_wuq"][l].rearrange("(kc p) n -> p kc n", p=128), w=["wuq"])
        wkk = sb(ph, "wkk", [128, 8, 64], BF16); wkv = sb(ph, "wkv", [128, 8, 64], BF16)
        wv4 = W["mla_wukv"][l].rearrange("r (h x) -> r h x", x=128)
        P.dma("pool", wkk[:], wv4[:, :, 0:64], w=["wkk"])
        P.dma("pool", wkv[:], wv4[:, :, 64:128], w=["wkv"])
        qng = ppar(ph, "qng", W["mla_qnorm"][l])
        kvg = sb(ph, "kvg", [128, 1], F32)
        P.dma("sp", kvg[:], W["mla_kvnorm"][l].rearrange("(p o) -> p o", o=1), w=["kvg"], allow_slow_non_contiguous=True)
        if ctx:
            cosT = sb(ph, "cosT", [128, 512], F32); sinT = sb(ph, "sinT", [128, 512], F32)

            def load_tables(p0, n):
                P.dma("sp", cosT[64:96, 0:n], env["k_cos"][:, p0:p0 + n], w=["cosT"])
                P.dma("sp", sinT[64:96, 0:n], env["k_sin"][:, p0:p0 + n], w=["sinT"])
            w_krs = sb(ph, "w_krs", [128, 8, 96], BF16)
            P.memset("pool", w_krs[:], 0.0, w=["w_krs"])
            P.ts1("dve", w_krs[:, :, 64:96:2], w_kr[:, :, 65:96:2], -1.0, ALU.mult, r=["w_kr"], w=["w_krs"])
            P.cp("dve", w_krs[:, :, 65:96:2], w_kr[:, :, 64:96:2], r=["w_kr"], w=["w_krs"])
            wuqs = sb(ph, "wuqs", [128, 2, 768], BF16)
            P.memset("pool", wuqs[:], 0.0, w=["wuqs"])
            for kc in range(2):
                v4 = wuq[:, kc, :].rearrange("p (h x) -> p h x", x=96)
                v4s = wuqs[:, kc, :].rearrange("p (h x) -> p h x", x=96)
                P.ts1("dve", v4s[:, :, 64:96:2], v4[:, :, 65:96:2], -1.0, ALU.mult, r=["wuq"], w=["wuqs"])
                P.cp("dve", v4s[:, :, 65:96:2], v4[:, :, 64:96:2], r=["wuq"], w=["wuqs"])
        ckvn = sb(ph, "ckvn", [128, Lk], BF16); krall = sb(ph, "krall", [128, Lk], BF16)
        Kt = sb(ph, "Kt", [128, 8, Lk], BF16); V = sb(ph, "Vv", [128, KT, 8, 65], BF16)
        cqn = sb(ph, "cqn", [128, 2, L], BF16); sgm = [sb(ph, "sgm%d" % i, [128, QB], F32) for i in range(3 if KT < 6 else 2)]
        ck = sb(ph, "mck", [128, BW], F32); sq = sb(ph, "msq", [128, BW], F32); sq1 = sb(ph, "msq1", [128, BW], F32)
        rstd = sb(ph, "mrstd", [128, BW], F32); t1 = sb(ph, "mt1", [128, BW], F32); t2 = sb(ph, "mt2", [128, BW], F32)
        if g.nseq == 1:
            sqq = sb(ph, "msqq", [128, BW], F32); rstdq = sb(ph, "mrstdq", [128, BW], F32); t2q = sb(ph, "mt2q", [128, BW], F32)
            kq_ = ("msqq", "mrstdq", "mt2q")
        else:
            sqq, rstdq, t2q = sq, rstd, t2
            kq_ = ("msq", "mrstd", "mt2")
        Pt = [sb(ph, "Pt%d" % i, [128, QB], BF16) for i in range(5 if KT >= 6 else 4)]
        Qh = [sb(ph, "Qh%d" % i, [128, QB], BF16) for i in range(3 if KT < 6 else 2)]
        accs, rden, tmpy = ck, sq, sq1
        P.memset("pool", V[:, :, :, 64:65], 1.0, w=["Vones"])
        if ctx:
            ckc = sb(ph, "ckc", [128, 2, 128], F32); krc = sb(ph, "krc", [128, 2, 96], F32)
        else:
            otile = sb(ph, "otile", [128, CB, 128], F32); okr = sb(ph, "okr", [128, CB, 32], F32)
        qi = 0
        for s in range(g.nseq):
            tok0 = s * L
            if ctx:
                P.dma("sp", ckc[:], env["c_ckv"][l].rearrange("(t p) r -> p t r", p=128), w=["ckc"])
                P.memset("dve", krc[:], 0.0, w=["krc"])
                P.dma("sp", krc[:, :, 64:96], env["c_kr"][l].rearrange("(t p) r -> p t r", p=128), w=["krc"])
                for t in range(2):
                    b = P.bank(); bk = "ps%d" % b
                    P.mm(ps[b][:, 0:128], ckc[:, t, :], identf[:], r=["ckc", "identf"], w=[bk])
                    P.cp("act", ckvn[:, t * 128:(t + 1) * 128], ps[b][:, 0:128], r=[bk], w=["ckvn"])
                    P.release(b)
                    b = P.bank(); bk = "ps%d" % b
                    P.mm(ps[b][0:96, 0:128], krc[:, t, :], identf[:], r=["krc", "identf"], w=[bk])
                    P.cp("act", krall[64:96, t * 128:(t + 1) * 128], ps[b][64:96, 0:128], r=[bk], w=["krall"])
                    P.release(b)
            for blk in range(NB):
                t0 = tok0 + blk * BW
                lsl = slice(blk * BW, (blk + 1) * BW)
                ksl = slice(K0 + blk * BW, K0 + (blk + 1) * BW)
                if ctx:
                    load_tables(blk * BW, BW)
                def ckv_gen(blk=blk, t0=t0, lsl=lsl, ksl=ksl):
                    yield from need(P, 3)
                    b = P.bank(); bk = "ps%d" % b
                    proj_fm(P, ps[b][:, 0:BW], w_ck, 0, 128, u, t0, BW, ["w_ck"] + ukeys, bk)
                    P.cp("act", ck[:], ps[b][:, 0:BW], r=[bk], w=["mck"])
                    P.act(sq[:], ps[b][:, 0:BW], AF.Square, r=[bk], w=["msq"])
                    P.release(b)
                    yield
                    b2 = P.bank(); bk2 = "ps%d" % b2
                    P.mm(ps[b2][:, 0:BW], onesf[:], sq[:], r=["onesf", "msq"], w=[bk2])
                    P.rsqrt(rstd[:], ps[b2][:, 0:BW], 1e-6, 1.0 / 128.0, t1[:], t2[:], r=[bk2], w=["mrstd"], kx="mt1", kt="mt2")
                    P.release(b2)
                    yield
                    P.tt("dve", ck[:], ck[:], rstd[:], ALU.mult, r=["mck", "mrstd"], w=["mck"])
                    P.ts1("dve", ck[:], ck[:], kvg[:, 0:1], ALU.mult, r=["mck", "kvg"], w=["mck"])
                    P.cp("act", ckvn[:, ksl], ck[:], r=["mck"], w=["ckvn"])
                    if not ctx:
                        b3 = P.bank(); bk3 = "ps%d" % b3
                        for ci in range(CB):
                            P.mm(ps[b3][:, ci * 128:(ci + 1) * 128], ck[:, ci * 128:(ci + 1) * 128], identf[:], r=["mck", "identf"], w=[bk3])
                        P.cp("act", otile[:], ps[b3][:, 0:BW].rearrange("p (c r) -> p c r", r=128), r=[bk3], w=["otile"])
                        P.release(b3)
                        P.dma("act", env["o_ckv"][s, l, blk * BW:(blk + 1) * BW, :].rearrange("(c p) r -> p c r", p=128), otile[:], r=["otile"])
                def kr_gen(blk=blk, t0=t0, lsl=lsl, ksl=ksl):
                    yield from need(P, 3)
                    b = P.bank(); bk = "ps%d" % b
                    proj_fm(P, ps[b][0:96, 0:BW], w_kr, 0, 96, u, t0, BW, ["w_kr"] + ukeys, bk)
                    if ctx:
                        b2 = P.bank(); bk2 = "ps%d" % b2
                        proj_fm(P, ps[b2][0:96, 0:BW], w_krs, 0, 96, u, t0, BW, ["w_krs"] + ukeys, bk2)
                        P.tt("dve", t1[64:96, :], ps[b][64:96, 0:BW], cosT[64:96, 0:BW], ALU.mult, r=[bk, "cosT"], w=["mt1"])
                        P.tt("dve", t2[64:96, :], ps[b2][64:96, 0:BW], sinT[64:96, 0:BW], ALU.mult, r=[bk2, "sinT"], w=["mt2"])
                        P.release(b); P.release(b2)
                        P.tt("pool", krall[64:96, ksl], t1[64:96, :], t2[64:96, :], ALU.add, r=["mt1", "mt2"], w=["krall"])
                    else:
                        P.cp("act", t1[64:96, :], ps[b][64:96, 0:BW], r=[bk], w=["mt1"])
                        P.release(b)
                        P.cp("pool", krall[64:96, ksl], t1[64:96, :], r=["mt1"], w=["krall"])
                        b3 = P.bank(); bk3 = "ps%d" % b3
                        for ci in range(CB):
                            P.mm(ps[b3][:, ci * 32:(ci + 1) * 32], t1[64:96, ci * 128:(ci + 1) * 128], identf[64:96, 64:96], r=["mt1", "identf"], w=[bk3])
                        P.cp("act", okr[:], ps[b3][:, 0:CB * 32].rearrange("p (c r) -> p c r", r=32), r=[bk3], w=["okr"])
                        P.release(b3)
                        P.dma("act", env["o_kr"][s, l, blk * BW:(blk + 1) * BW, :].rearrange("(c p) r -> p c r", p=128), okr[:], r=["okr"])
                def cq_gen(blk=blk, t0=t0, lsl=lsl, ksl=ksl):
                    yield from need(P, 3)
                    bq = [P.bank(), P.bank()]
                    for kc in range(2):
                        proj_fm(P, ps[bq[kc]][:, 0:BW], w_cq, kc * 128, 128, u, t0, BW, ["w_cq"] + ukeys, "ps%d" % bq[kc])
                    P.act(sqq[:], ps[bq[0]][:, 0:BW], AF.Square, r=["ps%d" % bq[0]], w=[kq_[0]])
                    P.act(sq1[:], ps[bq[1]][:, 0:BW], AF.Square, r=["ps%d" % bq[1]], w=["msq1"])
                    yield
                    b2 = P.bank(); bk2 = "ps%d" % b2
                    P.mm(ps[b2][:, 0:BW], onesf[:], sqq[:], start=True, stop=False, r=["onesf", kq_[0]], w=[bk2])
                    P.mm(ps[b2][:, 0:BW], onesf[:], sq1[:], start=False, stop=True, r=["onesf", "msq1"], w=[bk2])
                    P.rsqrt(rstdq[:], ps[b2][:, 0:BW], 1e-6, 1.0 / 256.0, t1[:], t2[:], r=[bk2], w=[kq_[1]], kx="mt1", kt="mt2")
                    P.release(b2)
                    yield
                    for kc in range(2):
                        P.tt("dve", t2q[:], ps[bq[kc]][:, 0:BW], rstdq[:], ALU.mult, r=["ps%d" % bq[kc], kq_[1]], w=[kq_[2]])
                        P.release(bq[kc])
                        P.act(cqn[:, kc, lsl], t2q[:], AF.Identity, scale=qng[:, kc:kc + 1], r=[kq_[2], "qng"], w=["cqn"])
                if g.nseq == 1:
                    yield from interleave(ckv_gen(), kr_gen(), cq_gen())
                else:
                    yield from ckv_gen()
                    yield
                    yield from kr_gen()
                    yield
                    yield from cq_gen()
            yield
            nkb = (Lk + 511) // 512
            for kb in range(nkb):
                k0 = kb * 512
                n = min(512, Lk - k0)
                for h in range(8):
                    yield from need(P, 1)
                    b = P.bank(); bk = "ps%d" % b
                    P.mm(ps[b][0:64, 0:n], wkk[:, h, :], ckvn[:, k0:k0 + n], r=["wkk", "ckvn"], w=[bk])
                    P.cp("act" if h % 2 == 0 else "dve", Kt[0:64, h, k0:k0 + n], ps[b][0:64, 0:n], r=[bk], w=["Kt"])
                    P.release(b)
            for h in range(8):
                P.cp("dve" if h % 2 == 0 else "act", Kt[64:96, h, :], krall[64:96, :], r=["krall"], w=["Kt"])
            wkv2 = wkv[:].rearrange("p h x -> p (h x)")
            yield
            for kt in range(KT):
                yield from need(P, 1)
                b = P.bank(); bk = "ps%d" % b
                P.mm(ps[b][:, 0:512], ckvn[:, kt * 128:(kt + 1) * 128], wkv2, r=["wkv", "ckvn"], w=[bk])
                P.cp("act" if kt % 2 == 0 else "dve", V[:, kt, :, 0:64], ps[b][:, 0:512].rearrange("p (h x) -> p h x", x=64), r=[bk], w=["Vv"])
                P.release(b)
            PRE = 3 if KT >= 6 else 2
            DEP = 2 if KT < 6 else 1
            NQ = DEP + 1
            items = [(qb, h) for qb in range(L // QB) for h in range(8)]
            prep = {}
            esets = [(accs, rden, tmpy, "mck", "msq", "msq1")]
            if DEP == 2:
                esets.append((rstd, t1, t2, "mrstd", "mt1", "mt2"))

            def prepare(it):
                qb, h = it
                qs = slice(qb * QB, (qb + 1) * QB)
                if ctx and h == 0:
                    load_tables(qb * QB, QB)
                if h % 2 == 0:
                    gi = (h // 2) % len(sgm)
                    sgt = sgm[gi]; sgk = "sgm%d" % gi
                    b = P.bank(); bk = "ps%d" % b
                    proj_fm(P, ps[b][:, 0:QB], w_gm, (h // 2) * 128, 128, u, tok0 + qb * QB, QB, ["w_gm"] + ukeys, bk)
                    P.act(sgt[:], ps[b][:, 0:QB], AF.Tanh, scale=0.5, r=[bk], w=[sgk])
                    P.stt("dve", sgt[:], sgt[:], 1.0, ps[b][:, 0:QB], ALU.add, ALU.mult, r=[sgk, bk], w=[sgk])
                    P.release(b)
                idx = items.index(it)
                qh = Qh[idx % NQ]; qk = "Qh%d" % (idx % NQ)
                bq_ = P.bank(); bqk = "ps%d" % bq_
                P.mm(ps[bq_][0:96, 0:QB], wuq[:, 0, h * 96:(h + 1) * 96], cqn[:, 0, qs], start=True, stop=False, r=["wuq", "cqn"], w=[bqk])
                P.mm(ps[bq_][0:96, 0:QB], wuq[:, 1, h * 96:(h + 1) * 96], cqn[:, 1, qs], start=False, stop=True, r=["wuq", "cqn"], w=[bqk])
                if ctx:
                    bs_ = P.bank(); bsk = "ps%d" % bs_
                    P.mm(ps[bs_][0:96, 0:QB], wuqs[:, 0, h * 96:(h + 1) * 96], cqn[:, 0, qs], start=True, stop=False, r=["wuqs", "cqn"], w=[bsk])
                    P.mm(ps[bs_][0:96, 0:QB], wuqs[:, 1, h * 96:(h + 1) * 96], cqn[:, 1, qs], start=False, stop=True, r=["wuqs", "cqn"], w=[bsk])
                    P.cp("act", qh[0:64, :], ps[bq_][0:64, 0:QB], r=[bqk], w=[qk])
                    P.tt("dve", t1[64:96, 0:QB], ps[bq_][64:96, 0:QB], cosT[64:96, 0:QB], ALU.mult, r=[bqk, "cosT"], w=["mt1"])
                    P.tt("dve", t2[64:96, 0:QB], ps[bs_][64:96, 0:QB], sinT[64:96, 0:QB], ALU.mult, r=[bsk, "sinT"], w=["mt2"])
                    P.release(bq_); P.release(bs_)
                    P.tt("pool", qh[64:96, :], t1[64:96, 0:QB], t2[64:96, 0:QB], ALU.add, r=["mt1", "mt2"], w=[qk])
                else:
                    P.cp("act", qh[0:96, :], ps[bq_][0:96, 0:QB], r=[bqk], w=[qk])
                    P.release(bq_)
                prep[it] = (qh, qk)

            def tail1(p):
                ac, rd, tm, ka, kr_, km = esets[p["idx"] % len(esets)]
                back = "ps%d" % p["bacc"]
                P.cp("act", ac[0:65, 0:QB], ps[p["bacc"]][0:65, 0:QB], r=[back], w=[ka])
                P.release(p["bacc"])
                P.op("dve", lambda E, ac=ac, rd=rd: E.reciprocal(rd[64:65, 0:QB], ac[64:65, 0:QB]), r=[ka], w=[kr_])

            def tail2(p):
                ac, rd, tm, ka, kr_, km = esets[p["idx"] % len(esets)]
                qb, h = p["it"]
                bb = P.bank(); bbk = "ps%d" % bb
                P.mm(ps[bb][0:64, 0:QB], onesf[64:65, 0:64], rd[64:65, 0:QB], r=["onesf", kr_], w=[bbk])
                pb = (h % 2) * 64; ch = 2 + h // 2
                P.tt("dve", tm[pb:pb + 64, 0:QB], ac[0:64, 0:QB], ps[bb][0:64, 0:QB], ALU.mult, r=[ka, bbk], w=[km])
                P.release(bb)
                gi = (h // 2) % len(sgm)
                sgt = sgm[gi]; sgk = "sgm%d" % gi
                P.stt("dve", y[pb:pb + 64, ch - 2, tok0 + qb * QB:tok0 + (qb + 1) * QB], tm[pb:pb + 64, 0:QB], 0.5, sgt[pb:pb + 64, :], ALU.mult, ALU.mult,
                      r=[km, sgk], w=["y:%d" % ch])

            yield
            yield from need(P, 3)
            for i in range(min(DEP, len(items))):
                prepare(items[i])
            pending = []
            K1 = min(2, KT - 1)
            GAP2 = 10 if KT >= 14 else KT

            def run_pending(slot):
                for p in list(pending):
                    if p["stage"] == 0 and slot >= p["s1"]:
                        tail1(p); p["stage"] = 1
                    if p["stage"] == 1 and slot >= p["s1"] + GAP2:
                        tail2(p); pending.remove(p)

            for idx, it in enumerate(items):
                yield
                yield from need(P, 6 if KT >= 6 else 5)
                qb, h = it
                qh, qk = prep.pop(it)
                sbank = {}

                def issue_S(kt, qh=qh, qk=qk, h=h):
                    bS = P.bank()
                    P.mm(ps[bS][:, 0:QB], Kt[0:96, h, kt * 128:(kt + 1) * 128], qh[0:96, :], r=["Kt", qk], w=["ps%d" % bS])
                    sbank[kt] = bS

                for kt in range(min(PRE, KT)):
                    issue_S(kt)
                if idx + DEP < len(items):
                    prepare(items[idx + DEP])
                bacc = P.bank(); back = "ps%d" % bacc
                for kt in range(KT):
                    if kt + PRE < KT:
                        issue_S(kt + PRE)
                    run_pending(idx * KT + kt)
                    bS = sbank.pop(kt)
                    pt = Pt[kt % len(Pt)]; pk = "Pt%d" % (kt % len(Pt))
                    P.act(pt[:], ps[bS][:, 0:QB], AF.Exp, scale=SM_SCALE, r=["ps%d" % bS], w=[pk])
                    P.release(bS)
                    P.mm(ps[bacc][0:65, 0:QB], V[:, kt, h, :], pt[:], start=(kt == 0), stop=(kt == KT - 1), r=["Vv", "Vones", pk], w=[back])
                pending.append(dict(it=it, idx=idx, bacc=bacc, stage=0, s1=(idx + 1) * KT + K1))
            run_pending(10 ** 9)
        if not threaded:
            dump("ymla_" + tag, y[:, :, :], [128, 4, g.T], ["y:2", "y:3", "y:4", "y:5"])
            P.barrier()


def interleave(*gens):
    gens = list(gens)
    while gens:
        alive = []
        for gn in gens:
            try:
                next(gn)
                alive.append(gn)
            except StopIteration:
                pass
        gens = alive
        yield


class _NoClose:
    def __init__(self, st):
        self.st = st

    def __enter__(self):
        return self.st

    def __exit__(self, *a):
        return False


def need(P, n):
    while len(P.free_banks) < n:
        yield


def interleave_w(pairs):
    pairs = list(pairs)
    while pairs:
        alive = []
        for gn, wgt in pairs:
            ok = True
            for _ in range(wgt):
                try:
                    next(gn)
                except StopIteration:
                    ok = False
                    break
            if ok:
                alive.append((gn, wgt))
        pairs = alive
        yield


def run_gen(gn):
    for _ in gn:
        pass


def phase_rwkv(env, g, l, u, ur, y, ukeys, urkeys, wslice, ppar, tag, threaded=False, ext=None):
    nc, P, sb, dump, W = _unpack(env)
    ps = P.ps
    identf, identb, jrevb, bones, onesf = env["identf"], env["identb"], env["jrevb"], env["bones"], env["onesf"]
    mask1, mask2, mask3 = env["mask1"], env["mask2"], env["mask3"]
    L, LT = g.L, g.LT
    BW, CB = 256, 2
    NB = L // BW
    nth = 2 if (g.nseq > 1 and not threaded) else 1
    NBP = 2 if NB > 1 else 1
    NTR = 2 if g.nseq == 1 else 1

    def rk(b, i):
        return "ps%d" % b

    def rks(b, n):
        return ["ps%d" % b]

    with (_NoClose(ext) if ext is not None else ExitStack()) as ph:
        w_rk = sb(ph, "w_rk", [128, 8, 512], BF16); P.dma("pool", w_rk[:], wslice(C_R, C_R + 512), w=["w_rk"])
        w_vg = sb(ph, "w_vg", [128, 8, 256], BF16); P.dma("pool", w_vg[:], wslice(C_V, C_V + 256), w=["w_vg"])
        w_g_own = sb(ph, "w_g_own", [128, 8, 256], BF16); P.dma("pool", w_g_own[:], wslice(C_G, C_G + 256), w=["w_g_own"])
        lo1 = [sb(ph, "lo1_%d" % e, [128, 8, 128], BF16) for e in range(2)]
        lo2 = [sb(ph, "lo2_%d" % e, [128, 256], BF16) for e in range(2)]
        for e in range(2):
            P.dma("pool", lo1[e][:, :, 0:64], W["rw_w1"][l, e].rearrange("(kc p) r -> p kc r", p=128), w=["lo1_%d" % e])
            P.dma("pool", lo1[e][:, :, 64:128], W["rw_a1"][l, e].rearrange("(kc p) r -> p kc r", p=128), w=["lo1_%d" % e])
            P.dma("pool", lo2[e][0:64, :], W["rw_w2"][l, e], w=["lo2_%d" % e])
            P.dma("pool", lo2[e][64:128, :], W["rw_a2"][l, e], w=["lo2_%d" % e])
        pw0 = [ppar(ph, "pw0_%d" % e, W["rw_w0"][l, e]) for e in range(2)]
        pa0 = [ppar(ph, "pa0_%d" % e, W["rw_a0"][l, e]) for e in range(2)]
        hpw0 = [sb(ph, "hpw0_%d" % e, [128, 2], F32) for e in range(2)]
        hpa0 = [sb(ph, "hpa0_%d" % e, [128, 2], F32) for e in range(2)]
        for e in range(2):
            P.ts1("dve", hpw0[e][:], pw0[e][:], 0.5, ALU.mult, r=["pw0_%d" % e], w=["hpw0_%d" % e])
            P.ts1("dve", hpa0[e][:], pa0[e][:], 0.5, ALU.mult, r=["pa0_%d" % e], w=["hpa0_%d" % e])
        pkk = ppar(ph, "pkk", W["rw_kk"][l]); pka = ppar(ph, "pka", W["rw_ka"][l]); prk = ppar(ph, "prk", W["rw_rk"][l])
        plg = ppar(ph, "plg", W["rw_lnx_g"][l]); plb = ppar(ph, "plb", W["rw_lnx_b"][l])

        class BufSet:
            pass

        ND = env["dbg"].get("ndummy", 0)
        dbank = P.bank() if ND else None

        def dummies(n):
            for _ in range(n * ND):
                P.pe_cont = True
                P.op("pe", lambda E: E.matmul(ps[dbank][:, :], identb[:], mask1[:], start=False, stop=False), (), ())
                P.pe_cont = False

        def alloc_set(tid):
            B = BufSet()
            B.tid = tid
            B.k = lambda name: "%s@%d" % (name, tid)
            n = lambda name: "%s_t%d" % (name, tid)
            B.QR = [[sb(ph, n("QR%d%d" % (bp, e)), [128, 2, CB, 2, 128], BF16) for e in range(2)] for bp in range(NBP)]
            B.AK = [[sb(ph, n("AK%d%d" % (bp, e)), [128, 2, CB, 2, 128], BF16) for e in range(2)] for bp in range(NBP)]
            B.AKgT = [[sb(ph, n("AKgT%d%d" % (bp, e)), [128, CB, 2, 2, 128], BF16) for e in range(2)] for bp in range(NBP)]
            B.Vt = [[sb(ph, n("Vt%d%d" % (bp, e)), [128, CB, 256], BF16) for e in range(2)] for bp in range(NBP)]
            B.GE = [sb(ph, n("GE%d" % bp), [128, 2, 2, CB], F32) for bp in range(NBP)]
            B.yT = [sb(ph, n("yT%d" % e), [128, LT, 256], BF16) for e in range(2)]
            B.tr = []
            for ti in range(NTR):
                T = BufSet()
                T.sfx = "#%d" % ti
                T.tl = sb(ph, n("tl%d" % ti), [128, BW], BF16)
                for nm in ("rT", "kT", "kkT", "sgw", "aT", "cs", "Em", "tq"):
                    setattr(T, nm, sb(ph, n("%s%d" % (nm, ti)), [128, BW], F32))
                T.Ebuf = sb(ph, n("Ebuf%d" % ti), [128, CB, 129], F32)
                T.Ag = sb(ph, n("Ag%d" % ti), [128, BW], BF16); T.Kg = sb(ph, n("Kg%d" % ti), [128, BW], BF16)
                P.memset("dve", T.Ebuf[:, :, 0:1], 1.0, w=[B.k("Ebuf0" + T.sfx)])
                B.tr.append(T)
            B.MA = [[sb(ph, n("MA%d%d" % (par, e)), [128, 4, 2, 128], BF16) for e in range(2)] for par in range(2)]
            B.LK = [[sb(ph, n("LK%d%d" % (par, e)), [128, 4, 2, 128], BF16) for e in range(2)] for par in range(2)]
            B.Pm = [[sb(ph, n("Pm%d%d" % (par, e)), [128, 4, 128], BF16) for e in range(2)] for par in range(2)]
            B.Nn = [sb(ph, n("Nn%d" % e), [128, 4, 128], BF16) for e in range(2)]
            B.M2 = [[sb(ph, n("M2_%d%d" % (e, i)), [128, 4, 128], BF16) for i in range(2)] for e in range(2)]
            B.N2 = [[sb(ph, n("N2_%d%d" % (e, i)), [128, 4, 128], BF16) for i in range(2)] for e in range(2)]
            B.W1 = sb(ph, n("W1"), [128, 512], BF16); B.Uu = sb(ph, n("Uu"), [128, 512], BF16)
            B.H32 = sb(ph, n("H32"), [128, 4, 64], F32); B.Hbf = sb(ph, n("Hbf"), [128, 4, 64], BF16)
            if not g.ctx:
                B.SS = sb(ph, n("SS"), [128, 512], F32)
                B.ssk = B.k("SS")
                B.So = B.SS[0:64, 0:512].rearrange("p (a h k) -> p a h k", a=4, h=2)
            return B

        def v3(t):
            return t[:].rearrange("p (c t) -> p c t", t=128)

        def derive_gen(B, s, blk, e):
            T = B.tr[e % NTR]
            bp = blk % NBP
            k = lambda name: B.k(name + T.sfx)
            kb = lambda name: B.k("%s.%d" % (name, bp))
            tok0 = s * L
            src, skeys = (u, ukeys) if e == 0 else (ur, urkeys)
            t0 = tok0 + blk * BW
            ek = "%d" % e
            rT, kT, kkT, sgw, aT, cs, Em, tq, tl, Ebuf = T.rT, T.kT, T.kkT, T.sgw, T.aT, T.cs, T.Em, T.tq, T.tl, T.Ebuf
            QRe, AKe, AKgTe, Vte, GEb = B.QR[bp][e], B.AK[bp][e], B.AKgT[bp][e], B.Vt[bp][e], B.GE[bp]
            Eend, kka, kdir = sgw, tq, cs
            cs3, Em3, Eend3 = v3(cs), v3(Em), v3(Eend)
            for half in range((CB + 1) // 2):
                yield from need(P, 2)
                b = P.bank(); bk = "ps%d" % b
                nci = min(2, CB - half * 2)
                for q in range(nci):
                    ci = half * 2 + q
                    for kc in range(8):
                        P.mm(ps[b][:, q * 256:(q + 1) * 256], src[:, kc, t0 + ci * 128:t0 + (ci + 1) * 128], w_vg[:, kc, 0:256],
                             start=(kc == 0), stop=(kc == 7), r=["w_vg"] + skeys, w=[rk(b, q)])
                P.cp("act", Vte[:, half * 2:half * 2 + nci, :], ps[b][:, 0:nci * 256].rearrange("p (c v) -> p c v", v=256),
                     r=rks(b, nci), w=[kb("Vt" + ek)])
                P.release(b)
            b = P.bank(); bk = "ps%d" % b
            proj_fm(P, ps[b][:, 0:BW], lo1[e], 0, 128, src, t0, BW, ["lo1_" + ek] + skeys, bk)
            P.act(tl[0:64, :], ps[b][0:64, 0:BW], AF.Tanh, r=[bk], w=[k("tl")])
            P.cp("dve", tl[64:128, :], ps[b][64:128, 0:BW], r=[bk], w=[k("tl")])
            P.release(b)
            yield
            for fc in range(2):
                fs = slice(fc * 128, (fc + 1) * 128)
                yield from need(P, 4)
                bw_ = P.bank(); ba_ = P.bank()
                P.mm(ps[bw_][:, 0:BW], lo2[e][0:64, fs], tl[0:64, :], r=["lo2_" + ek, k("tl")], w=["ps%d" % bw_])
                P.mm(ps[ba_][:, 0:BW], lo2[e][64:128, fs], tl[64:128, :], r=["lo2_" + ek, k("tl")], w=["ps%d" % ba_])
                br_ = P.bank(); bk_ = P.bank()
                proj_fm(P, ps[br_][:, 0:BW], w_rk, fc * 128, 128, src, t0, BW, ["w_rk"] + skeys, "ps%d" % br_)
                proj_fm(P, ps[bk_][:, 0:BW], w_rk, 256 + fc * 128, 128, src, t0, BW, ["w_rk"] + skeys, "ps%d" % bk_)
                P.act(sgw[:], ps[bw_][:, 0:BW], AF.Tanh, bias=hpw0[e][:, fc:fc + 1], scale=0.5, r=["ps%d" % bw_, "hpw0_" + ek], w=[k("sgw")])
                P.act(aT[:], ps[ba_][:, 0:BW], AF.Tanh, bias=hpa0[e][:, fc:fc + 1], scale=0.5, r=["ps%d" % ba_, "hpa0_" + ek], w=[k("aT")])
                P.release(bw_); P.release(ba_)
                P.ts("pool", sgw[:], sgw[:], 0.5, 0.5, ALU.mult, ALU.add, r=[k("sgw")], w=[k("sgw")])
                P.ts("pool", aT[:], aT[:], 0.5, 0.5, ALU.mult, ALU.add, r=[k("aT")], w=[k("aT")])
                P.cp("act", rT[:], ps[br_][:, 0:BW], r=["ps%d" % br_], w=[k("rT")])
                P.cp("dve", kT[:], ps[bk_][:, 0:BW], r=["ps%d" % bk_], w=[k("kT")])
                P.release(br_); P.release(bk_)
                yield
                P.ts1("dve", kkT[:], kT[:], pkk[:, fc:fc + 1], ALU.mult, r=[k("kT"), "pkk"], w=[k("kkT")])
                P.tt("pool", tq[:], kkT[:], kkT[:], ALU.mult, r=[k("kkT")], w=[k("tq")])
                bs_ = P.bank()
                P.mm(ps[bs_][:, 0:BW], bones[:], tq[:], r=["bones", k("tq")], w=["ps%d" % bs_])
                P.rsqrt(tq[:], ps[bs_][:, 0:BW], 1e-24, 1.0, Em[:], cs[:], r=["ps%d" % bs_], w=[k("tq")], kx=k("Em"), kt=k("cs"))
                P.release(bs_)
                for ci in range(CB):
                    P.op("dve", lambda E, ci=ci: E.tensor_tensor_scan(cs[:, ci * 128:(ci + 1) * 128], onesf[:, 0:128],
                                                                      sgw[:, ci * 128:(ci + 1) * 128], 0.0, ALU.mult, ALU.add),
                         r=["onesf", k("sgw")], w=[k("cs")])
                yield
                P.tt("dve", kkT[:], kkT[:], tq[:], ALU.mult, r=[k("kkT"), k("tq")], w=[k("kkT")])
                P.act(Ebuf[:, :, 1:129], cs3, AF.Exp, scale=-WDEC, r=[k("cs")], w=[k("Ebuf")])
                P.act(Em[:], cs[:], AF.Exp, scale=WDEC, r=[k("cs")], w=[k("Em")])
                P.cp("dve", GEb[:, e, fc, :], Ebuf[:, :, 128], r=[k("Ebuf")], w=[kb("GE")])
                P.tt("dve", Eend3, Em3, GEb[:, e, fc, :].unsqueeze(2).to_broadcast([128, CB, 128]), ALU.mult, r=[k("Em"), kb("GE")], w=[k("sgw")])
                P.tt("pool", kka[:], kkT[:], aT[:], ALU.mult, r=[k("kkT"), k("aT")], w=[k("tq")])
                P.ts("dve", aT[:], aT[:], 1.0, pka[:, fc:fc + 1], ALU.subtract, ALU.mult, r=[k("aT"), "pka"], w=[k("aT")])
                P.stt("dve", kdir[:], aT[:], 1.0, kT[:], ALU.add, ALU.mult, r=[k("aT"), k("kT")], w=[k("cs")])
                yield
                P.tt("dve", QRe[:, fc, :, 0, :], v3(kkT), Ebuf[:, :, 0:128], ALU.mult, r=[k("kkT"), k("Ebuf"), k("Ebuf0")], w=[kb("QR" + ek)])
                P.tt("dve", QRe[:, fc, :, 1, :], v3(rT), Ebuf[:, :, 1:129], ALU.mult, r=[k("rT"), k("Ebuf")], w=[kb("QR" + ek)])
                P.tt("pool", AKe[:, fc, :, 0, :], v3(kka), Em3, ALU.mult, r=[k("tq"), k("Em")], w=[kb("AK" + ek)])
                P.tt("pool", AKe[:, fc, :, 1, :], v3(kdir), Em3, ALU.mult, r=[k("cs"), k("Em")], w=[kb("AK" + ek)])
                P.tt("dve", T.Ag[:], kka[:], Eend[:], ALU.mult, r=[k("tq"), k("sgw")], w=[k("Ag")])
                P.tt("pool", T.Kg[:], kdir[:], Eend[:], ALU.mult, r=[k("cs"), k("sgw")], w=[k("Kg")])
                yield
                for ci in range(CB):
                    yield from need(P, 1)
                    b = P.bank()
                    P.mm(ps[b][:, 0:128], T.Ag[:, ci * 128:(ci + 1) * 128], identb[:], r=[k("Ag"), "identb"], w=[rk(b, 0)])
                    P.mm(ps[b][:, 128:256], T.Kg[:, ci * 128:(ci + 1) * 128], identb[:], r=[k("Kg"), "identb"], w=[rk(b, 1)])
                    P.cp("act", AKgTe[:, ci, :, fc, :], ps[b][:, 0:256].rearrange("p (a x) -> p a x", x=128), r=rks(b, 2), w=[kb("AKgT" + ek)])
                    P.release(b)
                yield

        def gn_gen(B, blk, ci, par):
            bp = blk % NBP
            k0 = B.k
            k = lambda name: (B.k("%s.%d" % (name, bp)) if name[:2] in ("QR", "AK", "Vt", "GE") else B.k(name))
            QR, AK, MA, LK, Pm, Nn, M2, N2 = B.QR[bp], B.AK[bp], B.MA[par], B.LK[par], B.Pm[par], B.Nn, B.M2, B.N2
            pk = "%d" % par
            for e in range(2):
                ek = "%d" % e
                for fc in range(2):
                    yield from need(P, 3)
                    b1 = P.bank(); b2 = P.bank()
                    for hh in range(2):
                        pb = hh * 64
                        qr2 = QR[e][pb:pb + 64, fc, ci, :, :].rearrange("p a t -> p (a t)")
                        P.mm(ps[b1][:, hh * 256:(hh + 1) * 256], AK[e][pb:pb + 64, fc, ci, 0, :], qr2, r=[k("AK" + ek), k("QR" + ek)], w=[rk(b1, hh)])
                        P.mm(ps[b2][:, hh * 256:(hh + 1) * 256], AK[e][pb:pb + 64, fc, ci, 1, :], qr2, r=[k("AK" + ek), k("QR" + ek)], w=[rk(b2, hh)])
                    P.tt("dve", MA[e][:, fc * 2:fc * 2 + 2, :, :].rearrange("p h a t -> p (h a t)"), ps[b1][:, :], mask1[:], ALU.mult,
                         r=rks(b1, 2) + ["mask1"], w=[k("MA" + pk + ek)])
                    P.tt("dve", LK[e][:, fc * 2:fc * 2 + 2, :, :].rearrange("p h a t -> p (h a t)"), ps[b2][:, :], mask2[:], ALU.mult,
                         r=rks(b2, 2) + ["mask2"], w=[k("LK" + pk + ek)])
                    P.release(b1); P.release(b2)
                b3 = P.bank()
                for h in range(4):
                    fc, pb = h // 2, (h % 2) * 64
                    P.mm(ps[b3][:, h * 128:(h + 1) * 128], QR[e][pb:pb + 64, fc, ci, 0, :], AK[e][pb:pb + 64, fc, ci, 0, :],
                         r=[k("AK" + ek), k("QR" + ek)], w=[rk(b3, h)])
                P.tt("dve", Nn[e][:].rearrange("p h t -> p (h t)"), ps[b3][:, :], mask3[:], ALU.mult, r=rks(b3, 4) + ["mask3"], w=[k("Nn" + ek)])
                P.release(b3)
                yield
            Mc, Nc, Mk, Nk = {}, {}, {}, {}
            for e in range(2):
                ek = "%d" % e
                P.tt("pool", Pm[e][:], MA[e][:, :, 0, :], identb[:].unsqueeze(1).to_broadcast([128, 4, 128]), ALU.add,
                     r=[k("MA" + pk + ek), "identb"], w=[k("Pm" + pk + ek)])
                Mc[e] = [MA[e][:, h, 0, :] for h in range(4)]; Mk[e] = k("MA" + pk + ek)
                Nc[e] = [Nn[e][:, h, :] for h in range(4)]; Nk[e] = k("Nn" + ek)
            for lev in range(6):
                i2 = lev % 2
                bN, bM, bP = {}, {}, {}
                yield from need(P, 4)
                for e in range(2):
                    bN[e] = P.bank()
                    if lev < 5:
                        bM[e] = P.bank()
                for h in range(4):
                    for e in range(2):
                        P.mm(ps[bN[e]][:, h * 128:(h + 1) * 128], Mc[e][h], Nc[e][h], r=[Mk[e], Nk[e]], w=[rk(bN[e], h)])
                        if lev < 5:
                            P.mm(ps[bM[e]][:, h * 128:(h + 1) * 128], Nc[e][h], Mc[e][h], r=[Mk[e], Nk[e]], w=[rk(bM[e], h)])
                dummies(1)
                for e in range(2):
                    n2k = k("N2_%d%d" % (e, i2))
                    P.cp("act", N2[e][i2][:].rearrange("p h t -> p (h t)"), ps[bN[e]][:, :], r=rks(bN[e], 4), w=[n2k])
                    P.release(bN[e])
                if lev < 5:
                    for e in range(2):
                        m2k = k("M2_%d%d" % (e, i2))
                        P.cp("dve" if e == 0 else "act", M2[e][i2][:].rearrange("p h t -> p (h t)"), ps[bM[e]][:, :], r=rks(bM[e], 4), w=[m2k])
                        P.release(bM[e])
                yield
                yield from need(P, 2)
                for e in range(2):
                    bP[e] = P.bank()
                for h in range(4):
                    for e in range(2):
                        P.mm(ps[bP[e]][:, h * 128:(h + 1) * 128], N2[e][i2][:, h, :], Pm[e][:, h, :],
                             r=[k("N2_%d%d" % (e, i2)), k("Pm" + pk + "%d" % e)], w=[rk(bP[e], h)])
                for e in range(2):
                    P.tt("dve", Pm[e][:].rearrange("p h t -> p (h t)"), Pm[e][:].rearrange("p h t -> p (h t)"), ps[bP[e]][:, :], ALU.add,
                         r=[k("Pm" + pk + "%d" % e)] + rks(bP[e], 4), w=[k("Pm" + pk + "%d" % e)])
                    P.release(bP[e])
                for e in range(2):
                    Mc[e] = [M2[e][i2][:, h, :] for h in range(4)]; Mk[e] = k("M2_%d%d" % (e, i2))
                    Nc[e] = [N2[e][i2][:, h, :] for h in range(4)]; Nk[e] = k("N2_%d%d" % (e, i2))
                yield

        def rec_gen(B, blk, ci, par, tile_i):
            bp = blk % NBP
            k = lambda name: (B.k("%s.%d" % (name, bp)) if name[:2] in ("QR", "AK", "Vt", "GE") else B.k(name))
            QR, AKgT, Vt, MA, LK, Pm = B.QR[bp], B.AKgT[bp], B.Vt[bp], B.MA[par], B.LK[par], B.Pm[par]
            W1, Uu, H32, Hbf, GE = B.W1, B.Uu, B.H32, B.Hbf, B.GE[bp]
            pk = "%d" % par
            chains = [(e, h) for h in range(4) for e in range(2)]
            yield from need(P, 1)
            bW = P.bank()
            for (e, h) in chains:
                ek = "%d" % e
                fc, pb = h // 2, (h % 2) * 64
                c = e * 4 + h
                cb = c * 64
                P.mm(ps[bW][:, cb:cb + 64], QR[e][pb:pb + 64, fc, ci, 0, :], Hbf[pb:pb + 64, e * 2 + fc, :], start=True, stop=False,
                     r=[k("QR" + ek), k("Hbf")], w=[rk(bW, c)])
                P.mm(ps[bW][:, cb:cb + 64], LK[e][:, h, 0, :], Vt[e][:, ci, h * 64:(h + 1) * 64], start=False, stop=True,
                     r=[k("LK" + pk + ek), k("Vt" + ek)], w=[rk(bW, c)])
            P.cp("act", W1[:], ps[bW][:, :], r=rks(bW, 8), w=[k("W1")])
            P.release(bW)
            yield
            yield from need(P, 1)
            bU = P.bank()
            for (e, h) in chains:
                c = e * 4 + h
                cb = c * 64
                P.mm(ps[bU][:, cb:cb + 64], Pm[e][:, h, :], W1[:, cb:cb + 64], r=[k("Pm" + pk + "%d" % e), k("W1")], w=[rk(bU, c)])
            P.ts1("dve", Uu[:], ps[bU][:, :], -1.0, ALU.mult, r=rks(bU, 8), w=[k("Uu")])
            P.release(bU)
            yield
            yield from need(P, 2)
            bY = P.bank(); bH = P.bank()
            for fc in range(2):
                for e in range(2):
                    ek = "%d" % e
                    c0 = (e * 4 + fc * 2) * 64
                    P.mm(ps[bH][:, c0:c0 + 128], AKgT[e][:, ci, 0, fc, :], Uu[:, c0:c0 + 128], start=True, stop=False,
                         r=[k("AKgT" + ek), k("Uu")], w=[rk(bH, 0)])
                    P.mm(ps[bH][:, c0:c0 + 128], AKgT[e][:, ci, 1, fc, :], Vt[e][:, ci, fc * 128:(fc + 1) * 128], start=False, stop=True,
                         r=[k("AKgT" + ek), k("Vt" + ek)], w=[rk(bH, 0)])
            for (e, h) in chains:
                ek = "%d" % e
                fc, pb = h // 2, (h % 2) * 64
                c = e * 4 + h
                cb = c * 64
                P.mm(ps[bY][:, cb:cb + 64], QR[e][pb:pb + 64, fc, ci, 1, :], Hbf[pb:pb + 64, e * 2 + fc, :], start=True, stop=False,
                     r=[k("QR" + ek), k("Hbf")], w=[rk(bY, c)])
                P.mm(ps[bY][:, cb:cb + 64], MA[e][:, h, 1, :], Uu[:, cb:cb + 64], start=False, stop=False, r=[k("MA" + pk + ek), k("Uu")], w=[rk(bY, c)])
                P.mm(ps[bY][:, cb:cb + 64], LK[e][:, h, 1, :], Vt[e][:, ci, h * 64:(h + 1) * 64], start=False, stop=True,
                     r=[k("LK" + pk + ek), k("Vt" + ek)], w=[rk(bY, c)])
            yield
            bH4 = ps[bH][:, :].rearrange("p (a hh v) -> p a hh v", hh=2, v=64)
            for hh in range(2):
                pb = hh * 64
                gb = GE[pb:pb + 64, :, :, ci].rearrange("p e f -> p (e f)").unsqueeze(2).to_broadcast([64, 4, 64])
                P.tt("dve", H32[pb:pb + 64, :, :], H32[pb:pb + 64, :, :], gb, ALU.mult, r=[k("H32"), k("GE")], w=[k("H32")])
                P.tt("dve", H32[pb:pb + 64, :, :], H32[pb:pb + 64, :, :], bH4[pb:pb + 64, :, hh, :], ALU.add, r=[k("H32")] + rks(bH, 8), w=[k("H32")])
            P.release(bH)
            P.cp("act", Hbf[:], H32[:], r=[k("H32")], w=[k("Hbf")])
            for e in range(2):
                P.cp("act" if e == 0 else "pool", B.yT[e][:, tile_i, :], ps[bY][:, e * 256:(e + 1) * 256], r=rks(bY, 8), w=[k("yT%d" % e)]) \
                    if e == 0 else P.cp("act", B.yT[e][:, tile_i, :], ps[bY][:, e * 256:(e + 1) * 256], r=rks(bY, 8), w=[k("yT%d" % e)])
            P.release(bY)
            yield

        def epi_gen(B, s, fcs, ti):
            tok0 = s * L
            T = B.tr[ti]
            k = lambda name: B.k(name + T.sfx) if name in ("rT", "kT", "kkT", "sgw", "aT") else B.k(name)
            rT, kT, kkT, sgw, aT = T.rT, T.kT, T.kkT, T.sgw, T.aT
            yT = B.yT
            w_g, wgk = w_g_own[:], "w_g_own"
            for blk in range(NB):
                t0 = tok0 + blk * BW
                for fc in fcs:
                    fs = slice(fc * 128, (fc + 1) * 128)
                    yield from need(P, 3)
                    bC = P.bank()
                    for ci in range(CB):
                        t = blk * CB + ci
                        tr = LT - 1 - t
                        P.mm(ps[bC][:, ci * 128:(ci + 1) * 128], yT[0][:, t, fs], identb[:], start=True, stop=False, r=[k("yT0"), "identb"], w=[rk(bC, ci)])
                        P.mm(ps[bC][:, ci * 128:(ci + 1) * 128], yT[1][:, tr, fs], jrevb[:], start=False, stop=True, r=[k("yT1"), "jrevb"], w=[rk(bC, ci)])
                    P.cp("act", rT[:], ps[bC][:, 0:BW], r=rks(bC, CB), w=[k("rT")])
                    P.act(kT[:], ps[bC][:, 0:BW], AF.Square, r=rks(bC, CB), w=[k("kT")])
                    P.release(bC)
                    yield
                    yield from need(P, 3)
                    bm_ = P.bank(); bq_ = P.bank()
                    P.mm(ps[bm_][:, 0:BW], bones[:], rT[:], r=["bones", k("rT")], w=["ps%d" % bm_])
                    P.mm(ps[bq_][:, 0:BW], bones[:], kT[:], r=["bones", k("kT")], w=["ps%d" % bq_])
                    P.ts1("dve", kkT[:], ps[bm_][:, 0:BW], 1.0 / 64.0, ALU.mult, r=["ps%d" % bm_], w=[k("kkT")])
                    P.release(bm_)
                    P.tt("pool", sgw[:], kkT[:], kkT[:], ALU.mult, r=[k("kkT")], w=[k("sgw")])
                    P.stt("dve", aT[:], ps[bq_][:, 0:BW], 1.0 / 64.0, sgw[:], ALU.mult, ALU.subtract, r=["ps%d" % bq_, k("sgw")], w=[k("aT")])
                    P.release(bq_)
                    P.ts1("dve", aT[:], aT[:], 0.0, ALU.max, r=[k("aT")], w=[k("aT")])
                    P.rsqrt(aT[:], aT[:], GN_EPS, 1.0, T.Em[:], T.cs[:], r=[k("aT")], w=[k("aT")], kx=B.k("Em" + T.sfx), kt=B.k("cs" + T.sfx))
                    yield
                    P.tt("dve", rT[:], rT[:], kkT[:], ALU.subtract, r=[k("rT"), k("kkT")], w=[k("rT")])
                    P.tt("dve", rT[:], rT[:], aT[:], ALU.mult, r=[k("rT"), k("aT")], w=[k("rT")])
                    P.ts("dve", rT[:], rT[:], plg[:, fc:fc + 1], plb[:, fc:fc + 1], ALU.mult, ALU.add, r=[k("rT"), "plg", "plb"], w=[k("rT")])
                    yield from need(P, 3)
                    br_ = P.bank(); bk_ = P.bank()
                    proj_fm(P, ps[br_][:, 0:BW], w_rk, fc * 128, 128, u, t0, BW, ["w_rk"] + ukeys, "ps%d" % br_)
                    proj_fm(P, ps[bk_][:, 0:BW], w_rk, 256 + fc * 128, 128, u, t0, BW, ["w_rk"] + ukeys, "ps%d" % bk_)
                    P.cp("act", kT[:], ps[bk_][:, 0:BW], r=["ps%d" % bk_], w=[k("kT")])
                    P.release(bk_)
                    P.stt("dve", kT[:], ps[br_][:, 0:BW], prk[:, fc:fc + 1], kT[:], ALU.mult, ALU.mult, r=["ps%d" % br_, "prk", k("kT")], w=[k("kT")])
                    P.release(br_)
                    yield
                    yield from need(P, 3)
                    bv_ = P.bank(); bg_ = P.bank(); bb_ = P.bank()
                    proj_fm(P, ps[bv_][:, 0:BW], w_vg, fc * 128, 128, u, t0, BW, ["w_vg"] + ukeys, "ps%d" % bv_)
                    proj_fm(P, ps[bg_][:, 0:BW], w_g, fc * 128, 128, u, t0, BW, [wgk] + ukeys, "ps%d" % bg_)
                    P.mm(ps[bb_][:, 0:BW], bones[:], kT[:], r=["bones", k("kT")], w=["ps%d" % bb_])
                    P.cp("act", kkT[:], ps[bv_][:, 0:BW], r=["ps%d" % bv_], w=[k("kkT")])
                    P.release(bv_)
                    P.act(sgw[:], ps[bg_][:, 0:BW], AF.Tanh, scale=0.5, r=["ps%d" % bg_], w=[k("sgw")])
                    P.stt("dve", sgw[:], sgw[:], 1.0, ps[bg_][:, 0:BW], ALU.add, ALU.mult, r=[k("sgw"), "ps%d" % bg_], w=[k("sgw")])
                    P.release(bg_)
                    P.tt("dve", kkT[:], kkT[:], ps[bb_][:, 0:BW], ALU.mult, r=[k("kkT"), "ps%d" % bb_], w=[k("kkT")])
                    P.release(bb_)
                    P.tt("pool", rT[:], rT[:], kkT[:], ALU.add, r=[k("rT"), k("kkT")], w=[k("rT")])
                    P.stt("dve", y[:, fc, t0:t0 + BW], rT[:], 0.5, sgw[:], ALU.mult, ALU.mult, r=[k("rT"), k("sgw")], w=["y:%d" % fc])
                    yield

        def unit_gen(B, s):
            k = B.k
            H32, Hbf = B.H32, B.Hbf
            if g.ctx:
                S0 = [B.tr[e % NTR].rT[0:64, 0:256].rearrange("p (f h k) -> p f h k", f=2, h=2) for e in range(2)]
                s0k = [B.k("rT" + B.tr[e % NTR].sfx) for e in range(2)]
                yield from need(P, 1)
                b = P.bank()
                for e in range(2):
                    for fc in range(2):
                        P.dma("sp", S0[e][:, fc, :, :], env["st_rwkv"][l, e, 2 * fc:2 * fc + 2].rearrange("h v k -> v h k"), w=[s0k[e]])
                    for fc in range(2):
                        a = e * 2 + fc
                        P.mm(ps[b][:, a * 64:(a + 1) * 64], S0[e][:, fc, :, :].rearrange("v h k -> v (h k)"), identf[0:64, 0:64],
                             r=[s0k[e], "identf"], w=[rk(b, a)])
                P.cp("dve", H32[:], ps[b][:, 0:256].rearrange("p (a v) -> p a v", v=64), r=rks(b, 4), w=[k("H32")])
                P.release(b)
            else:
                P.memset("dve", H32[:], 0.0, w=[k("H32")])
            P.cp("act", Hbf[:], H32[:], r=[k("H32")], w=[k("Hbf")])
            yield
            gci = 0
            if NTR == 2:
                yield from interleave(derive_gen(B, s, 0, 0), derive_gen(B, s, 0, 1))
            else:
                for e in range(2):
                    yield from derive_gen(B, s, 0, e)
            yield from gn_gen(B, 0, 0, gci % 2)
            for blk in range(NB):
                for ci in range(CB):
                    gens = [rec_gen(B, blk, ci, gci % 2, blk * CB + ci)]
                    if ci + 1 < CB:
                        gens.append(gn_gen(B, blk, ci + 1, (gci + 1) % 2))
                        if blk + 1 < NB:
                            gens.append(derive_gen(B, s, blk + 1, 0))
                            gens.append(derive_gen(B, s, blk + 1, 1))
                    elif blk + 1 < NB:
                        gens.append(gn_gen(B, blk + 1, 0, (gci + 1) % 2))
                    yield from interleave(*gens)
                    gci += 1
            if not g.ctx:
                yield from need(P, 1)
                b = P.bank()
                for a in range(4):
                    P.mm(ps[b][0:64, a * 128:(a + 1) * 128], H32[:, a, :], identf[:], r=[k("H32"), "identf"], w=[rk(b, a)])
                P.cp("act", B.SS[0:64, 0:512], ps[b][0:64, :], r=rks(b, 4), w=[B.ssk])
                P.release(b)
                for e in range(2):
                    for fc in range(2):
                        P.dma("act", env["o_rwkv"][s, l, e, 2 * fc:2 * fc + 2].rearrange("h v k -> v h k"), B.So[:, e * 2 + fc, :, :], r=[B.ssk])
            yield
            if NTR == 2:
                yield from interleave(epi_gen(B, s, (0,), 0), epi_gen(B, s, (1,), 1))
            else:
                yield from epi_gen(B, s, (0, 1), 0)

        sets = [alloc_set(i) for i in range(nth)]
        for s0 in range(0, g.nseq, nth):
            yield from interleave(*[unit_gen(sets[i], s0 + i) for i in range(nth) if s0 + i < g.nseq])
        if dbank is not None:
            P.release(dbank)
        if not threaded:
            dump("yrw_" + tag, y[:, :, :], [128, 2, g.T], ["y:0", "y:1"])
            P.barrier()


def host_consts():
    ident = np.eye(128, dtype=np.float32)
    jrev = np.ascontiguousarray(ident[::-1])
    jj, ii = np.meshgrid(np.arange(128), np.arange(128), indexing="ij")
    mU_s = (jj < ii).astype(np.float32)
    mU_i = (jj <= ii).astype(np.float32)
    mL_s = (ii < jj).astype(np.float32)
    mask1 = np.tile(np.concatenate([-mU_s, mU_i], 1), (1, 2))
    mask2 = np.tile(np.concatenate([mU_s, mU_i], 1), (1, 2))
    mask3 = np.tile(-mL_s, (1, 4))
    bones = np.zeros((128, 128), np.float32)
    bones[:64, :64] = 1.0
    bones[64:, 64:] = 1.0
    Lq = 2048
    row = np.repeat(np.arange(Lq // 64), 64).astype(np.float32)
    col = np.tile(np.arange(64), Lq // 64).astype(np.float32)
    inv = (np.float32(10000.0) ** (-np.arange(8, dtype=np.float32) / np.float32(8))).astype(np.float32)
    ang = np.concatenate([row[:, None] * inv, col[:, None] * inv], -1).astype(np.float32)
    cosT = np.repeat(np.cos(ang).astype(np.float32).T, 2, axis=0)
    sinT = np.repeat(np.sin(ang).astype(np.float32).T, 2, axis=0)
    return dict(k_ident=ident, k_jrev=jrev, k_mask1=np.ascontiguousarray(mask1), k_mask2=np.ascontiguousarray(mask2),
                k_mask3=np.ascontiguousarray(mask3), k_bones=bones, k_cos=np.ascontiguousarray(cosT),
                k_sin=np.ascontiguousarray(sinT))


WEIGHT_NAMES = ["w_mod", "b_mod", "w_in", "rw_w0", "rw_w1", "rw_w2", "rw_a0", "rw_a1", "rw_a2", "rw_kk", "rw_ka", "rw_rk",
                "rw_lnx_g", "rw_lnx_b", "mla_qnorm", "mla_wuq", "mla_kvnorm", "mla_wukv", "lru_conv_w", "lru_conv_b",
                "lru_wa", "lru_ba", "lru_wx", "lru_bx", "lru_lambda", "w_out", "ln_g", "ln_b"]


def make_in_maps(inputs, cores):
    f = lambda a: np.ascontiguousarray(np.asarray(a, dtype=np.float32))
    shared = {k: f(inputs[k]) for k in WEIGHT_NAMES}
    shared["rw_rk"] = shared["rw_rk"].reshape(2, 256)
    shared.update(host_consts())
    maps = []
    for i in cores:
        m = dict(shared)
        m["xs"] = f(inputs["x_sample"][i])
        m["xp"] = f(inputs["x_prompt"][4 * i:4 * i + 4]).reshape(1024, D)
        m["st_rwkv"] = f(inputs["state_rwkv"][i])
        m["c_ckv"] = f(inputs["cache_mla_ckv"][i])
        m["c_kr"] = f(inputs["cache_mla_krope"][i])
        m["st_lru"] = f(inputs["state_lru"][i])
        m["cvec"] = f(np.stack([np.asarray(inputs["c"][i]), np.asarray(inputs["c_ctx"])]))
        maps.append(m)
    return maps


def kernel(**inputs):
    nc = build()[0]
    cores = list(range(8))
    res = run_bass_kernel_spmd(nc, make_in_maps(inputs, cores), core_ids=cores)
    R = res.results
    y_prompt = np.concatenate([r["y_p"].reshape(4, 256, D) for r in R], 0)
    y_sample = np.stack([r["y_s"] for r in R], 0)
    o_rwkv = np.concatenate([r["o_rwkv"] for r in R], 0)
    o_ckv = np.concatenate([r["o_ckv"] for r in R], 0)
    o_kr = np.concatenate([r["o_kr"] for r in R], 0)
    o_lru = np.concatenate([r["o_lru"] for r in R], 0)
    return (y_prompt.astype(np.float32), y_sample.astype(np.float32), o_rwkv.astype(np.float32),
            o_ckv.astype(np.float32), o_kr.astype(np.float32), o_lru.astype(np.float32))
```

```python
import numpy as np
from collections import deque
from contextlib import ExitStack
import concourse.bass as bass
import concourse.mybir as mybir
from concourse.bass_utils import run_bass_kernel_spmd

F32 = mybir.dt.float32
BF16 = mybir.dt.bfloat16
ALU = mybir.AluOpType
AF = mybir.ActivationFunctionType

D = 1024
DEPTH = 2
ALPHA = (2 * DEPTH) ** 0.25
LN_EPS = 1e-5
GN_EPS = 64e-5
SM_SCALE = 96.0 ** -0.5
WDEC = float(np.exp(-0.5))
C_R, C_K, C_V, C_G, C_CQ, C_CKV, C_KR, C_GM, C_XL, C_GL = 0, 256, 512, 768, 1024, 1280, 1408, 1440, 1952, 2208

ENGS = ("pe", "act", "dve", "pool", "sp")


class Prog:
    N_DSEM = 40

    def __init__(self, nc, stack, same_engine_sync=True):
        self.nc = nc
        self.ops = {e: [] for e in ENGS}
        self.cnt = {e: 0 for e in ENGS}
        self.esem = {e: stack.enter_context(nc.semaphore("es_" + e)) for e in ENGS}
        self.dsem = [stack.enter_context(nc.semaphore("ds%d" % i)) for i in range(self.N_DSEM)]
        self.dcum = [0] * self.N_DSEM
        self.dnext = 0
        self.know_e = {e: {f: 0 for f in ENGS} for e in ENGS}
        self.know_d = {e: [0] * self.N_DSEM for e in ENGS}
        self.last_w = {}
        self.readers = {}
        self.same_engine_sync = same_engine_sync
        self.ps = [stack.enter_context(nc.psum_tensor("psb%d" % i, [128, 512], F32)) for i in range(8)]
        self.free_banks = deque(range(8))

    def pefull(self, on):
        self.pe_full = on and self.allow_pe_full

    allow_pe_full = ()

    def bank(self):
        assert self.free_banks, "out of PSUM banks"
        return self.free_banks.popleft()

    def release(self, b):
        self.free_banks.append(b)

    def _need(self, eng, ev):
        if ev is None:
            return
        if ev[0] == "c":
            _, f, n = ev
            if self.know_e[eng][f] >= n:
                return
            self.know_e[eng][f] = n
            self.ops[eng].append(("wc", f, n))
        else:
            _, i, v = ev
            if self.know_d[eng][i] >= v:
                return
            self.know_d[eng][i] = v
            self.ops[eng].append(("wd", i, v))

    relax = ("pe", "act", "dve")

    pe_cont = False
    pe_full = False

    def _same(self, eng, ev):
        if eng == "pe":
            return (self.pe_cont or self.pe_full) and ev is not None and ev[0] == "c" and ev[1] == eng
        return (eng in self.relax) and ev is not None and ev[0] == "c" and ev[1] == eng

    def _deps(self, eng, reads, writes):
        for k in reads:
            ev = self.last_w.get(k)
            if self._same(eng, ev) and not self.same_engine_sync:
                continue
            self._need(eng, ev)
        for k in writes:
            ev = self.last_w.get(k)
            if not self._same(eng, ev):
                self._need(eng, ev)
            for ev in self.readers.get(k, ()):
                if not self._same(eng, ev):
                    self._need(eng, ev)

    def _commit(self, ev, reads, writes):
        for k in reads:
            self.readers.setdefault(k, []).append(ev)
        for k in writes:
            self.last_w[k] = ev
            self.readers[k] = []

    def op(self, eng, fn, r=(), w=()):
        self._deps(eng, r, w)
        self.cnt[eng] += 1
        n = self.cnt[eng]
        self.ops[eng].append(("ins", fn, n))
        self._commit(("c", eng, n), r, w)

    def dma(self, eng, out, in_, r=(), w=(), **kw):
        self._deps(eng, r, w)
        i = self.dnext
        self.dnext = (self.dnext + 1) % self.N_DSEM
        if self.dcum[i] > 0:
            self._need(eng, ("d", i, self.dcum[i]))
        self.dcum[i] += 16
        v = self.dcum[i]
        sem = self.dsem[i]
        self.ops[eng].append(
            ("raw", lambda E, sem=sem, out=out, in_=in_, kw=kw: E.dma_start(out=out, in_=in_, **kw).then_inc(sem, 16)))
        self._commit(("d", i, v), r, w)

    def barrier(self):
        for e in ENGS:
            for f in ENGS:
                if self.cnt[f] > 0:
                    self._need(e, ("c", f, self.cnt[f]))
            for i in range(self.N_DSEM):
                if self.dcum[i] > 0:
                    self._need(e, ("d", i, self.dcum[i]))
        self.last_w = {}
        self.readers = {}

    def emit(self):
        self.barrier()
        ops = self.ops
        marked = {e: set() for e in ENGS}
        for e in ENGS:
            for rec in ops[e]:
                if rec[0] == "wc":
                    marked[rec[1]].add(rec[2])
        val = {}
        for e in ENGS:
            c = 0
            for n in sorted(marked[e]):
                c += 1
                val[(e, n)] = c
            assert c < 60000, (e, c)
        esem, dsem = self.esem, self.dsem

        def run(E, e):
            sem = esem[e]
            for rec in ops[e]:
                k = rec[0]
                if k == "ins":
                    ins = rec[1](E)
                    if rec[2] in marked[e]:
                        ins.then_inc(sem, 1)
                elif k == "wc":
                    E.wait_ge(esem[rec[1]], val[(rec[1], rec[2])])
                elif k == "wd":
                    E.wait_ge(dsem[rec[1]], rec[2])
                else:
                    rec[1](E)

        with self.nc.Block() as block:
            @block.tensor
            def _(E):
                run(E, "pe")

            @block.scalar
            def _(E):
                run(E, "act")

            @block.vector
            def _(E):
                run(E, "dve")

            @block.gpsimd
            def _(E):
                run(E, "pool")

            @block.sync
            def _(E):
                run(E, "sp")

    def mm(self, out, lhsT, rhs, start=True, stop=True, r=(), w=()):
        self.pe_cont = (not start) and ("pe" in self.relax)
        self.op("pe", lambda E: E.matmul(out, lhsT, rhs, start=start, stop=stop), r, w)
        self.pe_cont = False

    def act(self, out, in_, func, r=(), w=(), bias=None, scale=None, accum=None):
        kw = {}
        if bias is not None:
            kw["bias"] = bias
        if scale is not None:
            kw["scale"] = scale
        if accum is not None:
            kw["accum_out"] = accum
        self.op("act", lambda E: E.activation(out, in_, func, **kw), r, w)

    def cp(self, eng, out, in_, r=(), w=()):
        if eng == "act":
            self.op("act", lambda E: E.copy(out, in_), r, w)
        else:
            self.op(eng, lambda E: E.tensor_copy(out, in_), r, w)

    def ts(self, eng, out, in0, s1, s2, op0, op1, r=(), w=()):
        self.op(eng, lambda E: E.tensor_scalar(out, in0, s1, s2, op0, op1), r, w)

    def ts1(self, eng, out, in0, s1, op0, r=(), w=()):
        self.op(eng, lambda E: E.tensor_single_scalar(out, in0, s1, op0), r, w)

    def tt(self, eng, out, in0, in1, op, r=(), w=()):
        self.op(eng, lambda E: E.tensor_tensor(out, in0, in1, op), r, w)

    def stt(self, eng, out, in0, scalar, in1, op0, op1, r=(), w=()):
        self.op(eng, lambda E: E.scalar_tensor_tensor(out, in0, scalar, in1, op0, op1), r, w)

    def rsqrt(self, out, in_, eps, scale, tx, tt_, r=(), w=(), kx=None, kt=None, small=False):
        self.act(out, in_, AF.Ln, bias=eps, scale=scale, r=r, w=w)
        self.act(out, out, AF.Exp, scale=-0.5, r=w, w=w)

    def memset(self, eng, ap, val, w=()):
        self.op(eng, lambda E: E.memset(ap, val), (), w)


class Group:
    def __init__(self, name, nseq, L, j, ctx):
        self.name, self.nseq, self.L, self.j, self.ctx = name, nseq, L, j, ctx
        self.T = nseq * L
        self.NT = self.T // 128
        self.LT = L // 128
        self.BW = min(512, L)
        self.CB = self.BW // 128
        self.NB = L // self.BW


def build(dbg=None):
    dbg = dbg or {}
    stop = dbg.get("stop", "")
    groups_sel = dbg.get("groups", "SP")
    nlayers = dbg.get("nlayers", DEPTH)
    nc = bass.Bass("TRN2", target_bir_lowering=False)

    def din(name, shape):
        return nc.dram_tensor(name, list(shape), F32, kind="ExternalInput").ap()

    def dout(name, shape):
        return nc.dram_tensor(name, list(shape), F32, kind="ExternalOutput").ap()

    xs = din("xs", [2048, D]); xp = din("xp", [1024, D])
    st_rwkv = din("st_rwkv", [2, 2, 4, 64, 64]); c_ckv = din("c_ckv", [2, 256, 128]); c_kr = din("c_kr", [2, 256, 32])
    st_lru = din("st_lru", [2, 2, 256]); cvec = din("cvec", [2, D])
    w_mod = din("w_mod", [2, D, 3 * D]); b_mod = din("b_mod", [2, 3 * D]); w_in = din("w_in", [2, D, 2464])
    rw_w0 = din("rw_w0", [2, 2, 256]); rw_w1 = din("rw_w1", [2, 2, D, 64]); rw_w2 = din("rw_w2", [2, 2, 64, 256])
    rw_a0 = din("rw_a0", [2, 2, 256]); rw_a1 = din("rw_a1", [2, 2, D, 64]); rw_a2 = din("rw_a2", [2, 2, 64, 256])
    rw_kk = din("rw_kk", [2, 256]); rw_ka = din("rw_ka", [2, 256]); rw_rk = din("rw_rk", [2, 256])
    rw_lnx_g = din("rw_lnx_g", [2, 256]); rw_lnx_b = din("rw_lnx_b", [2, 256])
    mla_qnorm = din("mla_qnorm", [2, 256]); mla_wuq = din("mla_wuq", [2, 256, 768])
    mla_kvnorm = din("mla_kvnorm", [2, 128]); mla_wukv = din("mla_wukv", [2, 128, 1024])
    lru_conv_w = din("lru_conv_w", [2, 4, 256]); lru_conv_b = din("lru_conv_b", [2, 256])
    lru_wa = din("lru_wa", [2, 2, 4, 64, 64]); lru_ba = din("lru_ba", [2, 2, 256])
    lru_wx = din("lru_wx", [2, 2, 4, 64, 64]); lru_bx = din("lru_bx", [2, 2, 256]); lru_lambda = din("lru_lambda", [2, 2, 256])
    w_out = din("w_out", [2, D, D]); ln_g = din("ln_g", [2, D]); ln_b = din("ln_b", [2, D])
    k_ident = din("k_ident", [128, 128]); k_jrev = din("k_jrev", [128, 128])
    k_mask1 = din("k_mask1", [128, 512]); k_mask2 = din("k_mask2", [128, 512]); k_mask3 = din("k_mask3", [128, 512])
    k_bones = din("k_bones", [128, 128]); k_cos = din("k_cos", [32, 2048]); k_sin = din("k_sin", [32, 2048])

    y_s = dout("y_s", [2048, D]); y_p = dout("y_p", [1024, D])
    o_rwkv = dout("o_rwkv", [4, 2, 2, 4, 64, 64]); o_ckv = dout("o_ckv", [4, 2, 256, 128])
    o_kr = dout("o_kr", [4, 2, 256, 32]); o_lru = dout("o_lru", [4, 2, 2, 256])
    xmid = {"S": nc.dram_tensor("xmid_s", [2048, D], F32).ap(), "P": nc.dram_tensor("xmid_p", [1024, D], F32).ap()}

    dumps = []

    with ExitStack() as st:
        P = Prog(nc, st, same_engine_sync=dbg.get("ses", True))
        P.relax = dbg.get("relax", Prog.relax)
        P.allow_pe_full = dbg.get("pe_full", Prog.allow_pe_full)
        ps = P.ps

        uniq = [0]

        def sb(stack, name, shape, dt):
            uniq[0] += 1
            return stack.enter_context(nc.sbuf_tensor("%s_%d" % (name, uniq[0]), list(shape), dt))

        def dump(name, ap, shape, rkeys):
            if name not in dbg.get("dump", ()):
                return
            o = dout("dbg_" + name, shape)
            P.dma("pool", o, ap, r=rkeys, allow_slow_non_contiguous=True)
            dumps.append(name)

        identf = sb(st, "identf", [128, 128], F32); jrevf = sb(st, "jrevf", [128, 128], F32)
        identb = sb(st, "identb", [128, 128], BF16); jrevb = sb(st, "jrevb", [128, 128], BF16)
        mask1 = sb(st, "mask1", [128, 512], BF16); mask2 = sb(st, "mask2", [128, 512], BF16); mask3 = sb(st, "mask3", [128, 512], BF16)
        bones = sb(st, "bones", [128, 128], F32); onesf = sb(st, "onesf", [128, 128], F32)
        P.dma("sp", identf[:], k_ident, w=["identf"]); P.dma("sp", jrevf[:], k_jrev, w=["jrevf"])
        P.dma("pool", identb[:], k_ident, w=["identb"]); P.dma("pool", jrevb[:], k_jrev, w=["jrevb"])
        P.dma("pool", mask1[:], k_mask1, w=["mask1"]); P.dma("pool", mask2[:], k_mask2, w=["mask2"]); P.dma("pool", mask3[:], k_mask3, w=["mask3"])
        P.dma("sp", bones[:], k_bones, w=["bones"])
        P.memset("dve", onesf[:], 1.0, w=["onesf"])
        mod = [sb(st, "mod%d" % l, [128, 24, 2], F32) for l in range(2)]
        sc1 = [sb(st, "sc1_%d" % l, [128, 8, 2], F32) for l in range(2)]
        gta = [sb(st, "gta%d" % l, [128, 8, 2], F32) for l in range(2)]

        with ExitStack() as ph:
            cv = sb(ph, "cv", [128, 8, 2], F32); scv = sb(ph, "scv", [128, 8, 2], F32)
            bm = sb(ph, "bm", [128, 24], F32)
            wm = [sb(ph, "wm%d" % i, [128, 8, 512], F32) for i in range(2)]
            for j in range(2):
                P.dma("sp", cv[:, :, j], cvec[j].rearrange("(kc p) -> p kc", p=128), w=["cv"], allow_slow_non_contiguous=True)
            P.act(scv[:], cv[:], AF.Silu, r=["cv"], w=["scv"])
            modrow = sb(ph, "modrow", [2, 3 * D], F32); bmrow = sb(ph, "bmrow", [2, 3 * D], F32)
            for l in range(2):
                P.dma("sp", bmrow[:], b_mod[l:l + 1, :].to_broadcast([2, 3 * D]), w=["bmrow"])
                for ob in range(6):
                    wmt = wm[ob % 2]; wk = "wm%d" % (ob % 2)
                    P.dma("sp", wmt[:], w_mod[l, :, ob * 512:(ob + 1) * 512].rearrange("(kc p) n -> p kc n", p=128), w=[wk])
                    b = P.bank(); bk = "ps%d" % b
                    for kc in range(8):
                        P.mm(ps[b][0:2, :], scv[:, kc, :], wmt[:, kc, :], start=(kc == 0), stop=(kc == 7), r=[wk, "scv"], w=[bk])
                    P.tt("dve", modrow[:, ob * 512:(ob + 1) * 512], ps[b][0:2, :], bmrow[:, ob * 512:(ob + 1) * 512], ALU.add,
                         r=[bk, "bmrow"], w=["modrow"])
                    P.release(b)
                b = P.bank(); bk = "ps%d" % b
                for oc in range(24):
                    P.mm(ps[b][:, oc * 2:oc * 2 + 2], modrow[0:2, oc * 128:(oc + 1) * 128], identf[0:2, 0:2], r=["modrow", "identf"], w=[bk])
                P.cp("dve", mod[l][:], ps[b][:, 0:48].rearrange("p (a b) -> p a b", b=2), r=[bk], w=["mod%d" % l])
                P.release(b)
                P.ts1("dve", sc1[l][:], mod[l][:, 8:16, :], 1.0, ALU.add, r=["mod%d" % l], w=["sc1_%d" % l])
                P.ts1("dve", gta[l][:], mod[l][:, 16:24, :], 1.0 / ALPHA, ALU.mult, r=["mod%d" % l], w=["gta%d" % l])
            dump("mod0", mod[0][:], [128, 24, 2], ["mod0"])
            P.barrier()
        if stop == "0":
            P.emit()
            return nc, dumps

        groups = []
        if "S" in groups_sel:
            groups.append(Group("S", 1, 2048, 0, True))
        if "P" in groups_sel:
            groups.append(Group("P", 4, 256, 1, False))
        xin0 = {"S": xs, "P": xp}
        xfin = {"S": y_s, "P": y_p}
        env = dict(nc=nc, P=P, sb=sb, dump=dump, stop=stop, identf=identf, jrevf=jrevf, identb=identb, jrevb=jrevb,
                   mask1=mask1, mask2=mask2, mask3=mask3, bones=bones, onesf=onesf, k_cos=k_cos, k_sin=k_sin,
                   mod=mod, sc1=sc1, gta=gta, dbg=dbg,
                   W=dict(w_in=w_in, rw_w0=rw_w0, rw_w1=rw_w1, rw_w2=rw_w2, rw_a0=rw_a0, rw_a1=rw_a1, rw_a2=rw_a2,
                          rw_kk=rw_kk, rw_ka=rw_ka, rw_rk=rw_rk, rw_lnx_g=rw_lnx_g, rw_lnx_b=rw_lnx_b,
                          mla_qnorm=mla_qnorm, mla_wuq=mla_wuq, mla_kvnorm=mla_kvnorm, mla_wukv=mla_wukv,
                          lru_conv_w=lru_conv_w, lru_conv_b=lru_conv_b, lru_wa=lru_wa, lru_ba=lru_ba, lru_wx=lru_wx,
                          lru_bx=lru_bx, lru_lambda=lru_lambda, w_out=w_out, ln_g=ln_g, ln_b=ln_b),
                   st_rwkv=st_rwkv, c_ckv=c_ckv, c_kr=c_kr, st_lru=st_lru,
                   o_rwkv=o_rwkv, o_ckv=o_ckv, o_kr=o_kr, o_lru=o_lru)
        for g in groups:
            for l in range(nlayers):
                x_in = xin0[g.name] if l == 0 else xmid[g.name]
                x_out = xfin[g.name] if l == nlayers - 1 else xmid[g.name]
                done = layer(env, g, l, x_in, x_out, "xin%s%d" % (g.name, l), "xin%s%d" % (g.name, l + 1))
                if done:
                    P.emit()
                    return nc, dumps
        P.emit()
    return nc, dumps


def layer(env, g, l, x_in, x_out, kin, kout):
    nc, P, sb, dump, stop = env["nc"], env["P"], env["sb"], env["dump"], env["stop"]
    ps = P.ps
    W = env["W"]
    identb, jrevb, identf, jrevf = env["identb"], env["jrevb"], env["identf"], env["jrevf"]
    T, NT, LT, L, j = g.T, g.NT, g.LT, g.L, g.j
    tag = "%s%d" % (g.name, l)
    w_in = W["w_in"]

    def wslice(c0, c1):
        return w_in[l, :, c0:c1].rearrange("(kc p) n -> p kc n", p=128)

    def ppar(stack, name, src):
        t = sb(stack, name, [128, 2], F32)
        P.dma("sp", t[:], src.rearrange("(fc p) -> p fc", p=128), w=[name], allow_slow_non_contiguous=True)
        return t

    with ExitStack() as lay:
        u = sb(lay, "u", [128, 8, T], BF16)
        y_rw = sb(lay, "y_rw", [128, 2, T], BF16)
        y_lru = sb(lay, "y_lru", [128, 2, T], BF16)
        with ExitStack() as revs:
            ur = sb(revs, "ur", [128, 8, T], BF16)
            with ExitStack() as ph:
                NBUF = 3
                xb = [sb(ph, "xa%d" % i, [128, D], F32) for i in range(NBUF)]
                xn = [sb(ph, "xn%d" % i, [128, D], BF16) for i in range(NBUF)]
                st6 = sb(ph, "st6", [128, 12], F32); mv = sb(ph, "mv", [128, 2], F32); rs = sb(ph, "rs", [128, 1], F32)
                rq1 = sb(ph, "rq1", [128, 1], F32); rq2 = sb(ph, "rq2", [128, 1], F32)

                def stage1(tt):
                    xt = xb[tt % NBUF]; xk = "xa%d" % (tt % NBUF); xnt = xn[tt % NBUF]; xnk = "xn%d" % (tt % NBUF)
                    P.dma("sp", xt[:], x_in[tt * 128:(tt + 1) * 128, :], r=["%s:%d" % (kin, tt)], w=[xk])
                    P.op("dve", lambda E, xt=xt: E.bn_stats(st6[:, 0:6], xt[:, 0:512]), r=[xk], w=["st6a"])
                    P.op("dve", lambda E, xt=xt: E.bn_stats(st6[:, 6:12], xt[:, 512:1024]), r=[xk], w=["st6b"])
                    P.op("dve", lambda E: E.bn_aggr(mv[:], st6[:]), r=["st6a", "st6b"], w=["mv"])
                    P.rsqrt(rs[:], mv[:, 1:2], LN_EPS, 1.0, rq1[:], rq2[:], r=["mv"], w=["rs"], kx="rq1", kt="rq2", small=True)
                    P.ts("dve", xnt[:], xt[:], mv[:, 0:1], rs[:], ALU.subtract, ALU.mult, r=[xk, "mv", "rs"], w=[xnk])

                def stage2(tt):
                    s_, i = divmod(tt, LT)
                    rt = s_ * LT + (LT - 1 - i)
                    xnt = xn[tt % NBUF]; xnk = "xn%d" % (tt % NBUF)
                    for (dst, dk, perm, pk, tpos) in ((u, "u", identb, "identb", tt), (ur, "ur", jrevb, "jrevb", rt)):
                        for half in range(2):
                            b = P.bank(); bk = "ps%d" % b
                            for q in range(4):
                                kc = half * 4 + q
                                P.mm(ps[b][:, q * 128:(q + 1) * 128], xnt[:, kc * 128:(kc + 1) * 128], perm[:], r=[xnk, pk], w=[bk])
                            for q in range(4):
                                kc = half * 4 + q
                                if dk == "u":
                                    P.act(dst[:, kc, tpos * 128:(tpos + 1) * 128], ps[b][:, q * 128:(q + 1) * 128], AF.Identity,
                                          scale=env["sc1"][l][:, kc, j:j + 1], bias=env["mod"][l][:, kc, j:j + 1],
                                          r=[bk, "sc1_%d" % l, "mod%d" % l], w=["%s:%d" % (dk, tpos)])
                                else:
                                    P.ts("dve", dst[:, kc, tpos * 128:(tpos + 1) * 128], ps[b][:, q * 128:(q + 1) * 128],
                                         env["sc1"][l][:, kc, j:j + 1], env["mod"][l][:, kc, j:j + 1], ALU.mult, ALU.add,
                                         r=[bk, "sc1_%d" % l, "mod%d" % l], w=["%s:%d" % (dk, tpos)])
                            P.release(b)

                for tt in range(NT + 1):
                    if tt < NT:
                        stage1(tt)
                    if tt >= 1:
                        stage2(tt - 1)
                dump("u_" + tag, u[:], [128, 8, T], ["u:%d" % t for t in range(NT)])
                dump("ur_" + tag, ur[:], [128, 8, T], ["ur:%d" % t for t in range(NT)])
                P.barrier()
            if stop == "A":
                return True
            ukeys = ["u:%d" % t for t in range(NT)]
            urkeys = ["ur:%d" % t for t in range(NT)]
            skip = env["dbg"].get("skip", "")
            threaded = (g.nseq > 1) and skip == "" and stop in ("", "E") and env["dbg"].get("threads", True)
            if threaded:
                y_mla = sb(revs, "y_mla", [128, 4, T], BF16)
                with ExitStack() as shared:
                    tw = env["dbg"].get("tw", (6, 1, 1))
                    run_gen(interleave_w([(phase_rwkv(env, g, l, u, ur, y_rw, ukeys, urkeys, wslice, ppar, tag, threaded=True, ext=shared), tw[0]),
                                          (phase_lru(env, g, l, u, ur, y_lru, ukeys, urkeys, wslice, ppar, tag, threaded=True, ext=shared), tw[1]),
                                          (phase_mla(env, g, l, u, y_mla, ukeys, wslice, ppar, tag, threaded=True, ext=shared), tw[2])]))
                    P.barrier()
                phase_out(env, g, l, (y_rw, y_mla, y_lru), x_in, x_out, kin, kout, tag)
                return stop == "E"
            if "B" not in skip:
                run_gen(phase_rwkv(env, g, l, u, ur, y_rw, ukeys, urkeys, wslice, ppar, tag))
            if stop == "B":
                return True
            if "C" not in skip:
                run_gen(phase_lru(env, g, l, u, ur, y_lru, ukeys, urkeys, wslice, ppar, tag))
            if stop == "C":
                return True
        y_mla = sb(lay, "y_mla", [128, 4, T], BF16)
        if "D" not in skip:
            run_gen(phase_mla(env, g, l, u, y_mla, ukeys, wslice, ppar, tag))
        if stop == "D":
            return True
        phase_out(env, g, l, (y_rw, y_mla, y_lru), x_in, x_out, kin, kout, tag)
        if stop == "E":
            return True
    return False


def _unpack(env):
    return env["nc"], env["P"], env["sb"], env["dump"], env["W"]


def proj_fm(P, out, w, c0, M, src, t0, N, r, wk):
    for kc in range(8):
        P.mm(out, w[:, kc, c0:c0 + M], src[:, kc, t0:t0 + N], start=(kc == 0), stop=(kc == 7), r=r, w=[wk])


def phase_out(env, g, l, y, x_in, x_out, kin, kout, tag):
    nc, P, sb, dump, W = _unpack(env)
    ps = P.ps
    identf = env["identf"]
    NT, j = g.NT, g.j
    ykeys = ["y:%d" % c for c in range(8)]
    with ExitStack() as ph:
        w_o = sb(ph, "w_o", [128, 8, D], BF16)
        P.dma("pool", w_o[:], W["w_out"][l].rearrange("(kc p) n -> p kc n", p=128), w=["w_o"])
        gate_bc = sb(ph, "gate_bc", [128, D], F32); g_bc = sb(ph, "g_bc", [128, D], F32); b_bc = sb(ph, "b_bc", [128, D], F32)
        P.dma("sp", g_bc[:], W["ln_g"][l:l + 1, :].to_broadcast([128, D]), w=["g_bc"])
        P.dma("sp", b_bc[:], W["ln_b"][l:l + 1, :].to_broadcast([128, D]), w=["b_bc"])
        for half in range(2):
            b = P.bank(); bk = "ps%d" % b
            for q in range(4):
                kc = half * 4 + q
                P.mm(ps[b][:, q * 128:(q + 1) * 128], env["gta"][l][:, kc, j:j + 1].to_broadcast([128, 128]), identf[:],
                     r=["gta%d" % l, "identf"], w=[bk])
            P.cp("act", gate_bc[:, half * 512:(half + 1) * 512], ps[b][:], r=[bk], w=["gate_bc"])
            P.release(b)
        xb = [sb(ph, "xo%d" % i, [128, D], F32) for i in range(3)]
        t1 = [sb(ph, "t1o%d" % i, [128, D], F32) for i in range(3)]
        st6 = sb(ph, "st6o", [128, 12], F32); mv = sb(ph, "mvo", [128, 2], F32); rs = sb(ph, "rso", [128, 1], F32)
        nmr = sb(ph, "nmr", [128, 1], F32); rq1 = sb(ph, "rq1o", [128, 1], F32); rq2 = sb(ph, "rq2o", [128, 1], F32)
        def stage_a(tt):
            xt = xb[tt % 3]; xk = "xo%d" % (tt % 3); tt1 = t1[tt % 3]; tk = "t1o%d" % (tt % 3)
            P.dma("sp", xt[:], x_in[tt * 128:(tt + 1) * 128, :], r=["%s:%d" % (kin, tt)], w=[xk])
            bs = [P.bank(), P.bank()]
            for half in range(2):
                for kc in range(8):
                    yt_, yc_ = (y[0], kc) if kc < 2 else ((y[1], kc - 2) if kc < 6 else (y[2], kc - 6))
                    P.mm(ps[bs[half]][:, :], yt_[:, yc_, tt * 128:(tt + 1) * 128], w_o[:, kc, half * 512:(half + 1) * 512],
                         start=(kc == 0), stop=(kc == 7), r=[ykeys[kc], "w_o"], w=["ps%d" % bs[half]])
            for half in range(2):
                P.tt("dve", tt1[:, half * 512:(half + 1) * 512], ps[bs[half]][:, :], gate_bc[:, half * 512:(half + 1) * 512], ALU.mult,
                     r=["ps%d" % bs[half], "gate_bc"], w=[tk])
                P.release(bs[half])
            P.tt("pool", tt1[:, 0:512], tt1[:, 0:512], xt[:, 0:512], ALU.add, r=[tk, xk], w=[tk + "L"])
            P.tt("dve", tt1[:, 512:1024], tt1[:, 512:1024], xt[:, 512:1024], ALU.add, r=[tk, xk], w=[tk + "R"])

        def stage_a2(tt):
            tt1 = t1[tt % 3]; tk = "t1o%d" % (tt % 3)
            P.op("dve", lambda E, a=tt1: E.bn_stats(st6[:, 6:12], a[:, 512:1024]), r=[tk, tk + "R"], w=["st6ob"])
            P.op("dve", lambda E, a=tt1: E.bn_stats(st6[:, 0:6], a[:, 0:512]), r=[tk, tk + "L"], w=["st6oa"])
            P.op("dve", lambda E: E.bn_aggr(mv[:], st6[:]), r=["st6oa", "st6ob"], w=["mvo"])
            P.rsqrt(rs[:], mv[:, 1:2], LN_EPS / (ALPHA * ALPHA), 1.0, rq1[:], rq2[:], r=["mvo"], w=["rso"], kx="rq1o", kt="rq2o", small=True)
            P.stt("dve", nmr[:], mv[:, 0:1], -1.0, rs[:], ALU.mult, ALU.mult, r=["mvo", "rso"], w=["nmr"])
            P.act(tt1[:], tt1[:], AF.Identity, scale=rs[:, 0:1], bias=nmr[:, 0:1], r=[tk, tk + "L", tk + "R", "rso", "nmr"], w=[tk, tk + "L", tk + "R"])

        def stage_b(tt):
            tt1 = t1[tt % 3]; tk = "t1o%d" % (tt % 3)
            P.tt("dve", tt1[:], tt1[:], g_bc[:], ALU.mult, r=[tk, "g_bc"], w=[tk])
            P.tt("pool", tt1[:], tt1[:], b_bc[:], ALU.add, r=[tk, "b_bc"], w=[tk])
            P.dma("pool", x_out[tt * 128:(tt + 1) * 128, :], tt1[:], r=[tk], w=["%s:%d" % (kout, tt)])

        for tt in range(NT + 1):
            if tt < NT:
                stage_a(tt)
            if tt >= 1:
                stage_b(tt - 1)
            if tt < NT:
                stage_a2(tt)
        P.barrier()


def phase_lru(env, g, l, u, ur, y, ukeys, urkeys, wslice, ppar, tag, threaded=False, ext=None):
    nc, P, sb, dump, W = _unpack(env)
    ps = P.ps
    identf, jrevf = env["identf"], env["jrevf"]
    L, LT, BW, NB, CB = g.L, g.LT, g.BW, g.NB, g.CB
    NS = "NSCAN"
    with (_NoClose(ext) if ext is not None else ExitStack()) as ph:
        w_xl = sb(ph, "w_xl", [128, 8, 256], BF16); w_gl = sb(ph, "w_gl", [128, 8, 256], BF16)
        P.dma("pool", w_xl[:], wslice(C_XL, C_XL + 256), w=["w_xl"])
        P.dma("pool", w_gl[:], wslice(C_GL, C_GL + 256), w=["w_gl"])
        cw = sb(ph, "cw", [128, 2, 4], F32)
        for jj in range(4):
            P.dma("sp", cw[:, :, jj], W["lru_conv_w"][l, jj].rearrange("(fc p) -> p fc", p=128), w=["cw"], allow_slow_non_contiguous=True)
        cbias = ppar(ph, "cbias", W["lru_conv_b"][l])
        Wbd = {}
        for e in range(2):
            for nm, src in (("a", W["lru_wa"]), ("x", W["lru_wx"])):
                t = sb(ph, "wbd%s%d" % (nm, e), [128, 2, 128], BF16)
                k = "wbd%s%d" % (nm, e)
                P.memset("pool", t[:], 0.0, w=[k])
                for hh in range(2):
                    P.dma("pool", t[hh * 64:(hh + 1) * 64, :, hh * 64:(hh + 1) * 64],
                          src[l, e].rearrange("(fc hh) i o -> hh i fc o", hh=2)[hh], w=[k])
                Wbd[(e, nm)] = (t, k)
        ba = [ppar(ph, "lba%d" % e, W["lru_ba"][l, e]) for e in range(2)]
        bx = [ppar(ph, "lbx%d" % e, W["lru_bx"][l, e]) for e in range(2)]
        lam = [ppar(ph, "lam%d" % e, W["lru_lambda"][l, e]) for e in range(2)]
        hba = [sb(ph, "hba%d" % e, [128, 2], F32) for e in range(2)]
        hbx = [sb(ph, "hbx%d" % e, [128, 2], F32) for e in range(2)]
        for e in range(2):
            P.ts1("dve", hba[e][:], ba[e][:], 0.5, ALU.mult, r=["lba%d" % e], w=["hba%d" % e])
            P.ts1("dve", hbx[e][:], bx[e][:], 0.5, ALU.mult, r=["lbx%d" % e], w=["hbx%d" % e])
        cneg = [sb(ph, "cneg%d" % e, [128, 2], F32) for e in range(2)]
        hcn = [sb(ph, "hcn%d" % e, [128, 2], F32) for e in range(2)]
        for e in range(2):
            k = "cneg%d" % e
            P.act(cneg[e][:], lam[e][:], AF.Exp, scale=-1.0, r=["lam%d" % e], w=[k])
            P.act(cneg[e][:], cneg[e][:], AF.Ln, bias=1.0, r=[k], w=[k])
            P.ts1("dve", cneg[e][:], cneg[e][:], -8.0, ALU.mult, r=[k], w=[k])
            P.ts1("dve", hcn[e][:], cneg[e][:], 0.5, ALU.mult, r=[k], w=["hcn%d" % e])
        h0 = None
        if g.ctx:
            h0 = [ppar(ph, "lh0%d" % e, env["st_lru"][l, e]) for e in range(2)]
        xlp = sb(ph, "xlp", [128, 2, L + 4], F32); xc = sb(ph, "xc", [128, 2, L], F32); xcb = sb(ph, "xcb", [128, 2, L], BF16)
        hh_ = [sb(ph, "lh%d" % e, [128, 2, L], F32) for e in range(2)]
        NT2 = 2 if g.nseq == 1 else 1
        scr = []
        for i in range(NT2):
            scr.append({nm: sb(ph, "l%s%d" % (nm, i), [128, BW], F32) for nm in ("sr", "si", "aa", "a2", "bt", "hT", "sg", "tsum")})
        git = [0, 0]
        P.memset("dve", xlp[:, :, 0:2], 0.0, w=["xlp"])
        P.memset("dve", xlp[:, :, L + 2:L + 4], 0.0, w=["xlp"])
        for s in range(g.nseq):
            tok0 = s * L
            for e in range(2):
                src, skeys = (u, ukeys) if e == 0 else (ur, urkeys)
                hk = "lh%d" % e
                for blk in range(NB):
                    for fc in range(2):
                        yield from need(P, 1)
                        b = P.bank(); bk = "ps%d" % b
                        proj_fm(P, ps[b][:, 0:BW], w_xl, fc * 128, 128, src, tok0 + blk * BW, BW, ["w_xl"] + skeys, bk)
                        P.cp("act", xlp[:, fc, 2 + blk * BW:2 + (blk + 1) * BW], ps[b][:, 0:BW], r=[bk], w=["xlp"])
                        P.release(b)
                        yield
                dd = [jj - 2 for jj in range(4)] if e == 0 else [2 - jj for jj in range(4)]
                for fc in range(2):
                    P.ts("dve", xc[:, fc, :], xlp[:, fc, 2 + dd[0]:2 + dd[0] + L], cw[:, fc, 0:1], cbias[:, fc:fc + 1], ALU.mult, ALU.add,
                         r=["xlp", "cw", "cbias"], w=["xc"])
                    for jj in range(1, 4):
                        P.stt("dve", xc[:, fc, :], xlp[:, fc, 2 + dd[jj]:2 + dd[jj] + L], cw[:, fc, jj:jj + 1], xc[:, fc, :], ALU.mult, ALU.add,
                              r=["xlp", "cw", "xc"], w=["xc"])
                P.cp("act", xcb[:, 0, :], xc[:, 0, :], r=["xc"], w=["xcb"])
                P.cp("dve", xcb[:, 1, :], xc[:, 1, :], r=["xc"], w=["xcb"])
                yield
                wa_t, wa_k = Wbd[(e, "a")]; wx_t, wx_k = Wbd[(e, "x")]

                def gate_s1(fc, blk, S_, sx):
                    sl = slice(blk * BW, (blk + 1) * BW)
                    sr, si, aa, a2 = S_["sr"], S_["si"], S_["aa"], S_["a2"]
                    b1 = P.bank(); b2 = P.bank()
                    P.mm(ps[b1][:, 0:BW], wa_t[:, fc, :], xcb[:, fc, sl], r=[wa_k, "xcb"], w=["ps%d" % b1])
                    P.mm(ps[b2][:, 0:BW], wx_t[:, fc, :], xcb[:, fc, sl], r=[wx_k, "xcb"], w=["ps%d" % b2])
                    P.act(sr[:], ps[b1][:, 0:BW], AF.Tanh, bias=hba[e][:, fc:fc + 1], scale=0.5, r=["ps%d" % b1, "hba%d" % e], w=["lsr" + sx])
                    P.act(si[:], ps[b2][:, 0:BW], AF.Tanh, bias=hbx[e][:, fc:fc + 1], scale=0.5, r=["ps%d" % b2, "hbx%d" % e], w=["lsi" + sx])
                    P.release(b1); P.release(b2)
                    P.act(aa[:], sr[:], AF.Exp, scale=hcn[e][:, fc:fc + 1], bias=hcn[e][:, fc:fc + 1], r=["lsr" + sx, "hcn%d" % e], w=["laa" + sx])
                    P.act(a2[:], sr[:], AF.Exp, scale=cneg[e][:, fc:fc + 1], bias=cneg[e][:, fc:fc + 1], r=["lsr" + sx, "cneg%d" % e], w=["la2" + sx])

                def gate_s2(fc, blk, S_, sx):
                    sl = slice(blk * BW, (blk + 1) * BW)
                    si, aa, a2, bt = S_["si"], S_["aa"], S_["a2"], S_["bt"]
                    P.act(a2[:], a2[:], AF.Sqrt, bias=1.0, scale=-1.0, r=["la2" + sx], w=["la2" + sx])
                    P.stt("dve", bt[:], si[:], 1.0, a2[:], ALU.add, ALU.mult, r=["la2" + sx, "lsi" + sx], w=["lbt" + sx])
                    P.stt("dve", bt[:], bt[:], 0.5, xc[:, fc, sl], ALU.mult, ALU.mult, r=["lbt" + sx, "xc"], w=["lbt" + sx])
                    if blk == 0:
                        init = h0[e][:, fc:fc + 1] if h0 is not None else 0.0
                        rk = ["lh0%d" % e] if h0 is not None else []
                    else:
                        init = hh_[e][:, fc, blk * BW - 1:blk * BW]
                        rk = [hk]
                    P.op("dve", lambda E, o=hh_[e][:, fc, sl], i0=init, aa=aa, bt=bt: E.tensor_tensor_scan(o, aa[:], bt[:], i0, ALU.mult, ALU.add),
                         r=["laa" + sx, "lbt" + sx] + rk, w=[hk])

                for blk in range(NB):
                    if NT2 == 2:
                        yield from need(P, 4)
                        for fc in range(2):
                            gate_s1(fc, blk, scr[fc], "%d" % fc)
                        yield
                        for fc in range(2):
                            gate_s2(fc, blk, scr[fc], "%d" % fc)
                        yield
                    else:
                        for fc in range(2):
                            yield from need(P, 2)
                            gate_s1(fc, blk, scr[0], "0")
                            gate_s2(fc, blk, scr[0], "0")
                            yield
                if not g.ctx:
                    P.dma("pool", env["o_lru"][s, l, e].rearrange("(fc p) -> p fc", p=128), hh_[e][:, :, L - 1], r=[hk], allow_slow_non_contiguous=True)
            for blk in range(NB):
                sl = slice(blk * BW, (blk + 1) * BW)
                for fc in range(2):
                    S_ = scr[git[1] % NT2]; sx = "%d" % (git[1] % NT2); git[1] += 1
                    hT, sg, tsum = S_["hT"], S_["sg"], S_["tsum"]
                    yield from need(P, 3)
                    bT = P.bank()
                    for ci in range(CB):
                        tr = LT - 1 - (blk * CB + ci)
                        P.mm(ps[bT][:, ci * 128:(ci + 1) * 128], hh_[1][:, fc, tr * 128:(tr + 1) * 128], identf[:], r=["lh1", "identf"], w=["ps%d" % bT])
                    P.cp("act", hT[:], ps[bT][:, 0:BW], r=["ps%d" % bT], w=["lhT" + sx])
                    P.release(bT)
                    bF = P.bank()
                    for ci in range(CB):
                        P.mm(ps[bF][:, ci * 128:(ci + 1) * 128], hT[:, ci * 128:(ci + 1) * 128], jrevf[:], r=["lhT" + sx, "jrevf"], w=["ps%d" % bF])
                    bG = P.bank()
                    proj_fm(P, ps[bG][:, 0:BW], w_gl, fc * 128, 128, u, tok0 + blk * BW, BW, ["w_gl"] + ukeys, "ps%d" % bG)
                    P.act(sg[:], ps[bG][:, 0:BW], AF.Tanh, scale=0.5, r=["ps%d" % bG], w=["lsg" + sx])
                    P.stt("dve", sg[:], sg[:], 1.0, ps[bG][:, 0:BW], ALU.add, ALU.mult, r=["lsg" + sx, "ps%d" % bG], w=["lsg" + sx])
                    P.release(bG)
                    P.tt("dve", tsum[:], ps[bF][:, 0:BW], hh_[0][:, fc, sl], ALU.add, r=["ps%d" % bF, "lh0"], w=["ltsum" + sx])
                    P.release(bF)
                    P.stt("dve", y[:, fc, tok0 + blk * BW:tok0 + (blk + 1) * BW], tsum[:], 0.5, sg[:], ALU.mult, ALU.mult, r=["ltsum" + sx, "lsg" + sx], w=["y:%d" % (6 + fc)])
                    yield
        if not threaded:
            dump("ylru_" + tag, y[:, :, :], [128, 2, g.T], ["y:6", "y:7"])
            P.barrier()


def phase_mla(env, g, l, u, y, ukeys, wslice, ppar, tag, threaded=False, ext=None):
    nc, P, sb, dump, W = _unpack(env)
    ps = P.ps
    identf, onesf = env["identf"], env["onesf"]
    L, LT, BW, NB, CB = g.L, g.LT, g.BW, g.NB, g.CB
    ctx = g.ctx
    K0 = 256 if ctx else 0
    Lk = L + K0
    KT = Lk // 128
    QB = BW
    with (_NoClose(ext) if ext is not None else ExitStack()) as ph:
        w_cq = sb(ph, "w_cq", [128, 8, 256], BF16); P.dma("pool", w_cq[:], wslice(C_CQ, C_CQ + 256), w=["w_cq"])
        w_ck = sb(ph, "w_ck", [128, 8, 128], BF16); P.dma("pool", w_ck[:], wslice(C_CKV, C_CKV + 128), w=["w_ck"])
        w_kr = sb(ph, "w_kr", [128, 8, 96], BF16)
        P.memset("pool", w_kr[:], 0.0, w=["w_kr"])
        P.dma("pool", w_kr[:, :, 64:96], wslice(C_KR, C_KR + 32), w=["w_kr"])
        w_gm = sb(ph, "w_gm", [128, 8, 512], BF16); P.dma("pool", w_gm[:], wslice(C_GM, C_GM + 512), w=["w_gm"])
        wuq = sb(ph, "wuq", [128, 2, 768], BF16)
        P.dma("pool", wuq[:], W["mla_wuq"][l].rearrange("(kc p) n -> p kc n", p=128), w=["wuq"])
        wkk = sb(ph, "wkk", [128, 8, 64], BF16); wkv = sb(ph, "wkv", [128, 8, 64], BF16)
        wv4 = W["mla_wukv"][l].rearrange("r (h x) -> r h x", x=128)
        P.dma("pool", wkk[:], wv4[:, :, 0:64], w=["wkk"])
        P.dma("pool", wkv[:], wv4[:, :, 64:128], w=["wkv"])
        qng = ppar(ph, "qng", W["mla_qnorm"][l])
        kvg = sb(ph, "kvg", [128, 1], F32)
        P.dma("sp", kvg[:], W["mla_kvnorm"][l].rearrange("(p o) -> p o", o=1), w=["kvg"], allow_slow_non_contiguous=True)
        if ctx:
            cosT = sb(ph, "cosT", [128, 512], F32); sinT = sb(ph, "sinT", [128, 512], F32)

            def load_tables(p0, n):
                P.dma("sp", cosT[64:96, 0:n], env["k_cos"][:, p0:p0 + n], w=["cosT"])
                P.dma("sp", sinT[64:96, 0:n], env["k_sin"][:, p0:p0 + n], w=["sinT"])
            w_krs = sb(ph, "w_krs", [128, 8, 96], BF16)
            P.memset("pool", w_krs[:], 0.0, w=["w_krs"])
            P.ts1("dve", w_krs[:, :, 64:96:2], w_kr[:, :, 65:96:2], -1.0, ALU.mult, r=["w_kr"], w=["w_krs"])
            P.cp("dve", w_krs[:, :, 65:96:2], w_kr[:, :, 64:96:2], r=["w_kr"], w=["w_krs"])
            wuqs = sb(ph, "wuqs", [128, 2, 768], BF16)
            P.memset("pool", wuqs[:], 0.0, w=["wuqs"])
            for kc in range(2):
                v4 = wuq[:, kc, :].rearrange("p (h x) -> p h x", x=96)
                v4s = wuqs[:, kc, :].rearrange("p (h x) -> p h x", x=96)
                P.ts1("dve", v4s[:, :, 64:96:2], v4[:, :, 65:96:2], -1.0, ALU.mult, r=["wuq"], w=["wuqs"])
                P.cp("dve", v4s[:, :, 65:96:2], v4[:, :, 64:96:2], r=["wuq"], w=["wuqs"])
        ckvn = sb(ph, "ckvn", [128, Lk], BF16); krall = sb(ph, "krall", [128, Lk], BF16)
        Kt = sb(ph, "Kt", [128, 8, Lk], BF16); V = sb(ph, "Vv", [128, KT, 8, 65], BF16)
        cqn = sb(ph, "cqn", [128, 2, L], BF16); sgm = [sb(ph, "sgm%d" % i, [128, QB], F32) for i in range(3 if KT < 6 else 2)]
        ck = sb(ph, "mck", [128, BW], F32); sq = sb(ph, "msq", [128, BW], F32); sq1 = sb(ph, "msq1", [128, BW], F32)
        rstd = sb(ph, "mrstd", [128, BW], F32); t1 = sb(ph, "mt1", [128, BW], F32); t2 = sb(ph, "mt2", [128, BW], F32)
        if g.nseq == 1:
            sqq = sb(ph, "msqq", [128, BW], F32); rstdq = sb(ph, "mrstdq", [128, BW], F32); t2q = sb(ph, "mt2q", [128, BW], F32)
            kq_ = ("msqq", "mrstdq", "mt2q")
        else:
            sqq, rstdq, t2q = sq, rstd, t2
            kq_ = ("msq", "mrstd", "mt2")
        Pt = [sb(ph, "Pt%d" % i, [128, QB], BF16) for i in range(5 if KT >= 6 else 4)]
        Qh = [sb(ph, "Qh%d" % i, [128, QB], BF16) for i in range(3 if KT < 6 else 2)]
        accs, rden, tmpy = ck, sq, sq1
        P.memset("pool", V[:, :, :, 64:65], 1.0, w=["Vones"])
        if ctx:
            ckc = sb(ph, "ckc", [128, 2, 128], F32); krc = sb(ph, "krc", [128, 2, 96], F32)
        else:
            otile = sb(ph, "otile", [128, CB, 128], F32); okr = sb(ph, "okr", [128, CB, 32], F32)
        qi = 0
        for s in range(g.nseq):
            tok0 = s * L
            if ctx:
                P.dma("sp", ckc[:], env["c_ckv"][l].rearrange("(t p) r -> p t r", p=128), w=["ckc"])
                P.memset("dve", krc[:], 0.0, w=["krc"])
                P.dma("sp", krc[:, :, 64:96], env["c_kr"][l].rearrange("(t p) r -> p t r", p=128), w=["krc"])
                for t in range(2):
                    b = P.bank(); bk = "ps%d" % b
                    P.mm(ps[b][:, 0:128], ckc[:, t, :], identf[:], r=["ckc", "identf"], w=[bk])
                    P.cp("act", ckvn[:, t * 128:(t + 1) * 128], ps[b][:, 0:128], r=[bk], w=["ckvn"])
                    P.release(b)
                    b = P.bank(); bk = "ps%d" % b
                    P.mm(ps[b][0:96, 0:128], krc[:, t, :], identf[:], r=["krc", "identf"], w=[bk])
                    P.cp("act", krall[64:96, t * 128:(t + 1) * 128], ps[b][64:96, 0:128], r=[bk], w=["krall"])
                    P.release(b)
            for blk in range(NB):
                t0 = tok0 + blk * BW
                lsl = slice(blk * BW, (blk + 1) * BW)
                ksl = slice(K0 + blk * BW, K0 + (blk + 1) * BW)
                if ctx:
                    load_tables(blk * BW, BW)
                def ckv_gen(blk=blk, t0=t0, lsl=lsl, ksl=ksl):
                    yield from need(P, 3)
                    b = P.bank(); bk = "ps%d" % b
                    proj_fm(P, ps[b][:, 0:BW], w_ck, 0, 128, u, t0, BW, ["w_ck"] + ukeys, bk)
                    P.cp("act", ck[:], ps[b][:, 0:BW], r=[bk], w=["mck"])
                    P.act(sq[:], ps[b][:, 0:BW], AF.Square, r=[bk], w=["msq"])
                    P.release(b)
                    yield
                    b2 = P.bank(); bk2 = "ps%d" % b2
                    P.mm(ps[b2][:, 0:BW], onesf[:], sq[:], r=["onesf", "msq"], w=[bk2])
                    P.rsqrt(rstd[:], ps[b2][:, 0:BW], 1e-6, 1.0 / 128.0, t1[:], t2[:], r=[bk2], w=["mrstd"], kx="mt1", kt="mt2")
                    P.release(b2)
                    yield
                    P.tt("dve", ck[:], ck[:], rstd[:], ALU.mult, r=["mck", "mrstd"], w=["mck"])
                    P.ts1("dve", ck[:], ck[:], kvg[:, 0:1], ALU.mult, r=["mck", "kvg"], w=["mck"])
                    P.cp("act", ckvn[:, ksl], ck[:], r=["mck"], w=["ckvn"])
                    if not ctx:
                        b3 = P.bank(); bk3 = "ps%d" % b3
                        for ci in range(CB):
                            P.mm(ps[b3][:, ci * 128:(ci + 1) * 128], ck[:, ci * 128:(ci + 1) * 128], identf[:], r=["mck", "identf"], w=[bk3])
                        P.cp("act", otile[:], ps[b3][:, 0:BW].rearrange("p (c r) -> p c r", r=128), r=[bk3], w=["otile"])
                        P.release(b3)
                        P.dma("act", env["o_ckv"][s, l, blk * BW:(blk + 1) * BW, :].rearrange("(c p) r -> p c r", p=128), otile[:], r=["otile"])
                def kr_gen(blk=blk, t0=t0, lsl=lsl, ksl=ksl):
                    yield from need(P, 3)
                    b = P.bank(); bk = "ps%d" % b
                    proj_fm(P, ps[b][0:96, 0:BW], w_kr, 0, 96, u, t0, BW, ["w_kr"] + ukeys, bk)
                    if ctx:
                        b2 = P.bank(); bk2 = "ps%d" % b2
                        proj_fm(P, ps[b2][0:96, 0:BW], w_krs, 0, 96, u, t0, BW, ["w_krs"] + ukeys, bk2)
                        P.tt("dve", t1[64:96, :], ps[b][64:96, 0:BW], cosT[64:96, 0:BW], ALU.mult, r=[bk, "cosT"], w=["mt1"])
                        P.tt("dve", t2[64:96, :], ps[b2][64:96, 0:BW], sinT[64:96, 0:BW], ALU.mult, r=[bk2, "sinT"], w=["mt2"])
                        P.release(b); P.release(b2)
                        P.tt("pool", krall[64:96, ksl], t1[64:96, :], t2[64:96, :], ALU.add, r=["mt1", "mt2"], w=["krall"])
                    else:
                        P.cp("act", t1[64:96, :], ps[b][64:96, 0:BW], r=[bk], w=["mt1"])
                        P.release(b)
                        P.cp("pool", krall[64:96, ksl], t1[64:96, :], r=["mt1"], w=["krall"])
                        b3 = P.bank(); bk3 = "ps%d" % b3
                        for ci in range(CB):
                            P.mm(ps[b3][:, ci * 32:(ci + 1) * 32], t1[64:96, ci * 128:(ci + 1) * 128], identf[64:96, 64:96], r=["mt1", "identf"], w=[bk3])
                        P.cp("act", okr[:], ps[b3][:, 0:CB * 32].rearrange("p (c r) -> p c r", r=32), r=[bk3], w=["okr"])
                        P.release(b3)
                        P.dma("act", env["o_kr"][s, l, blk * BW:(blk + 1) * BW, :].rearrange("(c p) r -> p c r", p=128), okr[:], r=["okr"])
                def cq_gen(blk=blk, t0=t0, lsl=lsl, ksl=ksl):
                    yield from need(P, 3)
                    bq = [P.bank(), P.bank()]
                    for kc in range(2):
                        proj_fm(P, ps[bq[kc]][:, 0:BW], w_cq, kc * 128, 128, u, t0, BW, ["w_cq"] + ukeys, "ps%d" % bq[kc])
                    P.act(sqq[:], ps[bq[0]][:, 0:BW], AF.Square, r=["ps%d" % bq[0]], w=[kq_[0]])
                    P.act(sq1[:], ps[bq[1]][:, 0:BW], AF.Square, r=["ps%d" % bq[1]], w=["msq1"])
                    yield
                    b2 = P.bank(); bk2 = "ps%d" % b2
                    P.mm(ps[b2][:, 0:BW], onesf[:], sqq[:], start=True, stop=False, r=["onesf", kq_[0]], w=[bk2])
                    P.mm(ps[b2][:, 0:BW], onesf[:], sq1[:], start=False, stop=True, r=["onesf", "msq1"], w=[bk2])
                    P.rsqrt(rstdq[:], ps[b2][:, 0:BW], 1e-6, 1.0 / 256.0, t1[:], t2[:], r=[bk2], w=[kq_[1]], kx="mt1", kt="mt2")
                    P.release(b2)
                    yield
                    for kc in range(2):
                        P.tt("dve", t2q[:], ps[bq[kc]][:, 0:BW], rstdq[:], ALU.mult, r=["ps%d" % bq[kc], kq_[1]], w=[kq_[2]])
                        P.release(bq[kc])
                        P.act(cqn[:, kc, lsl], t2q[:], AF.Identity, scale=qng[:, kc:kc + 1], r=[kq_[2], "qng"], w=["cqn"])
                if g.nseq == 1:
                    yield from interleave(ckv_gen(), kr_gen(), cq_gen())
                else:
                    yield from ckv_gen()
                    yield
                    yield from kr_gen()
                    yield
                    yield from cq_gen()
            yield
            nkb = (Lk + 511) // 512
            for kb in range(nkb):
                k0 = kb * 512
                n = min(512, Lk - k0)
                for h in range(8):
                    yield from need(P, 1)
                    b = P.bank(); bk = "ps%d" % b
                    P.mm(ps[b][0:64, 0:n], wkk[:, h, :], ckvn[:, k0:k0 + n], r=["wkk", "ckvn"], w=[bk])
                    P.cp("act" if h % 2 == 0 else "dve", Kt[0:64, h, k0:k0 + n], ps[b][0:64, 0:n], r=[bk], w=["Kt"])
                    P.release(b)
            for h in range(8):
                P.cp("dve" if h % 2 == 0 else "act", Kt[64:96, h, :], krall[64:96, :], r=["krall"], w=["Kt"])
            wkv2 = wkv[:].rearrange("p h x -> p (h x)")
            yield
            for kt in range(KT):
                yield from need(P, 1)
                b = P.bank(); bk = "ps%d" % b
                P.mm(ps[b][:, 0:512], ckvn[:, kt * 128:(kt + 1) * 128], wkv2, r=["wkv", "ckvn"], w=[bk])
                P.cp("act" if kt % 2 == 0 else "dve", V[:, kt, :, 0:64], ps[b][:, 0:512].rearrange("p (h x) -> p h x", x=64), r=[bk], w=["Vv"])
                P.release(b)
            PRE = 3 if KT >= 6 else 2
            DEP = 2 if KT < 6 else 1
            NQ = DEP + 1
            items = [(qb, h) for qb in range(L // QB) for h in range(8)]
            prep = {}
            esets = [(accs, rden, tmpy, "mck", "msq", "msq1")]
            if DEP == 2:
                esets.append((rstd, t1, t2, "mrstd", "mt1", "mt2"))

            def prepare(it):
                qb, h = it
                qs = slice(qb * QB, (qb + 1) * QB)
                if ctx and h == 0:
                    load_tables(qb * QB, QB)
                if h % 2 == 0:
                    gi = (h // 2) % len(sgm)
                    sgt = sgm[gi]; sgk = "sgm%d" % gi
                    b = P.bank(); bk = "ps%d" % b
                    proj_fm(P, ps[b][:, 0:QB], w_gm, (h // 2) * 128, 128, u, tok0 + qb * QB, QB, ["w_gm"] + ukeys, bk)
                    P.act(sgt[:], ps[b][:, 0:QB], AF.Tanh, scale=0.5, r=[bk], w=[sgk])
                    P.stt("dve", sgt[:], sgt[:], 1.0, ps[b][:, 0:QB], ALU.add, ALU.mult, r=[sgk, bk], w=[sgk])
                    P.release(b)
                idx = items.index(it)
                qh = Qh[idx % NQ]; qk = "Qh%d" % (idx % NQ)
                bq_ = P.bank(); bqk = "ps%d" % bq_
                P.mm(ps[bq_][0:96, 0:QB], wuq[:, 0, h * 96:(h + 1) * 96], cqn[:, 0, qs], start=True, stop=False, r=["wuq", "cqn"], w=[bqk])
                P.mm(ps[bq_][0:96, 0:QB], wuq[:, 1, h * 96:(h + 1) * 96], cqn[:, 1, qs], start=False, stop=True, r=["wuq", "cqn"], w=[bqk])
                if ctx:
                    bs_ = P.bank(); bsk = "ps%d" % bs_
                    P.mm(ps[bs_][0:96, 0:QB], wuqs[:, 0, h * 96:(h + 1) * 96], cqn[:, 0, qs], start=True, stop=False, r=["wuqs", "cqn"], w=[bsk])
                    P.mm(ps[bs_][0:96, 0:QB], wuqs[:, 1, h * 96:(h + 1) * 96], cqn[:, 1, qs], start=False, stop=True, r=["wuqs", "cqn"], w=[bsk])
                    P.cp("act", qh[0:64, :], ps[bq_][0:64, 0:QB], r=[bqk], w=[qk])
                    P.tt("dve", t1[64:96, 0:QB], ps[bq_][64:96, 0:QB], cosT[64:96, 0:QB], ALU.mult, r=[bqk, "cosT"], w=["mt1"])
                    P.tt("dve", t2[64:96, 0:QB], ps[bs_][64:96, 0:QB], sinT[64:96, 0:QB], ALU.mult, r=[bsk, "sinT"], w=["mt2"])
                    P.release(bq_); P.release(bs_)
                    P.tt("pool", qh[64:96, :], t1[64:96, 0:QB], t2[64:96, 0:QB], ALU.add, r=["mt1", "mt2"], w=[qk])
                else:
                    P.cp("act", qh[0:96, :], ps[bq_][0:96, 0:QB], r=[bqk], w=[qk])
                    P.release(bq_)
                prep[it] = (qh, qk)

            def tail1(p):
                ac, rd, tm, ka, kr_, km = esets[p["idx"] % len(esets)]
                back = "ps%d" % p["bacc"]
                P.cp("act", ac[0:65, 0:QB], ps[p["bacc"]][0:65, 0:QB], r=[back], w=[ka])
                P.release(p["bacc"])
                P.op("dve", lambda E, ac=ac, rd=rd: E.reciprocal(rd[64:65, 0:QB], ac[64:65, 0:QB]), r=[ka], w=[kr_])

            def tail2(p):
                ac, rd, tm, ka, kr_, km = esets[p["idx"] % len(esets)]
                qb, h = p["it"]
                bb = P.bank(); bbk = "ps%d" % bb
                P.mm(ps[bb][0:64, 0:QB], onesf[64:65, 0:64], rd[64:65, 0:QB], r=["onesf", kr_], w=[bbk])
                pb = (h % 2) * 64; ch = 2 + h // 2
                P.tt("dve", tm[pb:pb + 64, 0:QB], ac[0:64, 0:QB], ps[bb][0:64, 0:QB], ALU.mult, r=[ka, bbk], w=[km])
                P.release(bb)
                gi = (h // 2) % len(sgm)
                sgt = sgm[gi]; sgk = "sgm%d" % gi
                P.stt("dve", y[pb:pb + 64, ch - 2, tok0 + qb * QB:tok0 + (qb + 1) * QB], tm[pb:pb + 64, 0:QB], 0.5, sgt[pb:pb + 64, :], ALU.mult, ALU.mult,
                      r=[km, sgk], w=["y:%d" % ch])

            yield
            yield from need(P, 3)
            for i in range(min(DEP, len(items))):
                prepare(items[i])
            pending = []
            K1 = min(2, KT - 1)
            GAP2 = 10 if KT >= 14 else KT

            def run_pending(slot):
                for p in list(pending):
                    if p["stage"] == 0 and slot >= p["s1"]:
                        tail1(p); p["stage"] = 1
                    if p["stage"] == 1 and slot >= p["s1"] + GAP2:
                        tail2(p); pending.remove(p)

            for idx, it in enumerate(items):
                yield
                yield from need(P, 6 if KT >= 6 else 5)
                qb, h = it
                qh, qk = prep.pop(it)
                sbank = {}

                def issue_S(kt, qh=qh, qk=qk, h=h):
                    bS = P.bank()
                    P.mm(ps[bS][:, 0:QB], Kt[0:96, h, kt * 128:(kt + 1) * 128], qh[0:96, :], r=["Kt", qk], w=["ps%d" % bS])
                    sbank[kt] = bS

                for kt in range(min(PRE, KT)):
                    issue_S(kt)
                if idx + DEP < len(items):
                    prepare(items[idx + DEP])
                bacc = P.bank(); back = "ps%d" % bacc
                for kt in range(KT):
                    if kt + PRE < KT:
                        issue_S(kt + PRE)
                    run_pending(idx * KT + kt)
                    bS = sbank.pop(kt)
                    pt = Pt[kt % len(Pt)]; pk = "Pt%d" % (kt % len(Pt))
                    P.act(pt[:], ps[bS][:, 0:QB], AF.Exp, scale=SM_SCALE, r=["ps%d" % bS], w=[pk])
                    P.release(bS)
                    P.mm(ps[bacc][0:65, 0:QB], V[:, kt, h, :], pt[:], start=(kt == 0), stop=(kt == KT - 1), r=["Vv", "Vones", pk], w=[back])
                pending.append(dict(it=it, idx=idx, bacc=bacc, stage=0, s1=(idx + 1) * KT + K1))
            run_pending(10 ** 9)
        if not threaded:
            dump("ymla_" + tag, y[:, :, :], [128, 4, g.T], ["y:2", "y:3", "y:4", "y:5"])
            P.barrier()


def interleave(*gens):
    gens = list(gens)
    while gens:
        alive = []
        for gn in gens:
            try:
                next(gn)
                alive.append(gn)
            except StopIteration:
                pass
        gens = alive
        yield


class _NoClose:
    def __init__(self, st):
        self.st = st

    def __enter__(self):
        return self.st

    def __exit__(self, *a):
        return False


def need(P, n):
    while len(P.free_banks) < n:
        yield


def interleave_w(pairs):
    pairs = list(pairs)
    while pairs:
        alive = []
        for gn, wgt in pairs:
            ok = True
            for _ in range(wgt):
                try:
                    next(gn)
                except StopIteration:
                    ok = False
                    break
            if ok:
                alive.append((gn, wgt))
        pairs = alive
        yield


def run_gen(gn):
    for _ in gn:
        pass


def phase_rwkv(env, g, l, u, ur, y, ukeys, urkeys, wslice, ppar, tag, threaded=False, ext=None):
    nc, P, sb, dump, W = _unpack(env)
    ps = P.ps
    identf, identb, jrevb, bones, onesf = env["identf"], env["identb"], env["jrevb"], env["bones"], env["onesf"]
    mask1, mask2, mask3 = env["mask1"], env["mask2"], env["mask3"]
    L, LT = g.L, g.LT
    BW, CB = 256, 2
    NB = L // BW
    nth = 2 if (g.nseq > 1 and not threaded) else 1
    NBP = 2 if NB > 1 else 1
    NTR = 2 if g.nseq == 1 else 1

    def rk(b, i):
        return "ps%d" % b

    def rks(b, n):
        return ["ps%d" % b]

    with (_NoClose(ext) if ext is not None else ExitStack()) as ph:
        w_rk = sb(ph, "w_rk", [128, 8, 512], BF16); P.dma("pool", w_rk[:], wslice(C_R, C_R + 512), w=["w_rk"])
        w_vg = sb(ph, "w_vg", [128, 8, 256], BF16); P.dma("pool", w_vg[:], wslice(C_V, C_V + 256), w=["w_vg"])
        w_g_own = sb(ph, "w_g_own", [128, 8, 256], BF16); P.dma("pool", w_g_own[:], wslice(C_G, C_G + 256), w=["w_g_own"])
        lo1 = [sb(ph, "lo1_%d" % e, [128, 8, 128], BF16) for e in range(2)]
        lo2 = [sb(ph, "lo2_%d" % e, [128, 256], BF16) for e in range(2)]
        for e in range(2):
            P.dma("pool", lo1[e][:, :, 0:64], W["rw_w1"][l, e].rearrange("(kc p) r -> p kc r", p=128), w=["lo1_%d" % e])
            P.dma("pool", lo1[e][:, :, 64:128], W["rw_a1"][l, e].rearrange("(kc p) r -> p kc r", p=128), w=["lo1_%d" % e])
            P.dma("pool", lo2[e][0:64, :], W["rw_w2"][l, e], w=["lo2_%d" % e])
            P.dma("pool", lo2[e][64:128, :], W["rw_a2"][l, e], w=["lo2_%d" % e])
        pw0 = [ppar(ph, "pw0_%d" % e, W["rw_w0"][l, e]) for e in range(2)]
        pa0 = [ppar(ph, "pa0_%d" % e, W["rw_a0"][l, e]) for e in range(2)]
        hpw0 = [sb(ph, "hpw0_%d" % e, [128, 2], F32) for e in range(2)]
        hpa0 = [sb(ph, "hpa0_%d" % e, [128, 2], F32) for e in range(2)]
        for e in range(2):
            P.ts1("dve", hpw0[e][:], pw0[e][:], 0.5, ALU.mult, r=["pw0_%d" % e], w=["hpw0_%d" % e])
            P.ts1("dve", hpa0[e][:], pa0[e][:], 0.5, ALU.mult, r=["pa0_%d" % e], w=["hpa0_%d" % e])
        pkk = ppar(ph, "pkk", W["rw_kk"][l]); pka = ppar(ph, "pka", W["rw_ka"][l]); prk = ppar(ph, "prk", W["rw_rk"][l])
        plg = ppar(ph, "plg", W["rw_lnx_g"][l]); plb = ppar(ph, "plb", W["rw_lnx_b"][l])

        class BufSet:
            pass

        ND = env["dbg"].get("ndummy", 0)
        dbank = P.bank() if ND else None

        def dummies(n):
            for _ in range(n * ND):
                P.pe_cont = True
                P.op("pe", lambda E: E.matmul(ps[dbank][:, :], identb[:], mask1[:], start=False, stop=False), (), ())
                P.pe_cont = False

        def alloc_set(tid):
            B = BufSet()
            B.tid = tid
            B.k = lambda name: "%s@%d" % (name, tid)
            n = lambda name: "%s_t%d" % (name, tid)
            B.QR = [[sb(ph, n("QR%d%d" % (bp, e)), [128, 2, CB, 2, 128], BF16) for e in range(2)] for bp in range(NBP)]
            B.AK = [[sb(ph, n("AK%d%d" % (bp, e)), [128, 2, CB, 2, 128], BF16) for e in range(2)] for bp in range(NBP)]
            B.AKgT = [[sb(ph, n("AKgT%d%d" % (bp, e)), [128, CB, 2, 2, 128], BF16) for e in range(2)] for bp in range(NBP)]
            B.Vt = [[sb(ph, n("Vt%d%d" % (bp, e)), [128, CB, 256], BF16) for e in range(2)] for bp in range(NBP)]
            B.GE = [sb(ph, n("GE%d" % bp), [128, 2, 2, CB], F32) for bp in range(NBP)]
            B.yT = [sb(ph, n("yT%d" % e), [128, LT, 256], BF16) for e in range(2)]
            B.tr = []
            for ti in range(NTR):
                T = BufSet()
                T.sfx = "#%d" % ti
                T.tl = sb(ph, n("tl%d" % ti), [128, BW], BF16)
                for nm in ("rT", "kT", "kkT", "sgw", "aT", "cs", "Em", "tq"):
                    setattr(T, nm, sb(ph, n("%s%d" % (nm, ti)), [128, BW], F32))
                T.Ebuf = sb(ph, n("Ebuf%d" % ti), [128, CB, 129], F32)
                T.Ag = sb(ph, n("Ag%d" % ti), [128, BW], BF16); T.Kg = sb(ph, n("Kg%d" % ti), [128, BW], BF16)
                P.memset("dve", T.Ebuf[:, :, 0:1], 1.0, w=[B.k("Ebuf0" + T.sfx)])
                B.tr.append(T)
            B.MA = [[sb(ph, n("MA%d%d" % (par, e)), [128, 4, 2, 128], BF16) for e in range(2)] for par in range(2)]
            B.LK = [[sb(ph, n("LK%d%d" % (par, e)), [128, 4, 2, 128], BF16) for e in range(2)] for par in range(2)]
            B.Pm = [[sb(ph, n("Pm%d%d" % (par, e)), [128, 4, 128], BF16) for e in range(2)] for par in range(2)]
            B.Nn = [sb(ph, n("Nn%d" % e), [128, 4, 128], BF16) for e in range(2)]
            B.M2 = [[sb(ph, n("M2_%d%d" % (e, i)), [128, 4, 128], BF16) for i in range(2)] for e in range(2)]
            B.N2 = [[sb(ph, n("N2_%d%d" % (e, i)), [128, 4, 128], BF16) for i in range(2)] for e in range(2)]
            B.W1 = sb(ph, n("W1"), [128, 512], BF16); B.Uu = sb(ph, n("Uu"), [128, 512], BF16)
            B.H32 = sb(ph, n("H32"), [128, 4, 64], F32); B.Hbf = sb(ph, n("Hbf"), [128, 4, 64], BF16)
            if not g.ctx:
                B.SS = sb(ph, n("SS"), [128, 512], F32)
                B.ssk = B.k("SS")
                B.So = B.SS[0:64, 0:512].rearrange("p (a h k) -> p a h k", a=4, h=2)
            return B

        def v3(t):
            return t[:].rearrange("p (c t) -> p c t", t=128)

        def derive_gen(B, s, blk, e):
            T = B.tr[e % NTR]
            bp = blk % NBP
            k = lambda name: B.k(name + T.sfx)
            kb = lambda name: B.k("%s.%d" % (name, bp))
            tok0 = s * L
            src, skeys = (u, ukeys) if e == 0 else (ur, urkeys)
            t0 = tok0 + blk * BW
            ek = "%d" % e
            rT, kT, kkT, sgw, aT, cs, Em, tq, tl, Ebuf = T.rT, T.kT, T.kkT, T.sgw, T.aT, T.cs, T.Em, T.tq, T.tl, T.Ebuf
            QRe, AKe, AKgTe, Vte, GEb = B.QR[bp][e], B.AK[bp][e], B.AKgT[bp][e], B.Vt[bp][e], B.GE[bp]
            Eend, kka, kdir = sgw, tq, cs
            cs3, Em3, Eend3 = v3(cs), v3(Em), v3(Eend)
            for half in range((CB + 1) // 2):
                yield from need(P, 2)
                b = P.bank(); bk = "ps%d" % b
                nci = min(2, CB - half * 2)
                for q in range(nci):
                    ci = half * 2 + q
                    for kc in range(8):
                        P.mm(ps[b][:, q * 256:(q + 1) * 256], src[:, kc, t0 + ci * 128:t0 + (ci + 1) * 128], w_vg[:, kc, 0:256],
                             start=(kc == 0), stop=(kc == 7), r=["w_vg"] + skeys, w=[rk(b, q)])
                P.cp("act", Vte[:, half * 2:half * 2 + nci, :], ps[b][:, 0:nci * 256].rearrange("p (c v) -> p c v", v=256),
                     r=rks(b, nci), w=[kb("Vt" + ek)])
                P.release(b)
            b = P.bank(); bk = "ps%d" % b
            proj_fm(P, ps[b][:, 0:BW], lo1[e], 0, 128, src, t0, BW, ["lo1_" + ek] + skeys, bk)
            P.act(tl[0:64, :], ps[b][0:64, 0:BW], AF.Tanh, r=[bk], w=[k("tl")])
            P.cp("dve", tl[64:128, :], ps[b][64:128, 0:BW], r=[bk], w=[k("tl")])
            P.release(b)
            yield
            for fc in range(2):
                fs = slice(fc * 128, (fc + 1) * 128)
                yield from need(P, 4)
                bw_ = P.bank(); ba_ = P.bank()
                P.mm(ps[bw_][:, 0:BW], lo2[e][0:64, fs], tl[0:64, :], r=["lo2_" + ek, k("tl")], w=["ps%d" % bw_])
                P.mm(ps[ba_][:, 0:BW], lo2[e][64:128, fs], tl[64:128, :], r=["lo2_" + ek, k("tl")], w=["ps%d" % ba_])
                br_ = P.bank(); bk_ = P.bank()
                proj_fm(P, ps[br_][:, 0:BW], w_rk, fc * 128, 128, src, t0, BW, ["w_rk"] + skeys, "ps%d" % br_)
                proj_fm(P, ps[bk_][:, 0:BW], w_rk, 256 + fc * 128, 128, src, t0, BW, ["w_rk"] + skeys, "ps%d" % bk_)
                P.act(sgw[:], ps[bw_][:, 0:BW], AF.Tanh, bias=hpw0[e][:, fc:fc + 1], scale=0.5, r=["ps%d" % bw_, "hpw0_" + ek], w=[k("sgw")])
                P.act(aT[:], ps[ba_][:, 0:BW], AF.Tanh, bias=hpa0[e][:, fc:fc + 1], scale=0.5, r=["ps%d" % ba_, "hpa0_" + ek], w=[k("aT")])
                P.release(bw_); P.release(ba_)
                P.ts("pool", sgw[:], sgw[:], 0.5, 0.5, ALU.mult, ALU.add, r=[k("sgw")], w=[k("sgw")])
                P.ts("pool", aT[:], aT[:], 0.5, 0.5, ALU.mult, ALU.add, r=[k("aT")], w=[k("aT")])
                P.cp("act", rT[:], ps[br_][:, 0:BW], r=["ps%d" % br_], w=[k("rT")])
                P.cp("dve", kT[:], ps[bk_][:, 0:BW], r=["ps%d" % bk_], w=[k("kT")])
                P.release(br_); P.release(bk_)
                yield
                P.ts1("dve", kkT[:], kT[:], pkk[:, fc:fc + 1], ALU.mult, r=[k("kT"), "pkk"], w=[k("kkT")])
                P.tt("pool", tq[:], kkT[:], kkT[:], ALU.mult, r=[k("kkT")], w=[k("tq")])
                bs_ = P.bank()
                P.mm(ps[bs_][:, 0:BW], bones[:], tq[:], r=["bones", k("tq")], w=["ps%d" % bs_])
                P.rsqrt(tq[:], ps[bs_][:, 0:BW], 1e-24, 1.0, Em[:], cs[:], r=["ps%d" % bs_], w=[k("tq")], kx=k("Em"), kt=k("cs"))
                P.release(bs_)
                for ci in range(CB):
                    P.op("dve", lambda E, ci=ci: E.tensor_tensor_scan(cs[:, ci * 128:(ci + 1) * 128], onesf[:, 0:128],
                                                                      sgw[:, ci * 128:(ci + 1) * 128], 0.0, ALU.mult, ALU.add),
                         r=["onesf", k("sgw")], w=[k("cs")])
                yield
                P.tt("dve", kkT[:], kkT[:], tq[:], ALU.mult, r=[k("kkT"), k("tq")], w=[k("kkT")])
                P.act(Ebuf[:, :, 1:129], cs3, AF.Exp, scale=-WDEC, r=[k("cs")], w=[k("Ebuf")])
                P.act(Em[:], cs[:], AF.Exp, scale=WDEC, r=[k("cs")], w=[k("Em")])
                P.cp("dve", GEb[:, e, fc, :], Ebuf[:, :, 128], r=[k("Ebuf")], w=[kb("GE")])
                P.tt("dve", Eend3, Em3, GEb[:, e, fc, :].unsqueeze(2).to_broadcast([128, CB, 128]), ALU.mult, r=[k("Em"), kb("GE")], w=[k("sgw")])
                P.tt("pool", kka[:], kkT[:], aT[:], ALU.mult, r=[k("kkT"), k("aT")], w=[k("tq")])
                P.ts("dve", aT[:], aT[:], 1.0, pka[:, fc:fc + 1], ALU.subtract, ALU.mult, r=[k("aT"), "pka"], w=[k("aT")])
                P.stt("dve", kdir[:], aT[:], 1.0, kT[:], ALU.add, ALU.mult, r=[k("aT"), k("kT")], w=[k("cs")])
                yield
                P.tt("dve", QRe[:, fc, :, 0, :], v3(kkT), Ebuf[:, :, 0:128], ALU.mult, r=[k("kkT"), k("Ebuf"), k("Ebuf0")], w=[kb("QR" + ek)])
                P.tt("dve", QRe[:, fc, :, 1, :], v3(rT), Ebuf[:, :, 1:129], ALU.mult, r=[k("rT"), k("Ebuf")], w=[kb("QR" + ek)])
                P.tt("pool", AKe[:, fc, :, 0, :], v3(kka), Em3, ALU.mult, r=[k("tq"), k("Em")], w=[kb("AK" + ek)])
                P.tt("pool", AKe[:, fc, :, 1, :], v3(kdir), Em3, ALU.mult, r=[k("cs"), k("Em")], w=[kb("AK" + ek)])
                P.tt("dve", T.Ag[:], kka[:], Eend[:], ALU.mult, r=[k("tq"), k("sgw")], w=[k("Ag")])
                P.tt("pool", T.Kg[:], kdir[:], Eend[:], ALU.mult, r=[k("cs"), k("sgw")], w=[k("Kg")])
                yield
                for ci in range(CB):
                    yield from need(P, 1)
                    b = P.bank()
                    P.mm(ps[b][:, 0:128], T.Ag[:, ci * 128:(ci + 1) * 128], identb[:], r=[k("Ag"), "identb"], w=[rk(b, 0)])
                    P.mm(ps[b][:, 128:256], T.Kg[:, ci * 128:(ci + 1) * 128], identb[:], r=[k("Kg"), "identb"], w=[rk(b, 1)])
                    P.cp("act", AKgTe[:, ci, :, fc, :], ps[b][:, 0:256].rearrange("p (a x) -> p a x", x=128), r=rks(b, 2), w=[kb("AKgT" + ek)])
                    P.release(b)
                yield

        def gn_gen(B, blk, ci, par):
            bp = blk % NBP
            k0 = B.k
            k = lambda name: (B.k("%s.%d" % (name, bp)) if name[:2] in ("QR", "AK", "Vt", "GE") else B.k(name))
            QR, AK, MA, LK, Pm, Nn, M2, N2 = B.QR[bp], B.AK[bp], B.MA[par], B.LK[par], B.Pm[par], B.Nn, B.M2, B.N2
            pk = "%d" % par
            for e in range(2):
                ek = "%d" % e
                for fc in range(2):
                    yield from need(P, 3)
                    b1 = P.bank(); b2 = P.bank()
                    for hh in range(2):
                        pb = hh * 64
                        qr2 = QR[e][pb:pb + 64, fc, ci, :, :].rearrange("p a t -> p (a t)")
                        P.mm(ps[b1][:, hh * 256:(hh + 1) * 256], AK[e][pb:pb + 64, fc, ci, 0, :], qr2, r=[k("AK" + ek), k("QR" + ek)], w=[rk(b1, hh)])
                        P.mm(ps[b2][:, hh * 256:(hh + 1) * 256], AK[e][pb:pb + 64, fc, ci, 1, :], qr2, r=[k("AK" + ek), k("QR" + ek)], w=[rk(b2, hh)])
                    P.tt("dve", MA[e][:, fc * 2:fc * 2 + 2, :, :].rearrange("p h a t -> p (h a t)"), ps[b1][:, :], mask1[:], ALU.mult,
                         r=rks(b1, 2) + ["mask1"], w=[k("MA" + pk + ek)])
                    P.tt("dve", LK[e][:, fc * 2:fc * 2 + 2, :, :].rearrange("p h a t -> p (h a t)"), ps[b2][:, :], mask2[:], ALU.mult,
                         r=rks(b2, 2) + ["mask2"], w=[k("LK" + pk + ek)])
                    P.release(b1); P.release(b2)
                b3 = P.bank()
                for h in range(4):
                    fc, pb = h // 2, (h % 2) * 64
                    P.mm(ps[b3][:, h * 128:(h + 1) * 128], QR[e][pb:pb + 64, fc, ci, 0, :], AK[e][pb:pb + 64, fc, ci, 0, :],
                         r=[k("AK" + ek), k("QR" + ek)], w=[rk(b3, h)])
                P.tt("dve", Nn[e][:].rearrange("p h t -> p (h t)"), ps[b3][:, :], mask3[:], ALU.mult, r=rks(b3, 4) + ["mask3"], w=[k("Nn" + ek)])
                P.release(b3)
                yield
            Mc, Nc, Mk, Nk = {}, {}, {}, {}
            for e in range(2):
                ek = "%d" % e
                P.tt("pool", Pm[e][:], MA[e][:, :, 0, :], identb[:].unsqueeze(1).to_broadcast([128, 4, 128]), ALU.add,
                     r=[k("MA" + pk + ek), "identb"], w=[k("Pm" + pk + ek)])
                Mc[e] = [MA[e][:, h, 0, :] for h in range(4)]; Mk[e] = k("MA" + pk + ek)
                Nc[e] = [Nn[e][:, h, :] for h in range(4)]; Nk[e] = k("Nn" + ek)
            for lev in range(6):
                i2 = lev % 2
                bN, bM, bP = {}, {}, {}
                yield from need(P, 4)
                for e in range(2):
                    bN[e] = P.bank()
                    if lev < 5:
                        bM[e] = P.bank()
                for h in range(4):
                    for e in range(2):
                        P.mm(ps[bN[e]][:, h * 128:(h + 1) * 128], Mc[e][h], Nc[e][h], r=[Mk[e], Nk[e]], w=[rk(bN[e], h)])
                        if lev < 5:
                            P.mm(ps[bM[e]][:, h * 128:(h + 1) * 128], Nc[e][h], Mc[e][h], r=[Mk[e], Nk[e]], w=[rk(bM[e], h)])
                dummies(1)
                for e in range(2):
                    n2k = k("N2_%d%d" % (e, i2))
                    P.cp("act", N2[e][i2][:].rearrange("p h t -> p (h t)"), ps[bN[e]][:, :], r=rks(bN[e], 4), w=[n2k])
                    P.release(bN[e])
                if lev < 5:
                    for e in range(2):
                        m2k = k("M2_%d%d" % (e, i2))
                        P.cp("dve" if e == 0 else "act", M2[e][i2][:].rearrange("p h t -> p (h t)"), ps[bM[e]][:, :], r=rks(bM[e], 4), w=[m2k])
                        P.release(bM[e])
                yield
                yield from need(P, 2)
                for e in range(2):
                    bP[e] = P.bank()
                for h in range(4):
                    for e in range(2):
                        P.mm(ps[bP[e]][:, h * 128:(h + 1) * 128], N2[e][i2][:, h, :], Pm[e][:, h, :],
                             r=[k("N2_%d%d" % (e, i2)), k("Pm" + pk + "%d" % e)], w=[rk(bP[e], h)])
                for e in range(2):
                    P.tt("dve", Pm[e][:].rearrange("p h t -> p (h t)"), Pm[e][:].rearrange("p h t -> p (h t)"), ps[bP[e]][:, :], ALU.add,
                         r=[k("Pm" + pk + "%d" % e)] + rks(bP[e], 4), w=[k("Pm" + pk + "%d" % e)])
                    P.release(bP[e])
                for e in range(2):
                    Mc[e] = [M2[e][i2][:, h, :] for h in range(4)]; Mk[e] = k("M2_%d%d" % (e, i2))
                    Nc[e] = [N2[e][i2][:, h, :] for h in range(4)]; Nk[e] = k("N2_%d%d" % (e, i2))
                yield

        def rec_gen(B, blk, ci, par, tile_i):
            bp = blk % NBP
            k = lambda name: (B.k("%s.%d" % (name, bp)) if name[:2] in ("QR", "AK", "Vt", "GE") else B.k(name))
            QR, AKgT, Vt, MA, LK, Pm = B.QR[bp], B.AKgT[bp], B.Vt[bp], B.MA[par], B.LK[par], B.Pm[par]
            W1, Uu, H32, Hbf, GE = B.W1, B.Uu, B.H32, B.Hbf, B.GE[bp]
            pk = "%d" % par
            chains = [(e, h) for h in range(4) for e in range(2)]
            yield from need(P, 1)
            bW = P.bank()
            for (e, h) in chains:
                ek = "%d" % e
                fc, pb = h // 2, (h % 2) * 64
                c = e * 4 + h
                cb = c * 64
                P.mm(ps[bW][:, cb:cb + 64], QR[e][pb:pb + 64, fc, ci, 0, :], Hbf[pb:pb + 64, e * 2 + fc, :], start=True, stop=False,
                     r=[k("QR" + ek), k("Hbf")], w=[rk(bW, c)])
                P.mm(ps[bW][:, cb:cb + 64], LK[e][:, h, 0, :], Vt[e][:, ci, h * 64:(h + 1) * 64], start=False, stop=True,
                     r=[k("LK" + pk + ek), k("Vt" + ek)], w=[rk(bW, c)])
            P.cp("act", W1[:], ps[bW][:, :], r=rks(bW, 8), w=[k("W1")])
            P.release(bW)
            yield
            yield from need(P, 1)
            bU = P.bank()
            for (e, h) in chains:
                c = e * 4 + h
                cb = c * 64
                P.mm(ps[bU][:, cb:cb + 64], Pm[e][:, h, :], W1[:, cb:cb + 64], r=[k("Pm" + pk + "%d" % e), k("W1")], w=[rk(bU, c)])
            P.ts1("dve", Uu[:], ps[bU][:, :], -1.0, ALU.mult, r=rks(bU, 8), w=[k("Uu")])
            P.release(bU)
            yield
            yield from need(P, 2)
            bY = P.bank(); bH = P.bank()
            for fc in range(2):
                for e in range(2):
                    ek = "%d" % e
                    c0 = (e * 4 + fc * 2) * 64
                    P.mm(ps[bH][:, c0:c0 + 128], AKgT[e][:, ci, 0, fc, :], Uu[:, c0:c0 + 128], start=True, stop=False,
                         r=[k("AKgT" + ek), k("Uu")], w=[rk(bH, 0)])
                    P.mm(ps[bH][:, c0:c0 + 128], AKgT[e][:, ci, 1, fc, :], Vt[e][:, ci, fc * 128:(fc + 1) * 128], start=False, stop=True,
                         r=[k("AKgT" + ek), k("Vt" + ek)], w=[rk(bH, 0)])
            for (e, h) in chains:
                ek = "%d" % e
                fc, pb = h // 2, (h % 2) * 64
                c = e * 4 + h
                cb = c * 64
                P.mm(ps[bY][:, cb:cb + 64], QR[e][pb:pb + 64, fc, ci, 1, :], Hbf[pb:pb + 64, e * 2 + fc, :], start=True, stop=False,
                     r=[k("QR" + ek), k("Hbf")], w=[rk(bY, c)])
                P.mm(ps[bY][:, cb:cb + 64], MA[e][:, h, 1, :], Uu[:, cb:cb + 64], start=False, stop=False, r=[k("MA" + pk + ek), k("Uu")], w=[rk(bY, c)])
                P.mm(ps[bY][:, cb:cb + 64], LK[e][:, h, 1, :], Vt[e][:, ci, h * 64:(h + 1) * 64], start=False, stop=True,
                     r=[k("LK" + pk + ek), k("Vt" + ek)], w=[rk(bY, c)])
            yield
            bH4 = ps[bH][:, :].rearrange("p (a hh v) -> p a hh v", hh=2, v=64)
            for hh in range(2):
                pb = hh * 64
                gb = GE[pb:pb + 64, :, :, ci].rearrange("p e f -> p (e f)").unsqueeze(2).to_broadcast([64, 4, 64])
                P.tt("dve", H32[pb:pb + 64, :, :], H32[pb:pb + 64, :, :], gb, ALU.mult, r=[k("H32"), k("GE")], w=[k("H32")])
                P.tt("dve", H32[pb:pb + 64, :, :], H32[pb:pb + 64, :, :], bH4[pb:pb + 64, :, hh, :], ALU.add, r=[k("H32")] + rks(bH, 8), w=[k("H32")])
            P.release(bH)
            P.cp("act", Hbf[:], H32[:], r=[k("H32")], w=[k("Hbf")])
            for e in range(2):
                P.cp("act" if e == 0 else "pool", B.yT[e][:, tile_i, :], ps[bY][:, e * 256:(e + 1) * 256], r=rks(bY, 8), w=[k("yT%d" % e)]) \
                    if e == 0 else P.cp("act", B.yT[e][:, tile_i, :], ps[bY][:, e * 256:(e + 1) * 256], r=rks(bY, 8), w=[k("yT%d" % e)])
            P.release(bY)
            yield

        def epi_gen(B, s, fcs, ti):
            tok0 = s * L
            T = B.tr[ti]
            k = lambda name: B.k(name + T.sfx) if name in ("rT", "kT", "kkT", "sgw", "aT") else B.k(name)
            rT, kT, kkT, sgw, aT = T.rT, T.kT, T.kkT, T.sgw, T.aT
            yT = B.yT
            w_g, wgk = w_g_own[:], "w_g_own"
            for blk in range(NB):
                t0 = tok0 + blk * BW
                for fc in fcs:
                    fs = slice(fc * 128, (fc + 1) * 128)
                    yield from need(P, 3)
                    bC = P.bank()
                    for ci in range(CB):
                        t = blk * CB + ci
                        tr = LT - 1 - t
                        P.mm(ps[bC][:, ci * 128:(ci + 1) * 128], yT[0][:, t, fs], identb[:], start=True, stop=False, r=[k("yT0"), "identb"], w=[rk(bC, ci)])
                        P.mm(ps[bC][:, ci * 128:(ci + 1) * 128], yT[1][:, tr, fs], jrevb[:], start=False, stop=True, r=[k("yT1"), "jrevb"], w=[rk(bC, ci)])
                    P.cp("act", rT[:], ps[bC][:, 0:BW], r=rks(bC, CB), w=[k("rT")])
                    P.act(kT[:], ps[bC][:, 0:BW], AF.Square, r=rks(bC, CB), w=[k("kT")])
                    P.release(bC)
                    yield
                    yield from need(P, 3)
                    bm_ = P.bank(); bq_ = P.bank()
                    P.mm(ps[bm_][:, 0:BW], bones[:], rT[:], r=["bones", k("rT")], w=["ps%d" % bm_])
                    P.mm(ps[bq_][:, 0:BW], bones[:], kT[:], r=["bones", k("kT")], w=["ps%d" % bq_])
                    P.ts1("dve", kkT[:], ps[bm_][:, 0:BW], 1.0 / 64.0, ALU.mult, r=["ps%d" % bm_], w=[k("kkT")])
                    P.release(bm_)
                    P.tt("pool", sgw[:], kkT[:], kkT[:], ALU.mult, r=[k("kkT")], w=[k("sgw")])
                    P.stt("dve", aT[:], ps[bq_][:, 0:BW], 1.0 / 64.0, sgw[:], ALU.mult, ALU.subtract, r=["ps%d" % bq_, k("sgw")], w=[k("aT")])
                    P.release(bq_)
                    P.ts1("dve", aT[:], aT[:], 0.0, ALU.max, r=[k("aT")], w=[k("aT")])
                    P.rsqrt(aT[:], aT[:], GN_EPS, 1.0, T.Em[:], T.cs[:], r=[k("aT")], w=[k("aT")], kx=B.k("Em" + T.sfx), kt=B.k("cs" + T.sfx))
                    yield
                    P.tt("dve", rT[:], rT[:], kkT[:], ALU.subtract, r=[k("rT"), k("kkT")], w=[k("rT")])
                    P.tt("dve", rT[:], rT[:], aT[:], ALU.mult, r=[k("rT"), k("aT")], w=[k("rT")])
                    P.ts("dve", rT[:], rT[:], plg[:, fc:fc + 1], plb[:, fc:fc + 1], ALU.mult, ALU.add, r=[k("rT"), "plg", "plb"], w=[k("rT")])
                    yield from need(P, 3)
                    br_ = P.bank(); bk_ = P.bank()
                    proj_fm(P, ps[br_][:, 0:BW], w_rk, fc * 128, 128, u, t0, BW, ["w_rk"] + ukeys, "ps%d" % br_)
                    proj_fm(P, ps[bk_][:, 0:BW], w_rk, 256 + fc * 128, 128, u, t0, BW, ["w_rk"] + ukeys, "ps%d" % bk_)
                    P.cp("act", kT[:], ps[bk_][:, 0:BW], r=["ps%d" % bk_], w=[k("kT")])
                    P.release(bk_)
                    P.stt("dve", kT[:], ps[br_][:, 0:BW], prk[:, fc:fc + 1], kT[:], ALU.mult, ALU.mult, r=["ps%d" % br_, "prk", k("kT")], w=[k("kT")])
                    P.release(br_)
                    yield
                    yield from need(P, 3)
                    bv_ = P.bank(); bg_ = P.bank(); bb_ = P.bank()
                    proj_fm(P, ps[bv_][:, 0:BW], w_vg, fc * 128, 128, u, t0, BW, ["w_vg"] + ukeys, "ps%d" % bv_)
                    proj_fm(P, ps[bg_][:, 0:BW], w_g, fc * 128, 128, u, t0, BW, [wgk] + ukeys, "ps%d" % bg_)
                    P.mm(ps[bb_][:, 0:BW], bones[:], kT[:], r=["bones", k("kT")], w=["ps%d" % bb_])
                    P.cp("act", kkT[:], ps[bv_][:, 0:BW], r=["ps%d" % bv_], w=[k("kkT")])
                    P.release(bv_)
                    P.act(sgw[:], ps[bg_][:, 0:BW], AF.Tanh, scale=0.5, r=["ps%d" % bg_], w=[k("sgw")])
                    P.stt("dve", sgw[:], sgw[:], 1.0, ps[bg_][:, 0:BW], ALU.add, ALU.mult, r=[k("sgw"), "ps%d" % bg_], w=[k("sgw")])
                    P.release(bg_)
                    P.tt("dve", kkT[:], kkT[:], ps[bb_][:, 0:BW], ALU.mult, r=[k("kkT"), "ps%d" % bb_], w=[k("kkT")])
                    P.release(bb_)
                    P.tt("pool", rT[:], rT[:], kkT[:], ALU.add, r=[k("rT"), k("kkT")], w=[k("rT")])
                    P.stt("dve", y[:, fc, t0:t0 + BW], rT[:], 0.5, sgw[:], ALU.mult, ALU.mult, r=[k("rT"), k("sgw")], w=["y:%d" % fc])
                    yield

        def unit_gen(B, s):
            k = B.k
            H32, Hbf = B.H32, B.Hbf
            if g.ctx:
                S0 = [B.tr[e % NTR].rT[0:64, 0:256].rearrange("p (f h k) -> p f h k", f=2, h=2) for e in range(2)]
                s0k = [B.k("rT" + B.tr[e % NTR].sfx) for e in range(2)]
                yield from need(P, 1)
                b = P.bank()
                for e in range(2):
                    for fc in range(2):
                        P.dma("sp", S0[e][:, fc, :, :], env["st_rwkv"][l, e, 2 * fc:2 * fc + 2].rearrange("h v k -> v h k"), w=[s0k[e]])
                    for fc in range(2):
                        a = e * 2 + fc
                        P.mm(ps[b][:, a * 64:(a + 1) * 64], S0[e][:, fc, :, :].rearrange("v h k -> v (h k)"), identf[0:64, 0:64],
                             r=[s0k[e], "identf"], w=[rk(b, a)])
                P.cp("dve", H32[:], ps[b][:, 0:256].rearrange("p (a v) -> p a v", v=64), r=rks(b, 4), w=[k("H32")])
                P.release(b)
            else:
                P.memset("dve", H32[:], 0.0, w=[k("H32")])
            P.cp("act", Hbf[:], H32[:], r=[k("H32")], w=[k("Hbf")])
            yield
            gci = 0
            if NTR == 2:
                yield from interleave(derive_gen(B, s, 0, 0), derive_gen(B, s, 0, 1))
            else:
                for e in range(2):
                    yield from derive_gen(B, s, 0, e)
            yield from gn_gen(B, 0, 0, gci % 2)
            for blk in range(NB):
                for ci in range(CB):
                    gens = [rec_gen(B, blk, ci, gci % 2, blk * CB + ci)]
                    if ci + 1 < CB:
                        gens.append(gn_gen(B, blk, ci + 1, (gci + 1) % 2))
                        if blk + 1 < NB:
                            gens.append(derive_gen(B, s, blk + 1, 0))
                            gens.append(derive_gen(B, s, blk + 1, 1))
                    elif blk + 1 < NB:
                        gens.append(gn_gen(B, blk + 1, 0, (gci + 1) % 2))
                    yield from interleave(*gens)
                    gci += 1
            if not g.ctx:
                yield from need(P, 1)
                b = P.bank()
                for a in range(4):
                    P.mm(ps[b][0:64, a * 128:(a + 1) * 128], H32[:, a, :], identf[:], r=[k("H32"), "identf"], w=[rk(b, a)])
                P.cp("act", B.SS[0:64, 0:512], ps[b][0:64, :], r=rks(b, 4), w=[B.ssk])
                P.release(b)
                for e in range(2):
                    for fc in range(2):
                        P.dma("act", env["o_rwkv"][s, l, e, 2 * fc:2 * fc + 2].rearrange("h v k -> v h k"), B.So[:, e * 2 + fc, :, :], r=[B.ssk])
            yield
            if NTR == 2:
                yield from interleave(epi_gen(B, s, (0,), 0), epi_gen(B, s, (1,), 1))
            else:
                yield from epi_gen(B, s, (0, 1), 0)

        sets = [alloc_set(i) for i in range(nth)]
        for s0 in range(0, g.nseq, nth):
            yield from interleave(*[unit_gen(sets[i], s0 + i) for i in range(nth) if s0 + i < g.nseq])
        if dbank is not None:
            P.release(dbank)
        if not threaded:
            dump("yrw_" + tag, y[:, :, :], [128, 2, g.T], ["y:0", "y:1"])
            P.barrier()


def host_consts():
    ident = np.eye(128, dtype=np.float32)
    jrev = np.ascontiguousarray(ident[::-1])
    jj, ii = np.meshgrid(np.arange(128), np.arange(128), indexing="ij")
    mU_s = (jj < ii).astype(np.float32)
    mU_i = (jj <= ii).astype(np.float32)
    mL_s = (ii < jj).astype(np.float32)
    mask1 = np.tile(np.concatenate([-mU_s, mU_i], 1), (1, 2))
    mask2 = np.tile(np.concatenate([mU_s, mU_i], 1), (1, 2))
    mask3 = np.tile(-mL_s, (1, 4))
    bones = np.zeros((128, 128), np.float32)
    bones[:64, :64] = 1.0
    bones[64:, 64:] = 1.0
    Lq = 2048
    row = np.repeat(np.arange(Lq // 64), 64).astype(np.float32)
    col = np.tile(np.arange(64), Lq // 64).astype(np.float32)
    inv = (np.float32(10000.0) ** (-np.arange(8, dtype=np.float32) / np.float32(8))).astype(np.float32)
    ang = np.concatenate([row[:, None] * inv, col[:, None] * inv], -1).astype(np.float32)
    cosT = np.repeat(np.cos(ang).astype(np.float32).T, 2, axis=0)
    sinT = np.repeat(np.sin(ang).astype(np.float32).T, 2, axis=0)
    return dict(k_ident=ident, k_jrev=jrev, k_mask1=np.ascontiguousarray(mask1), k_mask2=np.ascontiguousarray(mask2),
                k_mask3=np.ascontiguousarray(mask3), k_bones=bones, k_cos=np.ascontiguousarray(cosT),
                k_sin=np.ascontiguousarray(sinT))


WEIGHT_NAMES = ["w_mod", "b_mod", "w_in", "rw_w0", "rw_w1", "rw_w2", "rw_a0", "rw_a1", "rw_a2", "rw_kk", "rw_ka", "rw_rk",
                "rw_lnx_g", "rw_lnx_b", "mla_qnorm", "mla_wuq", "mla_kvnorm", "mla_wukv", "lru_conv_w", "lru_conv_b",
                "lru_wa", "lru_ba", "lru_wx", "lru_bx", "lru_lambda", "w_out", "ln_g", "ln_b"]


def make_in_maps(inputs, cores):
    f = lambda a: np.ascontiguousarray(np.asarray(a, dtype=np.float32))
    shared = {k: f(inputs[k]) for k in WEIGHT_NAMES}
    shared["rw_rk"] = shared["rw_rk"].reshape(2, 256)
    shared.update(host_consts())
    maps = []
    for i in cores:
        m = dict(shared)
        m["xs"] = f(inputs["x_sample"][i])
        m["xp"] = f(inputs["x_prompt"][4 * i:4 * i + 4]).reshape(1024, D)
        m["st_rwkv"] = f(inputs["state_rwkv"][i])
        m["c_ckv"] = f(inputs["cache_mla_ckv"][i])
        m["c_kr"] = f(inputs["cache_mla_krope"][i])
        m["st_lru"] = f(inputs["state_lru"][i])
        m["cvec"] = f(np.stack([np.asarray(inputs["c"][i]), np.asarray(inputs["c_ctx"])]))
        maps.append(m)
    return maps


def kernel(**inputs):
    nc = build()[0]
    cores = list(range(8))
    res = run_bass_kernel_spmd(nc, make_in_maps(inputs, cores), core_ids=cores)
    R = res.results
    y_prompt = np.concatenate([r["y_p"].reshape(4, 256, D) for r in R], 0)
    y_sample = np.stack([r["y_s"] for r in R], 0)
    o_rwkv = np.concatenate([r["o_rwkv"] for r in R], 0)
    o_ckv = np.concatenate([r["o_ckv"] for r in R], 0)
    o_kr = np.concatenate([r["o_kr"] for r in R], 0)
    o_lru = np.concatenate([r["o_lru"] for r in R], 0)
    return (y_prompt.astype(np.float32), y_sample.astype(np.float32), o_rwkv.astype(np.float32),
            o_ckv.astype(np.float32), o_kr.astype(np.float32), o_lru.astype(np.float32))
```

```python
import numpy as np
from collections import deque
from contextlib import ExitStack
import concourse.bass as bass
import concourse.mybir as mybir
from concourse.bass_utils import run_bass_kernel_spmd

F32 = mybir.dt.float32
BF16 = mybir.dt.bfloat16
ALU = mybir.AluOpType
AF = mybir.ActivationFunctionType

D = 1024
DEPTH = 2
ALPHA = (2 * DEPTH) ** 0.25
LN_EPS = 1e-5
GN_EPS = 64e-5
SM_SCALE = 96.0 ** -0.5
WDEC = float(np.exp(-0.5))
C_R, C_K, C_V, C_G, C_CQ, C_CKV, C_KR, C_GM, C_XL, C_GL = 0, 256, 512, 768, 1024, 1280, 1408, 1440, 1952, 2208

ENGS = ("pe", "act", "dve", "pool", "sp")


class Prog:
    N_DSEM = 40

    def __init__(self, nc, stack, same_engine_sync=True):
        self.nc = nc
        self.ops = {e: [] for e in ENGS}
        self.cnt = {e: 0 for e in ENGS}
        self.esem = {e: stack.enter_context(nc.semaphore("es_" + e)) for e in ENGS}
        self.dsem = [stack.enter_context(nc.semaphore("ds%d" % i)) for i in range(self.N_DSEM)]
        self.dcum = [0] * self.N_DSEM
        self.dnext = 0
        self.know_e = {e: {f: 0 for f in ENGS} for e in ENGS}
        self.know_d = {e: [0] * self.N_DSEM for e in ENGS}
        self.last_w = {}
        self.readers = {}
        self.same_engine_sync = same_engine_sync
        self.ps = [stack.enter_context(nc.psum_tensor("psb%d" % i, [128, 512], F32)) for i in range(8)]
        self.free_banks = deque(range(8))

    def pefull(self, on):
        self.pe_full = on and self.allow_pe_full

    allow_pe_full = ()

    def bank(self):
        assert self.free_banks, "out of PSUM banks"
        return self.free_banks.popleft()

    def release(self, b):
        self.free_banks.append(b)

    def _need(self, eng, ev):
        if ev is None:
            return
        if ev[0] == "c":
            _, f, n = ev
            if self.know_e[eng][f] >= n:
                return
            self.know_e[eng][f] = n
            self.ops[eng].append(("wc", f, n))
        else:
            _, i, v = ev
            if self.know_d[eng][i] >= v:
                return
            self.know_d[eng][i] = v
            self.ops[eng].append(("wd", i, v))

    relax = ("pe", "act", "dve")

    pe_cont = False
    pe_full = False

    def _same(self, eng, ev):
        if eng == "pe":
            return (self.pe_cont or self.pe_full) and ev is not None and ev[0] == "c" and ev[1] == eng
        return (eng in self.relax) and ev is not None and ev[0] == "c" and ev[1] == eng

    def _deps(self, eng, reads, writes):
        for k in reads:
            ev = self.last_w.get(k)
            if self._same(eng, ev) and not self.same_engine_sync:
                continue
            self._need(eng, ev)
        for k in writes:
            ev = self.last_w.get(k)
            if not self._same(eng, ev):
                self._need(eng, ev)
            for ev in self.readers.get(k, ()):
                if not self._same(eng, ev):
                    self._need(eng, ev)

    def _commit(self, ev, reads, writes):
        for k in reads:
            self.readers.setdefault(k, []).append(ev)
        for k in writes:
            self.last_w[k] = ev
            self.readers[k] = []

    def op(self, eng, fn, r=(), w=()):
        self._deps(eng, r, w)
        self.cnt[eng] += 1
        n = self.cnt[eng]
        self.ops[eng].append(("ins", fn, n))
        self._commit(("c", eng, n), r, w)

    def dma(self, eng, out, in_, r=(), w=(), **kw):
        self._deps(eng, r, w)
        i = self.dnext
        self.dnext = (self.dnext + 1) % self.N_DSEM
        if self.dcum[i] > 0:
            self._need(eng, ("d", i, self.dcum[i]))
        self.dcum[i] += 16
        v = self.dcum[i]
        sem = self.dsem[i]
        self.ops[eng].append(
            ("raw", lambda E, sem=sem, out=out, in_=in_, kw=kw: E.dma_start(out=out, in_=in_, **kw).then_inc(sem, 16)))
        self._commit(("d", i, v), r, w)

    def barrier(self):
        for e in ENGS:
            for f in ENGS:
                if self.cnt[f] > 0:
                    self._need(e, ("c", f, self.cnt[f]))
            for i in range(self.N_DSEM):
                if self.dcum[i] > 0:
                    self._need(e, ("d", i, self.dcum[i]))
        self.last_w = {}
        self.readers = {}

    def emit(self):
        self.barrier()
        ops = self.ops
        marked = {e: set() for e in ENGS}
        for e in ENGS:
            for rec in ops[e]:
                if rec[0] == "wc":
                    marked[rec[1]].add(rec[2])
        val = {}
        for e in ENGS:
            c = 0
            for n in sorted(marked[e]):
                c += 1
                val[(e, n)] = c
            assert c < 60000, (e, c)
        esem, dsem = self.esem, self.dsem

        def run(E, e):
            sem = esem[e]
            for rec in ops[e]:
                k = rec[0]
                if k == "ins":
                    ins = rec[1](E)
                    if rec[2] in marked[e]:
                        ins.then_inc(sem, 1)
                elif k == "wc":
                    E.wait_ge(esem[rec[1]], val[(rec[1], rec[2])])
                elif k == "wd":
                    E.wait_ge(dsem[rec[1]], rec[2])
                else:
                    rec[1](E)

        with self.nc.Block() as block:
            @block.tensor
            def _(E):
                run(E, "pe")

            @block.scalar
            def _(E):
                run(E, "act")

            @block.vector
            def _(E):
                run(E, "dve")

            @block.gpsimd
            def _(E):
                run(E, "pool")

            @block.sync
            def _(E):
                run(E, "sp")

    def mm(self, out, lhsT, rhs, start=True, stop=True, r=(), w=()):
        self.pe_cont = (not start) and ("pe" in self.relax)
        self.op("pe", lambda E: E.matmul(out, lhsT, rhs, start=start, stop=stop), r, w)
        self.pe_cont = False

    def act(self, out, in_, func, r=(), w=(), bias=None, scale=None, accum=None):
        kw = {}
        if bias is not None:
            kw["bias"] = bias
        if scale is not None:
            kw["scale"] = scale
        if accum is not None:
            kw["accum_out"] = accum
        self.op("act", lambda E: E.activation(out, in_, func, **kw), r, w)

    def cp(self, eng, out, in_, r=(), w=()):
        if eng == "act":
            self.op("act", lambda E: E.copy(out, in_), r, w)
        else:
            self.op(eng, lambda E: E.tensor_copy(out, in_), r, w)

    def ts(self, eng, out, in0, s1, s2, op0, op1, r=(), w=()):
        self.op(eng, lambda E: E.tensor_scalar(out, in0, s1, s2, op0, op1), r, w)

    def ts1(self, eng, out, in0, s1, op0, r=(), w=()):
        self.op(eng, lambda E: E.tensor_single_scalar(out, in0, s1, op0), r, w)

    def tt(self, eng, out, in0, in1, op, r=(), w=()):
        self.op(eng, lambda E: E.tensor_tensor(out, in0, in1, op), r, w)

    def stt(self, eng, out, in0, scalar, in1, op0, op1, r=(), w=()):
        self.op(eng, lambda E: E.scalar_tensor_tensor(out, in0, scalar, in1, op0, op1), r, w)

    def rsqrt(self, out, in_, eps, scale, tx, tt_, r=(), w=(), kx=None, kt=None, small=False):
        self.act(out, in_, AF.Ln, bias=eps, scale=scale, r=r, w=w)
        self.act(out, out, AF.Exp, scale=-0.5, r=w, w=w)

    def memset(self, eng, ap, val, w=()):
        self.op(eng, lambda E: E.memset(ap, val), (), w)


class Group:
    def __init__(self, name, nseq, L, j, ctx):
        self.name, self.nseq, self.L, self.j, self.ctx = name, nseq, L, j, ctx
        self.T = nseq * L
        self.NT = self.T // 128
        self.LT = L // 128
        self.BW = min(512, L)
        self.CB = self.BW // 128
        self.NB = L // self.BW


def build(dbg=None):
    dbg = dbg or {}
    stop = dbg.get("stop", "")
    groups_sel = dbg.get("groups", "SP")
    nlayers = dbg.get("nlayers", DEPTH)
    nc = bass.Bass("TRN2", target_bir_lowering=False)

    def din(name, shape):
        return nc.dram_tensor(name, list(shape), F32, kind="ExternalInput").ap()

    def dout(name, shape):
        return nc.dram_tensor(name, list(shape), F32, kind="ExternalOutput").ap()

    xs = din("xs", [2048, D]); xp = din("xp", [1024, D])
    st_rwkv = din("st_rwkv", [2, 2, 4, 64, 64]); c_ckv = din("c_ckv", [2, 256, 128]); c_kr = din("c_kr", [2, 256, 32])
    st_lru = din("st_lru", [2, 2, 256]); cvec = din("cvec", [2, D])
    w_mod = din("w_mod", [2, D, 3 * D]); b_mod = din("b_mod", [2, 3 * D]); w_in = din("w_in", [2, D, 2464])
    rw_w0 = din("rw_w0", [2, 2, 256]); rw_w1 = din("rw_w1", [2, 2, D, 64]); rw_w2 = din("rw_w2", [2, 2, 64, 256])
    rw_a0 = din("rw_a0", [2, 2, 256]); rw_a1 = din("rw_a1", [2, 2, D, 64]); rw_a2 = din("rw_a2", [2, 2, 64, 256])
    rw_kk = din("rw_kk", [2, 256]); rw_ka = din("rw_ka", [2, 256]); rw_rk = din("rw_rk", [2, 256])
    rw_lnx_g = din("rw_lnx_g", [2, 256]); rw_lnx_b = din("rw_lnx_b", [2, 256])
    mla_qnorm = din("mla_qnorm", [2, 256]); mla_wuq = din("mla_wuq", [2, 256, 768])
    mla_kvnorm = din("mla_kvnorm", [2, 128]); mla_wukv = din("mla_wukv", [2, 128, 1024])
    lru_conv_w = din("lru_conv_w", [2, 4, 256]); lru_conv_b = din("lru_conv_b", [2, 256])
    lru_wa = din("lru_wa", [2, 2, 4, 64, 64]); lru_ba = din("lru_ba", [2, 2, 256])
    lru_wx = din("lru_wx", [2, 2, 4, 64, 64]); lru_bx = din("lru_bx", [2, 2, 256]); lru_lambda = din("lru_lambda", [2, 2, 256])
    w_out = din("w_out", [2, D, D]); ln_g = din("ln_g", [2, D]); ln_b = din("ln_b", [2, D])
    k_ident = din("k_ident", [128, 128]); k_jrev = din("k_jrev", [128, 128])
    k_mask1 = din("k_mask1", [128, 512]); k_mask2 = din("k_mask2", [128, 512]); k_mask3 = din("k_mask3", [128, 512])
    k_bones = din("k_bones", [128, 128]); k_cos = din("k_cos", [32, 2048]); k_sin = din("k_sin", [32, 2048])

    y_s = dout("y_s", [2048, D]); y_p = dout("y_p", [1024, D])
    o_rwkv = dout("o_rwkv", [4, 2, 2, 4, 64, 64]); o_ckv = dout("o_ckv", [4, 2, 256, 128])
    o_kr = dout("o_kr", [4, 2, 256, 32]); o_lru = dout("o_lru", [4, 2, 2, 256])
    xmid = {"S": nc.dram_tensor("xmid_s", [2048, D], F32).ap(), "P": nc.dram_tensor("xmid_p", [1024, D], F32).ap()}

    dumps = []

    with ExitStack() as st:
        P = Prog(nc, st, same_engine_sync=dbg.get("ses", True))
        P.relax = dbg.get("relax", Prog.relax)
        P.allow_pe_full = dbg.get("pe_full", Prog.allow_pe_full)
        ps = P.ps

        uniq = [0]

        def sb(stack, name, shape, dt):
            uniq[0] += 1
            return stack.enter_context(nc.sbuf_tensor("%s_%d" % (name, uniq[0]), list(shape), dt))

        def dump(name, ap, shape, rkeys):
            if name not in dbg.get("dump", ()):
                return
            o = dout("dbg_" + name, shape)
            P.dma("pool", o, ap, r=rkeys, allow_slow_non_contiguous=True)
            dumps.append(name)

        identf = sb(st, "identf", [128, 128], F32); jrevf = sb(st, "jrevf", [128, 128], F32)
        identb = sb(st, "identb", [128, 128], BF16); jrevb = sb(st, "jrevb", [128, 128], BF16)
        mask1 = sb(st, "mask1", [128, 512], BF16); mask2 = sb(st, "mask2", [128, 512], BF16); mask3 = sb(st, "mask3", [128, 512], BF16)
        bones = sb(st, "bones", [128, 128], F32); onesf = sb(st, "onesf", [128, 128], F32)
        P.dma("sp", identf[:], k_ident, w=["identf"]); P.dma("sp", jrevf[:], k_jrev, w=["jrevf"])
        P.dma("pool", identb[:], k_ident, w=["identb"]); P.dma("pool", jrevb[:], k_jrev, w=["jrevb"])
        P.dma("pool", mask1[:], k_mask1, w=["mask1"]); P.dma("pool", mask2[:], k_mask2, w=["mask2"]); P.dma("pool", mask3[:], k_mask3, w=["mask3"])
        P.dma("sp", bones[:], k_bones, w=["bones"])
        P.memset("dve", onesf[:], 1.0, w=["onesf"])
        mod = [sb(st, "mod%d" % l, [128, 24, 2], F32) for l in range(2)]
        sc1 = [sb(st, "sc1_%d" % l, [128, 8, 2], F32) for l in range(2)]
        gta = [sb(st, "gta%d" % l, [128, 8, 2], F32) for l in range(2)]

        with ExitStack() as ph:
            cv = sb(ph, "cv", [128, 8, 2], F32); scv = sb(ph, "scv", [128, 8, 2], F32)
            bm = sb(ph, "bm", [128, 24], F32)
            wm = [sb(ph, "wm%d" % i, [128, 8, 512], F32) for i in range(3)]
            for j in range(2):
                P.dma("sp", cv[:, :, j], cvec[j].rearrange("(kc p) -> p kc", p=128), w=["cv"], allow_slow_non_contiguous=True)
            P.act(scv[:], cv[:], AF.Silu, r=["cv"], w=["scv"])
            modrow = sb(ph, "modrow", [2, 3 * D], F32); bmrow = sb(ph, "bmrow", [2, 3 * D], F32)
            for l in range(2):
                P.dma("sp", bmrow[:], b_mod[l:l + 1, :].to_broadcast([2, 3 * D]), w=["bmrow"])
                for ob in range(6):
                    wi = (l * 6 + ob) % 3
                    wmt = wm[wi]; wk = "wm%d" % wi
                    P.dma("sp" if ob % 2 == 0 else "act", wmt[:], w_mod[l, :, ob * 512:(ob + 1) * 512].rearrange("(kc p) n -> p kc n", p=128), w=[wk])
                    b = P.bank(); bk = "ps%d" % b
                    for kc in range(8):
                        P.mm(ps[b][0:2, :], scv[:, kc, :], wmt[:, kc, :], start=(kc == 0), stop=(kc == 7), r=[wk, "scv"], w=[bk])
                    P.tt("dve", modrow[:, ob * 512:(ob + 1) * 512], ps[b][0:2, :], bmrow[:, ob * 512:(ob + 1) * 512], ALU.add,
                         r=[bk, "bmrow"], w=["modrow"])
                    P.release(b)
                b = P.bank(); bk = "ps%d" % b
                for oc in range(24):
                    P.mm(ps[b][:, oc * 2:oc * 2 + 2], modrow[0:2, oc * 128:(oc + 1) * 128], identf[0:2, 0:2], r=["modrow", "identf"], w=[bk])
                P.cp("dve", mod[l][:], ps[b][:, 0:48].rearrange("p (a b) -> p a b", b=2), r=[bk], w=["mod%d" % l])
                P.release(b)
                P.ts1("dve", sc1[l][:], mod[l][:, 8:16, :], 1.0, ALU.add, r=["mod%d" % l], w=["sc1_%d" % l])
                P.ts1("dve", gta[l][:], mod[l][:, 16:24, :], 1.0 / ALPHA, ALU.mult, r=["mod%d" % l], w=["gta%d" % l])
            dump("mod0", mod[0][:], [128, 24, 2], ["mod0"])
            P.barrier()
        if stop == "0":
            P.emit()
            return nc, dumps

        groups = []
        if "S" in groups_sel:
            groups.append(Group("S", 1, 2048, 0, True))
        if "P" in groups_sel:
            groups.append(Group("P", 4, 256, 1, False))
        xin0 = {"S": xs, "P": xp}
        xfin = {"S": y_s, "P": y_p}
        env = dict(nc=nc, P=P, sb=sb, dump=dump, stop=stop, identf=identf, jrevf=jrevf, identb=identb, jrevb=jrevb,
                   mask1=mask1, mask2=mask2, mask3=mask3, bones=bones, onesf=onesf, k_cos=k_cos, k_sin=k_sin,
                   mod=mod, sc1=sc1, gta=gta, dbg=dbg,
                   W=dict(w_in=w_in, rw_w0=rw_w0, rw_w1=rw_w1, rw_w2=rw_w2, rw_a0=rw_a0, rw_a1=rw_a1, rw_a2=rw_a2,
                          rw_kk=rw_kk, rw_ka=rw_ka, rw_rk=rw_rk, rw_lnx_g=rw_lnx_g, rw_lnx_b=rw_lnx_b,
                          mla_qnorm=mla_qnorm, mla_wuq=mla_wuq, mla_kvnorm=mla_kvnorm, mla_wukv=mla_wukv,
                          lru_conv_w=lru_conv_w, lru_conv_b=lru_conv_b, lru_wa=lru_wa, lru_ba=lru_ba, lru_wx=lru_wx,
                          lru_bx=lru_bx, lru_lambda=lru_lambda, w_out=w_out, ln_g=ln_g, ln_b=ln_b),
                   st_rwkv=st_rwkv, c_ckv=c_ckv, c_kr=c_kr, st_lru=st_lru,
                   o_rwkv=o_rwkv, o_ckv=o_ckv, o_kr=o_kr, o_lru=o_lru)
        for g in groups:
            for l in range(nlayers):
                x_in = xin0[g.name] if l == 0 else xmid[g.name]
                x_out = xfin[g.name] if l == nlayers - 1 else xmid[g.name]
                done = layer(env, g, l, x_in, x_out, "xin%s%d" % (g.name, l), "xin%s%d" % (g.name, l + 1))
                if done:
                    P.emit()
                    return nc, dumps
        P.emit()
    return nc, dumps


def layer(env, g, l, x_in, x_out, kin, kout):
    nc, P, sb, dump, stop = env["nc"], env["P"], env["sb"], env["dump"], env["stop"]
    ps = P.ps
    W = env["W"]
    identb, jrevb, identf, jrevf = env["identb"], env["jrevb"], env["identf"], env["jrevf"]
    T, NT, LT, L, j = g.T, g.NT, g.LT, g.L, g.j
    tag = "%s%d" % (g.name, l)
    w_in = W["w_in"]

    def wslice(c0, c1):
        return w_in[l, :, c0:c1].rearrange("(kc p) n -> p kc n", p=128)

    def ppar(stack, name, src):
        t = sb(stack, name, [128, 2], F32)
        P.dma("sp", t[:], src.rearrange("(fc p) -> p fc", p=128), w=[name], allow_slow_non_contiguous=True)
        return t

    with ExitStack() as lay:
        u = sb(lay, "u", [128, 8, T], BF16)
        y_rw = sb(lay, "y_rw", [128, 2, T], BF16)
        y_lru = sb(lay, "y_lru", [128, 2, T], BF16)
        with ExitStack() as revs:
            ur = sb(revs, "ur", [128, 8, T], BF16)
            with ExitStack() as ph:
                NBUF = 3
                xb = [sb(ph, "xa%d" % i, [128, D], F32) for i in range(NBUF)]
                xn = [sb(ph, "xn%d" % i, [128, D], BF16) for i in range(NBUF)]
                st6 = sb(ph, "st6", [128, 12], F32); mv = sb(ph, "mv", [128, 2], F32); rs = sb(ph, "rs", [128, 1], F32)
                rq1 = sb(ph, "rq1", [128, 1], F32); rq2 = sb(ph, "rq2", [128, 1], F32)

                def stage1(tt):
                    xt = xb[tt % NBUF]; xk = "xa%d" % (tt % NBUF); xnt = xn[tt % NBUF]; xnk = "xn%d" % (tt % NBUF)
                    P.dma("sp", xt[:], x_in[tt * 128:(tt + 1) * 128, :], r=["%s:%d" % (kin, tt)], w=[xk])
                    P.op("dve", lambda E, xt=xt: E.bn_stats(st6[:, 0:6], xt[:, 0:512]), r=[xk], w=["st6a"])
                    P.op("dve", lambda E, xt=xt: E.bn_stats(st6[:, 6:12], xt[:, 512:1024]), r=[xk], w=["st6b"])
                    P.op("dve", lambda E: E.bn_aggr(mv[:], st6[:]), r=["st6a", "st6b"], w=["mv"])
                    P.rsqrt(rs[:], mv[:, 1:2], LN_EPS, 1.0, rq1[:], rq2[:], r=["mv"], w=["rs"], kx="rq1", kt="rq2", small=True)
                    P.ts("dve", xnt[:], xt[:], mv[:, 0:1], rs[:], ALU.subtract, ALU.mult, r=[xk, "mv", "rs"], w=[xnk])

                def stage2(tt):
                    s_, i = divmod(tt, LT)
                    rt = s_ * LT + (LT - 1 - i)
                    xnt = xn[tt % NBUF]; xnk = "xn%d" % (tt % NBUF)
                    for (dst, dk, perm, pk, tpos) in ((u, "u", identb, "identb", tt), (ur, "ur", jrevb, "jrevb", rt)):
                        for half in range(2):
                            b = P.bank(); bk = "ps%d" % b
                            for q in range(4):
                                kc = half * 4 + q
                                P.mm(ps[b][:, q * 128:(q + 1) * 128], xnt[:, kc * 128:(kc + 1) * 128], perm[:], r=[xnk, pk], w=[bk])
                            for q in range(4):
                                kc = half * 4 + q
                                if dk == "u":
                                    P.act(dst[:, kc, tpos * 128:(tpos + 1) * 128], ps[b][:, q * 128:(q + 1) * 128], AF.Identity,
                                          scale=env["sc1"][l][:, kc, j:j + 1], bias=env["mod"][l][:, kc, j:j + 1],
                                          r=[bk, "sc1_%d" % l, "mod%d" % l], w=["%s:%d" % (dk, tpos)])
                                else:
                                    P.ts("dve", dst[:, kc, tpos * 128:(tpos + 1) * 128], ps[b][:, q * 128:(q + 1) * 128],
                                         env["sc1"][l][:, kc, j:j + 1], env["mod"][l][:, kc, j:j + 1], ALU.mult, ALU.add,
                                         r=[bk, "sc1_%d" % l, "mod%d" % l], w=["%s:%d" % (dk, tpos)])
                            P.release(b)

                for tt in range(NT + 1):
                    if tt < NT:
                        stage1(tt)
                    if tt >= 1:
                        stage2(tt - 1)
                dump("u_" + tag, u[:], [128, 8, T], ["u:%d" % t for t in range(NT)])
                dump("ur_" + tag, ur[:], [128, 8, T], ["ur:%d" % t for t in range(NT)])
                P.barrier()
            if stop == "A":
                return True
            ukeys = ["u:%d" % t for t in range(NT)]
            urkeys = ["ur:%d" % t for t in range(NT)]
            skip = env["dbg"].get("skip", "")
            threaded = (g.nseq > 1) and skip == "" and stop in ("", "E") and env["dbg"].get("threads", True)
            if threaded:
                y_mla = sb(revs, "y_mla", [128, 4, T], BF16)
                with ExitStack() as shared:
                    tw = env["dbg"].get("tw", (6, 1, 1))
                    run_gen(interleave_w([(phase_rwkv(env, g, l, u, ur, y_rw, ukeys, urkeys, wslice, ppar, tag, threaded=True, ext=shared), tw[0]),
                                          (phase_lru(env, g, l, u, ur, y_lru, ukeys, urkeys, wslice, ppar, tag, threaded=True, ext=shared), tw[1]),
                                          (phase_mla(env, g, l, u, y_mla, ukeys, wslice, ppar, tag, threaded=True, ext=shared), tw[2])]))
                    P.barrier()
                phase_out(env, g, l, (y_rw, y_mla, y_lru), x_in, x_out, kin, kout, tag)
                return stop == "E"
            if "B" not in skip:
                run_gen(phase_rwkv(env, g, l, u, ur, y_rw, ukeys, urkeys, wslice, ppar, tag))
            if stop == "B":
                return True
            if "C" not in skip:
                run_gen(phase_lru(env, g, l, u, ur, y_lru, ukeys, urkeys, wslice, ppar, tag))
            if stop == "C":
                return True
        y_mla = sb(lay, "y_mla", [128, 4, T], BF16)
        if "D" not in skip:
            run_gen(phase_mla(env, g, l, u, y_mla, ukeys, wslice, ppar, tag))
        if stop == "D":
            return True
        phase_out(env, g, l, (y_rw, y_mla, y_lru), x_in, x_out, kin, kout, tag)
        if stop == "E":
            return True
    return False


def _unpack(env):
    return env["nc"], env["P"], env["sb"], env["dump"], env["W"]


def proj_fm(P, out, w, c0, M, src, t0, N, r, wk):
    for kc in range(8):
        P.mm(out, w[:, kc, c0:c0 + M], src[:, kc, t0:t0 + N], start=(kc == 0), stop=(kc == 7), r=r, w=[wk])


def phase_out(env, g, l, y, x_in, x_out, kin, kout, tag):
    nc, P, sb, dump, W = _unpack(env)
    ps = P.ps
    identf = env["identf"]
    NT, j = g.NT, g.j
    ykeys = ["y:%d" % c for c in range(8)]
    with ExitStack() as ph:
        w_o = sb(ph, "w_o", [128, 8, D], BF16)
        P.dma("pool", w_o[:], W["w_out"][l].rearrange("(kc p) n -> p kc n", p=128), w=["w_o"])
        gate_bc = sb(ph, "gate_bc", [128, D], F32); g_bc = sb(ph, "g_bc", [128, D], F32); b_bc = sb(ph, "b_bc", [128, D], F32)
        P.dma("sp", g_bc[:], W["ln_g"][l:l + 1, :].to_broadcast([128, D]), w=["g_bc"])
        P.dma("sp", b_bc[:], W["ln_b"][l:l + 1, :].to_broadcast([128, D]), w=["b_bc"])
        for half in range(2):
            b = P.bank(); bk = "ps%d" % b
            for q in range(4):
                kc = half * 4 + q
                P.mm(ps[b][:, q * 128:(q + 1) * 128], env["gta"][l][:, kc, j:j + 1].to_broadcast([128, 128]), identf[:],
                     r=["gta%d" % l, "identf"], w=[bk])
            P.cp("act", gate_bc[:, half * 512:(half + 1) * 512], ps[b][:], r=[bk], w=["gate_bc"])
            P.release(b)
        xb = [sb(ph, "xo%d" % i, [128, D], F32) for i in range(3)]
        t1 = [sb(ph, "t1o%d" % i, [128, D], F32) for i in range(3)]
        st6 = sb(ph, "st6o", [128, 12], F32); mv = sb(ph, "mvo", [128, 2], F32); rs = sb(ph, "rso", [128, 1], F32)
        nmr = sb(ph, "nmr", [128, 1], F32); rq1 = sb(ph, "rq1o", [128, 1], F32); rq2 = sb(ph, "rq2o", [128, 1], F32)
        def stage_a(tt):
            xt = xb[tt % 3]; xk = "xo%d" % (tt % 3); tt1 = t1[tt % 3]; tk = "t1o%d" % (tt % 3)
            P.dma("sp", xt[:], x_in[tt * 128:(tt + 1) * 128, :], r=["%s:%d" % (kin, tt)], w=[xk])
            bs = [P.bank(), P.bank()]
            for half in range(2):
                for kc in range(8):
                    yt_, yc_ = (y[0], kc) if kc < 2 else ((y[1], kc - 2) if kc < 6 else (y[2], kc - 6))
                    P.mm(ps[bs[half]][:, :], yt_[:, yc_, tt * 128:(tt + 1) * 128], w_o[:, kc, half * 512:(half + 1) * 512],
                         start=(kc == 0), stop=(kc == 7), r=[ykeys[kc], "w_o"], w=["ps%d" % bs[half]])
            for half in range(2):
                P.tt("dve", tt1[:, half * 512:(half + 1) * 512], ps[bs[half]][:, :], gate_bc[:, half * 512:(half + 1) * 512], ALU.mult,
                     r=["ps%d" % bs[half], "gate_bc"], w=[tk])
                P.release(bs[half])
            P.tt("pool", tt1[:, 0:512], tt1[:, 0:512], xt[:, 0:512], ALU.add, r=[tk, xk], w=[tk + "L"])
            P.tt("dve", tt1[:, 512:1024], tt1[:, 512:1024], xt[:, 512:1024], ALU.add, r=[tk, xk], w=[tk + "R"])

        def stage_a2(tt):
            tt1 = t1[tt % 3]; tk = "t1o%d" % (tt % 3)
            P.op("dve", lambda E, a=tt1: E.bn_stats(st6[:, 6:12], a[:, 512:1024]), r=[tk, tk + "R"], w=["st6ob"])
            P.op("dve", lambda E, a=tt1: E.bn_stats(st6[:, 0:6], a[:, 0:512]), r=[tk, tk + "L"], w=["st6oa"])
            P.op("dve", lambda E: E.bn_aggr(mv[:], st6[:]), r=["st6oa", "st6ob"], w=["mvo"])
            P.rsqrt(rs[:], mv[:, 1:2], LN_EPS / (ALPHA * ALPHA), 1.0, rq1[:], rq2[:], r=["mvo"], w=["rso"], kx="rq1o", kt="rq2o", small=True)
            P.stt("dve", nmr[:], mv[:, 0:1], -1.0, rs[:], ALU.mult, ALU.mult, r=["mvo", "rso"], w=["nmr"])
            P.act(tt1[:], tt1[:], AF.Identity, scale=rs[:, 0:1], bias=nmr[:, 0:1], r=[tk, tk + "L", tk + "R", "rso", "nmr"], w=[tk, tk + "L", tk + "R"])

        def stage_b(tt):
            tt1 = t1[tt % 3]; tk = "t1o%d" % (tt % 3)
            P.tt("dve", tt1[:], tt1[:], g_bc[:], ALU.mult, r=[tk, "g_bc"], w=[tk])
            P.tt("pool", tt1[:], tt1[:], b_bc[:], ALU.add, r=[tk, "b_bc"], w=[tk])
            P.dma("pool", x_out[tt * 128:(tt + 1) * 128, :], tt1[:], r=[tk], w=["%s:%d" % (kout, tt)])

        for tt in range(NT + 1):
            if tt < NT:
                stage_a(tt)
            if tt >= 1:
                stage_b(tt - 1)
            if tt < NT:
                stage_a2(tt)
        P.barrier()


def phase_lru(env, g, l, u, ur, y, ukeys, urkeys, wslice, ppar, tag, threaded=False, ext=None):
    nc, P, sb, dump, W = _unpack(env)
    ps = P.ps
    identf, jrevf = env["identf"], env["jrevf"]
    L, LT, BW, NB, CB = g.L, g.LT, g.BW, g.NB, g.CB
    NS = "NSCAN"
    with (_NoClose(ext) if ext is not None else ExitStack()) as ph:
        w_xl = sb(ph, "w_xl", [128, 8, 256], BF16); w_gl = sb(ph, "w_gl", [128, 8, 256], BF16)
        P.dma("pool", w_xl[:], wslice(C_XL, C_XL + 256), w=["w_xl"])
        P.dma("pool", w_gl[:], wslice(C_GL, C_GL + 256), w=["w_gl"])
        cw = sb(ph, "cw", [128, 2, 4], F32)
        for jj in range(4):
            P.dma("sp", cw[:, :, jj], W["lru_conv_w"][l, jj].rearrange("(fc p) -> p fc", p=128), w=["cw"], allow_slow_non_contiguous=True)
        cbias = ppar(ph, "cbias", W["lru_conv_b"][l])
        Wbd = {}
        for e in range(2):
            for nm, src in (("a", W["lru_wa"]), ("x", W["lru_wx"])):
                t = sb(ph, "wbd%s%d" % (nm, e), [128, 2, 128], BF16)
                k = "wbd%s%d" % (nm, e)
                P.memset("pool", t[:], 0.0, w=[k])
                for hh in range(2):
                    P.dma("pool", t[hh * 64:(hh + 1) * 64, :, hh * 64:(hh + 1) * 64],
                          src[l, e].rearrange("(fc hh) i o -> hh i fc o", hh=2)[hh], w=[k])
                Wbd[(e, nm)] = (t, k)
        ba = [ppar(ph, "lba%d" % e, W["lru_ba"][l, e]) for e in range(2)]
        bx = [ppar(ph, "lbx%d" % e, W["lru_bx"][l, e]) for e in range(2)]
        lam = [ppar(ph, "lam%d" % e, W["lru_lambda"][l, e]) for e in range(2)]
        hba = [sb(ph, "hba%d" % e, [128, 2], F32) for e in range(2)]
        hbx = [sb(ph, "hbx%d" % e, [128, 2], F32) for e in range(2)]
        for e in range(2):
            P.ts1("dve", hba[e][:], ba[e][:], 0.5, ALU.mult, r=["lba%d" % e], w=["hba%d" % e])
            P.ts1("dve", hbx[e][:], bx[e][:], 0.5, ALU.mult, r=["lbx%d" % e], w=["hbx%d" % e])
        cneg = [sb(ph, "cneg%d" % e, [128, 2], F32) for e in range(2)]
        hcn = [sb(ph, "hcn%d" % e, [128, 2], F32) for e in range(2)]
        for e in range(2):
            k = "cneg%d" % e
            P.act(cneg[e][:], lam[e][:], AF.Exp, scale=-1.0, r=["lam%d" % e], w=[k])
            P.act(cneg[e][:], cneg[e][:], AF.Ln, bias=1.0, r=[k], w=[k])
            P.ts1("dve", cneg[e][:], cneg[e][:], -8.0, ALU.mult, r=[k], w=[k])
            P.ts1("dve", hcn[e][:], cneg[e][:], 0.5, ALU.mult, r=[k], w=["hcn%d" % e])
        h0 = None
        if g.ctx:
            h0 = [ppar(ph, "lh0%d" % e, env["st_lru"][l, e]) for e in range(2)]
        xlp = sb(ph, "xlp", [128, 2, L + 4], F32); xc = sb(ph, "xc", [128, 2, L], F32); xcb = sb(ph, "xcb", [128, 2, L], BF16)
        hh_ = [sb(ph, "lh%d" % e, [128, 2, L], F32) for e in range(2)]
        NT2 = 2 if g.nseq == 1 else 1
        scr = []
        for i in range(NT2):
            scr.append({nm: sb(ph, "l%s%d" % (nm, i), [128, BW], F32) for nm in ("sr", "si", "aa", "a2", "bt", "hT", "sg", "tsum")})
        git = [0, 0]
        P.memset("dve", xlp[:, :, 0:2], 0.0, w=["xlp"])
        P.memset("dve", xlp[:, :, L + 2:L + 4], 0.0, w=["xlp"])
        for s in range(g.nseq):
            tok0 = s * L
            for e in range(2):
                src, skeys = (u, ukeys) if e == 0 else (ur, urkeys)
                hk = "lh%d" % e
                for blk in range(NB):
                    for fc in range(2):
                        yield from need(P, 1)
                        b = P.bank(); bk = "ps%d" % b
                        proj_fm(P, ps[b][:, 0:BW], w_xl, fc * 128, 128, src, tok0 + blk * BW, BW, ["w_xl"] + skeys, bk)
                        P.cp("act", xlp[:, fc, 2 + blk * BW:2 + (blk + 1) * BW], ps[b][:, 0:BW], r=[bk], w=["xlp"])
                        P.release(b)
                        yield
                dd = [jj - 2 for jj in range(4)] if e == 0 else [2 - jj for jj in range(4)]
                for fc in range(2):
                    P.ts("dve", xc[:, fc, :], xlp[:, fc, 2 + dd[0]:2 + dd[0] + L], cw[:, fc, 0:1], cbias[:, fc:fc + 1], ALU.mult, ALU.add,
                         r=["xlp", "cw", "cbias"], w=["xc"])
                    for jj in range(1, 4):
                        P.stt("dve", xc[:, fc, :], xlp[:, fc, 2 + dd[jj]:2 + dd[jj] + L], cw[:, fc, jj:jj + 1], xc[:, fc, :], ALU.mult, ALU.add,
                              r=["xlp", "cw", "xc"], w=["xc"])
                P.cp("act", xcb[:, 0, :], xc[:, 0, :], r=["xc"], w=["xcb"])
                P.cp("dve", xcb[:, 1, :], xc[:, 1, :], r=["xc"], w=["xcb"])
                yield
                wa_t, wa_k = Wbd[(e, "a")]; wx_t, wx_k = Wbd[(e, "x")]

                def gate_s1(fc, blk, S_, sx):
                    sl = slice(blk * BW, (blk + 1) * BW)
                    sr, si, aa, a2 = S_["sr"], S_["si"], S_["aa"], S_["a2"]
                    b1 = P.bank(); b2 = P.bank()
                    P.mm(ps[b1][:, 0:BW], wa_t[:, fc, :], xcb[:, fc, sl], r=[wa_k, "xcb"], w=["ps%d" % b1])
                    P.mm(ps[b2][:, 0:BW], wx_t[:, fc, :], xcb[:, fc, sl], r=[wx_k, "xcb"], w=["ps%d" % b2])
                    P.act(sr[:], ps[b1][:, 0:BW], AF.Tanh, bias=hba[e][:, fc:fc + 1], scale=0.5, r=["ps%d" % b1, "hba%d" % e], w=["lsr" + sx])
                    P.act(si[:], ps[b2][:, 0:BW], AF.Tanh, bias=hbx[e][:, fc:fc + 1], scale=0.5, r=["ps%d" % b2, "hbx%d" % e], w=["lsi" + sx])
                    P.release(b1); P.release(b2)
                    P.act(aa[:], sr[:], AF.Exp, scale=hcn[e][:, fc:fc + 1], bias=hcn[e][:, fc:fc + 1], r=["lsr" + sx, "hcn%d" % e], w=["laa" + sx])
                    P.act(a2[:], sr[:], AF.Exp, scale=cneg[e][:, fc:fc + 1], bias=cneg[e][:, fc:fc + 1], r=["lsr" + sx, "cneg%d" % e], w=["la2" + sx])

                def gate_s2(fc, blk, S_, sx):
                    sl = slice(blk * BW, (blk + 1) * BW)
                    si, aa, a2, bt = S_["si"], S_["aa"], S_["a2"], S_["bt"]
                    P.act(a2[:], a2[:], AF.Sqrt, bias=1.0, scale=-1.0, r=["la2" + sx], w=["la2" + sx])
                    P.stt("dve", bt[:], si[:], 1.0, a2[:], ALU.add, ALU.mult, r=["la2" + sx, "lsi" + sx], w=["lbt" + sx])
                    P.stt("dve", bt[:], bt[:], 0.5, xc[:, fc, sl], ALU.mult, ALU.mult, r=["lbt" + sx, "xc"], w=["lbt" + sx])
                    if blk == 0:
                        init = h0[e][:, fc:fc + 1] if h0 is not None else 0.0
                        rk = ["lh0%d" % e] if h0 is not None else []
                    else:
                        init = hh_[e][:, fc, blk * BW - 1:blk * BW]
                        rk = [hk]
                    P.op("dve", lambda E, o=hh_[e][:, fc, sl], i0=init, aa=aa, bt=bt: E.tensor_tensor_scan(o, aa[:], bt[:], i0, ALU.mult, ALU.add),
                         r=["laa" + sx, "lbt" + sx] + rk, w=[hk])

                for blk in range(NB):
                    if NT2 == 2:
                        yield from need(P, 4)
                        for fc in range(2):
                            gate_s1(fc, blk, scr[fc], "%d" % fc)
                        yield
                        for fc in range(2):
                            gate_s2(fc, blk, scr[fc], "%d" % fc)
                        yield
                    else:
                        for fc in range(2):
                            yield from need(P, 2)
                            gate_s1(fc, blk, scr[0], "0")
                            gate_s2(fc, blk, scr[0], "0")
                            yield
                if not g.ctx:
                    P.dma("pool", env["o_lru"][s, l, e].rearrange("(fc p) -> p fc", p=128), hh_[e][:, :, L - 1], r=[hk], allow_slow_non_contiguous=True)
            for blk in range(NB):
                sl = slice(blk * BW, (blk + 1) * BW)
                for fc in range(2):
                    S_ = scr[git[1] % NT2]; sx = "%d" % (git[1] % NT2); git[1] += 1
                    hT, sg, tsum = S_["hT"], S_["sg"], S_["tsum"]
                    yield from need(P, 3)
                    bT = P.bank()
                    for ci in range(CB):
                        tr = LT - 1 - (blk * CB + ci)
                        P.mm(ps[bT][:, ci * 128:(ci + 1) * 128], hh_[1][:, fc, tr * 128:(tr + 1) * 128], identf[:], r=["lh1", "identf"], w=["ps%d" % bT])
                    P.cp("act", hT[:], ps[bT][:, 0:BW], r=["ps%d" % bT], w=["lhT" + sx])
                    P.release(bT)
                    bF = P.bank()
                    for ci in range(CB):
                        P.mm(ps[bF][:, ci * 128:(ci + 1) * 128], hT[:, ci * 128:(ci + 1) * 128], jrevf[:], r=["lhT" + sx, "jrevf"], w=["ps%d" % bF])
                    bG = P.bank()
                    proj_fm(P, ps[bG][:, 0:BW], w_gl, fc * 128, 128, u, tok0 + blk * BW, BW, ["w_gl"] + ukeys, "ps%d" % bG)
                    P.act(sg[:], ps[bG][:, 0:BW], AF.Tanh, scale=0.5, r=["ps%d" % bG], w=["lsg" + sx])
                    P.stt("dve", sg[:], sg[:], 1.0, ps[bG][:, 0:BW], ALU.add, ALU.mult, r=["lsg" + sx, "ps%d" % bG], w=["lsg" + sx])
                    P.release(bG)
                    P.tt("dve", tsum[:], ps[bF][:, 0:BW], hh_[0][:, fc, sl], ALU.add, r=["ps%d" % bF, "lh0"], w=["ltsum" + sx])
                    P.release(bF)
                    P.stt("dve", y[:, fc, tok0 + blk * BW:tok0 + (blk + 1) * BW], tsum[:], 0.5, sg[:], ALU.mult, ALU.mult, r=["ltsum" + sx, "lsg" + sx], w=["y:%d" % (6 + fc)])
                    yield
        if not threaded:
            dump("ylru_" + tag, y[:, :, :], [128, 2, g.T], ["y:6", "y:7"])
            P.barrier()


def phase_mla(env, g, l, u, y, ukeys, wslice, ppar, tag, threaded=False, ext=None):
    nc, P, sb, dump, W = _unpack(env)
    ps = P.ps
    identf, onesf = env["identf"], env["onesf"]
    L, LT, BW, NB, CB = g.L, g.LT, g.BW, g.NB, g.CB
    ctx = g.ctx
    K0 = 256 if ctx else 0
    Lk = L + K0
    KT = Lk // 128
    QB = BW
    with (_NoClose(ext) if ext is not None else ExitStack()) as ph:
        w_cq = sb(ph, "w_cq", [128, 8, 256], BF16); P.dma("pool", w_cq[:], wslice(C_CQ, C_CQ + 256), w=["w_cq"])
        w_ck = sb(ph, "w_ck", [128, 8, 128], BF16); P.dma("pool", w_ck[:], wslice(C_CKV, C_CKV + 128), w=["w_ck"])
        w_kr = sb(ph, "w_kr", [128, 8, 96], BF16)
        P.memset("pool", w_kr[:], 0.0, w=["w_kr"])
        P.dma("pool", w_kr[:, :, 64:96], wslice(C_KR, C_KR + 32), w=["w_kr"])
        w_gm = sb(ph, "w_gm", [128, 8, 512], BF16); P.dma("pool", w_gm[:], wslice(C_GM, C_GM + 512), w=["w_gm"])
        wuq = sb(ph, "wuq", [128, 2, 768], BF16)
        P.dma("pool", wuq[:], W["mla_wuq"][l].rearrange("(kc p) n -> p kc n", p=128), w=["wuq"])
        wkk = sb(ph, "wkk", [128, 8, 64], BF16); wkv = sb(ph, "wkv", [128, 8, 64], BF16)
        wv4 = W["mla_wukv"][l].rearrange("r (h x) -> r h x", x=128)
        P.dma("pool", wkk[:], wv4[:, :, 0:64], w=["wkk"])
        P.dma("pool", wkv[:], wv4[:, :, 64:128], w=["wkv"])
        qng = ppar(ph, "qng", W["mla_qnorm"][l])
        kvg = sb(ph, "kvg", [128, 1], F32)
        P.dma("sp", kvg[:], W["mla_kvnorm"][l].rearrange("(p o) -> p o", o=1), w=["kvg"], allow_slow_non_contiguous=True)
        if ctx:
            cosT = sb(ph, "cosT", [128, 512], F32); sinT = sb(ph, "sinT", [128, 512], F32)

            def load_tables(p0, n):
                P.dma("sp", cosT[64:96, 0:n], env["k_cos"][:, p0:p0 + n], w=["cosT"])
                P.dma("sp", sinT[64:96, 0:n], env["k_sin"][:, p0:p0 + n], w=["sinT"])
            w_krs = sb(ph, "w_krs", [128, 8, 96], BF16)
            P.memset("pool", w_krs[:], 0.0, w=["w_krs"])
            P.ts1("dve", w_krs[:, :, 64:96:2], w_kr[:, :, 65:96:2], -1.0, ALU.mult, r=["w_kr"], w=["w_krs"])
            P.cp("dve", w_krs[:, :, 65:96:2], w_kr[:, :, 64:96:2], r=["w_kr"], w=["w_krs"])
            wuqs = sb(ph, "wuqs", [128, 2, 768], BF16)
            P.memset("pool", wuqs[:], 0.0, w=["wuqs"])
            for kc in range(2):
                v4 = wuq[:, kc, :].rearrange("p (h x) -> p h x", x=96)
                v4s = wuqs[:, kc, :].rearrange("p (h x) -> p h x", x=96)
                P.ts1("dve", v4s[:, :, 64:96:2], v4[:, :, 65:96:2], -1.0, ALU.mult, r=["wuq"], w=["wuqs"])
                P.cp("dve", v4s[:, :, 65:96:2], v4[:, :, 64:96:2], r=["wuq"], w=["wuqs"])
        ckvn = sb(ph, "ckvn", [128, Lk], BF16); krall = sb(ph, "krall", [128, Lk], BF16)
        Kt = sb(ph, "Kt", [128, 8, Lk], BF16); V = sb(ph, "Vv", [128, KT, 8, 65], BF16)
        cqn = sb(ph, "cqn", [128, 2, L], BF16); sgm = [sb(ph, "sgm%d" % i, [128, QB], F32) for i in range(3 if KT < 6 else 2)]
        ck = sb(ph, "mck", [128, BW], F32); sq = sb(ph, "msq", [128, BW], F32); sq1 = sb(ph, "msq1", [128, BW], F32)
        rstd = sb(ph, "mrstd", [128, BW], F32); t1 = sb(ph, "mt1", [128, BW], F32); t2 = sb(ph, "mt2", [128, BW], F32)
        if g.nseq == 1:
            sqq = sb(ph, "msqq", [128, BW], F32); rstdq = sb(ph, "mrstdq", [128, BW], F32); t2q = sb(ph, "mt2q", [128, BW], F32)
            kq_ = ("msqq", "mrstdq", "mt2q")
        else:
            sqq, rstdq, t2q = sq, rstd, t2
            kq_ = ("msq", "mrstd", "mt2")
        Pt = [sb(ph, "Pt%d" % i, [128, QB], BF16) for i in range(5 if KT >= 6 else 4)]
        Qh = [sb(ph, "Qh%d" % i, [128, QB], BF16) for i in range(3 if KT < 6 else 2)]
        accs, rden, tmpy = ck, sq, sq1
        P.memset("pool", V[:, :, :, 64:65], 1.0, w=["Vones"])
        if ctx:
            ckc = sb(ph, "ckc", [128, 2, 128], F32); krc = sb(ph, "krc", [128, 2, 96], F32)
        else:
            otile = sb(ph, "otile", [128, CB, 128], F32); okr = sb(ph, "okr", [128, CB, 32], F32)
        qi = 0
        for s in range(g.nseq):
            tok0 = s * L
            if ctx:
                P.dma("sp", ckc[:], env["c_ckv"][l].rearrange("(t p) r -> p t r", p=128), w=["ckc"])
                P.memset("dve", krc[:], 0.0, w=["krc"])
                P.dma("sp", krc[:, :, 64:96], env["c_kr"][l].rearrange("(t p) r -> p t r", p=128), w=["krc"])
                for t in range(2):
                    b = P.bank(); bk = "ps%d" % b
                    P.mm(ps[b][:, 0:128], ckc[:, t, :], identf[:], r=["ckc", "identf"], w=[bk])
                    P.cp("act", ckvn[:, t * 128:(t + 1) * 128], ps[b][:, 0:128], r=[bk], w=["ckvn"])
                    P.release(b)
                    b = P.bank(); bk = "ps%d" % b
                    P.mm(ps[b][0:96, 0:128], krc[:, t, :], identf[:], r=["krc", "identf"], w=[bk])
                    P.cp("act", krall[64:96, t * 128:(t + 1) * 128], ps[b][64:96, 0:128], r=[bk], w=["krall"])
                    P.release(b)
            for blk in range(NB):
                t0 = tok0 + blk * BW
                lsl = slice(blk * BW, (blk + 1) * BW)
                ksl = slice(K0 + blk * BW, K0 + (blk + 1) * BW)
                if ctx:
                    load_tables(blk * BW, BW)
                def ckv_gen(blk=blk, t0=t0, lsl=lsl, ksl=ksl):
                    yield from need(P, 3)
                    b = P.bank(); bk = "ps%d" % b
                    proj_fm(P, ps[b][:, 0:BW], w_ck, 0, 128, u, t0, BW, ["w_ck"] + ukeys, bk)
                    P.cp("act", ck[:], ps[b][:, 0:BW], r=[bk], w=["mck"])
                    P.act(sq[:], ps[b][:, 0:BW], AF.Square, r=[bk], w=["msq"])
                    P.release(b)
                    yield
                    b2 = P.bank(); bk2 = "ps%d" % b2
                    P.mm(ps[b2][:, 0:BW], onesf[:], sq[:], r=["onesf", "msq"], w=[bk2])
                    P.rsqrt(rstd[:], ps[b2][:, 0:BW], 1e-6, 1.0 / 128.0, t1[:], t2[:], r=[bk2], w=["mrstd"], kx="mt1", kt="mt2")
                    P.release(b2)
                    yield
                    P.tt("dve", ck[:], ck[:], rstd[:], ALU.mult, r=["mck", "mrstd"], w=["mck"])
                    P.ts1("dve", ck[:], ck[:], kvg[:, 0:1], ALU.mult, r=["mck", "kvg"], w=["mck"])
                    P.cp("act", ckvn[:, ksl], ck[:], r=["mck"], w=["ckvn"])
                    if not ctx:
                        b3 = P.bank(); bk3 = "ps%d" % b3
                        for ci in range(CB):
                            P.mm(ps[b3][:, ci * 128:(ci + 1) * 128], ck[:, ci * 128:(ci + 1) * 128], identf[:], r=["mck", "identf"], w=[bk3])
                        P.cp("act", otile[:], ps[b3][:, 0:BW].rearrange("p (c r) -> p c r", r=128), r=[bk3], w=["otile"])
                        P.release(b3)
                        P.dma("act", env["o_ckv"][s, l, blk * BW:(blk + 1) * BW, :].rearrange("(c p) r -> p c r", p=128), otile[:], r=["otile"])
                def kr_gen(blk=blk, t0=t0, lsl=lsl, ksl=ksl):
                    yield from need(P, 3)
                    b = P.bank(); bk = "ps%d" % b
                    proj_fm(P, ps[b][0:96, 0:BW], w_kr, 0, 96, u, t0, BW, ["w_kr"] + ukeys, bk)
                    if ctx:
                        b2 = P.bank(); bk2 = "ps%d" % b2
                        proj_fm(P, ps[b2][0:96, 0:BW], w_krs, 0, 96, u, t0, BW, ["w_krs"] + ukeys, bk2)
                        P.tt("dve", t1[64:96, :], ps[b][64:96, 0:BW], cosT[64:96, 0:BW], ALU.mult, r=[bk, "cosT"], w=["mt1"])
                        P.tt("dve", t2[64:96, :], ps[b2][64:96, 0:BW], sinT[64:96, 0:BW], ALU.mult, r=[bk2, "sinT"], w=["mt2"])
                        P.release(b); P.release(b2)
                        P.tt("pool", krall[64:96, ksl], t1[64:96, :], t2[64:96, :], ALU.add, r=["mt1", "mt2"], w=["krall"])
                    else:
                        P.cp("act", t1[64:96, :], ps[b][64:96, 0:BW], r=[bk], w=["mt1"])
                        P.release(b)
                        P.cp("pool", krall[64:96, ksl], t1[64:96, :], r=["mt1"], w=["krall"])
                        b3 = P.bank(); bk3 = "ps%d" % b3
                        for ci in range(CB):
                            P.mm(ps[b3][:, ci * 32:(ci + 1) * 32], t1[64:96, ci * 128:(ci + 1) * 128], identf[64:96, 64:96], r=["mt1", "identf"], w=[bk3])
                        P.cp("act", okr[:], ps[b3][:, 0:CB * 32].rearrange("p (c r) -> p c r", r=32), r=[bk3], w=["okr"])
                        P.release(b3)
                        P.dma("act", env["o_kr"][s, l, blk * BW:(blk + 1) * BW, :].rearrange("(c p) r -> p c r", p=128), okr[:], r=["okr"])
                def cq_gen(blk=blk, t0=t0, lsl=lsl, ksl=ksl):
                    yield from need(P, 3)
                    bq = [P.bank(), P.bank()]
                    for kc in range(2):
                        proj_fm(P, ps[bq[kc]][:, 0:BW], w_cq, kc * 128, 128, u, t0, BW, ["w_cq"] + ukeys, "ps%d" % bq[kc])
                    P.act(sqq[:], ps[bq[0]][:, 0:BW], AF.Square, r=["ps%d" % bq[0]], w=[kq_[0]])
                    P.act(sq1[:], ps[bq[1]][:, 0:BW], AF.Square, r=["ps%d" % bq[1]], w=["msq1"])
                    yield
                    b2 = P.bank(); bk2 = "ps%d" % b2
                    P.mm(ps[b2][:, 0:BW], onesf[:], sqq[:], start=True, stop=False, r=["onesf", kq_[0]], w=[bk2])
                    P.mm(ps[b2][:, 0:BW], onesf[:], sq1[:], start=False, stop=True, r=["onesf", "msq1"], w=[bk2])
                    P.rsqrt(rstdq[:], ps[b2][:, 0:BW], 1e-6, 1.0 / 256.0, t1[:], t2[:], r=[bk2], w=[kq_[1]], kx="mt1", kt="mt2")
                    P.release(b2)
                    yield
                    for kc in range(2):
                        P.tt("dve", t2q[:], ps[bq[kc]][:, 0:BW], rstdq[:], ALU.mult, r=["ps%d" % bq[kc], kq_[1]], w=[kq_[2]])
                        P.release(bq[kc])
                        P.act(cqn[:, kc, lsl], t2q[:], AF.Identity, scale=qng[:, kc:kc + 1], r=[kq_[2], "qng"], w=["cqn"])
                if g.nseq == 1:
                    yield from interleave(ckv_gen(), kr_gen(), cq_gen())
                else:
                    yield from ckv_gen()
                    yield
                    yield from kr_gen()
                    yield
                    yield from cq_gen()
            yield
            nkb = (Lk + 511) // 512
            for kb in range(nkb):
                k0 = kb * 512
                n = min(512, Lk - k0)
                for h in range(8):
                    yield from need(P, 1)
                    b = P.bank(); bk = "ps%d" % b
                    P.mm(ps[b][0:64, 0:n], wkk[:, h, :], ckvn[:, k0:k0 + n], r=["wkk", "ckvn"], w=[bk])
                    P.cp("act" if h % 2 == 0 else "dve", Kt[0:64, h, k0:k0 + n], ps[b][0:64, 0:n], r=[bk], w=["Kt"])
                    P.release(b)
            for h in range(8):
                P.cp("dve" if h % 2 == 0 else "act", Kt[64:96, h, :], krall[64:96, :], r=["krall"], w=["Kt"])
            wkv2 = wkv[:].rearrange("p h x -> p (h x)")
            yield
            for kt in range(KT):
                yield from need(P, 1)
                b = P.bank(); bk = "ps%d" % b
                P.mm(ps[b][:, 0:512], ckvn[:, kt * 128:(kt + 1) * 128], wkv2, r=["wkv", "ckvn"], w=[bk])
                P.cp("act" if kt % 2 == 0 else "dve", V[:, kt, :, 0:64], ps[b][:, 0:512].rearrange("p (h x) -> p h x", x=64), r=[bk], w=["Vv"])
                P.release(b)
            PRE = 3 if KT >= 6 else 2
            DEP = 2 if KT < 6 else 1
            NQ = DEP + 1
            items = [(qb, h) for qb in range(L // QB) for h in range(8)]
            prep = {}
            esets = [(accs, rden, tmpy, "mck", "msq", "msq1")]
            if DEP == 2:
                esets.append((rstd, t1, t2, "mrstd", "mt1", "mt2"))

            def prepare(it):
                qb, h = it
                qs = slice(qb * QB, (qb + 1) * QB)
                if ctx and h == 0:
                    load_tables(qb * QB, QB)
                if h % 2 == 0:
                    gi = (h // 2) % len(sgm)
                    sgt = sgm[gi]; sgk = "sgm%d" % gi
                    b = P.bank(); bk = "ps%d" % b
                    proj_fm(P, ps[b][:, 0:QB], w_gm, (h // 2) * 128, 128, u, tok0 + qb * QB, QB, ["w_gm"] + ukeys, bk)
                    P.act(sgt[:], ps[b][:, 0:QB], AF.Tanh, scale=0.5, r=[bk], w=[sgk])
                    P.stt("dve", sgt[:], sgt[:], 1.0, ps[b][:, 0:QB], ALU.add, ALU.mult, r=[sgk, bk], w=[sgk])
                    P.release(b)
                idx = items.index(it)
                qh = Qh[idx % NQ]; qk = "Qh%d" % (idx % NQ)
                bq_ = P.bank(); bqk = "ps%d" % bq_
                P.mm(ps[bq_][0:96, 0:QB], wuq[:, 0, h * 96:(h + 1) * 96], cqn[:, 0, qs], start=True, stop=False, r=["wuq", "cqn"], w=[bqk])
                P.mm(ps[bq_][0:96, 0:QB], wuq[:, 1, h * 96:(h + 1) * 96], cqn[:, 1, qs], start=False, stop=True, r=["wuq", "cqn"], w=[bqk])
                if ctx:
                    bs_ = P.bank(); bsk = "ps%d" % bs_
                    P.mm(ps[bs_][0:96, 0:QB], wuqs[:, 0, h * 96:(h + 1) * 96], cqn[:, 0, qs], start=True, stop=False, r=["wuqs", "cqn"], w=[bsk])
                    P.mm(ps[bs_][0:96, 0:QB], wuqs[:, 1, h * 96:(h + 1) * 96], cqn[:, 1, qs], start=False, stop=True, r=["wuqs", "cqn"], w=[bsk])
                    P.cp("act", qh[0:64, :], ps[bq_][0:64, 0:QB], r=[bqk], w=[qk])
                    P.tt("dve", t1[64:96, 0:QB], ps[bq_][64:96, 0:QB], cosT[64:96, 0:QB], ALU.mult, r=[bqk, "cosT"], w=["mt1"])
                    P.tt("dve", t2[64:96, 0:QB], ps[bs_][64:96, 0:QB], sinT[64:96, 0:QB], ALU.mult, r=[bsk, "sinT"], w=["mt2"])
                    P.release(bq_); P.release(bs_)
                    P.tt("pool", qh[64:96, :], t1[64:96, 0:QB], t2[64:96, 0:QB], ALU.add, r=["mt1", "mt2"], w=[qk])
                else:
                    P.cp("act", qh[0:96, :], ps[bq_][0:96, 0:QB], r=[bqk], w=[qk])
                    P.release(bq_)
                prep[it] = (qh, qk)

            def tail1(p):
                ac, rd, tm, ka, kr_, km = esets[p["idx"] % len(esets)]
                back = "ps%d" % p["bacc"]
                P.cp("act", ac[0:65, 0:QB], ps[p["bacc"]][0:65, 0:QB], r=[back], w=[ka])
                P.release(p["bacc"])
                P.op("dve", lambda E, ac=ac, rd=rd: E.reciprocal(rd[64:65, 0:QB], ac[64:65, 0:QB]), r=[ka], w=[kr_])

            def tail2(p):
                ac, rd, tm, ka, kr_, km = esets[p["idx"] % len(esets)]
                qb, h = p["it"]
                bb = P.bank(); bbk = "ps%d" % bb
                P.mm(ps[bb][0:64, 0:QB], onesf[64:65, 0:64], rd[64:65, 0:QB], r=["onesf", kr_], w=[bbk])
                pb = (h % 2) * 64; ch = 2 + h // 2
                P.tt("dve", tm[pb:pb + 64, 0:QB], ac[0:64, 0:QB], ps[bb][0:64, 0:QB], ALU.mult, r=[ka, bbk], w=[km])
                P.release(bb)
                gi = (h // 2) % len(sgm)
                sgt = sgm[gi]; sgk = "sgm%d" % gi
                P.stt("dve", y[pb:pb + 64, ch - 2, tok0 + qb * QB:tok0 + (qb + 1) * QB], tm[pb:pb + 64, 0:QB], 0.5, sgt[pb:pb + 64, :], ALU.mult, ALU.mult,
                      r=[km, sgk], w=["y:%d" % ch])

            yield
            yield from need(P, 3)
            for i in range(min(DEP, len(items))):
                prepare(items[i])
            pending = []
            K1 = min(2, KT - 1)
            GAP2 = 10 if KT >= 14 else KT

            def run_pending(slot):
                for p in list(pending):
                    if p["stage"] == 0 and slot >= p["s1"]:
                        tail1(p); p["stage"] = 1
                    if p["stage"] == 1 and slot >= p["s1"] + GAP2:
                        tail2(p); pending.remove(p)

            for idx, it in enumerate(items):
                yield
                yield from need(P, 6 if KT >= 6 else 5)
                qb, h = it
                qh, qk = prep.pop(it)
                sbank = {}

                def issue_S(kt, qh=qh, qk=qk, h=h):
                    bS = P.bank()
                    P.mm(ps[bS][:, 0:QB], Kt[0:96, h, kt * 128:(kt + 1) * 128], qh[0:96, :], r=["Kt", qk], w=["ps%d" % bS])
                    sbank[kt] = bS

                for kt in range(min(PRE, KT)):
                    issue_S(kt)
                if idx + DEP < len(items):
                    prepare(items[idx + DEP])
                bacc = P.bank(); back = "ps%d" % bacc
                for kt in range(KT):
                    if kt + PRE < KT:
                        issue_S(kt + PRE)
                    run_pending(idx * KT + kt)
                    bS = sbank.pop(kt)
                    pt = Pt[kt % len(Pt)]; pk = "Pt%d" % (kt % len(Pt))
                    P.act(pt[:], ps[bS][:, 0:QB], AF.Exp, scale=SM_SCALE, r=["ps%d" % bS], w=[pk])
                    P.release(bS)
                    P.mm(ps[bacc][0:65, 0:QB], V[:, kt, h, :], pt[:], start=(kt == 0), stop=(kt == KT - 1), r=["Vv", "Vones", pk], w=[back])
                pending.append(dict(it=it, idx=idx, bacc=bacc, stage=0, s1=(idx + 1) * KT + K1))
            run_pending(10 ** 9)
        if not threaded:
            dump("ymla_" + tag, y[:, :, :], [128, 4, g.T], ["y:2", "y:3", "y:4", "y:5"])
            P.barrier()


def interleave(*gens):
    gens = list(gens)
    while gens:
        alive = []
        for gn in gens:
            try:
                next(gn)
                alive.append(gn)
            except StopIteration:
                pass
        gens = alive
        yield


class _NoClose:
    def __init__(self, st):
        self.st = st

    def __enter__(self):
        return self.st

    def __exit__(self, *a):
        return False


def need(P, n):
    while len(P.free_banks) < n:
        yield


def interleave_w(pairs):
    pairs = list(pairs)
    while pairs:
        alive = []
        for gn, wgt in pairs:
            ok = True
            for _ in range(wgt):
                try:
                    next(gn)
                except StopIteration:
                    ok = False
                    break
            if ok:
                alive.append((gn, wgt))
        pairs = alive
        yield


def run_gen(gn):
    for _ in gn:
        pass


def phase_rwkv(env, g, l, u, ur, y, ukeys, urkeys, wslice, ppar, tag, threaded=False, ext=None):
    nc, P, sb, dump, W = _unpack(env)
    ps = P.ps
    identf, identb, jrevb, bones, onesf = env["identf"], env["identb"], env["jrevb"], env["bones"], env["onesf"]
    mask1, mask2, mask3 = env["mask1"], env["mask2"], env["mask3"]
    L, LT = g.L, g.LT
    BW, CB = 256, 2
    NB = L // BW
    nth = 2 if (g.nseq > 1 and not threaded) else 1
    NBP = 2 if NB > 1 else 1
    NTR = 2 if g.nseq == 1 else 1

    def rk(b, i):
        return "ps%d" % b

    def rks(b, n):
        return ["ps%d" % b]

    with (_NoClose(ext) if ext is not None else ExitStack()) as ph:
        w_rk = sb(ph, "w_rk", [128, 8, 512], BF16); P.dma("pool", w_rk[:], wslice(C_R, C_R + 512), w=["w_rk"])
        w_vg = sb(ph, "w_vg", [128, 8, 256], BF16); P.dma("pool", w_vg[:], wslice(C_V, C_V + 256), w=["w_vg"])
        w_g_own = sb(ph, "w_g_own", [128, 8, 256], BF16); P.dma("pool", w_g_own[:], wslice(C_G, C_G + 256), w=["w_g_own"])
        lo1 = [sb(ph, "lo1_%d" % e, [128, 8, 128], BF16) for e in range(2)]
        lo2 = [sb(ph, "lo2_%d" % e, [128, 256], BF16) for e in range(2)]
        for e in range(2):
            P.dma("pool", lo1[e][:, :, 0:64], W["rw_w1"][l, e].rearrange("(kc p) r -> p kc r", p=128), w=["lo1_%d" % e])
            P.dma("pool", lo1[e][:, :, 64:128], W["rw_a1"][l, e].rearrange("(kc p) r -> p kc r", p=128), w=["lo1_%d" % e])
            P.dma("pool", lo2[e][0:64, :], W["rw_w2"][l, e], w=["lo2_%d" % e])
            P.dma("pool", lo2[e][64:128, :], W["rw_a2"][l, e], w=["lo2_%d" % e])
        pw0 = [ppar(ph, "pw0_%d" % e, W["rw_w0"][l, e]) for e in range(2)]
        pa0 = [ppar(ph, "pa0_%d" % e, W["rw_a0"][l, e]) for e in range(2)]
        hpw0 = [sb(ph, "hpw0_%d" % e, [128, 2], F32) for e in range(2)]
        hpa0 = [sb(ph, "hpa0_%d" % e, [128, 2], F32) for e in range(2)]
        for e in range(2):
            P.ts1("dve", hpw0[e][:], pw0[e][:], 0.5, ALU.mult, r=["pw0_%d" % e], w=["hpw0_%d" % e])
            P.ts1("dve", hpa0[e][:], pa0[e][:], 0.5, ALU.mult, r=["pa0_%d" % e], w=["hpa0_%d" % e])
        pkk = ppar(ph, "pkk", W["rw_kk"][l]); pka = ppar(ph, "pka", W["rw_ka"][l]); prk = ppar(ph, "prk", W["rw_rk"][l])
        plg = ppar(ph, "plg", W["rw_lnx_g"][l]); plb = ppar(ph, "plb", W["rw_lnx_b"][l])

        class BufSet:
            pass

        ND = env["dbg"].get("ndummy", 0)
        dbank = P.bank() if ND else None

        def dummies(n):
            for _ in range(n * ND):
                P.pe_cont = True
                P.op("pe", lambda E: E.matmul(ps[dbank][:, :], identb[:], mask1[:], start=False, stop=False), (), ())
                P.pe_cont = False

        def alloc_set(tid):
            B = BufSet()
            B.tid = tid
            B.k = lambda name: "%s@%d" % (name, tid)
            n = lambda name: "%s_t%d" % (name, tid)
            B.QR = [[sb(ph, n("QR%d%d" % (bp, e)), [128, 2, CB, 2, 128], BF16) for e in range(2)] for bp in range(NBP)]
            B.AK = [[sb(ph, n("AK%d%d" % (bp, e)), [128, 2, CB, 2, 128], BF16) for e in range(2)] for bp in range(NBP)]
            B.AKgT = [[sb(ph, n("AKgT%d%d" % (bp, e)), [128, CB, 2, 2, 128], BF16) for e in range(2)] for bp in range(NBP)]
            B.Vt = [[sb(ph, n("Vt%d%d" % (bp, e)), [128, CB, 256], BF16) for e in range(2)] for bp in range(NBP)]
            B.GE = [sb(ph, n("GE%d" % bp), [128, 2, 2, CB], F32) for bp in range(NBP)]
            B.yT = [sb(ph, n("yT%d" % e), [128, LT, 256], BF16) for e in range(2)]
            B.tr = []
            for ti in range(NTR):
                T = BufSet()
                T.sfx = "#%d" % ti
                T.tl = sb(ph, n("tl%d" % ti), [128, BW], BF16)
                for nm in ("rT", "kT", "kkT", "sgw", "aT", "cs", "Em", "tq"):
                    setattr(T, nm, sb(ph, n("%s%d" % (nm, ti)), [128, BW], F32))
                T.Ebuf = sb(ph, n("Ebuf%d" % ti), [128, CB, 129], F32)
                T.Ag = sb(ph, n("Ag%d" % ti), [128, BW], BF16); T.Kg = sb(ph, n("Kg%d" % ti), [128, BW], BF16)
                P.memset("dve", T.Ebuf[:, :, 0:1], 1.0, w=[B.k("Ebuf0" + T.sfx)])
                B.tr.append(T)
            B.MA = [[sb(ph, n("MA%d%d" % (par, e)), [128, 4, 2, 128], BF16) for e in range(2)] for par in range(2)]
            B.LK = [[sb(ph, n("LK%d%d" % (par, e)), [128, 4, 2, 128], BF16) for e in range(2)] for par in range(2)]
            B.Pm = [[sb(ph, n("Pm%d%d" % (par, e)), [128, 4, 128], BF16) for e in range(2)] for par in range(2)]
            B.Nn = [sb(ph, n("Nn%d" % e), [128, 4, 128], BF16) for e in range(2)]
            B.M2 = [[sb(ph, n("M2_%d%d" % (e, i)), [128, 4, 128], BF16) for i in range(2)] for e in range(2)]
            B.N2 = [[sb(ph, n("N2_%d%d" % (e, i)), [128, 4, 128], BF16) for i in range(2)] for e in range(2)]
            B.W1 = sb(ph, n("W1"), [128, 512], BF16); B.Uu = sb(ph, n("Uu"), [128, 512], BF16)
            B.H32 = sb(ph, n("H32"), [128, 4, 64], F32); B.Hbf = sb(ph, n("Hbf"), [128, 4, 64], BF16)
            if not g.ctx:
                B.SS = sb(ph, n("SS"), [128, 512], F32)
                B.ssk = B.k("SS")
                B.So = B.SS[0:64, 0:512].rearrange("p (a h k) -> p a h k", a=4, h=2)
            return B

        def v3(t):
            return t[:].rearrange("p (c t) -> p c t", t=128)

        def derive_gen(B, s, blk, e):
            T = B.tr[e % NTR]
            bp = blk % NBP
            k = lambda name: B.k(name + T.sfx)
            kb = lambda name: B.k("%s.%d" % (name, bp))
            tok0 = s * L
            src, skeys = (u, ukeys) if e == 0 else (ur, urkeys)
            t0 = tok0 + blk * BW
            ek = "%d" % e
            rT, kT, kkT, sgw, aT, cs, Em, tq, tl, Ebuf = T.rT, T.kT, T.kkT, T.sgw, T.aT, T.cs, T.Em, T.tq, T.tl, T.Ebuf
            QRe, AKe, AKgTe, Vte, GEb = B.QR[bp][e], B.AK[bp][e], B.AKgT[bp][e], B.Vt[bp][e], B.GE[bp]
            Eend, kka, kdir = sgw, tq, cs
            cs3, Em3, Eend3 = v3(cs), v3(Em), v3(Eend)
            for half in range((CB + 1) // 2):
                yield from need(P, 2)
                b = P.bank(); bk = "ps%d" % b
                nci = min(2, CB - half * 2)
                for q in range(nci):
                    ci = half * 2 + q
                    for kc in range(8):
                        P.mm(ps[b][:, q * 256:(q + 1) * 256], src[:, kc, t0 + ci * 128:t0 + (ci + 1) * 128], w_vg[:, kc, 0:256],
                             start=(kc == 0), stop=(kc == 7), r=["w_vg"] + skeys, w=[rk(b, q)])
                P.cp("act", Vte[:, half * 2:half * 2 + nci, :], ps[b][:, 0:nci * 256].rearrange("p (c v) -> p c v", v=256),
                     r=rks(b, nci), w=[kb("Vt" + ek)])
                P.release(b)
            b = P.bank(); bk = "ps%d" % b
            proj_fm(P, ps[b][:, 0:BW], lo1[e], 0, 128, src, t0, BW, ["lo1_" + ek] + skeys, bk)
            P.act(tl[0:64, :], ps[b][0:64, 0:BW], AF.Tanh, r=[bk], w=[k("tl")])
            P.cp("dve", tl[64:128, :], ps[b][64:128, 0:BW], r=[bk], w=[k("tl")])
            P.release(b)
            yield
            for fc in range(2):
                fs = slice(fc * 128, (fc + 1) * 128)
                yield from need(P, 4)
                bw_ = P.bank(); ba_ = P.bank()
                P.mm(ps[bw_][:, 0:BW], lo2[e][0:64, fs], tl[0:64, :], r=["lo2_" + ek, k("tl")], w=["ps%d" % bw_])
                P.mm(ps[ba_][:, 0:BW], lo2[e][64:128, fs], tl[64:128, :], r=["lo2_" + ek, k("tl")], w=["ps%d" % ba_])
                br_ = P.bank(); bk_ = P.bank()
                proj_fm(P, ps[br_][:, 0:BW], w_rk, fc * 128, 128, src, t0, BW, ["w_rk"] + skeys, "ps%d" % br_)
                proj_fm(P, ps[bk_][:, 0:BW], w_rk, 256 + fc * 128, 128, src, t0, BW, ["w_rk"] + skeys, "ps%d" % bk_)
                P.act(sgw[:], ps[bw_][:, 0:BW], AF.Tanh, bias=hpw0[e][:, fc:fc + 1], scale=0.5, r=["ps%d" % bw_, "hpw0_" + ek], w=[k("sgw")])
                P.act(aT[:], ps[ba_][:, 0:BW], AF.Tanh, bias=hpa0[e][:, fc:fc + 1], scale=0.5, r=["ps%d" % ba_, "hpa0_" + ek], w=[k("aT")])
                P.release(bw_); P.release(ba_)
                P.ts("pool", sgw[:], sgw[:], 0.5, 0.5, ALU.mult, ALU.add, r=[k("sgw")], w=[k("sgw")])
                P.ts("pool", aT[:], aT[:], 0.5, 0.5, ALU.mult, ALU.add, r=[k("aT")], w=[k("aT")])
                P.cp("act", rT[:], ps[br_][:, 0:BW], r=["ps%d" % br_], w=[k("rT")])
                P.cp("dve", kT[:], ps[bk_][:, 0:BW], r=["ps%d" % bk_], w=[k("kT")])
                P.release(br_); P.release(bk_)
                yield
                P.ts1("dve", kkT[:], kT[:], pkk[:, fc:fc + 1], ALU.mult, r=[k("kT"), "pkk"], w=[k("kkT")])
                P.tt("pool", tq[:], kkT[:], kkT[:], ALU.mult, r=[k("kkT")], w=[k("tq")])
                bs_ = P.bank()
                P.mm(ps[bs_][:, 0:BW], bones[:], tq[:], r=["bones", k("tq")], w=["ps%d" % bs_])
                P.rsqrt(tq[:], ps[bs_][:, 0:BW], 1e-24, 1.0, Em[:], cs[:], r=["ps%d" % bs_], w=[k("tq")], kx=k("Em"), kt=k("cs"))
                P.release(bs_)
                for ci in range(CB):
                    P.op("dve", lambda E, ci=ci: E.tensor_tensor_scan(cs[:, ci * 128:(ci + 1) * 128], onesf[:, 0:128],
                                                                      sgw[:, ci * 128:(ci + 1) * 128], 0.0, ALU.mult, ALU.add),
                         r=["onesf", k("sgw")], w=[k("cs")])
                yield
                P.tt("dve", kkT[:], kkT[:], tq[:], ALU.mult, r=[k("kkT"), k("tq")], w=[k("kkT")])
                P.act(Ebuf[:, :, 1:129], cs3, AF.Exp, scale=-WDEC, r=[k("cs")], w=[k("Ebuf")])
                P.act(Em[:], cs[:], AF.Exp, scale=WDEC, r=[k("cs")], w=[k("Em")])
                P.cp("dve", GEb[:, e, fc, :], Ebuf[:, :, 128], r=[k("Ebuf")], w=[kb("GE")])
                P.tt("dve", Eend3, Em3, GEb[:, e, fc, :].unsqueeze(2).to_broadcast([128, CB, 128]), ALU.mult, r=[k("Em"), kb("GE")], w=[k("sgw")])
                P.tt("pool", kka[:], kkT[:], aT[:], ALU.mult, r=[k("kkT"), k("aT")], w=[k("tq")])
                P.ts("dve", aT[:], aT[:], 1.0, pka[:, fc:fc + 1], ALU.subtract, ALU.mult, r=[k("aT"), "pka"], w=[k("aT")])
                P.stt("dve", kdir[:], aT[:], 1.0, kT[:], ALU.add, ALU.mult, r=[k("aT"), k("kT")], w=[k("cs")])
                yield
                P.tt("dve", QRe[:, fc, :, 0, :], v3(kkT), Ebuf[:, :, 0:128], ALU.mult, r=[k("kkT"), k("Ebuf"), k("Ebuf0")], w=[kb("QR" + ek)])
                P.tt("dve", QRe[:, fc, :, 1, :], v3(rT), Ebuf[:, :, 1:129], ALU.mult, r=[k("rT"), k("Ebuf")], w=[kb("QR" + ek)])
                P.tt("pool", AKe[:, fc, :, 0, :], v3(kka), Em3, ALU.mult, r=[k("tq"), k("Em")], w=[kb("AK" + ek)])
                P.tt("pool", AKe[:, fc, :, 1, :], v3(kdir), Em3, ALU.mult, r=[k("cs"), k("Em")], w=[kb("AK" + ek)])
                P.tt("dve", T.Ag[:], kka[:], Eend[:], ALU.mult, r=[k("tq"), k("sgw")], w=[k("Ag")])
                P.tt("pool", T.Kg[:], kdir[:], Eend[:], ALU.mult, r=[k("cs"), k("sgw")], w=[k("Kg")])
                yield
                for ci in range(CB):
                    yield from need(P, 1)
                    b = P.bank()
                    P.mm(ps[b][:, 0:128], T.Ag[:, ci * 128:(ci + 1) * 128], identb[:], r=[k("Ag"), "identb"], w=[rk(b, 0)])
                    P.mm(ps[b][:, 128:256], T.Kg[:, ci * 128:(ci + 1) * 128], identb[:], r=[k("Kg"), "identb"], w=[rk(b, 1)])
                    P.cp("act", AKgTe[:, ci, :, fc, :], ps[b][:, 0:256].rearrange("p (a x) -> p a x", x=128), r=rks(b, 2), w=[kb("AKgT" + ek)])
                    P.release(b)
                yield

        def gn_gen(B, blk, ci, par):
            bp = blk % NBP
            k0 = B.k
            k = lambda name: (B.k("%s.%d" % (name, bp)) if name[:2] in ("QR", "AK", "Vt", "GE") else B.k(name))
            QR, AK, MA, LK, Pm, Nn, M2, N2 = B.QR[bp], B.AK[bp], B.MA[par], B.LK[par], B.Pm[par], B.Nn, B.M2, B.N2
            pk = "%d" % par
            for e in range(2):
                ek = "%d" % e
                for fc in range(2):
                    yield from need(P, 3)
                    b1 = P.bank(); b2 = P.bank()
                    for hh in range(2):
                        pb = hh * 64
                        qr2 = QR[e][pb:pb + 64, fc, ci, :, :].rearrange("p a t -> p (a t)")
                        P.mm(ps[b1][:, hh * 256:(hh + 1) * 256], AK[e][pb:pb + 64, fc, ci, 0, :], qr2, r=[k("AK" + ek), k("QR" + ek)], w=[rk(b1, hh)])
                        P.mm(ps[b2][:, hh * 256:(hh + 1) * 256], AK[e][pb:pb + 64, fc, ci, 1, :], qr2, r=[k("AK" + ek), k("QR" + ek)], w=[rk(b2, hh)])
                    P.tt("dve", MA[e][:, fc * 2:fc * 2 + 2, :, :].rearrange("p h a t -> p (h a t)"), ps[b1][:, :], mask1[:], ALU.mult,
                         r=rks(b1, 2) + ["mask1"], w=[k("MA" + pk + ek)])
                    P.tt("dve", LK[e][:, fc * 2:fc * 2 + 2, :, :].rearrange("p h a t -> p (h a t)"), ps[b2][:, :], mask2[:], ALU.mult,
                         r=rks(b2, 2) + ["mask2"], w=[k("LK" + pk + ek)])
                    P.release(b1); P.release(b2)
                b3 = P.bank()
                for h in range(4):
                    fc, pb = h // 2, (h % 2) * 64
                    P.mm(ps[b3][:, h * 128:(h + 1) * 128], QR[e][pb:pb + 64, fc, ci, 0, :], AK[e][pb:pb + 64, fc, ci, 0, :],
                         r=[k("AK" + ek), k("QR" + ek)], w=[rk(b3, h)])
                P.tt("dve", Nn[e][:].rearrange("p h t -> p (h t)"), ps[b3][:, :], mask3[:], ALU.mult, r=rks(b3, 4) + ["mask3"], w=[k("Nn" + ek)])
                P.release(b3)
                yield
            Mc, Nc, Mk, Nk = {}, {}, {}, {}
            for e in range(2):
                ek = "%d" % e
                P.tt("pool", Pm[e][:], MA[e][:, :, 0, :], identb[:].unsqueeze(1).to_broadcast([128, 4, 128]), ALU.add,
                     r=[k("MA" + pk + ek), "identb"], w=[k("Pm" + pk + ek)])
                Mc[e] = [MA[e][:, h, 0, :] for h in range(4)]; Mk[e] = k("MA" + pk + ek)
                Nc[e] = [Nn[e][:, h, :] for h in range(4)]; Nk[e] = k("Nn" + ek)
            for lev in range(6):
                i2 = lev % 2
                bN, bM, bP = {}, {}, {}
                yield from need(P, 4)
                for e in range(2):
                    bN[e] = P.bank()
                    if lev < 5:
                        bM[e] = P.bank()
                for h in range(4):
                    for e in range(2):
                        P.mm(ps[bN[e]][:, h * 128:(h + 1) * 128], Mc[e][h], Nc[e][h], r=[Mk[e], Nk[e]], w=[rk(bN[e], h)])
                        if lev < 5:
                            P.mm(ps[bM[e]][:, h * 128:(h + 1) * 128], Nc[e][h], Mc[e][h], r=[Mk[e], Nk[e]], w=[rk(bM[e], h)])
                dummies(1)
                for e in range(2):
                    n2k = k("N2_%d%d" % (e, i2))
                    P.cp("act", N2[e][i2][:].rearrange("p h t -> p (h t)"), ps[bN[e]][:, :], r=rks(bN[e], 4), w=[n2k])
                    P.release(bN[e])
                if lev < 5:
                    for e in range(2):
                        m2k = k("M2_%d%d" % (e, i2))
                        P.cp("dve" if e == 0 else "act", M2[e][i2][:].rearrange("p h t -> p (h t)"), ps[bM[e]][:, :], r=rks(bM[e], 4), w=[m2k])
                        P.release(bM[e])
                yield
                yield from need(P, 2)
                for e in range(2):
                    bP[e] = P.bank()
                for h in range(4):
                    for e in range(2):
                        P.mm(ps[bP[e]][:, h * 128:(h + 1) * 128], N2[e][i2][:, h, :], Pm[e][:, h, :],
                             r=[k("N2_%d%d" % (e, i2)), k("Pm" + pk + "%d" % e)], w=[rk(bP[e], h)])
                for e in range(2):
                    P.tt("dve", Pm[e][:].rearrange("p h t -> p (h t)"), Pm[e][:].rearrange("p h t -> p (h t)"), ps[bP[e]][:, :], ALU.add,
                         r=[k("Pm" + pk + "%d" % e)] + rks(bP[e], 4), w=[k("Pm" + pk + "%d" % e)])
                    P.release(bP[e])
                for e in range(2):
                    Mc[e] = [M2[e][i2][:, h, :] for h in range(4)]; Mk[e] = k("M2_%d%d" % (e, i2))
                    Nc[e] = [N2[e][i2][:, h, :] for h in range(4)]; Nk[e] = k("N2_%d%d" % (e, i2))
                yield

        def rec_gen(B, blk, ci, par, tile_i):
            bp = blk % NBP
            k = lambda name: (B.k("%s.%d" % (name, bp)) if name[:2] in ("QR", "AK", "Vt", "GE") else B.k(name))
            QR, AKgT, Vt, MA, LK, Pm = B.QR[bp], B.AKgT[bp], B.Vt[bp], B.MA[par], B.LK[par], B.Pm[par]
            W1, Uu, H32, Hbf, GE = B.W1, B.Uu, B.H32, B.Hbf, B.GE[bp]
            pk = "%d" % par
            chains = [(e, h) for h in range(4) for e in range(2)]
            yield from need(P, 1)
            bW = P.bank()
            for (e, h) in chains:
                ek = "%d" % e
                fc, pb = h // 2, (h % 2) * 64
                c = e * 4 + h
                cb = c * 64
                P.mm(ps[bW][:, cb:cb + 64], QR[e][pb:pb + 64, fc, ci, 0, :], Hbf[pb:pb + 64, e * 2 + fc, :], start=True, stop=False,
                     r=[k("QR" + ek), k("Hbf")], w=[rk(bW, c)])
                P.mm(ps[bW][:, cb:cb + 64], LK[e][:, h, 0, :], Vt[e][:, ci, h * 64:(h + 1) * 64], start=False, stop=True,
                     r=[k("LK" + pk + ek), k("Vt" + ek)], w=[rk(bW, c)])
            P.cp("act", W1[:], ps[bW][:, :], r=rks(bW, 8), w=[k("W1")])
            P.release(bW)
            yield
            yield from need(P, 1)
            bU = P.bank()
            for (e, h) in chains:
                c = e * 4 + h
                cb = c * 64
                P.mm(ps[bU][:, cb:cb + 64], Pm[e][:, h, :], W1[:, cb:cb + 64], r=[k("Pm" + pk + "%d" % e), k("W1")], w=[rk(bU, c)])
            P.ts1("dve", Uu[:], ps[bU][:, :], -1.0, ALU.mult, r=rks(bU, 8), w=[k("Uu")])
            P.release(bU)
            yield
            yield from need(P, 2)
            bY = P.bank(); bH = P.bank()
            for fc in range(2):
                for e in range(2):
                    ek = "%d" % e
                    c0 = (e * 4 + fc * 2) * 64
                    P.mm(ps[bH][:, c0:c0 + 128], AKgT[e][:, ci, 0, fc, :], Uu[:, c0:c0 + 128], start=True, stop=False,
                         r=[k("AKgT" + ek), k("Uu")], w=[rk(bH, 0)])
                    P.mm(ps[bH][:, c0:c0 + 128], AKgT[e][:, ci, 1, fc, :], Vt[e][:, ci, fc * 128:(fc + 1) * 128], start=False, stop=True,
                         r=[k("AKgT" + ek), k("Vt" + ek)], w=[rk(bH, 0)])
            for (e, h) in chains:
                ek = "%d" % e
                fc, pb = h // 2, (h % 2) * 64
                c = e * 4 + h
                cb = c * 64
                P.mm(ps[bY][:, cb:cb + 64], QR[e][pb:pb + 64, fc, ci, 1, :], Hbf[pb:pb + 64, e * 2 + fc, :], start=True, stop=False,
                     r=[k("QR" + ek), k("Hbf")], w=[rk(bY, c)])
                P.mm(ps[bY][:, cb:cb + 64], MA[e][:, h, 1, :], Uu[:, cb:cb + 64], start=False, stop=False, r=[k("MA" + pk + ek), k("Uu")], w=[rk(bY, c)])
                P.mm(ps[bY][:, cb:cb + 64], LK[e][:, h, 1, :], Vt[e][:, ci, h * 64:(h + 1) * 64], start=False, stop=True,
                     r=[k("LK" + pk + ek), k("Vt" + ek)], w=[rk(bY, c)])
            yield
            bH4 = ps[bH][:, :].rearrange("p (a hh v) -> p a hh v", hh=2, v=64)
            for hh in range(2):
                pb = hh * 64
                gb = GE[pb:pb + 64, :, :, ci].rearrange("p e f -> p (e f)").unsqueeze(2).to_broadcast([64, 4, 64])
                P.tt("dve", H32[pb:pb + 64, :, :], H32[pb:pb + 64, :, :], gb, ALU.mult, r=[k("H32"), k("GE")], w=[k("H32")])
                P.tt("dve", H32[pb:pb + 64, :, :], H32[pb:pb + 64, :, :], bH4[pb:pb + 64, :, hh, :], ALU.add, r=[k("H32")] + rks(bH, 8), w=[k("H32")])
            P.release(bH)
            P.cp("act", Hbf[:], H32[:], r=[k("H32")], w=[k("Hbf")])
            for e in range(2):
                P.cp("act" if e == 0 else "pool", B.yT[e][:, tile_i, :], ps[bY][:, e * 256:(e + 1) * 256], r=rks(bY, 8), w=[k("yT%d" % e)]) \
                    if e == 0 else P.cp("act", B.yT[e][:, tile_i, :], ps[bY][:, e * 256:(e + 1) * 256], r=rks(bY, 8), w=[k("yT%d" % e)])
            P.release(bY)
            yield

        def epi_gen(B, s, fcs, ti):
            tok0 = s * L
            T = B.tr[ti]
            k = lambda name: B.k(name + T.sfx) if name in ("rT", "kT", "kkT", "sgw", "aT") else B.k(name)
            rT, kT, kkT, sgw, aT = T.rT, T.kT, T.kkT, T.sgw, T.aT
            yT = B.yT
            w_g, wgk = w_g_own[:], "w_g_own"
            for blk in range(NB):
                t0 = tok0 + blk * BW
                for fc in fcs:
                    fs = slice(fc * 128, (fc + 1) * 128)
                    yield from need(P, 3)
                    bC = P.bank()
                    for ci in range(CB):
                        t = blk * CB + ci
                        tr = LT - 1 - t
                        P.mm(ps[bC][:, ci * 128:(ci + 1) * 128], yT[0][:, t, fs], identb[:], start=True, stop=False, r=[k("yT0"), "identb"], w=[rk(bC, ci)])
                        P.mm(ps[bC][:, ci * 128:(ci + 1) * 128], yT[1][:, tr, fs], jrevb[:], start=False, stop=True, r=[k("yT1"), "jrevb"], w=[rk(bC, ci)])
                    P.cp("act", rT[:], ps[bC][:, 0:BW], r=rks(bC, CB), w=[k("rT")])
                    P.act(kT[:], ps[bC][:, 0:BW], AF.Square, r=rks(bC, CB), w=[k("kT")])
                    P.release(bC)
                    yield
                    yield from need(P, 3)
                    bm_ = P.bank(); bq_ = P.bank()
                    P.mm(ps[bm_][:, 0:BW], bones[:], rT[:], r=["bones", k("rT")], w=["ps%d" % bm_])
                    P.mm(ps[bq_][:, 0:BW], bones[:], kT[:], r=["bones", k("kT")], w=["ps%d" % bq_])
                    P.ts1("dve", kkT[:], ps[bm_][:, 0:BW], 1.0 / 64.0, ALU.mult, r=["ps%d" % bm_], w=[k("kkT")])
                    P.release(bm_)
                    P.tt("pool", sgw[:], kkT[:], kkT[:], ALU.mult, r=[k("kkT")], w=[k("sgw")])
                    P.stt("dve", aT[:], ps[bq_][:, 0:BW], 1.0 / 64.0, sgw[:], ALU.mult, ALU.subtract, r=["ps%d" % bq_, k("sgw")], w=[k("aT")])
                    P.release(bq_)
                    P.ts1("dve", aT[:], aT[:], 0.0, ALU.max, r=[k("aT")], w=[k("aT")])
                    P.rsqrt(aT[:], aT[:], GN_EPS, 1.0, T.Em[:], T.cs[:], r=[k("aT")], w=[k("aT")], kx=B.k("Em" + T.sfx), kt=B.k("cs" + T.sfx))
                    yield
                    P.tt("dve", rT[:], rT[:], kkT[:], ALU.subtract, r=[k("rT"), k("kkT")], w=[k("rT")])
                    P.tt("dve", rT[:], rT[:], aT[:], ALU.mult, r=[k("rT"), k("aT")], w=[k("rT")])
                    P.ts("dve", rT[:], rT[:], plg[:, fc:fc + 1], plb[:, fc:fc + 1], ALU.mult, ALU.add, r=[k("rT"), "plg", "plb"], w=[k("rT")])
                    yield from need(P, 3)
                    br_ = P.bank(); bk_ = P.bank()
                    proj_fm(P, ps[br_][:, 0:BW], w_rk, fc * 128, 128, u, t0, BW, ["w_rk"] + ukeys, "ps%d" % br_)
                    proj_fm(P, ps[bk_][:, 0:BW], w_rk, 256 + fc * 128, 128, u, t0, BW, ["w_rk"] + ukeys, "ps%d" % bk_)
                    P.cp("act", kT[:], ps[bk_][:, 0:BW], r=["ps%d" % bk_], w=[k("kT")])
                    P.release(bk_)
                    P.stt("dve", kT[:], ps[br_][:, 0:BW], prk[:, fc:fc + 1], kT[:], ALU.mult, ALU.mult, r=["ps%d" % br_, "prk", k("kT")], w=[k("kT")])
                    P.release(br_)
                    yield
                    yield from need(P, 3)
                    bv_ = P.bank(); bg_ = P.bank(); bb_ = P.bank()
                    proj_fm(P, ps[bv_][:, 0:BW], w_vg, fc * 128, 128, u, t0, BW, ["w_vg"] + ukeys, "ps%d" % bv_)
                    proj_fm(P, ps[bg_][:, 0:BW], w_g, fc * 128, 128, u, t0, BW, [wgk] + ukeys, "ps%d" % bg_)
                    P.mm(ps[bb_][:, 0:BW], bones[:], kT[:], r=["bones", k("kT")], w=["ps%d" % bb_])
                    P.cp("act", kkT[:], ps[bv_][:, 0:BW], r=["ps%d" % bv_], w=[k("kkT")])
                    P.release(bv_)
                    P.act(sgw[:], ps[bg_][:, 0:BW], AF.Tanh, scale=0.5, r=["ps%d" % bg_], w=[k("sgw")])
                    P.stt("dve", sgw[:], sgw[:], 1.0, ps[bg_][:, 0:BW], ALU.add, ALU.mult, r=[k("sgw"), "ps%d" % bg_], w=[k("sgw")])
                    P.release(bg_)
                    P.tt("dve", kkT[:], kkT[:], ps[bb_][:, 0:BW], ALU.mult, r=[k("kkT"), "ps%d" % bb_], w=[k("kkT")])
                    P.release(bb_)
                    P.tt("pool", rT[:], rT[:], kkT[:], ALU.add, r=[k("rT"), k("kkT")], w=[k("rT")])
                    P.stt("dve", y[:, fc, t0:t0 + BW], rT[:], 0.5, sgw[:], ALU.mult, ALU.mult, r=[k("rT"), k("sgw")], w=["y:%d" % fc])
                    yield

        def unit_gen(B, s):
            k = B.k
            H32, Hbf = B.H32, B.Hbf
            if g.ctx:
                S0 = [B.tr[e % NTR].rT[0:64, 0:256].rearrange("p (f h k) -> p f h k", f=2, h=2) for e in range(2)]
                s0k = [B.k("rT" + B.tr[e % NTR].sfx) for e in range(2)]
                yield from need(P, 1)
                b = P.bank()
                for e in range(2):
                    for fc in range(2):
                        P.dma("sp", S0[e][:, fc, :, :], env["st_rwkv"][l, e, 2 * fc:2 * fc + 2].rearrange("h v k -> v h k"), w=[s0k[e]])
                    for fc in range(2):
                        a = e * 2 + fc
                        P.mm(ps[b][:, a * 64:(a + 1) * 64], S0[e][:, fc, :, :].rearrange("v h k -> v (h k)"), identf[0:64, 0:64],
                             r=[s0k[e], "identf"], w=[rk(b, a)])
                P.cp("dve", H32[:], ps[b][:, 0:256].rearrange("p (a v) -> p a v", v=64), r=rks(b, 4), w=[k("H32")])
                P.release(b)
            else:
                P.memset("dve", H32[:], 0.0, w=[k("H32")])
            P.cp("act", Hbf[:], H32[:], r=[k("H32")], w=[k("Hbf")])
            yield
            gci = 0
            if NTR == 2:
                yield from interleave(derive_gen(B, s, 0, 0), derive_gen(B, s, 0, 1))
            else:
                for e in range(2):
                    yield from derive_gen(B, s, 0, e)
            yield from gn_gen(B, 0, 0, gci % 2)
            for blk in range(NB):
                for ci in range(CB):
                    gens = [rec_gen(B, blk, ci, gci % 2, blk * CB + ci)]
                    if ci + 1 < CB:
                        gens.append(gn_gen(B, blk, ci + 1, (gci + 1) % 2))
                        if blk + 1 < NB:
                            gens.append(derive_gen(B, s, blk + 1, 0))
                            gens.append(derive_gen(B, s, blk + 1, 1))
                    elif blk + 1 < NB:
                        gens.append(gn_gen(B, blk + 1, 0, (gci + 1) % 2))
                    yield from interleave(*gens)
                    gci += 1
            if not g.ctx:
                yield from need(P, 1)
                b = P.bank()
                for a in range(4):
                    P.mm(ps[b][0:64, a * 128:(a + 1) * 128], H32[:, a, :], identf[:], r=[k("H32"), "identf"], w=[rk(b, a)])
                P.cp("act", B.SS[0:64, 0:512], ps[b][0:64, :], r=rks(b, 4), w=[B.ssk])
                P.release(b)
                for e in range(2):
                    for fc in range(2):
                        P.dma("act", env["o_rwkv"][s, l, e, 2 * fc:2 * fc + 2].rearrange("h v k -> v h k"), B.So[:, e * 2 + fc, :, :], r=[B.ssk])
            yield
            if NTR == 2:
                yield from interleave(epi_gen(B, s, (0,), 0), epi_gen(B, s, (1,), 1))
            else:
                yield from epi_gen(B, s, (0, 1), 0)

        sets = [alloc_set(i) for i in range(nth)]
        for s0 in range(0, g.nseq, nth):
            yield from interleave(*[unit_gen(sets[i], s0 + i) for i in range(nth) if s0 + i < g.nseq])
        if dbank is not None:
            P.release(dbank)
        if not threaded:
            dump("yrw_" + tag, y[:, :, :], [128, 2, g.T], ["y:0", "y:1"])
            P.barrier()


def host_consts():
    ident = np.eye(128, dtype=np.float32)
    jrev = np.ascontiguousarray(ident[::-1])
    jj, ii = np.meshgrid(np.arange(128), np.arange(128), indexing="ij")
    mU_s = (jj < ii).astype(np.float32)
    mU_i = (jj <= ii).astype(np.float32)
    mL_s = (ii < jj).astype(np.float32)
    mask1 = np.tile(np.concatenate([-mU_s, mU_i], 1), (1, 2))
    mask2 = np.tile(np.concatenate([mU_s, mU_i], 1), (1, 2))
    mask3 = np.tile(-mL_s, (1, 4))
    bones = np.zeros((128, 128), np.float32)
    bones[:64, :64] = 1.0
    bones[64:, 64:] = 1.0
    Lq = 2048
    row = np.repeat(np.arange(Lq // 64), 64).astype(np.float32)
    col = np.tile(np.arange(64), Lq // 64).astype(np.float32)
    inv = (np.float32(10000.0) ** (-np.arange(8, dtype=np.float32) / np.float32(8))).astype(np.float32)
    ang = np.concatenate([row[:, None] * inv, col[:, None] * inv], -1).astype(np.float32)
    cosT = np.repeat(np.cos(ang).astype(np.float32).T, 2, axis=0)
    sinT = np.repeat(np.sin(ang).astype(np.float32).T, 2, axis=0)
    return dict(k_ident=ident, k_jrev=jrev, k_mask1=np.ascontiguousarray(mask1), k_mask2=np.ascontiguousarray(mask2),
                k_mask3=np.ascontiguousarray(mask3), k_bones=bones, k_cos=np.ascontiguousarray(cosT),
                k_sin=np.ascontiguousarray(sinT))


WEIGHT_NAMES = ["w_mod", "b_mod", "w_in", "rw_w0", "rw_w1", "rw_w2", "rw_a0", "rw_a1", "rw_a2", "rw_kk", "rw_ka", "rw_rk",
                "rw_lnx_g", "rw_lnx_b", "mla_qnorm", "mla_wuq", "mla_kvnorm", "mla_wukv", "lru_conv_w", "lru_conv_b",
                "lru_wa", "lru_ba", "lru_wx", "lru_bx", "lru_lambda", "w_out", "ln_g", "ln_b"]


def make_in_maps(inputs, cores):
    f = lambda a: np.ascontiguousarray(np.asarray(a, dtype=np.float32))
    shared = {k: f(inputs[k]) for k in WEIGHT_NAMES}
    shared["rw_rk"] = shared["rw_rk"].reshape(2, 256)
    shared.update(host_consts())
    maps = []
    for i in cores:
        m = dict(shared)
        m["xs"] = f(inputs["x_sample"][i])
        m["xp"] = f(inputs["x_prompt"][4 * i:4 * i + 4]).reshape(1024, D)
        m["st_rwkv"] = f(inputs["state_rwkv"][i])
        m["c_ckv"] = f(inputs["cache_mla_ckv"][i])
        m["c_kr"] = f(inputs["cache_mla_krope"][i])
        m["st_lru"] = f(inputs["state_lru"][i])
        m["cvec"] = f(np.stack([np.asarray(inputs["c"][i]), np.asarray(inputs["c_ctx"])]))
        maps.append(m)
    return maps


def kernel(**inputs):
    nc = build()[0]
    cores = list(range(8))
    res = run_bass_kernel_spmd(nc, make_in_maps(inputs, cores), core_ids=cores)
    R = res.results
    y_prompt = np.concatenate([r["y_p"].reshape(4, 256, D) for r in R], 0)
    y_sample = np.stack([r["y_s"] for r in R], 0)
    o_rwkv = np.concatenate([r["o_rwkv"] for r in R], 0)
    o_ckv = np.concatenate([r["o_ckv"] for r in R], 0)
    o_kr = np.concatenate([r["o_kr"] for r in R], 0)
    o_lru = np.concatenate([r["o_lru"] for r in R], 0)
    return (y_prompt.astype(np.float32), y_sample.astype(np.float32), o_rwkv.astype(np.float32),
            o_ckv.astype(np.float32), o_kr.astype(np.float32), o_lru.astype(np.float32))
```

```python
import numpy as np
from collections import deque
from contextlib import ExitStack
import concourse.bass as bass
import concourse.mybir as mybir
from concourse.bass_utils import run_bass_kernel_spmd

F32 = mybir.dt.float32
BF16 = mybir.dt.bfloat16
ALU = mybir.AluOpType
AF = mybir.ActivationFunctionType

D = 1024
DEPTH = 2
ALPHA = (2 * DEPTH) ** 0.25
LN_EPS = 1e-5
GN_EPS = 64e-5
SM_SCALE = 96.0 ** -0.5
WDEC = float(np.exp(-0.5))
C_R, C_K, C_V, C_G, C_CQ, C_CKV, C_KR, C_GM, C_XL, C_GL = 0, 256, 512, 768, 1024, 1280, 1408, 1440, 1952, 2208

ENGS = ("pe", "act", "dve", "pool", "sp")


class Prog:
    N_DSEM = 40

    def __init__(self, nc, stack, same_engine_sync=True):
        self.nc = nc
        self.ops = {e: [] for e in ENGS}
        self.cnt = {e: 0 for e in ENGS}
        self.esem = {e: stack.enter_context(nc.semaphore("es_" + e)) for e in ENGS}
        self.dsem = [stack.enter_context(nc.semaphore("ds%d" % i)) for i in range(self.N_DSEM)]
        self.dcum = [0] * self.N_DSEM
        self.dnext = 0
        self.know_e = {e: {f: 0 for f in ENGS} for e in ENGS}
        self.know_d = {e: [0] * self.N_DSEM for e in ENGS}
        self.last_w = {}
        self.readers = {}
        self.same_engine_sync = same_engine_sync
        self.ps = [stack.enter_context(nc.psum_tensor("psb%d" % i, [128, 512], F32)) for i in range(8)]
        self.free_banks = deque(range(8))

    def pefull(self, on):
        self.pe_full = on and self.allow_pe_full

    allow_pe_full = ()

    def bank(self):
        assert self.free_banks, "out of PSUM banks"
        return self.free_banks.popleft()

    def release(self, b):
        self.free_banks.append(b)

    def _need(self, eng, ev):
        if ev is None:
            return
        if ev[0] == "c":
            _, f, n = ev
            if self.know_e[eng][f] >= n:
                return
            self.know_e[eng][f] = n
            self.ops[eng].append(("wc", f, n))
        else:
            _, i, v = ev
            if self.know_d[eng][i] >= v:
                return
            self.know_d[eng][i] = v
            self.ops[eng].append(("wd", i, v))

    relax = ("pe", "act", "dve")

    pe_cont = False
    pe_full = False

    def _same(self, eng, ev):
        if eng == "pe":
            return (self.pe_cont or self.pe_full) and ev is not None and ev[0] == "c" and ev[1] == eng
        return (eng in self.relax) and ev is not None and ev[0] == "c" and ev[1] == eng

    def _deps(self, eng, reads, writes):
        for k in reads:
            ev = self.last_w.get(k)
            if self._same(eng, ev) and not self.same_engine_sync:
                continue
            self._need(eng, ev)
        for k in writes:
            ev = self.last_w.get(k)
            if not self._same(eng, ev):
                self._need(eng, ev)
            for ev in self.readers.get(k, ()):
                if not self._same(eng, ev):
                    self._need(eng, ev)

    def _commit(self, ev, reads, writes):
        for k in reads:
            self.readers.setdefault(k, []).append(ev)
        for k in writes:
            self.last_w[k] = ev
            self.readers[k] = []

    def op(self, eng, fn, r=(), w=()):
        self._deps(eng, r, w)
        self.cnt[eng] += 1
        n = self.cnt[eng]
        self.ops[eng].append(("ins", fn, n))
        self._commit(("c", eng, n), r, w)

    def dma(self, eng, out, in_, r=(), w=(), **kw):
        self._deps(eng, r, w)
        i = self.dnext
        self.dnext = (self.dnext + 1) % self.N_DSEM
        if self.dcum[i] > 0:
            self._need(eng, ("d", i, self.dcum[i]))
        self.dcum[i] += 16
        v = self.dcum[i]
        sem = self.dsem[i]
        self.ops[eng].append(
            ("raw", lambda E, sem=sem, out=out, in_=in_, kw=kw: E.dma_start(out=out, in_=in_, **kw).then_inc(sem, 16)))
        self._commit(("d", i, v), r, w)

    def barrier(self):
        for e in ENGS:
            for f in ENGS:
                if self.cnt[f] > 0:
                    self._need(e, ("c", f, self.cnt[f]))
            for i in range(self.N_DSEM):
                if self.dcum[i] > 0:
                    self._need(e, ("d", i, self.dcum[i]))
        self.last_w = {}
        self.readers = {}

    def emit(self):
        self.barrier()
        ops = self.ops
        marked = {e: set() for e in ENGS}
        for e in ENGS:
            for rec in ops[e]:
                if rec[0] == "wc":
                    marked[rec[1]].add(rec[2])
        val = {}
        for e in ENGS:
            c = 0
            for n in sorted(marked[e]):
                c += 1
                val[(e, n)] = c
            assert c < 60000, (e, c)
        esem, dsem = self.esem, self.dsem

        def run(E, e):
            sem = esem[e]
            for rec in ops[e]:
                k = rec[0]
                if k == "ins":
                    ins = rec[1](E)
                    if rec[2] in marked[e]:
                        ins.then_inc(sem, 1)
                elif k == "wc":
                    E.wait_ge(esem[rec[1]], val[(rec[1], rec[2])])
                elif k == "wd":
                    E.wait_ge(dsem[rec[1]], rec[2])
                else:
                    rec[1](E)

        with self.nc.Block() as block:
            @block.tensor
            def _(E):
                run(E, "pe")

            @block.scalar
            def _(E):
                run(E, "act")

            @block.vector
            def _(E):
                run(E, "dve")

            @block.gpsimd
            def _(E):
                run(E, "pool")

            @block.sync
            def _(E):
                run(E, "sp")

    def mm(self, out, lhsT, rhs, start=True, stop=True, r=(), w=()):
        self.pe_cont = (not start) and ("pe" in self.relax)
        self.op("pe", lambda E: E.matmul(out, lhsT, rhs, start=start, stop=stop), r, w)
        self.pe_cont = False

    def act(self, out, in_, func, r=(), w=(), bias=None, scale=None, accum=None):
        kw = {}
        if bias is not None:
            kw["bias"] = bias
        if scale is not None:
            kw["scale"] = scale
        if accum is not None:
            kw["accum_out"] = accum
        self.op("act", lambda E: E.activation(out, in_, func, **kw), r, w)

    def cp(self, eng, out, in_, r=(), w=()):
        if eng == "act":
            self.op("act", lambda E: E.copy(out, in_), r, w)
        else:
            self.op(eng, lambda E: E.tensor_copy(out, in_), r, w)

    def ts(self, eng, out, in0, s1, s2, op0, op1, r=(), w=()):
        self.op(eng, lambda E: E.tensor_scalar(out, in0, s1, s2, op0, op1), r, w)

    def ts1(self, eng, out, in0, s1, op0, r=(), w=()):
        self.op(eng, lambda E: E.tensor_single_scalar(out, in0, s1, op0), r, w)

    def tt(self, eng, out, in0, in1, op, r=(), w=()):
        self.op(eng, lambda E: E.tensor_tensor(out, in0, in1, op), r, w)

    def stt(self, eng, out, in0, scalar, in1, op0, op1, r=(), w=()):
        self.op(eng, lambda E: E.scalar_tensor_tensor(out, in0, scalar, in1, op0, op1), r, w)

    def rsqrt(self, out, in_, eps, scale, tx, tt_, r=(), w=(), kx=None, kt=None, small=False):
        self.act(out, in_, AF.Ln, bias=eps, scale=scale, r=r, w=w)
        self.act(out, out, AF.Exp, scale=-0.5, r=w, w=w)

    def memset(self, eng, ap, val, w=()):
        self.op(eng, lambda E: E.memset(ap, val), (), w)


class Group:
    def __init__(self, name, nseq, L, j, ctx):
        self.name, self.nseq, self.L, self.j, self.ctx = name, nseq, L, j, ctx
        self.T = nseq * L
        self.NT = self.T // 128
        self.LT = L // 128
        self.BW = min(512, L)
        self.CB = self.BW // 128
        self.NB = L // self.BW


def build(dbg=None):
    dbg = dbg or {}
    stop = dbg.get("stop", "")
    groups_sel = dbg.get("groups", "SP")
    nlayers = dbg.get("nlayers", DEPTH)
    nc = bass.Bass("TRN2", target_bir_lowering=False)

    def din(name, shape):
        return nc.dram_tensor(name, list(shape), F32, kind="ExternalInput").ap()

    def dout(name, shape):
        return nc.dram_tensor(name, list(shape), F32, kind="ExternalOutput").ap()

    xs = din("xs", [2048, D]); xp = din("xp", [1024, D])
    st_rwkv = din("st_rwkv", [2, 2, 4, 64, 64]); c_ckv = din("c_ckv", [2, 256, 128]); c_kr = din("c_kr", [2, 256, 32])
    st_lru = din("st_lru", [2, 2, 256]); cvec = din("cvec", [2, D])
    w_mod = din("w_mod", [2, D, 3 * D]); b_mod = din("b_mod", [2, 3 * D]); w_in = din("w_in", [2, D, 2464])
    rw_w0 = din("rw_w0", [2, 2, 256]); rw_w1 = din("rw_w1", [2, 2, D, 64]); rw_w2 = din("rw_w2", [2, 2, 64, 256])
    rw_a0 = din("rw_a0", [2, 2, 256]); rw_a1 = din("rw_a1", [2, 2, D, 64]); rw_a2 = din("rw_a2", [2, 2, 64, 256])
    rw_kk = din("rw_kk", [2, 256]); rw_ka = din("rw_ka", [2, 256]); rw_rk = din("rw_rk", [2, 256])
    rw_lnx_g = din("rw_lnx_g", [2, 256]); rw_lnx_b = din("rw_lnx_b", [2, 256])
    mla_qnorm = din("mla_qnorm", [2, 256]); mla_wuq = din("mla_wuq", [2, 256, 768])
    mla_kvnorm = din("mla_kvnorm", [2, 128]); mla_wukv = din("mla_wukv", [2, 128, 1024])
    lru_conv_w = din("lru_conv_w", [2, 4, 256]); lru_conv_b = din("lru_conv_b", [2, 256])
    lru_wa = din("lru_wa", [2, 2, 4, 64, 64]); lru_ba = din("lru_ba", [2, 2, 256])
    lru_wx = din("lru_wx", [2, 2, 4, 64, 64]); lru_bx = din("lru_bx", [2, 2, 256]); lru_lambda = din("lru_lambda", [2, 2, 256])
    w_out = din("w_out", [2, D, D]); ln_g = din("ln_g", [2, D]); ln_b = din("ln_b", [2, D])
    k_ident = din("k_ident", [128, 128]); k_jrev = din("k_jrev", [128, 128])
    k_mask1 = din("k_mask1", [128, 512]); k_mask2 = din("k_mask2", [128, 512]); k_mask3 = din("k_mask3", [128, 512])
    k_bones = din("k_bones", [128, 128]); k_cos = din("k_cos", [32, 2048]); k_sin = din("k_sin", [32, 2048])

    y_s = dout("y_s", [2048, D]); y_p = dout("y_p", [1024, D])
    o_rwkv = dout("o_rwkv", [4, 2, 2, 4, 64, 64]); o_ckv = dout("o_ckv", [4, 2, 256, 128])
    o_kr = dout("o_kr", [4, 2, 256, 32]); o_lru = dout("o_lru", [4, 2, 2, 256])
    xmid = {"S": nc.dram_tensor("xmid_s", [2048, D], F32).ap(), "P": nc.dram_tensor("xmid_p", [1024, D], F32).ap()}

    dumps = []

    with ExitStack() as st:
        P = Prog(nc, st, same_engine_sync=dbg.get("ses", True))
        P.relax = dbg.get("relax", Prog.relax)
        P.allow_pe_full = dbg.get("pe_full", Prog.allow_pe_full)
        ps = P.ps

        uniq = [0]

        def sb(stack, name, shape, dt):
            uniq[0] += 1
            return stack.enter_context(nc.sbuf_tensor("%s_%d" % (name, uniq[0]), list(shape), dt))

        def dump(name, ap, shape, rkeys):
            if name not in dbg.get("dump", ()):
                return
            o = dout("dbg_" + name, shape)
            P.dma("pool", o, ap, r=rkeys, allow_slow_non_contiguous=True)
            dumps.append(name)

        identf = sb(st, "identf", [128, 128], F32); jrevf = sb(st, "jrevf", [128, 128], F32)
        identb = sb(st, "identb", [128, 128], BF16); jrevb = sb(st, "jrevb", [128, 128], BF16)
        mask1 = sb(st, "mask1", [128, 512], BF16); mask2 = sb(st, "mask2", [128, 512], BF16); mask3 = sb(st, "mask3", [128, 512], BF16)
        bones = sb(st, "bones", [128, 128], F32); onesf = sb(st, "onesf", [128, 128], F32)
        P.dma("sp", identf[:], k_ident, w=["identf"]); P.dma("sp", jrevf[:], k_jrev, w=["jrevf"])
        P.dma("pool", identb[:], k_ident, w=["identb"]); P.dma("pool", jrevb[:], k_jrev, w=["jrevb"])
        P.dma("pool", mask1[:], k_mask1, w=["mask1"]); P.dma("pool", mask2[:], k_mask2, w=["mask2"]); P.dma("pool", mask3[:], k_mask3, w=["mask3"])
        P.dma("sp", bones[:], k_bones, w=["bones"])
        P.memset("dve", onesf[:], 1.0, w=["onesf"])
        mod = [sb(st, "mod%d" % l, [128, 24, 2], F32) for l in range(2)]
        sc1 = [sb(st, "sc1_%d" % l, [128, 8, 2], F32) for l in range(2)]
        gta = [sb(st, "gta%d" % l, [128, 8, 2], F32) for l in range(2)]

        with ExitStack() as ph:
            cv = sb(ph, "cv", [128, 8, 2], F32); scv = sb(ph, "scv", [128, 8, 2], F32)
            bm = sb(ph, "bm", [128, 24], F32)
            wm = [sb(ph, "wm%d" % i, [128, 8, 512], F32) for i in range(3)]
            for j in range(2):
                P.dma("sp", cv[:, :, j], cvec[j].rearrange("(kc p) -> p kc", p=128), w=["cv"], allow_slow_non_contiguous=True)
            P.act(scv[:], cv[:], AF.Silu, r=["cv"], w=["scv"])
            modrow = sb(ph, "modrow", [2, 3 * D], F32); bmrow = sb(ph, "bmrow", [2, 3 * D], F32)
            for l in range(2):
                P.dma("sp", bmrow[:], b_mod[l:l + 1, :].to_broadcast([2, 3 * D]), w=["bmrow"])
                for ob in range(6):
                    wi = (l * 6 + ob) % 3
                    wmt = wm[wi]; wk = "wm%d" % wi
                    P.dma("sp" if ob % 2 == 0 else "act", wmt[:], w_mod[l, :, ob * 512:(ob + 1) * 512].rearrange("(kc p) n -> p kc n", p=128), w=[wk])
                    b = P.bank(); bk = "ps%d" % b
                    for kc in range(8):
                        P.mm(ps[b][0:2, :], scv[:, kc, :], wmt[:, kc, :], start=(kc == 0), stop=(kc == 7), r=[wk, "scv"], w=[bk])
                    P.tt("dve", modrow[:, ob * 512:(ob + 1) * 512], ps[b][0:2, :], bmrow[:, ob * 512:(ob + 1) * 512], ALU.add,
                         r=[bk, "bmrow"], w=["modrow"])
                    P.release(b)
                b = P.bank(); bk = "ps%d" % b
                for oc in range(24):
                    P.mm(ps[b][:, oc * 2:oc * 2 + 2], modrow[0:2, oc * 128:(oc + 1) * 128], identf[0:2, 0:2], r=["modrow", "identf"], w=[bk])
                P.cp("dve", mod[l][:], ps[b][:, 0:48].rearrange("p (a b) -> p a b", b=2), r=[bk], w=["mod%d" % l])
                P.release(b)
                P.ts1("dve", sc1[l][:], mod[l][:, 8:16, :], 1.0, ALU.add, r=["mod%d" % l], w=["sc1_%d" % l])
                P.ts1("dve", gta[l][:], mod[l][:, 16:24, :], 1.0 / ALPHA, ALU.mult, r=["mod%d" % l], w=["gta%d" % l])
            dump("mod0", mod[0][:], [128, 24, 2], ["mod0"])
            P.barrier()
        if stop == "0":
            P.emit()
            return nc, dumps

        groups = []
        if "S" in groups_sel:
            groups.append(Group("S", 1, 2048, 0, True))
        if "P" in groups_sel:
            groups.append(Group("P", 4, 256, 1, False))
        xin0 = {"S": xs, "P": xp}
        xfin = {"S": y_s, "P": y_p}
        env = dict(nc=nc, P=P, sb=sb, dump=dump, stop=stop, identf=identf, jrevf=jrevf, identb=identb, jrevb=jrevb,
                   mask1=mask1, mask2=mask2, mask3=mask3, bones=bones, onesf=onesf, k_cos=k_cos, k_sin=k_sin,
                   mod=mod, sc1=sc1, gta=gta, dbg=dbg,
                   W=dict(w_in=w_in, rw_w0=rw_w0, rw_w1=rw_w1, rw_w2=rw_w2, rw_a0=rw_a0, rw_a1=rw_a1, rw_a2=rw_a2,
                          rw_kk=rw_kk, rw_ka=rw_ka, rw_rk=rw_rk, rw_lnx_g=rw_lnx_g, rw_lnx_b=rw_lnx_b,
                          mla_qnorm=mla_qnorm, mla_wuq=mla_wuq, mla_kvnorm=mla_kvnorm, mla_wukv=mla_wukv,
                          lru_conv_w=lru_conv_w, lru_conv_b=lru_conv_b, lru_wa=lru_wa, lru_ba=lru_ba, lru_wx=lru_wx,
                          lru_bx=lru_bx, lru_lambda=lru_lambda, w_out=w_out, ln_g=ln_g, ln_b=ln_b),
                   st_rwkv=st_rwkv, c_ckv=c_ckv, c_kr=c_kr, st_lru=st_lru,
                   o_rwkv=o_rwkv, o_ckv=o_ckv, o_kr=o_kr, o_lru=o_lru)
        for g in groups:
            for l in range(nlayers):
                x_in = xin0[g.name] if l == 0 else xmid[g.name]
                x_out = xfin[g.name] if l == nlayers - 1 else xmid[g.name]
                done = layer(env, g, l, x_in, x_out, "xin%s%d" % (g.name, l), "xin%s%d" % (g.name, l + 1))
                if done:
                    P.emit()
                    return nc, dumps
        P.emit()
    return nc, dumps


def layer(env, g, l, x_in, x_out, kin, kout):
    nc, P, sb, dump, stop = env["nc"], env["P"], env["sb"], env["dump"], env["stop"]
    ps = P.ps
    W = env["W"]
    identb, jrevb, identf, jrevf = env["identb"], env["jrevb"], env["identf"], env["jrevf"]
    T, NT, LT, L, j = g.T, g.NT, g.LT, g.L, g.j
    tag = "%s%d" % (g.name, l)
    w_in = W["w_in"]

    def wslice(c0, c1):
        return w_in[l, :, c0:c1].rearrange("(kc p) n -> p kc n", p=128)

    def ppar(stack, name, src):
        t = sb(stack, name, [128, 2], F32)
        P.dma("sp", t[:], src.rearrange("(fc p) -> p fc", p=128), w=[name], allow_slow_non_contiguous=True)
        return t

    with ExitStack() as lay:
        u = sb(lay, "u", [128, 8, T], BF16)
        y_rw = sb(lay, "y_rw", [128, 2, T], BF16)
        y_lru = sb(lay, "y_lru", [128, 2, T], BF16)
        with ExitStack() as revs:
            ur = sb(revs, "ur", [128, 8, T], BF16)
            with ExitStack() as ph:
                NBUF = 3
                xb = [sb(ph, "xa%d" % i, [128, D], F32) for i in range(NBUF)]
                xn = [sb(ph, "xn%d" % i, [128, D], BF16) for i in range(NBUF)]
                st6 = sb(ph, "st6", [128, 12], F32); mv = sb(ph, "mv", [128, 2], F32); rs = sb(ph, "rs", [128, 1], F32)
                rq1 = sb(ph, "rq1", [128, 1], F32); rq2 = sb(ph, "rq2", [128, 1], F32)

                def stage1(tt):
                    xt = xb[tt % NBUF]; xk = "xa%d" % (tt % NBUF); xnt = xn[tt % NBUF]; xnk = "xn%d" % (tt % NBUF)
                    P.dma("sp" if tt % 2 == 0 else "pool", xt[:], x_in[tt * 128:(tt + 1) * 128, :], r=["%s:%d" % (kin, tt)], w=[xk])
                    P.op("dve", lambda E, xt=xt: E.bn_stats(st6[:, 0:6], xt[:, 0:512]), r=[xk], w=["st6a"])
                    P.op("dve", lambda E, xt=xt: E.bn_stats(st6[:, 6:12], xt[:, 512:1024]), r=[xk], w=["st6b"])
                    P.op("dve", lambda E: E.bn_aggr(mv[:], st6[:]), r=["st6a", "st6b"], w=["mv"])
                    P.rsqrt(rs[:], mv[:, 1:2], LN_EPS, 1.0, rq1[:], rq2[:], r=["mv"], w=["rs"], kx="rq1", kt="rq2", small=True)
                    P.ts("dve", xnt[:], xt[:], mv[:, 0:1], rs[:], ALU.subtract, ALU.mult, r=[xk, "mv", "rs"], w=[xnk])

                def stage2(tt):
                    s_, i = divmod(tt, LT)
                    rt = s_ * LT + (LT - 1 - i)
                    xnt = xn[tt % NBUF]; xnk = "xn%d" % (tt % NBUF)
                    for (dst, dk, perm, pk, tpos) in ((u, "u", identb, "identb", tt), (ur, "ur", jrevb, "jrevb", rt)):
                        for half in range(2):
                            b = P.bank(); bk = "ps%d" % b
                            for q in range(4):
                                kc = half * 4 + q
                                P.mm(ps[b][:, q * 128:(q + 1) * 128], xnt[:, kc * 128:(kc + 1) * 128], perm[:], r=[xnk, pk], w=[bk])
                            for q in range(4):
                                kc = half * 4 + q
                                if dk == "u":
                                    P.act(dst[:, kc, tpos * 128:(tpos + 1) * 128], ps[b][:, q * 128:(q + 1) * 128], AF.Identity,
                                          scale=env["sc1"][l][:, kc, j:j + 1], bias=env["mod"][l][:, kc, j:j + 1],
                                          r=[bk, "sc1_%d" % l, "mod%d" % l], w=["%s:%d" % (dk, tpos)])
                                else:
                                    P.ts("dve", dst[:, kc, tpos * 128:(tpos + 1) * 128], ps[b][:, q * 128:(q + 1) * 128],
                                         env["sc1"][l][:, kc, j:j + 1], env["mod"][l][:, kc, j:j + 1], ALU.mult, ALU.add,
                                         r=[bk, "sc1_%d" % l, "mod%d" % l], w=["%s:%d" % (dk, tpos)])
                            P.release(b)

                for tt in range(NT + 1):
                    if tt < NT:
                        stage1(tt)
                    if tt >= 1:
                        stage2(tt - 1)
                dump("u_" + tag, u[:], [128, 8, T], ["u:%d" % t for t in range(NT)])
                dump("ur_" + tag, ur[:], [128, 8, T], ["ur:%d" % t for t in range(NT)])
                P.barrier()
            if stop == "A":
                return True
            ukeys = ["u:%d" % t for t in range(NT)]
            urkeys = ["ur:%d" % t for t in range(NT)]
            skip = env["dbg"].get("skip", "")
            threaded = (g.nseq > 1) and skip == "" and stop in ("", "E") and env["dbg"].get("threads", True)
            if threaded:
                y_mla = sb(revs, "y_mla", [128, 4, T], BF16)
                with ExitStack() as shared:
                    tw = env["dbg"].get("tw", (6, 1, 1))
                    run_gen(interleave_w([(phase_rwkv(env, g, l, u, ur, y_rw, ukeys, urkeys, wslice, ppar, tag, threaded=True, ext=shared), tw[0]),
                                          (phase_lru(env, g, l, u, ur, y_lru, ukeys, urkeys, wslice, ppar, tag, threaded=True, ext=shared), tw[1]),
                                          (phase_mla(env, g, l, u, y_mla, ukeys, wslice, ppar, tag, threaded=True, ext=shared), tw[2])]))
                    P.barrier()
                phase_out(env, g, l, (y_rw, y_mla, y_lru), x_in, x_out, kin, kout, tag)
                return stop == "E"
            if "B" not in skip:
                run_gen(phase_rwkv(env, g, l, u, ur, y_rw, ukeys, urkeys, wslice, ppar, tag))
            if stop == "B":
                return True
            if "C" not in skip:
                run_gen(phase_lru(env, g, l, u, ur, y_lru, ukeys, urkeys, wslice, ppar, tag))
            if stop == "C":
                return True
        y_mla = sb(lay, "y_mla", [128, 4, T], BF16)
        if "D" not in skip:
            run_gen(phase_mla(env, g, l, u, y_mla, ukeys, wslice, ppar, tag))
        if stop == "D":
            return True
        phase_out(env, g, l, (y_rw, y_mla, y_lru), x_in, x_out, kin, kout, tag)
        if stop == "E":
            return True
    return False


def _unpack(env):
    return env["nc"], env["P"], env["sb"], env["dump"], env["W"]


def proj_fm(P, out, w, c0, M, src, t0, N, r, wk):
    for kc in range(8):
        P.mm(out, w[:, kc, c0:c0 + M], src[:, kc, t0:t0 + N], start=(kc == 0), stop=(kc == 7), r=r, w=[wk])


def phase_out(env, g, l, y, x_in, x_out, kin, kout, tag):
    nc, P, sb, dump, W = _unpack(env)
    ps = P.ps
    identf = env["identf"]
    NT, j = g.NT, g.j
    ykeys = ["y:%d" % c for c in range(8)]
    with ExitStack() as ph:
        w_o = sb(ph, "w_o", [128, 8, D], BF16)
        P.dma("pool", w_o[:], W["w_out"][l].rearrange("(kc p) n -> p kc n", p=128), w=["w_o"])
        gate_bc = sb(ph, "gate_bc", [128, D], F32); g_bc = sb(ph, "g_bc", [128, D], F32); b_bc = sb(ph, "b_bc", [128, D], F32)
        P.dma("sp", g_bc[:], W["ln_g"][l:l + 1, :].to_broadcast([128, D]), w=["g_bc"])
        P.dma("sp", b_bc[:], W["ln_b"][l:l + 1, :].to_broadcast([128, D]), w=["b_bc"])
        for half in range(2):
            b = P.bank(); bk = "ps%d" % b
            for q in range(4):
                kc = half * 4 + q
                P.mm(ps[b][:, q * 128:(q + 1) * 128], env["gta"][l][:, kc, j:j + 1].to_broadcast([128, 128]), identf[:],
                     r=["gta%d" % l, "identf"], w=[bk])
            P.cp("act", gate_bc[:, half * 512:(half + 1) * 512], ps[b][:], r=[bk], w=["gate_bc"])
            P.release(b)
        xb = [sb(ph, "xo%d" % i, [128, D], F32) for i in range(3)]
        t1 = [sb(ph, "t1o%d" % i, [128, D], F32) for i in range(3)]
        st6 = sb(ph, "st6o", [128, 12], F32); mv = sb(ph, "mvo", [128, 2], F32); rs = sb(ph, "rso", [128, 1], F32)
        nmr = sb(ph, "nmr", [128, 1], F32); rq1 = sb(ph, "rq1o", [128, 1], F32); rq2 = sb(ph, "rq2o", [128, 1], F32)
        def stage_a(tt):
            xt = xb[tt % 3]; xk = "xo%d" % (tt % 3); tt1 = t1[tt % 3]; tk = "t1o%d" % (tt % 3)
            P.dma("sp", xt[:], x_in[tt * 128:(tt + 1) * 128, :], r=["%s:%d" % (kin, tt)], w=[xk])
            bs = [P.bank(), P.bank()]
            for half in range(2):
                for kc in range(8):
                    yt_, yc_ = (y[0], kc) if kc < 2 else ((y[1], kc - 2) if kc < 6 else (y[2], kc - 6))
                    P.mm(ps[bs[half]][:, :], yt_[:, yc_, tt * 128:(tt + 1) * 128], w_o[:, kc, half * 512:(half + 1) * 512],
                         start=(kc == 0), stop=(kc == 7), r=[ykeys[kc], "w_o"], w=["ps%d" % bs[half]])
            for half in range(2):
                P.tt("dve", tt1[:, half * 512:(half + 1) * 512], ps[bs[half]][:, :], gate_bc[:, half * 512:(half + 1) * 512], ALU.mult,
                     r=["ps%d" % bs[half], "gate_bc"], w=[tk])
                P.release(bs[half])
            P.tt("pool", tt1[:, 0:512], tt1[:, 0:512], xt[:, 0:512], ALU.add, r=[tk, xk], w=[tk + "L"])
            P.tt("dve", tt1[:, 512:1024], tt1[:, 512:1024], xt[:, 512:1024], ALU.add, r=[tk, xk], w=[tk + "R"])

        def stage_a2(tt):
            tt1 = t1[tt % 3]; tk = "t1o%d" % (tt % 3)
            P.op("dve", lambda E, a=tt1: E.bn_stats(st6[:, 6:12], a[:, 512:1024]), r=[tk, tk + "R"], w=["st6ob"])
            P.op("dve", lambda E, a=tt1: E.bn_stats(st6[:, 0:6], a[:, 0:512]), r=[tk, tk + "L"], w=["st6oa"])
            P.op("dve", lambda E: E.bn_aggr(mv[:], st6[:]), r=["st6oa", "st6ob"], w=["mvo"])
            P.rsqrt(rs[:], mv[:, 1:2], LN_EPS / (ALPHA * ALPHA), 1.0, rq1[:], rq2[:], r=["mvo"], w=["rso"], kx="rq1o", kt="rq2o", small=True)
            P.stt("dve", nmr[:], mv[:, 0:1], -1.0, rs[:], ALU.mult, ALU.mult, r=["mvo", "rso"], w=["nmr"])
            P.act(tt1[:], tt1[:], AF.Identity, scale=rs[:, 0:1], bias=nmr[:, 0:1], r=[tk, tk + "L", tk + "R", "rso", "nmr"], w=[tk, tk + "L", tk + "R"])

        def stage_b(tt):
            tt1 = t1[tt % 3]; tk = "t1o%d" % (tt % 3)
            P.tt("dve", tt1[:], tt1[:], g_bc[:], ALU.mult, r=[tk, "g_bc"], w=[tk])
            P.tt("pool", tt1[:], tt1[:], b_bc[:], ALU.add, r=[tk, "b_bc"], w=[tk])
            P.dma("pool", x_out[tt * 128:(tt + 1) * 128, :], tt1[:], r=[tk], w=["%s:%d" % (kout, tt)])

        for tt in range(NT + 1):
            if tt < NT:
                stage_a(tt)
            if tt >= 1:
                stage_b(tt - 1)
            if tt < NT:
                stage_a2(tt)
        P.barrier()


def phase_lru(env, g, l, u, ur, y, ukeys, urkeys, wslice, ppar, tag, threaded=False, ext=None):
    nc, P, sb, dump, W = _unpack(env)
    ps = P.ps
    identf, jrevf = env["identf"], env["jrevf"]
    L, LT, BW, NB, CB = g.L, g.LT, g.BW, g.NB, g.CB
    NS = "NSCAN"
    with (_NoClose(ext) if ext is not None else ExitStack()) as ph:
        w_xl = sb(ph, "w_xl", [128, 8, 256], BF16); w_gl = sb(ph, "w_gl", [128, 8, 256], BF16)
        P.dma("pool", w_xl[:], wslice(C_XL, C_XL + 256), w=["w_xl"])
        P.dma("pool", w_gl[:], wslice(C_GL, C_GL + 256), w=["w_gl"])
        cw = sb(ph, "cw", [128, 2, 4], F32)
        for jj in range(4):
            P.dma("sp", cw[:, :, jj], W["lru_conv_w"][l, jj].rearrange("(fc p) -> p fc", p=128), w=["cw"], allow_slow_non_contiguous=True)
        cbias = ppar(ph, "cbias", W["lru_conv_b"][l])
        Wbd = {}
        for e in range(2):
            for nm, src in (("a", W["lru_wa"]), ("x", W["lru_wx"])):
                t = sb(ph, "wbd%s%d" % (nm, e), [128, 2, 128], BF16)
                k = "wbd%s%d" % (nm, e)
                P.memset("pool", t[:], 0.0, w=[k])
                for hh in range(2):
                    P.dma("pool", t[hh * 64:(hh + 1) * 64, :, hh * 64:(hh + 1) * 64],
                          src[l, e].rearrange("(fc hh) i o -> hh i fc o", hh=2)[hh], w=[k])
                Wbd[(e, nm)] = (t, k)
        ba = [ppar(ph, "lba%d" % e, W["lru_ba"][l, e]) for e in range(2)]
        bx = [ppar(ph, "lbx%d" % e, W["lru_bx"][l, e]) for e in range(2)]
        lam = [ppar(ph, "lam%d" % e, W["lru_lambda"][l, e]) for e in range(2)]
        hba = [sb(ph, "hba%d" % e, [128, 2], F32) for e in range(2)]
        hbx = [sb(ph, "hbx%d" % e, [128, 2], F32) for e in range(2)]
        for e in range(2):
            P.ts1("dve", hba[e][:], ba[e][:], 0.5, ALU.mult, r=["lba%d" % e], w=["hba%d" % e])
            P.ts1("dve", hbx[e][:], bx[e][:], 0.5, ALU.mult, r=["lbx%d" % e], w=["hbx%d" % e])
        cneg = [sb(ph, "cneg%d" % e, [128, 2], F32) for e in range(2)]
        hcn = [sb(ph, "hcn%d" % e, [128, 2], F32) for e in range(2)]
        for e in range(2):
            k = "cneg%d" % e
            P.act(cneg[e][:], lam[e][:], AF.Exp, scale=-1.0, r=["lam%d" % e], w=[k])
            P.act(cneg[e][:], cneg[e][:], AF.Ln, bias=1.0, r=[k], w=[k])
            P.ts1("dve", cneg[e][:], cneg[e][:], -8.0, ALU.mult, r=[k], w=[k])
            P.ts1("dve", hcn[e][:], cneg[e][:], 0.5, ALU.mult, r=[k], w=["hcn%d" % e])
        h0 = None
        if g.ctx:
            h0 = [ppar(ph, "lh0%d" % e, env["st_lru"][l, e]) for e in range(2)]
        xlp = sb(ph, "xlp", [128, 2, L + 4], F32); xc = sb(ph, "xc", [128, 2, L], F32); xcb = sb(ph, "xcb", [128, 2, L], BF16)
        hh_ = [sb(ph, "lh%d" % e, [128, 2, L], F32) for e in range(2)]
        NT2 = 2 if g.nseq == 1 else 1
        scr = []
        for i in range(NT2):
            scr.append({nm: sb(ph, "l%s%d" % (nm, i), [128, BW], F32) for nm in ("sr", "si", "aa", "a2", "bt", "hT", "sg", "tsum")})
        git = [0, 0]
        P.memset("dve", xlp[:, :, 0:2], 0.0, w=["xlp"])
        P.memset("dve", xlp[:, :, L + 2:L + 4], 0.0, w=["xlp"])
        for s in range(g.nseq):
            tok0 = s * L
            for e in range(2):
                src, skeys = (u, ukeys) if e == 0 else (ur, urkeys)
                hk = "lh%d" % e
                for blk in range(NB):
                    for fc in range(2):
                        yield from need(P, 1)
                        b = P.bank(); bk = "ps%d" % b
                        proj_fm(P, ps[b][:, 0:BW], w_xl, fc * 128, 128, src, tok0 + blk * BW, BW, ["w_xl"] + skeys, bk)
                        P.cp("act", xlp[:, fc, 2 + blk * BW:2 + (blk + 1) * BW], ps[b][:, 0:BW], r=[bk], w=["xlp"])
                        P.release(b)
                        yield
                dd = [jj - 2 for jj in range(4)] if e == 0 else [2 - jj for jj in range(4)]
                for fc in range(2):
                    P.ts("dve", xc[:, fc, :], xlp[:, fc, 2 + dd[0]:2 + dd[0] + L], cw[:, fc, 0:1], cbias[:, fc:fc + 1], ALU.mult, ALU.add,
                         r=["xlp", "cw", "cbias"], w=["xc"])
                    for jj in range(1, 4):
                        P.stt("dve", xc[:, fc, :], xlp[:, fc, 2 + dd[jj]:2 + dd[jj] + L], cw[:, fc, jj:jj + 1], xc[:, fc, :], ALU.mult, ALU.add,
                              r=["xlp", "cw", "xc"], w=["xc"])
                P.cp("act", xcb[:, 0, :], xc[:, 0, :], r=["xc"], w=["xcb"])
                P.cp("dve", xcb[:, 1, :], xc[:, 1, :], r=["xc"], w=["xcb"])
                yield
                wa_t, wa_k = Wbd[(e, "a")]; wx_t, wx_k = Wbd[(e, "x")]

                def gate_s1(fc, blk, S_, sx):
                    sl = slice(blk * BW, (blk + 1) * BW)
                    sr, si, aa, a2 = S_["sr"], S_["si"], S_["aa"], S_["a2"]
                    b1 = P.bank(); b2 = P.bank()
                    P.mm(ps[b1][:, 0:BW], wa_t[:, fc, :], xcb[:, fc, sl], r=[wa_k, "xcb"], w=["ps%d" % b1])
                    P.mm(ps[b2][:, 0:BW], wx_t[:, fc, :], xcb[:, fc, sl], r=[wx_k, "xcb"], w=["ps%d" % b2])
                    P.act(sr[:], ps[b1][:, 0:BW], AF.Tanh, bias=hba[e][:, fc:fc + 1], scale=0.5, r=["ps%d" % b1, "hba%d" % e], w=["lsr" + sx])
                    P.act(si[:], ps[b2][:, 0:BW], AF.Tanh, bias=hbx[e][:, fc:fc + 1], scale=0.5, r=["ps%d" % b2, "hbx%d" % e], w=["lsi" + sx])
                    P.release(b1); P.release(b2)
                    P.act(aa[:], sr[:], AF.Exp, scale=hcn[e][:, fc:fc + 1], bias=hcn[e][:, fc:fc + 1], r=["lsr" + sx, "hcn%d" % e], w=["laa" + sx])
                    P.act(a2[:], sr[:], AF.Exp, scale=cneg[e][:, fc:fc + 1], bias=cneg[e][:, fc:fc + 1], r=["lsr" + sx, "cneg%d" % e], w=["la2" + sx])

                def gate_s2(fc, blk, S_, sx):
                    sl = slice(blk * BW, (blk + 1) * BW)
                    si, aa, a2, bt = S_["si"], S_["aa"], S_["a2"], S_["bt"]
                    P.act(a2[:], a2[:], AF.Sqrt, bias=1.0, scale=-1.0, r=["la2" + sx], w=["la2" + sx])
                    P.stt("dve", bt[:], si[:], 1.0, a2[:], ALU.add, ALU.mult, r=["la2" + sx, "lsi" + sx], w=["lbt" + sx])
                    P.stt("dve", bt[:], bt[:], 0.5, xc[:, fc, sl], ALU.mult, ALU.mult, r=["lbt" + sx, "xc"], w=["lbt" + sx])
                    if blk == 0:
                        init = h0[e][:, fc:fc + 1] if h0 is not None else 0.0
                        rk = ["lh0%d" % e] if h0 is not None else []
                    else:
                        init = hh_[e][:, fc, blk * BW - 1:blk * BW]
                        rk = [hk]
                    P.op("dve", lambda E, o=hh_[e][:, fc, sl], i0=init, aa=aa, bt=bt: E.tensor_tensor_scan(o, aa[:], bt[:], i0, ALU.mult, ALU.add),
                         r=["laa" + sx, "lbt" + sx] + rk, w=[hk])

                for blk in range(NB):
                    if NT2 == 2:
                        yield from need(P, 4)
                        for fc in range(2):
                            gate_s1(fc, blk, scr[fc], "%d" % fc)
                        yield
                        for fc in range(2):
                            gate_s2(fc, blk, scr[fc], "%d" % fc)
                        yield
                    else:
                        for fc in range(2):
                            yield from need(P, 2)
                            gate_s1(fc, blk, scr[0], "0")
                            gate_s2(fc, blk, scr[0], "0")
                            yield
                if not g.ctx:
                    P.dma("pool", env["o_lru"][s, l, e].rearrange("(fc p) -> p fc", p=128), hh_[e][:, :, L - 1], r=[hk], allow_slow_non_contiguous=True)
            for blk in range(NB):
                sl = slice(blk * BW, (blk + 1) * BW)
                for fc in range(2):
                    S_ = scr[git[1] % NT2]; sx = "%d" % (git[1] % NT2); git[1] += 1
                    hT, sg, tsum = S_["hT"], S_["sg"], S_["tsum"]
                    yield from need(P, 3)
                    bT = P.bank()
                    for ci in range(CB):
                        tr = LT - 1 - (blk * CB + ci)
                        P.mm(ps[bT][:, ci * 128:(ci + 1) * 128], hh_[1][:, fc, tr * 128:(tr + 1) * 128], identf[:], r=["lh1", "identf"], w=["ps%d" % bT])
                    P.cp("act", hT[:], ps[bT][:, 0:BW], r=["ps%d" % bT], w=["lhT" + sx])
                    P.release(bT)
                    bF = P.bank()
                    for ci in range(CB):
                        P.mm(ps[bF][:, ci * 128:(ci + 1) * 128], hT[:, ci * 128:(ci + 1) * 128], jrevf[:], r=["lhT" + sx, "jrevf"], w=["ps%d" % bF])
                    bG = P.bank()
                    proj_fm(P, ps[bG][:, 0:BW], w_gl, fc * 128, 128, u, tok0 + blk * BW, BW, ["w_gl"] + ukeys, "ps%d" % bG)
                    P.act(sg[:], ps[bG][:, 0:BW], AF.Tanh, scale=0.5, r=["ps%d" % bG], w=["lsg" + sx])
                    P.stt("dve", sg[:], sg[:], 1.0, ps[bG][:, 0:BW], ALU.add, ALU.mult, r=["lsg" + sx, "ps%d" % bG], w=["lsg" + sx])
                    P.release(bG)
                    P.tt("dve", tsum[:], ps[bF][:, 0:BW], hh_[0][:, fc, sl], ALU.add, r=["ps%d" % bF, "lh0"], w=["ltsum" + sx])
                    P.release(bF)
                    P.stt("dve", y[:, fc, tok0 + blk * BW:tok0 + (blk + 1) * BW], tsum[:], 0.5, sg[:], ALU.mult, ALU.mult, r=["ltsum" + sx, "lsg" + sx], w=["y:%d" % (6 + fc)])
                    yield
        if not threaded:
            dump("ylru_" + tag, y[:, :, :], [128, 2, g.T], ["y:6", "y:7"])
            P.barrier()


def phase_mla(env, g, l, u, y, ukeys, wslice, ppar, tag, threaded=False, ext=None):
    nc, P, sb, dump, W = _unpack(env)
    ps = P.ps
    identf, onesf = env["identf"], env["onesf"]
    L, LT, BW, NB, CB = g.L, g.LT, g.BW, g.NB, g.CB
    ctx = g.ctx
    K0 = 256 if ctx else 0
    Lk = L + K0
    KT = Lk // 128
    QB = BW
    with (_NoClose(ext) if ext is not None else ExitStack()) as ph:
        w_cq = sb(ph, "w_cq", [128, 8, 256], BF16); P.dma("pool", w_cq[:], wslice(C_CQ, C_CQ + 256), w=["w_cq"])
        w_ck = sb(ph, "w_ck", [128, 8, 128], BF16); P.dma("pool", w_ck[:], wslice(C_CKV, C_CKV + 128), w=["w_ck"])
        w_kr = sb(ph, "w_kr", [128, 8, 96], BF16)
        P.memset("pool", w_kr[:], 0.0, w=["w_kr"])
        P.dma("pool", w_kr[:, :, 64:96], wslice(C_KR, C_KR + 32), w=["w_kr"])
        w_gm = sb(ph, "w_gm", [128, 8, 512], BF16); P.dma("pool", w_gm[:], wslice(C_GM, C_GM + 512), w=["w_gm"])
        wuq = sb(ph, "wuq", [128, 2, 768], BF16)
        P.dma("pool", wuq[:], W["mla_wuq"][l].rearrange("(kc p) n -> p kc n", p=128), w=["wuq"])
        wkk = sb(ph, "wkk", [128, 8, 64], BF16); wkv = sb(ph, "wkv", [128, 8, 64], BF16)
        wv4 = W["mla_wukv"][l].rearrange("r (h x) -> r h x", x=128)
        P.dma("pool", wkk[:], wv4[:, :, 0:64], w=["wkk"])
        P.dma("pool", wkv[:], wv4[:, :, 64:128], w=["wkv"])
        qng = ppar(ph, "qng", W["mla_qnorm"][l])
        kvg = sb(ph, "kvg", [128, 1], F32)
        P.dma("sp", kvg[:], W["mla_kvnorm"][l].rearrange("(p o) -> p o", o=1), w=["kvg"], allow_slow_non_contiguous=True)
        if ctx:
            cosT = sb(ph, "cosT", [128, 512], F32); sinT = sb(ph, "sinT", [128, 512], F32)

            def load_tables(p0, n):
                P.dma("sp", cosT[64:96, 0:n], env["k_cos"][:, p0:p0 + n], w=["cosT"])
                P.dma("sp", sinT[64:96, 0:n], env["k_sin"][:, p0:p0 + n], w=["sinT"])
            w_krs = sb(ph, "w_krs", [128, 8, 96], BF16)
            P.memset("pool", w_krs[:], 0.0, w=["w_krs"])
            P.ts1("dve", w_krs[:, :, 64:96:2], w_kr[:, :, 65:96:2], -1.0, ALU.mult, r=["w_kr"], w=["w_krs"])
            P.cp("dve", w_krs[:, :, 65:96:2], w_kr[:, :, 64:96:2], r=["w_kr"], w=["w_krs"])
            wuqs = sb(ph, "wuqs", [128, 2, 768], BF16)
            P.memset("pool", wuqs[:], 0.0, w=["wuqs"])
            for kc in range(2):
                v4 = wuq[:, kc, :].rearrange("p (h x) -> p h x", x=96)
                v4s = wuqs[:, kc, :].rearrange("p (h x) -> p h x", x=96)
                P.ts1("dve", v4s[:, :, 64:96:2], v4[:, :, 65:96:2], -1.0, ALU.mult, r=["wuq"], w=["wuqs"])
                P.cp("dve", v4s[:, :, 65:96:2], v4[:, :, 64:96:2], r=["wuq"], w=["wuqs"])
        ckvn = sb(ph, "ckvn", [128, Lk], BF16); krall = sb(ph, "krall", [128, Lk], BF16)
        Kt = sb(ph, "Kt", [128, 8, Lk], BF16); V = sb(ph, "Vv", [128, KT, 8, 65], BF16)
        cqn = sb(ph, "cqn", [128, 2, L], BF16); sgm = [sb(ph, "sgm%d" % i, [128, QB], F32) for i in range(3 if KT < 6 else 2)]
        ck = sb(ph, "mck", [128, BW], F32); sq = sb(ph, "msq", [128, BW], F32); sq1 = sb(ph, "msq1", [128, BW], F32)
        rstd = sb(ph, "mrstd", [128, BW], F32); t1 = sb(ph, "mt1", [128, BW], F32); t2 = sb(ph, "mt2", [128, BW], F32)
        if g.nseq == 1:
            sqq = sb(ph, "msqq", [128, BW], F32); rstdq = sb(ph, "mrstdq", [128, BW], F32); t2q = sb(ph, "mt2q", [128, BW], F32)
            kq_ = ("msqq", "mrstdq", "mt2q")
        else:
            sqq, rstdq, t2q = sq, rstd, t2
            kq_ = ("msq", "mrstd", "mt2")
        Pt = [sb(ph, "Pt%d" % i, [128, QB], BF16) for i in range(5 if KT >= 6 else 4)]
        Qh = [sb(ph, "Qh%d" % i, [128, QB], BF16) for i in range(3 if KT < 6 else 2)]
        accs, rden, tmpy = ck, sq, sq1
        P.memset("pool", V[:, :, :, 64:65], 1.0, w=["Vones"])
        if ctx:
            ckc = sb(ph, "ckc", [128, 2, 128], F32); krc = sb(ph, "krc", [128, 2, 96], F32)
        else:
            otile = sb(ph, "otile", [128, CB, 128], F32); okr = sb(ph, "okr", [128, CB, 32], F32)
        qi = 0
        for s in range(g.nseq):
            tok0 = s * L
            if ctx:
                P.dma("sp", ckc[:], env["c_ckv"][l].rearrange("(t p) r -> p t r", p=128), w=["ckc"])
                P.memset("dve", krc[:], 0.0, w=["krc"])
                P.dma("sp", krc[:, :, 64:96], env["c_kr"][l].rearrange("(t p) r -> p t r", p=128), w=["krc"])
                for t in range(2):
                    b = P.bank(); bk = "ps%d" % b
                    P.mm(ps[b][:, 0:128], ckc[:, t, :], identf[:], r=["ckc", "identf"], w=[bk])
                    P.cp("act", ckvn[:, t * 128:(t + 1) * 128], ps[b][:, 0:128], r=[bk], w=["ckvn"])
                    P.release(b)
                    b = P.bank(); bk = "ps%d" % b
                    P.mm(ps[b][0:96, 0:128], krc[:, t, :], identf[:], r=["krc", "identf"], w=[bk])
                    P.cp("act", krall[64:96, t * 128:(t + 1) * 128], ps[b][64:96, 0:128], r=[bk], w=["krall"])
                    P.release(b)
            for blk in range(NB):
                t0 = tok0 + blk * BW
                lsl = slice(blk * BW, (blk + 1) * BW)
                ksl = slice(K0 + blk * BW, K0 + (blk + 1) * BW)
                if ctx:
                    load_tables(blk * BW, BW)
                def ckv_gen(blk=blk, t0=t0, lsl=lsl, ksl=ksl):
                    yield from need(P, 3)
                    b = P.bank(); bk = "ps%d" % b
                    proj_fm(P, ps[b][:, 0:BW], w_ck, 0, 128, u, t0, BW, ["w_ck"] + ukeys, bk)
                    P.cp("act", ck[:], ps[b][:, 0:BW], r=[bk], w=["mck"])
                    P.act(sq[:], ps[b][:, 0:BW], AF.Square, r=[bk], w=["msq"])
                    P.release(b)
                    yield
                    b2 = P.bank(); bk2 = "ps%d" % b2
                    P.mm(ps[b2][:, 0:BW], onesf[:], sq[:], r=["onesf", "msq"], w=[bk2])
                    P.rsqrt(rstd[:], ps[b2][:, 0:BW], 1e-6, 1.0 / 128.0, t1[:], t2[:], r=[bk2], w=["mrstd"], kx="mt1", kt="mt2")
                    P.release(b2)
                    yield
                    P.tt("dve", ck[:], ck[:], rstd[:], ALU.mult, r=["mck", "mrstd"], w=["mck"])
                    P.ts1("dve", ck[:], ck[:], kvg[:, 0:1], ALU.mult, r=["mck", "kvg"], w=["mck"])
                    P.cp("act", ckvn[:, ksl], ck[:], r=["mck"], w=["ckvn"])
                    if not ctx:
                        b3 = P.bank(); bk3 = "ps%d" % b3
                        for ci in range(CB):
                            P.mm(ps[b3][:, ci * 128:(ci + 1) * 128], ck[:, ci * 128:(ci + 1) * 128], identf[:], r=["mck", "identf"], w=[bk3])
                        P.cp("act", otile[:], ps[b3][:, 0:BW].rearrange("p (c r) -> p c r", r=128), r=[bk3], w=["otile"])
                        P.release(b3)
                        P.dma("act", env["o_ckv"][s, l, blk * BW:(blk + 1) * BW, :].rearrange("(c p) r -> p c r", p=128), otile[:], r=["otile"])
                def kr_gen(blk=blk, t0=t0, lsl=lsl, ksl=ksl):
                    yield from need(P, 3)
                    b = P.bank(); bk = "ps%d" % b
                    proj_fm(P, ps[b][0:96, 0:BW], w_kr, 0, 96, u, t0, BW, ["w_kr"] + ukeys, bk)
                    if ctx:
                        b2 = P.bank(); bk2 = "ps%d" % b2
                        proj_fm(P, ps[b2][0:96, 0:BW], w_krs, 0, 96, u, t0, BW, ["w_krs"] + ukeys, bk2)
                        P.tt("dve", t1[64:96, :], ps[b][64:96, 0:BW], cosT[64:96, 0:BW], ALU.mult, r=[bk, "cosT"], w=["mt1"])
                        P.tt("dve", t2[64:96, :], ps[b2][64:96, 0:BW], sinT[64:96, 0:BW], ALU.mult, r=[bk2, "sinT"], w=["mt2"])
                        P.release(b); P.release(b2)
                        P.tt("pool", krall[64:96, ksl], t1[64:96, :], t2[64:96, :], ALU.add, r=["mt1", "mt2"], w=["krall"])
                    else:
                        P.cp("act", t1[64:96, :], ps[b][64:96, 0:BW], r=[bk], w=["mt1"])
                        P.release(b)
                        P.cp("pool", krall[64:96, ksl], t1[64:96, :], r=["mt1"], w=["krall"])
                        b3 = P.bank(); bk3 = "ps%d" % b3
                        for ci in range(CB):
                            P.mm(ps[b3][:, ci * 32:(ci + 1) * 32], t1[64:96, ci * 128:(ci + 1) * 128], identf[64:96, 64:96], r=["mt1", "identf"], w=[bk3])
                        P.cp("act", okr[:], ps[b3][:, 0:CB * 32].rearrange("p (c r) -> p c r", r=32), r=[bk3], w=["okr"])
                        P.release(b3)
                        P.dma("act", env["o_kr"][s, l, blk * BW:(blk + 1) * BW, :].rearrange("(c p) r -> p c r", p=128), okr[:], r=["okr"])
                def cq_gen(blk=blk, t0=t0, lsl=lsl, ksl=ksl):
                    yield from need(P, 3)
                    bq = [P.bank(), P.bank()]
                    for kc in range(2):
                        proj_fm(P, ps[bq[kc]][:, 0:BW], w_cq, kc * 128, 128, u, t0, BW, ["w_cq"] + ukeys, "ps%d" % bq[kc])
                    P.act(sqq[:], ps[bq[0]][:, 0:BW], AF.Square, r=["ps%d" % bq[0]], w=[kq_[0]])
                    P.act(sq1[:], ps[bq[1]][:, 0:BW], AF.Square, r=["ps%d" % bq[1]], w=["msq1"])
                    yield
                    b2 = P.bank(); bk2 = "ps%d" % b2
                    P.mm(ps[b2][:, 0:BW], onesf[:], sqq[:], start=True, stop=False, r=["onesf", kq_[0]], w=[bk2])
                    P.mm(ps[b2][:, 0:BW], onesf[:], sq1[:], start=False, stop=True, r=["onesf", "msq1"], w=[bk2])
                    P.rsqrt(rstdq[:], ps[b2][:, 0:BW], 1e-6, 1.0 / 256.0, t1[:], t2[:], r=[bk2], w=[kq_[1]], kx="mt1", kt="mt2")
                    P.release(b2)
                    yield
                    for kc in range(2):
                        P.tt("dve", t2q[:], ps[bq[kc]][:, 0:BW], rstdq[:], ALU.mult, r=["ps%d" % bq[kc], kq_[1]], w=[kq_[2]])
                        P.release(bq[kc])
                        P.act(cqn[:, kc, lsl], t2q[:], AF.Identity, scale=qng[:, kc:kc + 1], r=[kq_[2], "qng"], w=["cqn"])
                if g.nseq == 1:
                    yield from interleave(ckv_gen(), kr_gen(), cq_gen())
                else:
                    yield from ckv_gen()
                    yield
                    yield from kr_gen()
                    yield
                    yield from cq_gen()
            yield
            nkb = (Lk + 511) // 512
            for kb in range(nkb):
                k0 = kb * 512
                n = min(512, Lk - k0)
                for h in range(8):
                    yield from need(P, 1)
                    b = P.bank(); bk = "ps%d" % b
                    P.mm(ps[b][0:64, 0:n], wkk[:, h, :], ckvn[:, k0:k0 + n], r=["wkk", "ckvn"], w=[bk])
                    P.cp("act" if h % 2 == 0 else "dve", Kt[0:64, h, k0:k0 + n], ps[b][0:64, 0:n], r=[bk], w=["Kt"])
                    P.release(b)
            for h in range(8):
                P.cp("dve" if h % 2 == 0 else "act", Kt[64:96, h, :], krall[64:96, :], r=["krall"], w=["Kt"])
            wkv2 = wkv[:].rearrange("p h x -> p (h x)")
            yield
            for kt in range(KT):
                yield from need(P, 1)
                b = P.bank(); bk = "ps%d" % b
                P.mm(ps[b][:, 0:512], ckvn[:, kt * 128:(kt + 1) * 128], wkv2, r=["wkv", "ckvn"], w=[bk])
                P.cp("act" if kt % 2 == 0 else "dve", V[:, kt, :, 0:64], ps[b][:, 0:512].rearrange("p (h x) -> p h x", x=64), r=[bk], w=["Vv"])
                P.release(b)
            PRE = 3 if KT >= 6 else 2
            DEP = 2 if KT < 6 else 1
            NQ = DEP + 1
            items = [(qb, h) for qb in range(L // QB) for h in range(8)]
            prep = {}
            esets = [(accs, rden, tmpy, "mck", "msq", "msq1")]
            if DEP == 2:
                esets.append((rstd, t1, t2, "mrstd", "mt1", "mt2"))

            def prepare(it):
                qb, h = it
                qs = slice(qb * QB, (qb + 1) * QB)
                if ctx and h == 0:
                    load_tables(qb * QB, QB)
                if h % 2 == 0:
                    gi = (h // 2) % len(sgm)
                    sgt = sgm[gi]; sgk = "sgm%d" % gi
                    b = P.bank(); bk = "ps%d" % b
                    proj_fm(P, ps[b][:, 0:QB], w_gm, (h // 2) * 128, 128, u, tok0 + qb * QB, QB, ["w_gm"] + ukeys, bk)
                    P.act(sgt[:], ps[b][:, 0:QB], AF.Tanh, scale=0.5, r=[bk], w=[sgk])
                    P.stt("dve", sgt[:], sgt[:], 1.0, ps[b][:, 0:QB], ALU.add, ALU.mult, r=[sgk, bk], w=[sgk])
                    P.release(b)
                idx = items.index(it)
                qh = Qh[idx % NQ]; qk = "Qh%d" % (idx % NQ)
                bq_ = P.bank(); bqk = "ps%d" % bq_
                P.mm(ps[bq_][0:96, 0:QB], wuq[:, 0, h * 96:(h + 1) * 96], cqn[:, 0, qs], start=True, stop=False, r=["wuq", "cqn"], w=[bqk])
                P.mm(ps[bq_][0:96, 0:QB], wuq[:, 1, h * 96:(h + 1) * 96], cqn[:, 1, qs], start=False, stop=True, r=["wuq", "cqn"], w=[bqk])
                if ctx:
                    bs_ = P.bank(); bsk = "ps%d" % bs_
                    P.mm(ps[bs_][0:96, 0:QB], wuqs[:, 0, h * 96:(h + 1) * 96], cqn[:, 0, qs], start=True, stop=False, r=["wuqs", "cqn"], w=[bsk])
                    P.mm(ps[bs_][0:96, 0:QB], wuqs[:, 1, h * 96:(h + 1) * 96], cqn[:, 1, qs], start=False, stop=True, r=["wuqs", "cqn"], w=[bsk])
                    P.cp("act", qh[0:64, :], ps[bq_][0:64, 0:QB], r=[bqk], w=[qk])
                    P.tt("dve", t1[64:96, 0:QB], ps[bq_][64:96, 0:QB], cosT[64:96, 0:QB], ALU.mult, r=[bqk, "cosT"], w=["mt1"])
                    P.tt("dve", t2[64:96, 0:QB], ps[bs_][64:96, 0:QB], sinT[64:96, 0:QB], ALU.mult, r=[bsk, "sinT"], w=["mt2"])
                    P.release(bq_); P.release(bs_)
                    P.tt("pool", qh[64:96, :], t1[64:96, 0:QB], t2[64:96, 0:QB], ALU.add, r=["mt1", "mt2"], w=[qk])
                else:
                    P.cp("act", qh[0:96, :], ps[bq_][0:96, 0:QB], r=[bqk], w=[qk])
                    P.release(bq_)
                prep[it] = (qh, qk)

            def tail1(p):
                ac, rd, tm, ka, kr_, km = esets[p["idx"] % len(esets)]
                back = "ps%d" % p["bacc"]
                P.cp("act", ac[0:65, 0:QB], ps[p["bacc"]][0:65, 0:QB], r=[back], w=[ka])
                P.release(p["bacc"])
                P.op("dve", lambda E, ac=ac, rd=rd: E.reciprocal(rd[64:65, 0:QB], ac[64:65, 0:QB]), r=[ka], w=[kr_])

            def tail2(p):
                ac, rd, tm, ka, kr_, km = esets[p["idx"] % len(esets)]
                qb, h = p["it"]
                bb = P.bank(); bbk = "ps%d" % bb
                P.mm(ps[bb][0:64, 0:QB], onesf[64:65, 0:64], rd[64:65, 0:QB], r=["onesf", kr_], w=[bbk])
                pb = (h % 2) * 64; ch = 2 + h // 2
                P.tt("dve", tm[pb:pb + 64, 0:QB], ac[0:64, 0:QB], ps[bb][0:64, 0:QB], ALU.mult, r=[ka, bbk], w=[km])
                P.release(bb)
                gi = (h // 2) % len(sgm)
                sgt = sgm[gi]; sgk = "sgm%d" % gi
                P.stt("dve", y[pb:pb + 64, ch - 2, tok0 + qb * QB:tok0 + (qb + 1) * QB], tm[pb:pb + 64, 0:QB], 0.5, sgt[pb:pb + 64, :], ALU.mult, ALU.mult,
                      r=[km, sgk], w=["y:%d" % ch])

            yield
            yield from need(P, 3)
            for i in range(min(DEP, len(items))):
                prepare(items[i])
            pending = []
            K1 = min(2, KT - 1)
            GAP2 = 10 if KT >= 14 else KT

            def run_pending(slot):
                for p in list(pending):
                    if p["stage"] == 0 and slot >= p["s1"]:
                        tail1(p); p["stage"] = 1
                    if p["stage"] == 1 and slot >= p["s1"] + GAP2:
                        tail2(p); pending.remove(p)

            for idx, it in enumerate(items):
                yield
                yield from need(P, 6 if KT >= 6 else 5)
                qb, h = it
                qh, qk = prep.pop(it)
                sbank = {}

                def issue_S(kt, qh=qh, qk=qk, h=h):
                    bS = P.bank()
                    P.mm(ps[bS][:, 0:QB], Kt[0:96, h, kt * 128:(kt + 1) * 128], qh[0:96, :], r=["Kt", qk], w=["ps%d" % bS])
                    sbank[kt] = bS

                for kt in range(min(PRE, KT)):
                    issue_S(kt)
                if idx + DEP < len(items):
                    prepare(items[idx + DEP])
                bacc = P.bank(); back = "ps%d" % bacc
                for kt in range(KT):
                    if kt + PRE < KT:
                        issue_S(kt + PRE)
                    run_pending(idx * KT + kt)
                    bS = sbank.pop(kt)
                    pt = Pt[kt % len(Pt)]; pk = "Pt%d" % (kt % len(Pt))
                    P.act(pt[:], ps[bS][:, 0:QB], AF.Exp, scale=SM_SCALE, r=["ps%d" % bS], w=[pk])
                    P.release(bS)
                    P.mm(ps[bacc][0:65, 0:QB], V[:, kt, h, :], pt[:], start=(kt == 0), stop=(kt == KT - 1), r=["Vv", "Vones", pk], w=[back])
                pending.append(dict(it=it, idx=idx, bacc=bacc, stage=0, s1=(idx + 1) * KT + K1))
            run_pending(10 ** 9)
        if not threaded:
            dump("ymla_" + tag, y[:, :, :], [128, 4, g.T], ["y:2", "y:3", "y:4", "y:5"])
            P.barrier()


def interleave(*gens):
    gens = list(gens)
    while gens:
        alive = []
        for gn in gens:
            try:
                next(gn)
                alive.append(gn)
            except StopIteration:
                pass
        gens = alive
        yield


class _NoClose:
    def __init__(self, st):
        self.st = st

    def __enter__(self):
        return self.st

    def __exit__(self, *a):
        return False


def need(P, n):
    while len(P.free_banks) < n:
        yield


def interleave_w(pairs):
    pairs = list(pairs)
    while pairs:
        alive = []
        for gn, wgt in pairs:
            ok = True
            for _ in range(wgt):
                try:
                    next(gn)
                except StopIteration:
                    ok = False
                    break
            if ok:
                alive.append((gn, wgt))
        pairs = alive
        yield


def run_gen(gn):
    for _ in gn:
        pass


def phase_rwkv(env, g, l, u, ur, y, ukeys, urkeys, wslice, ppar, tag, threaded=False, ext=None):
    nc, P, sb, dump, W = _unpack(env)
    ps = P.ps
    identf, identb, jrevb, bones, onesf = env["identf"], env["identb"], env["jrevb"], env["bones"], env["onesf"]
    mask1, mask2, mask3 = env["mask1"], env["mask2"], env["mask3"]
    L, LT = g.L, g.LT
    BW, CB = 256, 2
    NB = L // BW
    nth = 2 if (g.nseq > 1 and not threaded) else 1
    NBP = 2 if NB > 1 else 1
    NTR = 2 if g.nseq == 1 else 1

    def rk(b, i):
        return "ps%d" % b

    def rks(b, n):
        return ["ps%d" % b]

    with (_NoClose(ext) if ext is not None else ExitStack()) as ph:
        w_rk = sb(ph, "w_rk", [128, 8, 512], BF16); P.dma("pool", w_rk[:], wslice(C_R, C_R + 512), w=["w_rk"])
        w_vg = sb(ph, "w_vg", [128, 8, 256], BF16); P.dma("pool", w_vg[:], wslice(C_V, C_V + 256), w=["w_vg"])
        w_g_own = sb(ph, "w_g_own", [128, 8, 256], BF16); P.dma("pool", w_g_own[:], wslice(C_G, C_G + 256), w=["w_g_own"])
        lo1 = [sb(ph, "lo1_%d" % e, [128, 8, 128], BF16) for e in range(2)]
        lo2 = [sb(ph, "lo2_%d" % e, [128, 256], BF16) for e in range(2)]
        for e in range(2):
            P.dma("pool", lo1[e][:, :, 0:64], W["rw_w1"][l, e].rearrange("(kc p) r -> p kc r", p=128), w=["lo1_%d" % e])
            P.dma("pool", lo1[e][:, :, 64:128], W["rw_a1"][l, e].rearrange("(kc p) r -> p kc r", p=128), w=["lo1_%d" % e])
            P.dma("pool", lo2[e][0:64, :], W["rw_w2"][l, e], w=["lo2_%d" % e])
            P.dma("pool", lo2[e][64:128, :], W["rw_a2"][l, e], w=["lo2_%d" % e])
        pw0 = [ppar(ph, "pw0_%d" % e, W["rw_w0"][l, e]) for e in range(2)]
        pa0 = [ppar(ph, "pa0_%d" % e, W["rw_a0"][l, e]) for e in range(2)]
        hpw0 = [sb(ph, "hpw0_%d" % e, [128, 2], F32) for e in range(2)]
        hpa0 = [sb(ph, "hpa0_%d" % e, [128, 2], F32) for e in range(2)]
        for e in range(2):
            P.ts1("dve", hpw0[e][:], pw0[e][:], 0.5, ALU.mult, r=["pw0_%d" % e], w=["hpw0_%d" % e])
            P.ts1("dve", hpa0[e][:], pa0[e][:], 0.5, ALU.mult, r=["pa0_%d" % e], w=["hpa0_%d" % e])
        pkk = ppar(ph, "pkk", W["rw_kk"][l]); pka = ppar(ph, "pka", W["rw_ka"][l]); prk = ppar(ph, "prk", W["rw_rk"][l])
        plg = ppar(ph, "plg", W["rw_lnx_g"][l]); plb = ppar(ph, "plb", W["rw_lnx_b"][l])

        class BufSet:
            pass

        ND = env["dbg"].get("ndummy", 0)
        dbank = P.bank() if ND else None

        def dummies(n):
            for _ in range(n * ND):
                P.pe_cont = True
                P.op("pe", lambda E: E.matmul(ps[dbank][:, :], identb[:], mask1[:], start=False, stop=False), (), ())
                P.pe_cont = False

        def alloc_set(tid):
            B = BufSet()
            B.tid = tid
            B.k = lambda name: "%s@%d" % (name, tid)
            n = lambda name: "%s_t%d" % (name, tid)
            B.QR = [[sb(ph, n("QR%d%d" % (bp, e)), [128, 2, CB, 2, 128], BF16) for e in range(2)] for bp in range(NBP)]
            B.AK = [[sb(ph, n("AK%d%d" % (bp, e)), [128, 2, CB, 2, 128], BF16) for e in range(2)] for bp in range(NBP)]
            B.AKgT = [[sb(ph, n("AKgT%d%d" % (bp, e)), [128, CB, 2, 2, 128], BF16) for e in range(2)] for bp in range(NBP)]
            B.Vt = [[sb(ph, n("Vt%d%d" % (bp, e)), [128, CB, 256], BF16) for e in range(2)] for bp in range(NBP)]
            B.GE = [sb(ph, n("GE%d" % bp), [128, 2, 2, CB], F32) for bp in range(NBP)]
            B.yT = [sb(ph, n("yT%d" % e), [128, LT, 256], BF16) for e in range(2)]
            B.tr = []
            for ti in range(NTR):
                T = BufSet()
                T.sfx = "#%d" % ti
                T.tl = sb(ph, n("tl%d" % ti), [128, BW], BF16)
                for nm in ("rT", "kT", "kkT", "sgw", "aT", "cs", "Em", "tq"):
                    setattr(T, nm, sb(ph, n("%s%d" % (nm, ti)), [128, BW], F32))
                T.Ebuf = sb(ph, n("Ebuf%d" % ti), [128, CB, 129], F32)
                T.Ag = sb(ph, n("Ag%d" % ti), [128, BW], BF16); T.Kg = sb(ph, n("Kg%d" % ti), [128, BW], BF16)
                P.memset("dve", T.Ebuf[:, :, 0:1], 1.0, w=[B.k("Ebuf0" + T.sfx)])
                B.tr.append(T)
            B.MA = [[sb(ph, n("MA%d%d" % (par, e)), [128, 4, 2, 128], BF16) for e in range(2)] for par in range(2)]
            B.LK = [[sb(ph, n("LK%d%d" % (par, e)), [128, 4, 2, 128], BF16) for e in range(2)] for par in range(2)]
            B.Pm = [[sb(ph, n("Pm%d%d" % (par, e)), [128, 4, 128], BF16) for e in range(2)] for par in range(2)]
            B.Nn = [sb(ph, n("Nn%d" % e), [128, 4, 128], BF16) for e in range(2)]
            B.M2 = [[sb(ph, n("M2_%d%d" % (e, i)), [128, 4, 128], BF16) for i in range(2)] for e in range(2)]
            B.N2 = [[sb(ph, n("N2_%d%d" % (e, i)), [128, 4, 128], BF16) for i in range(2)] for e in range(2)]
            B.W1 = sb(ph, n("W1"), [128, 512], BF16); B.Uu = sb(ph, n("Uu"), [128, 512], BF16)
            B.H32 = sb(ph, n("H32"), [128, 4, 64], F32); B.Hbf = sb(ph, n("Hbf"), [128, 4, 64], BF16)
            if not g.ctx:
                B.SS = sb(ph, n("SS"), [128, 512], F32)
                B.ssk = B.k("SS")
                B.So = B.SS[0:64, 0:512].rearrange("p (a h k) -> p a h k", a=4, h=2)
            return B

        def v3(t):
            return t[:].rearrange("p (c t) -> p c t", t=128)

        def derive_gen(B, s, blk, e):
            T = B.tr[e % NTR]
            bp = blk % NBP
            k = lambda name: B.k(name + T.sfx)
            kb = lambda name: B.k("%s.%d" % (name, bp))
            tok0 = s * L
            src, skeys = (u, ukeys) if e == 0 else (ur, urkeys)
            t0 = tok0 + blk * BW
            ek = "%d" % e
            rT, kT, kkT, sgw, aT, cs, Em, tq, tl, Ebuf = T.rT, T.kT, T.kkT, T.sgw, T.aT, T.cs, T.Em, T.tq, T.tl, T.Ebuf
            QRe, AKe, AKgTe, Vte, GEb = B.QR[bp][e], B.AK[bp][e], B.AKgT[bp][e], B.Vt[bp][e], B.GE[bp]
            Eend, kka, kdir = sgw, tq, cs
            cs3, Em3, Eend3 = v3(cs), v3(Em), v3(Eend)
            for half in range((CB + 1) // 2):
                yield from need(P, 2)
                b = P.bank(); bk = "ps%d" % b
                nci = min(2, CB - half * 2)
                for q in range(nci):
                    ci = half * 2 + q
                    for kc in range(8):
                        P.mm(ps[b][:, q * 256:(q + 1) * 256], src[:, kc, t0 + ci * 128:t0 + (ci + 1) * 128], w_vg[:, kc, 0:256],
                             start=(kc == 0), stop=(kc == 7), r=["w_vg"] + skeys, w=[rk(b, q)])
                P.cp("act", Vte[:, half * 2:half * 2 + nci, :], ps[b][:, 0:nci * 256].rearrange("p (c v) -> p c v", v=256),
                     r=rks(b, nci), w=[kb("Vt" + ek)])
                P.release(b)
            b = P.bank(); bk = "ps%d" % b
            proj_fm(P, ps[b][:, 0:BW], lo1[e], 0, 128, src, t0, BW, ["lo1_" + ek] + skeys, bk)
            P.act(tl[0:64, :], ps[b][0:64, 0:BW], AF.Tanh, r=[bk], w=[k("tl")])
            P.cp("dve", tl[64:128, :], ps[b][64:128, 0:BW], r=[bk], w=[k("tl")])
            P.release(b)
            yield
            for fc in range(2):
                fs = slice(fc * 128, (fc + 1) * 128)
                yield from need(P, 4)
                bw_ = P.bank(); ba_ = P.bank()
                P.mm(ps[bw_][:, 0:BW], lo2[e][0:64, fs], tl[0:64, :], r=["lo2_" + ek, k("tl")], w=["ps%d" % bw_])
                P.mm(ps[ba_][:, 0:BW], lo2[e][64:128, fs], tl[64:128, :], r=["lo2_" + ek, k("tl")], w=["ps%d" % ba_])
                br_ = P.bank(); bk_ = P.bank()
                proj_fm(P, ps[br_][:, 0:BW], w_rk, fc * 128, 128, src, t0, BW, ["w_rk"] + skeys, "ps%d" % br_)
                proj_fm(P, ps[bk_][:, 0:BW], w_rk, 256 + fc * 128, 128, src, t0, BW, ["w_rk"] + skeys, "ps%d" % bk_)
                P.act(sgw[:], ps[bw_][:, 0:BW], AF.Tanh, bias=hpw0[e][:, fc:fc + 1], scale=0.5, r=["ps%d" % bw_, "hpw0_" + ek], w=[k("sgw")])
                P.act(aT[:], ps[ba_][:, 0:BW], AF.Tanh, bias=hpa0[e][:, fc:fc + 1], scale=0.5, r=["ps%d" % ba_, "hpa0_" + ek], w=[k("aT")])
                P.release(bw_); P.release(ba_)
                P.ts("pool", sgw[:], sgw[:], 0.5, 0.5, ALU.mult, ALU.add, r=[k("sgw")], w=[k("sgw")])
                P.ts("pool", aT[:], aT[:], 0.5, 0.5, ALU.mult, ALU.add, r=[k("aT")], w=[k("aT")])
                P.cp("act", rT[:], ps[br_][:, 0:BW], r=["ps%d" % br_], w=[k("rT")])
                P.cp("dve", kT[:], ps[bk_][:, 0:BW], r=["ps%d" % bk_], w=[k("kT")])
                P.release(br_); P.release(bk_)
                yield
                P.ts1("dve", kkT[:], kT[:], pkk[:, fc:fc + 1], ALU.mult, r=[k("kT"), "pkk"], w=[k("kkT")])
                P.tt("pool", tq[:], kkT[:], kkT[:], ALU.mult, r=[k("kkT")], w=[k("tq")])
                bs_ = P.bank()
                P.mm(ps[bs_][:, 0:BW], bones[:], tq[:], r=["bones", k("tq")], w=["ps%d" % bs_])
                P.rsqrt(tq[:], ps[bs_][:, 0:BW], 1e-24, 1.0, Em[:], cs[:], r=["ps%d" % bs_], w=[k("tq")], kx=k("Em"), kt=k("cs"))
                P.release(bs_)
                for ci in range(CB):
                    P.op("dve", lambda E, ci=ci: E.tensor_tensor_scan(cs[:, ci * 128:(ci + 1) * 128], onesf[:, 0:128],
                                                                      sgw[:, ci * 128:(ci + 1) * 128], 0.0, ALU.mult, ALU.add),
                         r=["onesf", k("sgw")], w=[k("cs")])
                yield
                P.tt("dve", kkT[:], kkT[:], tq[:], ALU.mult, r=[k("kkT"), k("tq")], w=[k("kkT")])
                P.act(Ebuf[:, :, 1:129], cs3, AF.Exp, scale=-WDEC, r=[k("cs")], w=[k("Ebuf")])
                P.act(Em[:], cs[:], AF.Exp, scale=WDEC, r=[k("cs")], w=[k("Em")])
                P.cp("dve", GEb[:, e, fc, :], Ebuf[:, :, 128], r=[k("Ebuf")], w=[kb("GE")])
                P.tt("dve", Eend3, Em3, GEb[:, e, fc, :].unsqueeze(2).to_broadcast([128, CB, 128]), ALU.mult, r=[k("Em"), kb("GE")], w=[k("sgw")])
                P.tt("pool", kka[:], kkT[:], aT[:], ALU.mult, r=[k("kkT"), k("aT")], w=[k("tq")])
                P.ts("dve", aT[:], aT[:], 1.0, pka[:, fc:fc + 1], ALU.subtract, ALU.mult, r=[k("aT"), "pka"], w=[k("aT")])
                P.stt("dve", kdir[:], aT[:], 1.0, kT[:], ALU.add, ALU.mult, r=[k("aT"), k("kT")], w=[k("cs")])
                yield
                P.tt("dve", QRe[:, fc, :, 0, :], v3(kkT), Ebuf[:, :, 0:128], ALU.mult, r=[k("kkT"), k("Ebuf"), k("Ebuf0")], w=[kb("QR" + ek)])
                P.tt("dve", QRe[:, fc, :, 1, :], v3(rT), Ebuf[:, :, 1:129], ALU.mult, r=[k("rT"), k("Ebuf")], w=[kb("QR" + ek)])
                P.tt("pool", AKe[:, fc, :, 0, :], v3(kka), Em3, ALU.mult, r=[k("tq"), k("Em")], w=[kb("AK" + ek)])
                P.tt("pool", AKe[:, fc, :, 1, :], v3(kdir), Em3, ALU.mult, r=[k("cs"), k("Em")], w=[kb("AK" + ek)])
                P.tt("dve", T.Ag[:], kka[:], Eend[:], ALU.mult, r=[k("tq"), k("sgw")], w=[k("Ag")])
                P.tt("pool", T.Kg[:], kdir[:], Eend[:], ALU.mult, r=[k("cs"), k("sgw")], w=[k("Kg")])
                yield
                for ci in range(CB):
                    yield from need(P, 1)
                    b = P.bank()
                    P.mm(ps[b][:, 0:128], T.Ag[:, ci * 128:(ci + 1) * 128], identb[:], r=[k("Ag"), "identb"], w=[rk(b, 0)])
                    P.mm(ps[b][:, 128:256], T.Kg[:, ci * 128:(ci + 1) * 128], identb[:], r=[k("Kg"), "identb"], w=[rk(b, 1)])
                    P.cp("act", AKgTe[:, ci, :, fc, :], ps[b][:, 0:256].rearrange("p (a x) -> p a x", x=128), r=rks(b, 2), w=[kb("AKgT" + ek)])
                    P.release(b)
                yield

        def gn_gen(B, blk, ci, par):
            bp = blk % NBP
            k0 = B.k
            k = lambda name: (B.k("%s.%d" % (name, bp)) if name[:2] in ("QR", "AK", "Vt", "GE") else B.k(name))
            QR, AK, MA, LK, Pm, Nn, M2, N2 = B.QR[bp], B.AK[bp], B.MA[par], B.LK[par], B.Pm[par], B.Nn, B.M2, B.N2
            pk = "%d" % par
            for e in range(2):
                ek = "%d" % e
                for fc in range(2):
                    yield from need(P, 3)
                    b1 = P.bank(); b2 = P.bank()
                    for hh in range(2):
                        pb = hh * 64
                        qr2 = QR[e][pb:pb + 64, fc, ci, :, :].rearrange("p a t -> p (a t)")
                        P.mm(ps[b1][:, hh * 256:(hh + 1) * 256], AK[e][pb:pb + 64, fc, ci, 0, :], qr2, r=[k("AK" + ek), k("QR" + ek)], w=[rk(b1, hh)])
                        P.mm(ps[b2][:, hh * 256:(hh + 1) * 256], AK[e][pb:pb + 64, fc, ci, 1, :], qr2, r=[k("AK" + ek), k("QR" + ek)], w=[rk(b2, hh)])
                    P.tt("dve", MA[e][:, fc * 2:fc * 2 + 2, :, :].rearrange("p h a t -> p (h a t)"), ps[b1][:, :], mask1[:], ALU.mult,
                         r=rks(b1, 2) + ["mask1"], w=[k("MA" + pk + ek)])
                    P.tt("dve", LK[e][:, fc * 2:fc * 2 + 2, :, :].rearrange("p h a t -> p (h a t)"), ps[b2][:, :], mask2[:], ALU.mult,
                         r=rks(b2, 2) + ["mask2"], w=[k("LK" + pk + ek)])
                    P.release(b1); P.release(b2)
                b3 = P.bank()
                for h in range(4):
                    fc, pb = h // 2, (h % 2) * 64
                    P.mm(ps[b3][:, h * 128:(h + 1) * 128], QR[e][pb:pb + 64, fc, ci, 0, :], AK[e][pb:pb + 64, fc, ci, 0, :],
                         r=[k("AK" + ek), k("QR" + ek)], w=[rk(b3, h)])
                P.tt("dve", Nn[e][:].rearrange("p h t -> p (h t)"), ps[b3][:, :], mask3[:], ALU.mult, r=rks(b3, 4) + ["mask3"], w=[k("Nn" + ek)])
                P.release(b3)
                yield
            Mc, Nc, Mk, Nk = {}, {}, {}, {}
            for e in range(2):
                ek = "%d" % e
                P.tt("pool", Pm[e][:], MA[e][:, :, 0, :], identb[:].unsqueeze(1).to_broadcast([128, 4, 128]), ALU.add,
                     r=[k("MA" + pk + ek), "identb"], w=[k("Pm" + pk + ek)])
                Mc[e] = [MA[e][:, h, 0, :] for h in range(4)]; Mk[e] = k("MA" + pk + ek)
                Nc[e] = [Nn[e][:, h, :] for h in range(4)]; Nk[e] = k("Nn" + ek)
            for lev in range(6):
                i2 = lev % 2
                bN, bM, bP = {}, {}, {}
                yield from need(P, 4)
                for e in range(2):
                    bN[e] = P.bank()
                    if lev < 5:
                        bM[e] = P.bank()
                for h in range(4):
                    for e in range(2):
                        P.mm(ps[bN[e]][:, h * 128:(h + 1) * 128], Mc[e][h], Nc[e][h], r=[Mk[e], Nk[e]], w=[rk(bN[e], h)])
                        if lev < 5:
                            P.mm(ps[bM[e]][:, h * 128:(h + 1) * 128], Nc[e][h], Mc[e][h], r=[Mk[e], Nk[e]], w=[rk(bM[e], h)])
                dummies(1)
                for e in range(2):
                    n2k = k("N2_%d%d" % (e, i2))
                    P.cp("act", N2[e][i2][:].rearrange("p h t -> p (h t)"), ps[bN[e]][:, :], r=rks(bN[e], 4), w=[n2k])
                    P.release(bN[e])
                if lev < 5:
                    for e in range(2):
                        m2k = k("M2_%d%d" % (e, i2))
                        P.cp("dve" if e == 0 else "act", M2[e][i2][:].rearrange("p h t -> p (h t)"), ps[bM[e]][:, :], r=rks(bM[e], 4), w=[m2k])
                        P.release(bM[e])
                yield
                yield from need(P, 2)
                for e in range(2):
                    bP[e] = P.bank()
                for h in range(4):
                    for e in range(2):
                        P.mm(ps[bP[e]][:, h * 128:(h + 1) * 128], N2[e][i2][:, h, :], Pm[e][:, h, :],
                             r=[k("N2_%d%d" % (e, i2)), k("Pm" + pk + "%d" % e)], w=[rk(bP[e], h)])
                for e in range(2):
                    P.tt("dve", Pm[e][:].rearrange("p h t -> p (h t)"), Pm[e][:].rearrange("p h t -> p (h t)"), ps[bP[e]][:, :], ALU.add,
                         r=[k("Pm" + pk + "%d" % e)] + rks(bP[e], 4), w=[k("Pm" + pk + "%d" % e)])
                    P.release(bP[e])
                for e in range(2):
                    Mc[e] = [M2[e][i2][:, h, :] for h in range(4)]; Mk[e] = k("M2_%d%d" % (e, i2))
                    Nc[e] = [N2[e][i2][:, h, :] for h in range(4)]; Nk[e] = k("N2_%d%d" % (e, i2))
                yield

        def rec_gen(B, blk, ci, par, tile_i):
            bp = blk % NBP
            k = lambda name: (B.k("%s.%d" % (name, bp)) if name[:2] in ("QR", "AK", "Vt", "GE") else B.k(name))
            QR, AKgT, Vt, MA, LK, Pm = B.QR[bp], B.AKgT[bp], B.Vt[bp], B.MA[par], B.LK[par], B.Pm[par]
            W1, Uu, H32, Hbf, GE = B.W1, B.Uu, B.H32, B.Hbf, B.GE[bp]
            pk = "%d" % par
            chains = [(e, h) for h in range(4) for e in range(2)]
            yield from need(P, 1)
            bW = P.bank()
            for (e, h) in chains:
                ek = "%d" % e
                fc, pb = h // 2, (h % 2) * 64
                c = e * 4 + h
                cb = c * 64
                P.mm(ps[bW][:, cb:cb + 64], QR[e][pb:pb + 64, fc, ci, 0, :], Hbf[pb:pb + 64, e * 2 + fc, :], start=True, stop=False,
                     r=[k("QR" + ek), k("Hbf")], w=[rk(bW, c)])
                P.mm(ps[bW][:, cb:cb + 64], LK[e][:, h, 0, :], Vt[e][:, ci, h * 64:(h + 1) * 64], start=False, stop=True,
                     r=[k("LK" + pk + ek), k("Vt" + ek)], w=[rk(bW, c)])
            P.cp("act", W1[:], ps[bW][:, :], r=rks(bW, 8), w=[k("W1")])
            P.release(bW)
            yield
            yield from need(P, 1)
            bU = P.bank()
            for (e, h) in chains:
                c = e * 4 + h
                cb = c * 64
                P.mm(ps[bU][:, cb:cb + 64], Pm[e][:, h, :], W1[:, cb:cb + 64], r=[k("Pm" + pk + "%d" % e), k("W1")], w=[rk(bU, c)])
            P.ts1("dve", Uu[:], ps[bU][:, :], -1.0, ALU.mult, r=rks(bU, 8), w=[k("Uu")])
            P.release(bU)
            yield
            yield from need(P, 2)
            bY = P.bank(); bH = P.bank()
            for fc in range(2):
                for e in range(2):
                    ek = "%d" % e
                    c0 = (e * 4 + fc * 2) * 64
                    P.mm(ps[bH][:, c0:c0 + 128], AKgT[e][:, ci, 0, fc, :], Uu[:, c0:c0 + 128], start=True, stop=False,
                         r=[k("AKgT" + ek), k("Uu")], w=[rk(bH, 0)])
                    P.mm(ps[bH][:, c0:c0 + 128], AKgT[e][:, ci, 1, fc, :], Vt[e][:, ci, fc * 128:(fc + 1) * 128], start=False, stop=True,
                         r=[k("AKgT" + ek), k("Vt" + ek)], w=[rk(bH, 0)])
            for (e, h) in chains:
                ek = "%d" % e
                fc, pb = h // 2, (h % 2) * 64
                c = e * 4 + h
                cb = c * 64
                P.mm(ps[bY][:, cb:cb + 64], QR[e][pb:pb + 64, fc, ci, 1, :], Hbf[pb:pb + 64, e * 2 + fc, :], start=True, stop=False,
                     r=[k("QR" + ek), k("Hbf")], w=[rk(bY, c)])
                P.mm(ps[bY][:, cb:cb + 64], MA[e][:, h, 1, :], Uu[:, cb:cb + 64], start=False, stop=False, r=[k("MA" + pk + ek), k("Uu")], w=[rk(bY, c)])
                P.mm(ps[bY][:, cb:cb + 64], LK[e][:, h, 1, :], Vt[e][:, ci, h * 64:(h + 1) * 64], start=False, stop=True,
                     r=[k("LK" + pk + ek), k("Vt" + ek)], w=[rk(bY, c)])
            yield
            bH4 = ps[bH][:, :].rearrange("p (a hh v) -> p a hh v", hh=2, v=64)
            for hh in range(2):
                pb = hh * 64
                gb = GE[pb:pb + 64, :, :, ci].rearrange("p e f -> p (e f)").unsqueeze(2).to_broadcast([64, 4, 64])
                P.tt("dve", H32[pb:pb + 64, :, :], H32[pb:pb + 64, :, :], gb, ALU.mult, r=[k("H32"), k("GE")], w=[k("H32")])
                P.tt("dve", H32[pb:pb + 64, :, :], H32[pb:pb + 64, :, :], bH4[pb:pb + 64, :, hh, :], ALU.add, r=[k("H32")] + rks(bH, 8), w=[k("H32")])
            P.release(bH)
            P.cp("act", Hbf[:], H32[:], r=[k("H32")], w=[k("Hbf")])
            for e in range(2):
                P.cp("act" if e == 0 else "pool", B.yT[e][:, tile_i, :], ps[bY][:, e * 256:(e + 1) * 256], r=rks(bY, 8), w=[k("yT%d" % e)]) \
                    if e == 0 else P.cp("act", B.yT[e][:, tile_i, :], ps[bY][:, e * 256:(e + 1) * 256], r=rks(bY, 8), w=[k("yT%d" % e)])
            P.release(bY)
            yield

        def epi_gen(B, s, fcs, ti):
            tok0 = s * L
            T = B.tr[ti]
            k = lambda name: B.k(name + T.sfx) if name in ("rT", "kT", "kkT", "sgw", "aT") else B.k(name)
            rT, kT, kkT, sgw, aT = T.rT, T.kT, T.kkT, T.sgw, T.aT
            yT = B.yT
            w_g, wgk = w_g_own[:], "w_g_own"
            for blk in range(NB):
                t0 = tok0 + blk * BW
                for fc in fcs:
                    fs = slice(fc * 128, (fc + 1) * 128)
                    yield from need(P, 3)
                    bC = P.bank()
                    for ci in range(CB):
                        t = blk * CB + ci
                        tr = LT - 1 - t
                        P.mm(ps[bC][:, ci * 128:(ci + 1) * 128], yT[0][:, t, fs], identb[:], start=True, stop=False, r=[k("yT0"), "identb"], w=[rk(bC, ci)])
                        P.mm(ps[bC][:, ci * 128:(ci + 1) * 128], yT[1][:, tr, fs], jrevb[:], start=False, stop=True, r=[k("yT1"), "jrevb"], w=[rk(bC, ci)])
                    P.cp("act", rT[:], ps[bC][:, 0:BW], r=rks(bC, CB), w=[k("rT")])
                    P.act(kT[:], ps[bC][:, 0:BW], AF.Square, r=rks(bC, CB), w=[k("kT")])
                    P.release(bC)
                    yield
                    yield from need(P, 3)
                    bm_ = P.bank(); bq_ = P.bank()
                    P.mm(ps[bm_][:, 0:BW], bones[:], rT[:], r=["bones", k("rT")], w=["ps%d" % bm_])
                    P.mm(ps[bq_][:, 0:BW], bones[:], kT[:], r=["bones", k("kT")], w=["ps%d" % bq_])
                    P.ts1("dve", kkT[:], ps[bm_][:, 0:BW], 1.0 / 64.0, ALU.mult, r=["ps%d" % bm_], w=[k("kkT")])
                    P.release(bm_)
                    P.tt("pool", sgw[:], kkT[:], kkT[:], ALU.mult, r=[k("kkT")], w=[k("sgw")])
                    P.stt("dve", aT[:], ps[bq_][:, 0:BW], 1.0 / 64.0, sgw[:], ALU.mult, ALU.subtract, r=["ps%d" % bq_, k("sgw")], w=[k("aT")])
                    P.release(bq_)
                    P.ts1("dve", aT[:], aT[:], 0.0, ALU.max, r=[k("aT")], w=[k("aT")])
                    P.rsqrt(aT[:], aT[:], GN_EPS, 1.0, T.Em[:], T.cs[:], r=[k("aT")], w=[k("aT")], kx=B.k("Em" + T.sfx), kt=B.k("cs" + T.sfx))
                    yield
                    P.tt("dve", rT[:], rT[:], kkT[:], ALU.subtract, r=[k("rT"), k("kkT")], w=[k("rT")])
                    P.tt("dve", rT[:], rT[:], aT[:], ALU.mult, r=[k("rT"), k("aT")], w=[k("rT")])
                    P.ts("dve", rT[:], rT[:], plg[:, fc:fc + 1], plb[:, fc:fc + 1], ALU.mult, ALU.add, r=[k("rT"), "plg", "plb"], w=[k("rT")])
                    yield from need(P, 3)
                    br_ = P.bank(); bk_ = P.bank()
                    proj_fm(P, ps[br_][:, 0:BW], w_rk, fc * 128, 128, u, t0, BW, ["w_rk"] + ukeys, "ps%d" % br_)
                    proj_fm(P, ps[bk_][:, 0:BW], w_rk, 256 + fc * 128, 128, u, t0, BW, ["w_rk"] + ukeys, "ps%d" % bk_)
                    P.cp("act", kT[:], ps[bk_][:, 0:BW], r=["ps%d" % bk_], w=[k("kT")])
                    P.release(bk_)
                    P.stt("dve", kT[:], ps[br_][:, 0:BW], prk[:, fc:fc + 1], kT[:], ALU.mult, ALU.mult, r=["ps%d" % br_, "prk", k("kT")], w=[k("kT")])
                    P.release(br_)
                    yield
                    yield from need(P, 3)
                    bv_ = P.bank(); bg_ = P.bank(); bb_ = P.bank()
                    proj_fm(P, ps[bv_][:, 0:BW], w_vg, fc * 128, 128, u, t0, BW, ["w_vg"] + ukeys, "ps%d" % bv_)
                    proj_fm(P, ps[bg_][:, 0:BW], w_g, fc * 128, 128, u, t0, BW, [wgk] + ukeys, "ps%d" % bg_)
                    P.mm(ps[bb_][:, 0:BW], bones[:], kT[:], r=["bones", k("kT")], w=["ps%d" % bb_])
                    P.cp("act", kkT[:], ps[bv_][:, 0:BW], r=["ps%d" % bv_], w=[k("kkT")])
                    P.release(bv_)
                    P.act(sgw[:], ps[bg_][:, 0:BW], AF.Tanh, scale=0.5, r=["ps%d" % bg_], w=[k("sgw")])
                    P.stt("dve", sgw[:], sgw[:], 1.0, ps[bg_][:, 0:BW], ALU.add, ALU.mult, r=[k("sgw"), "ps%d" % bg_], w=[k("sgw")])
                    P.release(bg_)
                    P.tt("dve", kkT[:], kkT[:], ps[bb_][:, 0:BW], ALU.mult, r=[k("kkT"), "ps%d" % bb_], w=[k("kkT")])
                    P.release(bb_)
                    P.tt("pool", rT[:], rT[:], kkT[:], ALU.add, r=[k("rT"), k("kkT")], w=[k("rT")])
                    P.stt("dve", y[:, fc, t0:t0 + BW], rT[:], 0.5, sgw[:], ALU.mult, ALU.mult, r=[k("rT"), k("sgw")], w=["y:%d" % fc])
                    yield

        def unit_gen(B, s):
            k = B.k
            H32, Hbf = B.H32, B.Hbf
            if g.ctx:
                S0 = [B.tr[e % NTR].rT[0:64, 0:256].rearrange("p (f h k) -> p f h k", f=2, h=2) for e in range(2)]
                s0k = [B.k("rT" + B.tr[e % NTR].sfx) for e in range(2)]
                yield from need(P, 1)
                b = P.bank()
                for e in range(2):
                    for fc in range(2):
                        P.dma("sp", S0[e][:, fc, :, :], env["st_rwkv"][l, e, 2 * fc:2 * fc + 2].rearrange("h v k -> v h k"), w=[s0k[e]])
                    for fc in range(2):
                        a = e * 2 + fc
                        P.mm(ps[b][:, a * 64:(a + 1) * 64], S0[e][:, fc, :, :].rearrange("v h k -> v (h k)"), identf[0:64, 0:64],
                             r=[s0k[e], "identf"], w=[rk(b, a)])
                P.cp("dve", H32[:], ps[b][:, 0:256].rearrange("p (a v) -> p a v", v=64), r=rks(b, 4), w=[k("H32")])
                P.release(b)
            else:
                P.memset("dve", H32[:], 0.0, w=[k("H32")])
            P.cp("act", Hbf[:], H32[:], r=[k("H32")], w=[k("Hbf")])
            yield
            gci = 0
            if NTR == 2:
                yield from interleave(derive_gen(B, s, 0, 0), derive_gen(B, s, 0, 1))
            else:
                for e in range(2):
                    yield from derive_gen(B, s, 0, e)
            yield from gn_gen(B, 0, 0, gci % 2)
            for blk in range(NB):
                for ci in range(CB):
                    gens = [rec_gen(B, blk, ci, gci % 2, blk * CB + ci)]
                    if ci + 1 < CB:
                        gens.append(gn_gen(B, blk, ci + 1, (gci + 1) % 2))
                        if blk + 1 < NB:
                            gens.append(derive_gen(B, s, blk + 1, 0))
                            gens.append(derive_gen(B, s, blk + 1, 1))
                    elif blk + 1 < NB:
                        gens.append(gn_gen(B, blk + 1, 0, (gci + 1) % 2))
                    yield from interleave(*gens)
                    gci += 1
            if not g.ctx:
                yield from need(P, 1)
                b = P.bank()
                for a in range(4):
                    P.mm(ps[b][0:64, a * 128:(a + 1) * 128], H32[:, a, :], identf[:], r=[k("H32"), "identf"], w=[rk(b, a)])
                P.cp("act", B.SS[0:64, 0:512], ps[b][0:64, :], r=rks(b, 4), w=[B.ssk])
                P.release(b)
                for e in range(2):
                    for fc in range(2):
                        P.dma("act", env["o_rwkv"][s, l, e, 2 * fc:2 * fc + 2].rearrange("h v k -> v h k"), B.So[:, e * 2 + fc, :, :], r=[B.ssk])
            yield
            if NTR == 2:
                yield from interleave(epi_gen(B, s, (0,), 0), epi_gen(B, s, (1,), 1))
            else:
                yield from epi_gen(B, s, (0, 1), 0)

        sets = [alloc_set(i) for i in range(nth)]
        for s0 in range(0, g.nseq, nth):
            yield from interleave(*[unit_gen(sets[i], s0 + i) for i in range(nth) if s0 + i < g.nseq])
        if dbank is not None:
            P.release(dbank)
        if not threaded:
            dump("yrw_" + tag, y[:, :, :], [128, 2, g.T], ["y:0", "y:1"])
            P.barrier()


def host_consts():
    ident = np.eye(128, dtype=np.float32)
    jrev = np.ascontiguousarray(ident[::-1])
    jj, ii = np.meshgrid(np.arange(128), np.arange(128), indexing="ij")
    mU_s = (jj < ii).astype(np.float32)
    mU_i = (jj <= ii).astype(np.float32)
    mL_s = (ii < jj).astype(np.float32)
    mask1 = np.tile(np.concatenate([-mU_s, mU_i], 1), (1, 2))
    mask2 = np.tile(np.concatenate([mU_s, mU_i], 1), (1, 2))
    mask3 = np.tile(-mL_s, (1, 4))
    bones = np.zeros((128, 128), np.float32)
    bones[:64, :64] = 1.0
    bones[64:, 64:] = 1.0
    Lq = 2048
    row = np.repeat(np.arange(Lq // 64), 64).astype(np.float32)
    col = np.tile(np.arange(64), Lq // 64).astype(np.float32)
    inv = (np.float32(10000.0) ** (-np.arange(8, dtype=np.float32) / np.float32(8))).astype(np.float32)
    ang = np.concatenate([row[:, None] * inv, col[:, None] * inv], -1).astype(np.float32)
    cosT = np.repeat(np.cos(ang).astype(np.float32).T, 2, axis=0)
    sinT = np.repeat(np.sin(ang).astype(np.float32).T, 2, axis=0)
    return dict(k_ident=ident, k_jrev=jrev, k_mask1=np.ascontiguousarray(mask1), k_mask2=np.ascontiguousarray(mask2),
                k_mask3=np.ascontiguousarray(mask3), k_bones=bones, k_cos=np.ascontiguousarray(cosT),
                k_sin=np.ascontiguousarray(sinT))


WEIGHT_NAMES = ["w_mod", "b_mod", "w_in", "rw_w0", "rw_w1", "rw_w2", "rw_a0", "rw_a1", "rw_a2", "rw_kk", "rw_ka", "rw_rk",
                "rw_lnx_g", "rw_lnx_b", "mla_qnorm", "mla_wuq", "mla_kvnorm", "mla_wukv", "lru_conv_w", "lru_conv_b",
                "lru_wa", "lru_ba", "lru_wx", "lru_bx", "lru_lambda", "w_out", "ln_g", "ln_b"]


def make_in_maps(inputs, cores):
    f = lambda a: np.ascontiguousarray(np.asarray(a, dtype=np.float32))
    shared = {k: f(inputs[k]) for k in WEIGHT_NAMES}
    shared["rw_rk"] = shared["rw_rk"].reshape(2, 256)
    shared.update(host_consts())
    maps = []
    for i in cores:
        m = dict(shared)
        m["xs"] = f(inputs["x_sample"][i])
        m["xp"] = f(inputs["x_prompt"][4 * i:4 * i + 4]).reshape(1024, D)
        m["st_rwkv"] = f(inputs["state_rwkv"][i])
        m["c_ckv"] = f(inputs["cache_mla_ckv"][i])
        m["c_kr"] = f(inputs["cache_mla_krope"][i])
        m["st_lru"] = f(inputs["state_lru"][i])
        m["cvec"] = f(np.stack([np.asarray(inputs["c"][i]), np.asarray(inputs["c_ctx"])]))
        maps.append(m)
    return maps


def kernel(**inputs):
    nc = build()[0]
    cores = list(range(8))
    res = run_bass_kernel_spmd(nc, make_in_maps(inputs, cores), core_ids=cores)
    R = res.results
    y_prompt = np.concatenate([r["y_p"].reshape(4, 256, D) for r in R], 0)
    y_sample = np.stack([r["y_s"] for r in R], 0)
    o_rwkv = np.concatenate([r["o_rwkv"] for r in R], 0)
    o_ckv = np.concatenate([r["o_ckv"] for r in R], 0)
    o_kr = np.concatenate([r["o_kr"] for r in R], 0)
    o_lru = np.concatenate([r["o_lru"] for r in R], 0)
    return (y_prompt.astype(np.float32), y_sample.astype(np.float32), o_rwkv.astype(np.float32),
            o_ckv.astype(np.float32), o_kr.astype(np.float32), o_lru.astype(np.float32))
```
